# Optimizing a Trainium2 kernel written in Bass

```python
import jax, jax.numpy as jnp
from jax import lax
import numpy as np

D_MODEL = 1024
BATCH = 4
SEQ = 8192
DEPTH = 4

GRID_W = 64
CTX_LEN = 256
N_MIXERS = 3
CHUNK = 128
NORM_EPS = 1e-6
RET_HEADS = 4
RET_QK_DIM = 256
RET_V_DIM = 512
RET_QK_WIDTH = RET_HEADS * RET_QK_DIM
RET_WIDTH = RET_HEADS * RET_V_DIM
ROPE_BASE = 10000.0
GM_WIDTH = 2 * D_MODEL
GM_GROUPS = 8
RW_HEAD = 64
RW_WIDTH = D_MODEL
RW_HEADS = RW_WIDTH // RW_HEAD
RW_LORA = 64
RW_LNX_EPS = 64e-5

N_RET = len(range(0, DEPTH, N_MIXERS))
N_GM = len(range(1, DEPTH, N_MIXERS))
N_RW = len(range(2, DEPTH, N_MIXERS))

kernel_name = 'hybrid_retention_gmlp_rwkv7_prefix_dit'

F32 = jnp.float32


def rmsnorm(x, g):
    xf = x.astype(F32)
    y = xf * lax.rsqrt(jnp.mean(xf * xf, axis=-1, keepdims=True) + NORM_EPS)
    return (y * g.astype(F32)).astype(x.dtype)


def to_heads(t, n_heads):
    b, l, _ = t.shape
    return t.reshape(b, l, n_heads, -1).transpose(0, 2, 1, 3).astype(F32)


def axial_rope(t, row, col):
    dk = t.shape[-1]
    n_freq = dk // 4
    freqs = ROPE_BASE ** (-jnp.arange(n_freq, dtype=F32) / n_freq)
    ang = jnp.concatenate([row.astype(F32)[:, None] * freqs, col.astype(F32)[:, None] * freqs], axis=-1)
    cos, sin = jnp.cos(ang), jnp.sin(ang)
    t1, t2 = t[..., :dk // 2], t[..., dk // 2:]
    return jnp.concatenate([t1 * cos - t2 * sin, t1 * sin + t2 * cos], axis=-1)


def retention_chunkwise(q, k, v, log_g, state0):
    b, h, l, _ = q.shape
    n = l // CHUNK

    def blocks(t):
        return jnp.moveaxis(t.reshape(b, h, n, CHUNK, t.shape[-1]), 2, 0)

    idx = jnp.arange(CHUNK, dtype=F32)
    diff = idx[:, None] - idx[None, :]
    intra = jnp.where(diff >= 0, jnp.exp(jnp.maximum(diff, 0.0)[None] * log_g[:, None, None]), 0.0)
    q_decay = jnp.exp((idx + 1.0)[None, :] * log_g[:, None])[:, :, None]
    k_decay = jnp.exp((CHUNK - 1.0 - idx)[None, :] * log_g[:, None])[:, :, None]
    chunk_decay = jnp.exp(CHUNK * log_g)[:, None, None]

    def step(s, blk):
        qb, kb, vb = blk
        scores = jnp.einsum('bhid,bhjd->bhij', qb, kb) * intra
        o = jnp.einsum('bhij,bhjv->bhiv', scores, vb) + jnp.einsum('bhid,bhdv->bhiv', qb * q_decay, s)
        s = s * chunk_decay + jnp.einsum('bhjd,bhjv->bhdv', kb * k_decay, vb)
        return s, o

    s_final, o = lax.scan(step, state0, (blocks(q), blocks(k), blocks(v)))
    o = jnp.moveaxis(o, 0, 2).reshape(b, h, l, v.shape[-1])
    return o, s_final


def retention_bidirectional(q, k, v, log_g, s_fwd, s_bwd):
    o_f, st_f = retention_chunkwise(q, k, v, log_g[0], s_fwd)
    o_b, st_b = retention_chunkwise(jnp.flip(q, 2), jnp.flip(k, 2), jnp.flip(v, 2), log_g[1], s_bwd)
    return o_f + jnp.flip(o_b, 2), st_f, st_b


def retention_project(h, w_in):
    a, bq, cq = RET_QK_WIDTH, 2 * RET_QK_WIDTH, 2 * RET_QK_WIDTH + RET_WIDTH
    q = to_heads(h @ w_in[:, :a], RET_HEADS)
    k = to_heads(h @ w_in[:, a:bq], RET_HEADS) * (RET_QK_DIM ** -0.5)
    v = to_heads(h @ w_in[:, bq:cq], RET_HEADS)
    z = h @ w_in[:, cq:]
    return q, k, v, z


def retention_output(o, z, w_out):
    o = o * lax.rsqrt(jnp.mean(o * o, axis=-1, keepdims=True) + NORM_EPS)
    b, h, l, dv = o.shape
    o = o.transpose(0, 2, 1, 3).reshape(b, l, h * dv).astype(z.dtype)
    return (o * jax.nn.silu(z)) @ w_out


def retention_mixer(hx, hc, w_in, decay_logit, w_out, need_ctx):
    l = hx.shape[1]
    t = jnp.arange(l, dtype=jnp.int32)
    row, col = t // GRID_W, t % GRID_W
    log_g = jax.nn.log_sigmoid(decay_logit.astype(F32))
    qc, kc, vc, zc = retention_project(hc, w_in)
    qx, kx, vx, zx = retention_project(hx, w_in)
    qx, kx = axial_rope(qx, row, col), axial_rope(kx, row, col)
    zero = jnp.zeros((hx.shape[0], RET_HEADS, RET_QK_DIM, RET_V_DIM), F32)
    oc, s_f, s_b = retention_bidirectional(qc, kc, vc, log_g, zero, zero)
    ox, _, _ = retention_bidirectional(qx, kx, vx, log_g, s_f, s_b)
    out_x = retention_output(ox, zx, w_out)
    out_c = retention_output(oc, zc, w_out) if need_ctx else None
    return out_x, out_c


def gmlp_chunk_mixer(h, w_in, vnorm_g, w_s, b_s, w_out):
    u = h @ w_in[:, :GM_WIDTH]
    v = (h @ w_in[:, GM_WIDTH:2 * GM_WIDTH]).astype(F32)
    z = h @ w_in[:, 2 * GM_WIDTH:]
    v = v - jnp.mean(v, axis=-1, keepdims=True)
    v = v * lax.rsqrt(jnp.mean(v * v, axis=-1, keepdims=True) + NORM_EPS) * vnorm_g.astype(F32)
    b, l, w = v.shape
    vg = v.reshape(b, l // CHUNK, CHUNK, GM_GROUPS, w // GM_GROUPS)
    mixed = jnp.einsum('gij,bnjgc->bnigc', w_s.astype(F32), vg) + b_s.astype(F32).T[:, :, None]
    y = u * mixed.reshape(b, l, w).astype(u.dtype)
    return (y * jax.nn.silu(z)) @ w_out


def token_shift_grid(h):
    l, b, d = h.shape
    rows = l // GRID_W
    g = h.reshape(rows, GRID_W, b, d)
    q = d // 4
    left = jnp.pad(g[:, :-1, :, :q], ((0, 0), (1, 0), (0, 0), (0, 0)))
    right = jnp.pad(g[:, 1:, :, q:2 * q], ((0, 0), (0, 1), (0, 0), (0, 0)))
    up = jnp.pad(g[:-1, :, :, 2 * q:3 * q], ((1, 0), (0, 0), (0, 0), (0, 0)))
    down = jnp.pad(g[1:, :, :, 3 * q:], ((0, 1), (0, 0), (0, 0), (0, 0)))
    return jnp.concatenate([left, right, up, down], axis=-1).reshape(l, b, d)


def token_shift_seq(h):
    half = h.shape[-1] // 2
    prev = jnp.pad(h[:-1, :, :half], ((1, 0), (0, 0), (0, 0)))
    nxt = jnp.pad(h[1:, :, half:], ((0, 1), (0, 0), (0, 0)))
    return jnp.concatenate([prev, nxt], axis=-1)


def rwkv_prepare(h, shifted, mu, w_rkvg, w0, w1, w2, a0, a1, a2, k_k, k_a):
    l, b, _ = h.shape
    xx = shifted - h

    def mix(p):
        return h + xx * mu[p]

    def heads(t):
        return t.astype(F32).reshape(l, b, RW_HEADS, RW_HEAD)

    r = heads(mix(0) @ w_rkvg[0])
    k = heads(mix(2) @ w_rkvg[1])
    v = heads(mix(3) @ w_rkvg[2])
    z = mix(5) @ w_rkvg[3]
    xw, xa = mix(1), mix(4)
    kk = k * k_k.astype(F32).reshape(RW_HEADS, RW_HEAD)
    kk = kk / jnp.maximum(jnp.sqrt(jnp.sum(kk * kk, axis=-1, keepdims=True)), 1e-12)
    k_a_h = k_a.astype(F32).reshape(RW_HEADS, RW_HEAD)
    dirs = []
    for d in range(2):
        w_log = -jax.nn.softplus(-(w0[d] + jnp.tanh(xw @ w1[d]) @ w2[d]).astype(F32)) - 0.5
        dec = heads(jnp.exp(-jnp.exp(w_log)))
        a = heads(jax.nn.sigmoid((a0[d] + (xa @ a1[d]) @ a2[d]).astype(F32)))
        dirs.append((dec, a, k * (1.0 + (a - 1.0) * k_a_h)))
    return r, v, kk, z, dirs


def wkv_scan(state0, r, dec, k, v, kk, a, reverse):
    def step(s, inp):
        r_t, w_t, k_t, v_t, kk_t, a_t = inp
        sa = jnp.einsum('bhvk,bhk->bhv', s, -kk_t)
        s = s * w_t[:, :, None, :] + sa[..., None] * (kk_t * a_t)[:, :, None, :] + v_t[..., None] * k_t[:, :, None, :]
        return s, jnp.einsum('bhvk,bhk->bhv', s, r_t)

    s_final, y = lax.scan(step, state0, (r, dec, k, v, kk, a), reverse=reverse)
    return y, s_final


def rwkv_bidirectional(prep, states):
    r, v, kk, _, dirs = prep
    y_f, s_f = wkv_scan(states[0], r, dirs[0][0], dirs[0][2], v, kk, dirs[0][1], False)
    y_b, s_b = wkv_scan(states[1], r, dirs[1][0], dirs[1][2], v, kk, dirs[1][1], True)
    return y_f + y_b, (s_f, s_b)


def rwkv_output(prep, y, r_k, lnx_g, lnx_b, w_out):
    r, v, _, z, dirs = prep
    l, b, h, d = y.shape
    yc = y - jnp.mean(y, axis=-1, keepdims=True)
    yn = yc * lax.rsqrt(jnp.mean(yc * yc, axis=-1, keepdims=True) + RW_LNX_EPS)
    yn = yn.reshape(l, b, h * d) * lnx_g.astype(F32) + lnx_b.astype(F32)
    rk = r_k.astype(F32)
    bonus = (jnp.sum(r * dirs[0][2] * rk, axis=-1, keepdims=True)
             + jnp.sum(r * dirs[1][2] * rk, axis=-1, keepdims=True)) * v
    o = (yn + bonus.reshape(l, b, h * d)).astype(z.dtype) * jax.nn.silu(z)
    return (o @ w_out).transpose(1, 0, 2)


def rwkv_mixer(hx, hc, mu, w_rkvg, w0, w1, w2, a0, a1, a2, k_k, k_a, r_k, lnx_g, lnx_b, w_out, need_ctx):
    hx_t = hx.transpose(1, 0, 2)
    hc_t = hc.transpose(1, 0, 2)
    prep_c = rwkv_prepare(hc_t, token_shift_seq(hc_t), mu, w_rkvg, w0, w1, w2, a0, a1, a2, k_k, k_a)
    prep_x = rwkv_prepare(hx_t, token_shift_grid(hx_t), mu, w_rkvg, w0, w1, w2, a0, a1, a2, k_k, k_a)
    zero = jnp.zeros((hx.shape[0], RW_HEADS, RW_HEAD, RW_HEAD), F32)
    y_c, s_c = rwkv_bidirectional(prep_c, (zero, zero))
    y_x, _ = rwkv_bidirectional(prep_x, s_c)
    out_x = rwkv_output(prep_x, y_x, r_k, lnx_g, lnx_b, w_out)
    out_c = rwkv_output(prep_c, y_c, r_k, lnx_g, lnx_b, w_out) if need_ctx else None
    return out_x, out_c


def setup_inputs(seed: int = 0) -> dict:
    key = jax.random.key(seed)
    ks = iter(jax.random.split(key, 48))
    D = D_MODEL

    def nrm(shape, s):
        return s * jax.random.normal(next(ks), shape, F32)

    x = nrm((BATCH, SEQ, D), 1.0)
    c = nrm((BATCH, D), 1.0)
    ctx = nrm((BATCH, CTX_LEN, D), 1.0)
    c_ctx = nrm((D,), 1.0)
    ada_w = nrm((DEPTH, D, 3 * D), 0.5 * D ** -0.5)
    ada_b = jnp.concatenate([nrm((DEPTH, 2 * D), 0.02), 1.0 + nrm((DEPTH, D), 0.02)], axis=-1)
    norm_g = 1.0 + nrm((DEPTH, D), 0.02)
    final_g = 1.0 + nrm((D,), 0.02)
    ret_w_in = nrm((N_RET, D, 2 * RET_QK_WIDTH + 2 * RET_WIDTH), D ** -0.5)
    gamma = 1.0 - 2.0 ** (-5.0 - jnp.arange(RET_HEADS, dtype=F32))
    ret_decay = (jnp.log(gamma) - jnp.log1p(-gamma)) + nrm((N_RET, 2, RET_HEADS), 0.1)
    ret_w_out = nrm((N_RET, RET_WIDTH, D), RET_WIDTH ** -0.5)
    gm_w_in = nrm((N_GM, D, 3 * GM_WIDTH), D ** -0.5)
    gm_vnorm_g = 1.0 + nrm((N_GM, GM_WIDTH), 0.02)
    gm_w_s = nrm((N_GM, GM_GROUPS, CHUNK, CHUNK), CHUNK ** -0.5)
    gm_b_s = 1.0 + nrm((N_GM, GM_GROUPS, CHUNK), 0.02)
    gm_w_out = nrm((N_GM, GM_WIDTH, D), GM_WIDTH ** -0.5)
    rw_mu = jax.random.uniform(next(ks), (N_RW, 6, D), F32)
    rw_w_rkvg = nrm((N_RW, 4, D, RW_WIDTH), D ** -0.5)
    rw_w0 = jnp.linspace(-6.0, -1.0, RW_WIDTH, dtype=F32) + nrm((N_RW, 2, RW_WIDTH), 0.1)
    rw_w1 = nrm((N_RW, 2, D, RW_LORA), D ** -0.5)
    rw_w2 = nrm((N_RW, 2, RW_LORA, RW_WIDTH), 0.1 * RW_LORA ** -0.5)
    rw_a0 = nrm((N_RW, 2, RW_WIDTH), 0.1)
    rw_a1 = nrm((N_RW, 2, D, RW_LORA), D ** -0.5)
    rw_a2 = nrm((N_RW, 2, RW_LORA, RW_WIDTH), 0.1 * RW_LORA ** -0.5)
    rw_k_k = 0.85 + nrm((N_RW, RW_WIDTH), 0.02)
    rw_k_a = 1.0 + nrm((N_RW, RW_WIDTH), 0.02)
    rw_r_k = nrm((N_RW, RW_HEADS, RW_HEAD), 0.1)
    rw_lnx_g = 1.0 + nrm((N_RW, RW_WIDTH), 0.02)
    rw_lnx_b = nrm((N_RW, RW_WIDTH), 0.02)
    rw_w_out = nrm((N_RW, RW_WIDTH, D), RW_WIDTH ** -0.5)
    return {'x': x, 'c': c, 'ctx': ctx, 'c_ctx': c_ctx,
            'ada_w': ada_w, 'ada_b': ada_b, 'norm_g': norm_g, 'final_g': final_g,
            'ret_w_in': ret_w_in, 'ret_decay': ret_decay, 'ret_w_out': ret_w_out,
            'gm_w_in': gm_w_in, 'gm_vnorm_g': gm_vnorm_g, 'gm_w_s': gm_w_s, 'gm_b_s': gm_b_s, 'gm_w_out': gm_w_out,
            'rw_mu': rw_mu, 'rw_w_rkvg': rw_w_rkvg, 'rw_w0': rw_w0, 'rw_w1': rw_w1, 'rw_w2': rw_w2,
            'rw_a0': rw_a0, 'rw_a1': rw_a1, 'rw_a2': rw_a2, 'rw_k_k': rw_k_k, 'rw_k_a': rw_k_a,
            'rw_r_k': rw_r_k, 'rw_lnx_g': rw_lnx_g, 'rw_lnx_b': rw_lnx_b, 'rw_w_out': rw_w_out}


def reference(x, c, ctx, c_ctx, ada_w, ada_b, norm_g, final_g,
              ret_w_in, ret_decay, ret_w_out,
              gm_w_in, gm_vnorm_g, gm_w_s, gm_b_s, gm_w_out,
              rw_mu, rw_w_rkvg, rw_w0, rw_w1, rw_w2, rw_a0, rw_a1, rw_a2,
              rw_k_k, rw_k_a, rw_r_k, rw_lnx_g, rw_lnx_b, rw_w_out):
    silu_c = jax.nn.silu(c)
    silu_cc = jax.nn.silu(c_ctx)
    for i in range(DEPTH):
        kind, j = i % N_MIXERS, i // N_MIXERS
        need_ctx = i < DEPTH - 1
        shift_x, scale_x, gate_x = jnp.split(silu_c @ ada_w[i] + ada_b[i], 3, axis=-1)
        shift_c, scale_c, gate_c = jnp.split(silu_cc @ ada_w[i] + ada_b[i], 3, axis=-1)
        hx = rmsnorm(x, norm_g[i]) * (1 + scale_x[:, None]) + shift_x[:, None]
        hc = rmsnorm(ctx, norm_g[i]) * (1 + scale_c) + shift_c
        if kind == 0:
            out_x, out_c = retention_mixer(hx, hc, ret_w_in[j], ret_decay[j], ret_w_out[j], need_ctx)
        elif kind == 1:
            out_x = gmlp_chunk_mixer(hx, gm_w_in[j], gm_vnorm_g[j], gm_w_s[j], gm_b_s[j], gm_w_out[j])
            out_c = gmlp_chunk_mixer(hc, gm_w_in[j], gm_vnorm_g[j], gm_w_s[j], gm_b_s[j], gm_w_out[j]) if need_ctx else None
        else:
            out_x, out_c = rwkv_mixer(hx, hc, rw_mu[j], rw_w_rkvg[j], rw_w0[j], rw_w1[j], rw_w2[j],
                                      rw_a0[j], rw_a1[j], rw_a2[j], rw_k_k[j], rw_k_a[j], rw_r_k[j],
                                      rw_lnx_g[j], rw_lnx_b[j], rw_w_out[j], need_ctx)
        x = x + gate_x[:, None] * out_x
        if need_ctx:
            ctx = ctx + gate_c * out_c
    return rmsnorm(x, final_g)
```

```python
from contextlib import ExitStack
import numpy as np
import concourse.bass as bass
import concourse.mybir as mybir
from concourse.bass_utils import run_bass_kernel_spmd

F32 = mybir.dt.float32
BF16 = mybir.dt.bfloat16
AF = mybir.ActivationFunctionType
ALU = mybir.AluOpType
AX = mybir.AxisListType

D = 1024
CH = 128
NCORES = 8
_UID = [0]


class Buf:
    __slots__ = ("name", "last_w", "readers", "excl")

    def __init__(self, name, excl=False):
        self.name = name
        self.last_w = None
        self.readers = []
        self.excl = excl


class Tile:
    def __init__(self, t, name, excl=False):
        self.t = t
        self.b = Buf(name, excl)

    def __getitem__(self, k):
        return self.t[k]


class Rot:
    def __init__(self, tiles):
        self.tiles = tiles
        self.i = 0

    def next(self):
        t = self.tiles[self.i % len(self.tiles)]
        self.slot = self.i % len(self.tiles)
        self.i += 1
        return t


class Op:
    __slots__ = ("eng", "fn", "dma_key", "waits", "signal", "sem", "val", "idx", "prog")


def _b(x):
    return x.b if isinstance(x, Tile) else x


class Prog:
    ENGS = ("pe", "act", "dve", "pool", "sp")

    def __init__(self):
        self.ops = []

    def add(self, eng, fn, reads=(), writes=(), dma_key=None):
        op = Op()
        op.eng, op.fn, op.dma_key = eng, fn, dma_key
        op.signal, op.sem, op.val = False, None, 0
        op.idx, op.prog = len(self.ops), self
        reads = [_b(x) for x in reads]
        writes = [_b(x) for x in writes]
        deps = []
        for b in reads:
            if b.last_w is not None:
                deps.append((b.last_w, True))
            if b.excl:
                deps.extend((r, False) for r in b.readers)
        for b in writes:
            if b.last_w is not None:
                deps.append((b.last_w, False))
            deps.extend((r, False) for r in b.readers)
        need, seen = [], set()
        for d, raw in deps:
            if d.prog is not self:
                continue
            if d.dma_key is None and dma_key is None and d.eng == eng:
                if eng == "pe" or not raw:
                    continue
            if d.idx in seen:
                continue
            seen.add(d.idx)
            d.signal = True
            need.append(d)
        op.waits = need
        for b in reads:
            b.readers.append(op)
        for b in writes:
            b.last_w = op
            b.readers = []
        self.ops.append(op)
        return op

    def pe(self, fn, reads=(), writes=()):
        return self.add("pe", fn, reads, writes)

    def act(self, fn, reads=(), writes=()):
        return self.add("act", fn, reads, writes)

    def dve(self, fn, reads=(), writes=()):
        return self.add("dve", fn, reads, writes)

    def pool(self, fn, reads=(), writes=()):
        return self.add("pool", fn, reads, writes)

    def dma(self, out, in_, key, reads=(), writes=(), q="sp", slow=False):
        if slow:
            return self.add(q, lambda e: e.dma_start(out=out, in_=in_, allow_slow_non_contiguous=True),
                            reads, writes, dma_key=key)
        return self.add(q, lambda e: e.dma_start(out=out, in_=in_), reads, writes, dma_key=key)

    def emit(self, nc, stack):
        def keyof(op):
            return ("dma", op.dma_key) if op.dma_key is not None else ("eng", op.eng)
        last = {}
        for op in self.ops:
            last[keyof(op)] = op
        for op in last.values():
            op.signal = True
        cnt, sems = {}, {}
        for op in self.ops:
            if not op.signal:
                continue
            k = keyof(op)
            cnt[k] = cnt.get(k, 0) + (16 if op.dma_key is not None else 1)
            op.val = cnt[k]
            if k not in sems:
                _UID[0] += 1
                sems[k] = nc.alloc_semaphore(name="s%d" % _UID[0])
            op.sem = sems[k]
        self.n_sems = len(sems)
        per_eng = {e: [] for e in self.ENGS}
        for op in self.ops:
            per_eng[op.eng].append(op)
        finals = [(sems[k], cnt[k]) for k in sems]

        def run(engname, e):
            waited = {}
            for op in per_eng[engname]:
                for d in op.waits:
                    key = id(d.sem)
                    if waited.get(key, 0) >= d.val:
                        continue
                    e.wait_ge(d.sem, d.val)
                    waited[key] = d.val
                ins = op.fn(e)
                if op.signal:
                    ins.then_inc(op.sem, 16 if op.dma_key is not None else 1)
            for s, v in finals:
                if waited.get(id(s), 0) < v:
                    e.wait_ge(s, v)

        with nc.Block() as block:
            block.tensor(lambda e: run("pe", e))
            block.scalar(lambda e: run("act", e))
            block.vector(lambda e: run("dve", e))
            block.gpsimd(lambda e: run("pool", e))
            block.sync(lambda e: run("sp", e))
        nc.clear_and_free_semaphores(list(sems.values()))
        nc.all_engine_barrier()


class Phase:
    def __init__(self, nc):
        self.nc = nc
        self.st = ExitStack()
        self.P = Prog()

    def sb(self, name, shape, dt):
        _UID[0] += 1
        t = self.st.enter_context(self.nc.sbuf_tensor("%s_%d" % (name, _UID[0]), list(shape), dt))
        return Tile(t, name)

    def ps(self, name, shape, dt=F32):
        _UID[0] += 1
        t = self.st.enter_context(self.nc.psum_tensor("%s_%d" % (name, _UID[0]), list(shape), dt))
        return Tile(t, name, excl=True)

    def rot(self, name, shape, dt, n):
        return Rot([self.sb("%s%d" % (name, i), shape, dt) for i in range(n)])

    def finish(self):
        self.P.emit(self.nc, self.st)
        self.st.close()


KINDS = (0, 1, 2, 0)
NORM_EPS = 1e-6


class Model:
    def __init__(self, L, CL=256, layers=(0, 1, 2, 3), final_norm=True):
        self.L, self.CL = L, CL
        self.NX, self.NC = L // CH, CL // CH
        self.layers = tuple(layers)
        self.final_norm = final_norm
        self.nc = nc = bass.Bass("TRN2", target_bir_lowering=False)
        self.inputs = {}
        self.outer = ExitStack()
        NT = L + CL

        def din(name, shape):
            self.inputs[name] = tuple(shape)
            return nc.dram_tensor(name, list(shape), F32, kind="ExternalInput").ap()

        self.x_in = din("x", [L, D])
        self.c_in = din("ctx", [CL, D])
        self.cc = din("cc", [128, 8, 2])
        self.ident_in = din("ident", [128, 128])
        self.rope_in = din("rope", [NT, 2, 128])
        self.dmat_in = din("dmat", [128, 6, 128])
        self.jcol_in = din("jcol", [128, 2])
        self.sel_in = din("sel2", [2, 2, 128])
        self.fg_in = din("final_g_bc", [128, D])
        self.lw = {}
        for i in self.layers:
            w = {}
            w["ada_w"] = din("ada_w%d" % i, [D, 3 * D])
            w["ada_bcol"] = din("ada_bcol%d" % i, [128, 24])
            w["ada_brow"] = din("ada_brow%d" % i, [2, D])
            w["ng_col"] = din("ng_col%d" % i, [128, 8])
            k = KINDS[i]
            if k == 0:
                w["w_in"] = din("ret_w_in%d" % i, [D, 6144])
                w["w_out"] = din("ret_w_out%d" % i, [2048, D])
                w["decay"] = din("ret_decay%d" % i, [128, 8])
            elif k == 1:
                w["w_in"] = din("gm_w_in%d" % i, [D, 6144])
                w["w_out"] = din("gm_w_out%d" % i, [2048, D])
                w["vg"] = din("gm_vg%d" % i, [128, 2048])
                w["wsT"] = din("gm_wsT%d" % i, [128, 8, 128])
                w["bsT"] = din("gm_bsT%d" % i, [128, 8])
            else:
                self._rw_inputs(w, i, din)
            self.lw[i] = w
        self.out = nc.dram_tensor("out", [L, D], F32, kind="ExternalOutput").ap()
        self.xs = [nc.dram_tensor("xs%d" % i, [NT, D], F32).ap() for i in range(2)]
        o = self.outer
        self.ident = Tile(o.enter_context(nc.sbuf_tensor("identb", [128, 128], BF16)), "ident")
        self.ident32 = Tile(o.enter_context(nc.sbuf_tensor("ident32", [128, 128], F32)), "ident32")
        self.mod = Tile(o.enter_context(nc.sbuf_tensor("mod", [128, 2, 2, 8], F32)), "mod")
        self.gate_dram = nc.dram_tensor("gate_dram", [2, 128, D], F32).ap()
        self.sel = Tile(o.enter_context(nc.sbuf_tensor("sel", [2, 2, 128], F32)), "sel")
        self._build()
        self.outer.close()

    def chunks_fwd(self):
        return [("c", i) for i in range(self.NC)] + [("x", i) for i in range(self.NX)]

    def chunks_bwd(self):
        return [("c", i) for i in reversed(range(self.NC))] + [("x", i) for i in reversed(range(self.NX))]

    def row0(self, ck):
        return ck[1] * CH if ck[0] == "c" else self.CL + ck[1] * CH

    def src_ap(self, li, ck):
        r0 = self.row0(ck)
        if li == 0:
            return (self.c_in if ck[0] == "c" else self.x_in)[ck[1] * CH:(ck[1] + 1) * CH, :]
        return self.xs[(li - 1) % 2][r0:r0 + CH, :]

    def dst_ap(self, li, ck):
        r0 = self.row0(ck)
        return self.xs[li % 2][r0:r0 + CH, :]

    def _build(self):
        nc = self.nc
        ph = Phase(nc)
        P = ph.P
        t32 = ph.sb("id32", [128, 128], F32)
        P.dma(self.ident32[:], self.ident_in[:, :], "c0", writes=[self.ident32])
        P.dve(lambda e: e.tensor_copy(out=self.ident[:], in_=self.ident32[:]), [self.ident32], [self.ident])
        P.dma(self.sel[:], self.sel_in[:, :, :], "c1", writes=[self.sel])
        ph.finish()
        for li, i in enumerate(self.layers):
            last = (li == len(self.layers) - 1)
            k = KINDS[i]
            if k == 0:
                self.retention_layer(li, i, last)
            elif k == 1:
                self.gmlp_layer(li, i, last)
            else:
                self.rwkv_layer(li, i, last)

    def emit_mod(self, ph, i):
        P, w = ph.P, self.lw[i]
        cc = ph.sb("cc", [128, 8, 2], F32)
        sg = ph.sb("sg", [128, 8, 2], F32)
        bcol = ph.sb("bcol", [128, 24], F32)
        brow = ph.sb("brow", [2, D], F32)
        ng = ph.sb("ng", [128, 8], F32)
        grow = ph.sb("grow", [2, D], F32)
        gate_t = [ph.sb("gate_t%d" % j, [128, D], F32) for j in range(2)]
        aw = ph.sb("adaw", [128, 8, 3 * D], F32)
        awk = [Tile(aw.t[:, k, :], "adaw%d" % k) for k in range(8)]
        pcol = ph.ps("pcol", [128, 16, 2], F32)
        prow = [ph.ps("prow%d" % n, [128, 512], F32) for n in range(2)]
        P.dma(cc[:], self.cc[:, :, :], "m0", writes=[cc])
        P.dma(bcol[:], w["ada_bcol"][:, :], "m1", writes=[bcol])
        P.dma(brow[:], w["ada_brow"][:, :], "m2", writes=[brow])
        P.dma(ng[:], w["ng_col"][:, :], "m3", writes=[ng])
        for k in range(8):
            P.dma(awk[k][:], w["ada_w"][k * 128:(k + 1) * 128, :], "ada%d" % k, writes=[awk[k]], q=("sp" if k % 2 == 0 else "pool"))
        P.act(lambda e: e.activation(out=sg[:], in_=cc[:], func=AF.Sigmoid), [cc], [sg])
        P.dve(lambda e: e.tensor_tensor(out=sg[:], in0=sg[:], in1=cc[:], op=ALU.mult), [sg, cc], [sg])
        for j in range(16):
            for k in range(8):
                P.pe(lambda e, j=j, k=k: e.matmul(pcol[:, j, :], lhsT=awk[k][:, j * 128:(j + 1) * 128], rhs=sg[:, k, :],
                                                  start=(k == 0), stop=(k == 7)), [awk[k], sg], [pcol])
        for n in range(2):
            for k in range(8):
                P.pe(lambda e, n=n, k=k: e.matmul(prow[n][0:2, :], lhsT=sg[:, k, :], rhs=awk[k][:, 2048 + n * 512:2048 + (n + 1) * 512],
                                                  start=(k == 0), stop=(k == 7)), [awk[k], sg], [prow[n]])
        tmp = ph.sb("modtmp", [128, 16, 2], F32)
        P.dve(lambda e: e.tensor_tensor(out=tmp[:], in0=pcol[:], in1=bcol[:, 0:16].unsqueeze(2).to_broadcast([128, 16, 2]), op=ALU.add),
              [pcol, bcol], [tmp])
        mod = self.mod
        for j in range(2):
            P.dve(lambda e, j=j: e.scalar_tensor_tensor(out=mod[:, 0, j, :], in0=tmp[:, 8:16, j], scalar=1.0, in1=ng[:], op0=ALU.add, op1=ALU.mult),
                  [tmp, ng], [mod])
            P.dve(lambda e, j=j: e.tensor_copy(out=mod[:, 1, j, :], in_=tmp[:, 0:8, j]), [tmp], [mod])
        for n in range(2):
            P.dve(lambda e, n=n: e.tensor_tensor(out=grow[:, n * 512:(n + 1) * 512], in0=prow[n][0:2, :], in1=brow[:, n * 512:(n + 1) * 512], op=ALU.add),
                  [prow[n], brow], [grow])
        for j in range(2):
            for n in range(2):
                P.pe(lambda e, j=j, n=n: e.matmul(prow[n][:, :], lhsT=self.sel[:, j, :], rhs=grow[:, n * 512:(n + 1) * 512], start=True, stop=True),
                     [self.sel, grow], [prow[n]])
                P.act(lambda e, j=j, n=n: e.activation(out=gate_t[j][:, n * 512:(n + 1) * 512], in_=prow[n][:, :], func=AF.Copy),
                      [prow[n]], [gate_t[j]])
        for j in range(2):
            P.dma(self.gate_dram[j], gate_t[j][:], "gst%d" % j, reads=[gate_t[j]], q="pool")

    def load_w(self, ph, Wt, src, col0, ncols, KT):
        P = ph.P
        views = []
        for k in range(KT):
            v = Tile(Wt.t[:, k, :], "%s_k%d" % (Wt.b.name, k))
            views.append(v)
            step = 2048
            for c0 in range(0, ncols, step):
                wdt = min(step, ncols - c0)
                P.dma(v.t[:, c0:c0 + wdt], src[k * 128:(k + 1) * 128, col0 + c0:col0 + c0 + wdt],
                      "w%s%d_%d" % (Wt.b.name, k, c0), writes=[v], q="pool")
        return views

    def front(self, ph, R, src, which, src_reads=()):
        P = ph.P
        mod = self.mod
        xt = R["xt"].next()
        P.dma(xt[:], src, "xt%d" % R["xt"].slot, reads=list(src_reads), writes=[xt])
        st = R["st"].next()
        junk = R["junk"]
        P.act(lambda e: e.activation(out=junk[:], in_=xt[:], func=AF.Square, accum_out=st[:, 0:1]), [xt], [junk, st])
        P.act(lambda e: e.activation(out=st[:, 1:2], in_=st[:, 0:1], func=AF.Sqrt, scale=1.0 / D, bias=NORM_EPS), [st], [st])
        P.dve(lambda e: e.reciprocal(out=st[:, 2:3], in_=st[:, 1:2]), [st], [st])
        xn = R["xn"].next()
        P.dve(lambda e: e.tensor_scalar(out=xn[:], in0=xt[:], scalar1=st[:, 2:3], scalar2=None, op0=ALU.mult), [xt, st], [xn])
        ptr = R["ptrx"]
        for k in range(8):
            P.pe(lambda e, k=k: e.transpose(out=ptr[:, k, :], in_=xn[:, k * 128:(k + 1) * 128], identity=self.ident[:]),
                 [xn, self.ident], [ptr])
        hT = R["hT"].next()
        for k in range(8):
            P.act(lambda e, k=k: e.activation(out=hT[:, k, :], in_=ptr[:, k, :], func=AF.Identity,
                                              scale=mod[:, 0, which, k:k + 1], bias=mod[:, 1, which, k:k + 1]),
                  [ptr, mod], [hT])
        return hT, xt

    def front_bufs(self, ph, n_xn=2, n_xt=2, n_hT=2, junk=True):
        return {
            "xt": ph.rot("xt", [128, D], F32, n_xt),
            "st": ph.rot("st", [128, 4], F32, 4),
            "junk": ph.sb("junk", [128, D], BF16) if junk else None,
            "xn": ph.rot("xn", [128, D], BF16, n_xn),
            "hT": ph.rot("hT", [128, 8, 128], BF16, n_hT),
            "ptrx": ph.ps("ptrx", [128, 8, 128], BF16),
        }

    def tail(self, ph, R, gated, Wo, li, ck, xres, last, KT=16):
        P = ph.P
        which = 1 if ck[0] == "c" else 0
        gT = R["gT"].next()
        for half in range(KT // 8):
            ptg = R["ptg"][half]
            for k in range(8):
                kk = half * 8 + k
                P.pe(lambda e, k=k, kk=kk, ptg=ptg: e.transpose(out=ptg[:, k, :], in_=gated[:, kk * 128:(kk + 1) * 128], identity=self.ident[:]),
                     [gated, self.ident], [ptg])
            if half == 0:
                P.act(lambda e, half=half, ptg=ptg: e.activation(out=gT[:, half * 8:(half + 1) * 8, :], in_=ptg[:], func=AF.Copy), [ptg], [gT])
            else:
                P.dve(lambda e, half=half, ptg=ptg: e.tensor_copy(out=gT[:, half * 8:(half + 1) * 8, :], in_=ptg[:]), [ptg], [gT])
        xo = R["xo"].next()
        for n in range(2):
            po = R["pout"]
            for k in range(KT):
                P.pe(lambda e, n=n, k=k: e.matmul(po[:], lhsT=gT[:, k, :], rhs=Wo[k][:, n * 512:(n + 1) * 512], start=(k == 0), stop=(k == KT - 1)),
                     [gT, Wo[k]], [po])
            P.dve(lambda e, n=n: e.tensor_tensor(out=xo[:, n * 512:(n + 1) * 512], in0=po[:], in1=self.gate[which][:, n * 512:(n + 1) * 512], op=ALU.mult),
                  [po, self.gate[which]], [xo])
        P.pool(lambda e: e.tensor_tensor(out=xo[:], in0=xo[:], in1=xres[:], op=ALU.add), [xo, xres], [xo])
        if last and self.final_norm:
            st = R["st"].next()
            junk = R["junk"]
            P.act(lambda e: e.activation(out=junk[:], in_=xo[:], func=AF.Square, accum_out=st[:, 0:1]), [xo], [junk, st])
            P.act(lambda e: e.activation(out=st[:, 1:2], in_=st[:, 0:1], func=AF.Sqrt, scale=1.0 / D, bias=NORM_EPS), [st], [st])
            P.dve(lambda e: e.reciprocal(out=st[:, 2:3], in_=st[:, 1:2]), [st], [st])
            P.dve(lambda e: e.scalar_tensor_tensor(out=xo[:], in0=xo[:], scalar=st[:, 2:3], in1=self.fg[:], op0=ALU.mult, op1=ALU.mult),
                  [xo, st, self.fg], [xo])
            dst = self.out[ck[1] * CH:(ck[1] + 1) * CH, :]
        elif last:
            dst = self.out[ck[1] * CH:(ck[1] + 1) * CH, :]
        else:
            dst = self.dst_ap(li, ck)
        P.dma(dst, xo[:], "xo%d" % R["xo"].slot, reads=[xo], q="pool")

    def tail_bufs(self, ph, n_gT=2, n_xo=2, last=False):
        if last and self.final_norm:
            self.fg = ph.sb("fg", [128, D], F32)
            ph.P.dma(self.fg[:], self.fg_in[:, :], "fgld", writes=[self.fg])
        self.gate = [ph.sb("gate%d" % j, [128, D], F32) for j in range(2)]
        for j in range(2):
            ph.P.dma(self.gate[j][:], self.gate_dram[j], "gld%d" % j, writes=[self.gate[j]])
        return {
            "gT": ph.rot("gT", [128, 16, 128], BF16, n_gT),
            "ptg": [ph.ps("ptg%d" % h, [128, 8, 128], BF16) for h in range(2)],
            "pout": ph.ps("pout", [128, 512], F32),
            "xo": ph.rot("xo", [128, D], F32, n_xo),
        }

    def retention_layer(self, li, i, last):
        nc, w = self.nc, self.lw[i]
        NCH = self.NX + self.NC
        if not hasattr(self, "scr_q"):
            self.scr_q = nc.dram_tensor("scr_q", [NCH, 128, 1024], BF16).ap()
            self.scr_k = nc.dram_tensor("scr_k", [NCH, 128, 1024], BF16).ap()
            self.scr_v = nc.dram_tensor("scr_v", [NCH, 128, 2048], BF16).ap()
            self.scr_o = nc.dram_tensor("scr_o", [NCH, 128, 2048], F32).ap()
            self.rt_cd = Tile(self.outer.enter_context(nc.sbuf_tensor("rt_cd", [128, 8], F32)), "rt_cd")
        ph = Phase(nc)
        self.emit_mod(ph, i)
        ph.finish()

        ph = Phase(nc)
        P = ph.P
        Wt = ph.sb("Wqkv", [128, 8, 4096], BF16)
        Wk = self.load_w(ph, Wt, w["w_in"], 0, 4096, 8)
        R = self.front_bufs(ph)
        dec = ph.sb("dec", [128, 8], F32)
        lg = ph.sb("lg", [128, 8], F32)
        dm = ph.sb("dm", [128, 6, 128], F32)
        jc = ph.sb("jc", [128, 2], F32)
        tA = ph.sb("tA", [128, 128], F32)
        tB = ph.sb("tB", [128, 128], F32)
        MT = ph.sb("MT", [128, 4, 128], F32)
        QDf = ph.sb("QDf", [128, 8, 128], BF16)
        QDb = ph.sb("QDb", [128, 8, 128], BF16)
        kdec = ph.sb("kdec", [128, 8], F32)
        cd = self.rt_cd
        P.dma(dec[:], w["decay"][:, :], "t0", writes=[dec])
        P.dma(dm[:], self.dmat_in[:, :, :], "t1", writes=[dm])
        P.dma(jc[:], self.jcol_in[:, :], "t2", writes=[jc])
        P.act(lambda e: e.activation(out=lg[:], in_=dec[:], func=AF.Exp, scale=-1.0), [dec], [lg])
        P.act(lambda e: e.activation(out=lg[:], in_=lg[:], func=AF.Ln, bias=1.0), [lg], [lg])
        P.dve(lambda e: e.tensor_scalar(out=lg[:], in0=lg[:], scalar1=-1.0, scalar2=None, op0=ALU.mult), [lg], [lg])
        P.act(lambda e: e.activation(out=cd[:], in_=lg[:], func=AF.Exp, scale=128.0), [lg], [cd])
        P.dve(lambda e: e.tensor_scalar(out=kdec[:, 0:4], in0=lg[:, 0:4], scalar1=jc[:, 0:1], scalar2=None, op0=ALU.mult), [lg, jc], [kdec])
        P.dve(lambda e: e.tensor_scalar(out=kdec[:, 4:8], in0=lg[:, 4:8], scalar1=jc[:, 1:2], scalar2=None, op0=ALU.mult), [lg, jc], [kdec])
        P.act(lambda e: e.activation(out=kdec[:], in_=kdec[:], func=AF.Exp), [kdec], [kdec])
        P.dve(lambda e: e.tensor_scalar(out=kdec[:], in0=kdec[:], scalar1=0.0625, scalar2=None, op0=ALU.mult), [kdec], [kdec])
        for h in range(4):
            P.act(lambda e, h=h: e.activation(out=tA[:], in_=dm[:, 0, :], func=AF.Exp, scale=lg[:, h:h + 1]), [dm, lg, MT], [tA])
            P.act(lambda e, h=h: e.activation(out=tB[:], in_=dm[:, 1, :], func=AF.Exp, scale=lg[:, 4 + h:5 + h]), [dm, lg, MT], [tB])
            P.dve(lambda e: e.tensor_tensor(out=tA[:], in0=tA[:], in1=dm[:, 2, :], op=ALU.mult), [tA, dm], [tA])
            P.dve(lambda e: e.tensor_tensor(out=tB[:], in0=tB[:], in1=dm[:, 3, :], op=ALU.mult), [tB, dm], [tB])
            P.dve(lambda e: e.tensor_tensor(out=tA[:], in0=tA[:], in1=tB[:], op=ALU.add), [tA, tB], [tA])
            P.dve(lambda e, h=h: e.tensor_scalar(out=MT[:, h, :], in0=tA[:], scalar1=0.0625, scalar2=None, op0=ALU.mult), [tA], [MT])
            for r in range(2):
                P.act(lambda e, h=h, r=r: e.activation(out=QDf[:, 2 * h + r, :], in_=dm[:, 4, :], func=AF.Exp, scale=lg[:, h:h + 1]), [dm, lg], [QDf])
                P.act(lambda e, h=h, r=r: e.activation(out=QDb[:, 2 * h + r, :], in_=dm[:, 5, :], func=AF.Exp, scale=lg[:, 4 + h:5 + h]), [dm, lg], [QDb])
        mm = Rot([ph.ps("mm%d" % j, [128, 512], F32) for j in range(2)])
        ptq = ph.ps("ptq", [128, 8, 128], BF16)
        ptk = ph.ps("ptk", [128, 8, 128], BF16)
        psc = ph.ps("psc", [128, 4, 128], F32)
        po = ph.ps("po", [128, 512], F32)
        pst = ph.ps("pst", [128, 512], F32)
        cs = ph.rot("cs", [128, 2, 2, 128], F32, 2)
        rtmp = ph.rot("rtmp", [128, 4, 2, 128], F32, 2)
        q_r = ph.sb("q_r", [128, 1024], BF16)
        k_r = ph.sb("k_r", [128, 1024], BF16)
        qT = ph.rot("qT", [128, 8, 128], BF16, 2)
        kT = ph.rot("kT", [128, 8, 128], BF16, 2)
        qTf = ph.rot("qTf", [128, 8, 128], BF16, 2)
        qTb = ph.rot("qTb", [128, 8, 128], BF16, 2)
        kf = ph.rot("kf", [128, 1024], BF16, 2)
        kb = ph.rot("kb", [128, 1024], BF16, 2)
        vsb = ph.rot("vsb", [128, 2048], BF16, 2)
        sT = ph.rot("sT", [128, 4, 128], BF16, 2)
        osb = ph.rot("osb", [128, 2048], F32, 2)
        S = ph.sb("S", [128, 8, 512], F32)
        Sbf = ph.sb("Sbf", [128, 8, 512], BF16)
        Sk = [Tile(S.t[:, k, :], "S%d" % k) for k in range(8)]
        Sbk = [Tile(Sbf.t[:, k, :], "Sb%d" % k) for k in range(8)]
        for k in range(8):
            P.dve(lambda e, k=k: e.memset(Sk[k][:], 0.0), [], [Sk[k]])
            P.pool(lambda e, k=k: e.memset(Sbk[k][:], 0.0), [], [Sbk[k]])

        for ck in self.chunks_fwd():
            which = 1 if ck[0] == "c" else 0
            r0 = self.row0(ck)
            cidx = r0 // CH
            hT, xt = self.front(ph, R, self.src_ap(li, ck), which)
            c_ = cs.next()
            for r in range(2):
                P.dma(c_[:, :, r, :], self.rope_in[r0:r0 + CH, :, :], "cs%d_%d" % (cs.slot, r), writes=[c_])
            v_ = vsb.next()
            for n in range(8):
                bank = mm.next()
                for k in range(8):
                    P.pe(lambda e, bank=bank, n=n, k=k, hT=hT: e.matmul(bank[:], lhsT=hT[:, k, :], rhs=Wk[k][:, n * 512:(n + 1) * 512],
                                                                        start=(k == 0), stop=(k == 7)), [hT, Wk[k]], [bank])
                if n < 4:
                    dst = q_r if n < 2 else k_r
                    tmp = rtmp.next()
                    pb = bank[:].rearrange("p (h t c) -> p h t c", h=2, t=2)
                    t1, t2 = pb[:, :, 0, :], pb[:, :, 1, :]
                    cos2, sin2 = c_[:, 0, :, :], c_[:, 1, :, :]
                    P.dve(lambda e, tmp=tmp, t1=t1, cos2=cos2: e.tensor_tensor(out=tmp[:, 0], in0=t1, in1=cos2, op=ALU.mult), [bank, c_], [tmp])
                    P.dve(lambda e, tmp=tmp, t2=t2, sin2=sin2: e.tensor_tensor(out=tmp[:, 1], in0=t2, in1=sin2, op=ALU.mult), [bank, c_], [tmp])
                    P.dve(lambda e, tmp=tmp, t1=t1, sin2=sin2: e.tensor_tensor(out=tmp[:, 2], in0=t1, in1=sin2, op=ALU.mult), [bank, c_], [tmp])
                    P.dve(lambda e, tmp=tmp, t2=t2, cos2=cos2: e.tensor_tensor(out=tmp[:, 3], in0=t2, in1=cos2, op=ALU.mult), [bank, c_], [tmp])
                    dv = dst[:].rearrange("p (h t c) -> p h t c", h=4, t=2)
                    h0 = 2 * (n % 2)
                    P.pool(lambda e, tmp=tmp, dv=dv, h0=h0: e.tensor_tensor(out=dv[:, h0:h0 + 2, 0, :], in0=tmp[:, 0], in1=tmp[:, 1], op=ALU.subtract),
                           [tmp], [dst])
                    P.pool(lambda e, tmp=tmp, dv=dv, h0=h0: e.tensor_tensor(out=dv[:, h0:h0 + 2, 1, :], in0=tmp[:, 2], in1=tmp[:, 3], op=ALU.add),
                           [tmp], [dst])
                else:
                    P.act(lambda e, bank=bank, n=n, v_=v_: e.activation(out=v_[:, (n - 4) * 512:(n - 3) * 512], in_=bank[:], func=AF.Copy), [bank], [v_])
            qT_, kT_, qTf_, qTb_ = qT.next(), kT.next(), qTf.next(), qTb.next()
            for k in range(8):
                P.pe(lambda e, k=k: e.transpose(out=ptq[:, k, :], in_=q_r[:, k * 128:(k + 1) * 128], identity=self.ident[:]), [q_r, self.ident], [ptq])
            P.act(lambda e, qT_=qT_: e.activation(out=qT_[:], in_=ptq[:], func=AF.Copy), [ptq], [qT_])
            for k in range(8):
                P.pe(lambda e, k=k: e.transpose(out=ptk[:, k, :], in_=k_r[:, k * 128:(k + 1) * 128], identity=self.ident[:]), [k_r, self.ident], [ptk])
            P.act(lambda e, kT_=kT_: e.activation(out=kT_[:], in_=ptk[:], func=AF.Copy), [ptk], [kT_])
            P.pool(lambda e, qT_=qT_, qTf_=qTf_: e.tensor_tensor(out=qTf_[:], in0=qT_[:], in1=QDf[:], op=ALU.mult), [qT_, QDf], [qTf_])
            P.pool(lambda e, qT_=qT_, qTb_=qTb_: e.tensor_tensor(out=qTb_[:], in0=qT_[:], in1=QDb[:], op=ALU.mult), [qT_, QDb], [qTb_])
            kf_, kb_ = kf.next(), kb.next()
            krv = k_r[:].rearrange("p (h c) -> p h c", h=4)
            P.pool(lambda e, kf_=kf_: e.tensor_tensor(out=kf_[:].rearrange("p (h c) -> p h c", h=4), in0=krv,
                                                      in1=kdec[:, 0:4].unsqueeze(2).to_broadcast([128, 4, 256]), op=ALU.mult), [k_r, kdec], [kf_])
            P.pool(lambda e, kb_=kb_: e.tensor_tensor(out=kb_[:].rearrange("p (h c) -> p h c", h=4), in0=krv,
                                                      in1=kdec[:, 4:8].unsqueeze(2).to_broadcast([128, 4, 256]), op=ALU.mult), [k_r, kdec], [kb_])
            for h in range(4):
                for hf in range(2):
                    P.pe(lambda e, h=h, hf=hf, kT_=kT_, qT_=qT_: e.matmul(psc[:, h, :], lhsT=kT_[:, 2 * h + hf, :], rhs=qT_[:, 2 * h + hf, :],
                                                                          start=(hf == 0), stop=(hf == 1)), [kT_, qT_], [psc])
            sT_ = sT.next()
            P.dve(lambda e, sT_=sT_: e.tensor_tensor(out=sT_[:], in0=psc[:], in1=MT[:], op=ALU.mult), [psc, MT], [sT_])
            o_ = osb.next()
            for h in range(4):
                P.pe(lambda e, h=h, sT_=sT_, v_=v_: e.matmul(po[:], lhsT=sT_[:, h, :], rhs=v_[:, h * 512:(h + 1) * 512], start=True, stop=False), [sT_, v_], [po])
                for hf in range(2):
                    kt = 2 * h + hf
                    P.pe(lambda e, kt=kt, hf=hf, qTf_=qTf_: e.matmul(po[:], lhsT=qTf_[:, kt, :], rhs=Sbk[kt][:], start=False, stop=(hf == 1)),
                         [qTf_, Sbk[kt]], [po])
                P.act(lambda e, h=h, o_=o_: e.activation(out=o_[:, h * 512:(h + 1) * 512], in_=po[:], func=AF.Copy), [po], [o_])
            self.ret_state_update(P, pst, kf_, v_, Sk, Sbk, cd, 0)
            P.dma(self.scr_q[cidx].rearrange("p (k t) -> p k t", k=8), qTb_[:], "sq%d" % qTb.slot, reads=[qTb_], q="pool")
            P.dma(self.scr_k[cidx], kb_[:], "sk%d" % kb.slot, reads=[kb_], q="pool")
            P.dma(self.scr_v[cidx], v_[:], "sv%d" % vsb.slot, reads=[v_], q="pool")
            P.dma(self.scr_o[cidx], o_[:], "so%d" % osb.slot, reads=[o_], q="pool")
        ph.finish()

        ph = Phase(nc)
        P = ph.P
        Wzt = ph.sb("Wz", [128, 8, 2048], BF16)
        Wz = self.load_w(ph, Wzt, w["w_in"], 4096, 2048, 8)
        Wot = ph.sb("Wo", [128, 16, 1024], BF16)
        Wo = self.load_w(ph, Wot, w["w_out"], 0, 1024, 16)
        R = self.front_bufs(ph)
        R.update(self.tail_bufs(ph, last=last))
        mm = Rot([ph.ps("mm%d" % j, [128, 512], F32) for j in range(2)])
        po = ph.ps("po", [128, 512], F32)
        pst = ph.ps("pst", [128, 512], F32)
        qTb = ph.rot("qTb", [128, 8, 128], BF16, 2)
        kb = ph.rot("kb", [128, 1024], BF16, 2)
        vsb = ph.rot("vsb", [128, 2048], BF16, 2)
        osb = ph.rot("osb", [128, 2048], F32, 2)
        sz = ph.rot("sz", [128, 2048], F32, 2)
        gated = ph.rot("gated", [128, 2048], BF16, 2)
        st4 = ph.rot("st4", [128, 12], F32, 2)
        S = ph.sb("S", [128, 8, 512], F32)
        Sbf = ph.sb("Sbf", [128, 8, 512], BF16)
        Sk = [Tile(S.t[:, k, :], "S%d" % k) for k in range(8)]
        Sbk = [Tile(Sbf.t[:, k, :], "Sb%d" % k) for k in range(8)]
        for k in range(8):
            P.dve(lambda e, k=k: e.memset(Sk[k][:], 0.0), [], [Sk[k]])
            P.pool(lambda e, k=k: e.memset(Sbk[k][:], 0.0), [], [Sbk[k]])
        cd = self.rt_cd
        for ck in self.chunks_bwd():
            which = 1 if ck[0] == "c" else 0
            cidx = self.row0(ck) // CH
            need_out = not (last and ck[0] == "c")
            kb_, v_ = kb.next(), vsb.next()
            P.dma(kb_[:], self.scr_k[cidx], "lk%d" % kb.slot, writes=[kb_])
            P.dma(v_[:], self.scr_v[cidx], "lv%d" % vsb.slot, writes=[v_])
            if need_out:
                q_, o_ = qTb.next(), osb.next()
                P.dma(q_[:], self.scr_q[cidx].rearrange("p (k t) -> p k t", k=8), "lq%d" % qTb.slot, writes=[q_])
                P.dma(o_[:], self.scr_o[cidx], "lo%d" % osb.slot, writes=[o_])
                hT, xt = self.front(ph, R, self.src_ap(li, ck), which)
                sz_ = sz.next()
                for n in range(4):
                    bank = mm.next()
                    for k in range(8):
                        P.pe(lambda e, bank=bank, n=n, k=k, hT=hT: e.matmul(bank[:], lhsT=hT[:, k, :], rhs=Wz[k][:, n * 512:(n + 1) * 512],
                                                                            start=(k == 0), stop=(k == 7)), [hT, Wz[k]], [bank])
                    P.act(lambda e, bank=bank, n=n, sz_=sz_: e.activation(out=sz_[:, n * 512:(n + 1) * 512], in_=bank[:], func=AF.Silu), [bank], [sz_])
                s4 = st4.next()
                junk = R["junk"]
                for h in range(4):
                    for hf in range(2):
                        kt = 2 * h + hf
                        P.pe(lambda e, kt=kt, hf=hf, q_=q_: e.matmul(po[:], lhsT=q_[:, kt, :], rhs=Sbk[kt][:], start=(hf == 0), stop=(hf == 1)),
                             [q_, Sbk[kt]], [po])
                    P.dve(lambda e, h=h, o_=o_: e.tensor_tensor(out=o_[:, h * 512:(h + 1) * 512], in0=po[:], in1=o_[:, h * 512:(h + 1) * 512], op=ALU.add),
                          [po, o_], [o_])
                    P.act(lambda e, h=h, o_=o_, s4=s4: e.activation(out=junk[:, 0:512], in_=o_[:, h * 512:(h + 1) * 512], func=AF.Square, accum_out=s4[:, h:h + 1]),
                          [o_], [junk, s4])
                P.act(lambda e, s4=s4: e.activation(out=s4[:, 4:8], in_=s4[:, 0:4], func=AF.Sqrt, scale=1.0 / 512, bias=NORM_EPS), [s4], [s4])
                P.dve(lambda e, s4=s4: e.reciprocal(out=s4[:, 8:12], in_=s4[:, 4:8]), [s4], [s4])
                g_ = gated.next()
                for h in range(4):
                    P.dve(lambda e, h=h, o_=o_, s4=s4, sz_=sz_, g_=g_: e.scalar_tensor_tensor(
                        out=g_[:, h * 512:(h + 1) * 512], in0=o_[:, h * 512:(h + 1) * 512], scalar=s4[:, 8 + h:9 + h],
                        in1=sz_[:, h * 512:(h + 1) * 512], op0=ALU.mult, op1=ALU.mult), [o_, s4, sz_], [g_])
                self.tail(ph, R, g_, Wo, li, ck, xt, last)
            self.ret_state_update(P, pst, kb_, v_, Sk, Sbk, cd, 4)
        ph.finish()

    def ret_state_update(self, P, pst, kd, v_, Sk, Sbk, cd, c0):
        for kt in range(8):
            h = kt // 2
            P.pe(lambda e, kt=kt, h=h: e.matmul(pst[:], lhsT=kd[:, kt * 128:(kt + 1) * 128], rhs=v_[:, h * 512:(h + 1) * 512], start=True, stop=True),
                 [kd, v_], [pst])
            P.dve(lambda e, kt=kt, h=h: e.scalar_tensor_tensor(out=Sk[kt][:], in0=Sk[kt][:], scalar=cd[:, c0 + h:c0 + h + 1], in1=pst[:],
                                                                op0=ALU.mult, op1=ALU.add), [Sk[kt], pst, cd], [Sk[kt]])
            P.pool(lambda e, kt=kt: e.tensor_copy(out=Sbk[kt][:], in_=Sk[kt][:]), [Sk[kt]], [Sbk[kt]])

    def gmlp_layer(self, li, i, last):
        nc, w = self.nc, self.lw[i]
        ph = Phase(nc)
        self.emit_mod(ph, i)
        ph.finish()
        ph = Phase(nc)
        P = ph.P
        Wt = ph.sb("Wuvz", [128, 8, 6144], BF16)
        Wk = self.load_w(ph, Wt, w["w_in"], 0, 6144, 8)
        Wot = ph.sb("Wo", [128, 16, 1024], BF16)
        Wo = self.load_w(ph, Wot, w["w_out"], 0, 1024, 16)
        R = self.front_bufs(ph, n_xn=1)
        R.update(self.tail_bufs(ph, n_gT=1, n_xo=1, last=last))
        mm = Rot([ph.ps("mm%d" % j, [128, 512], F32) for j in range(2)])
        spb = Rot([ph.ps("spb%d" % j, [128, 512], F32) for j in range(2)])
        wsT = ph.sb("wsT", [128, 8, 128], BF16)
        bsT = ph.sb("bsT", [128, 8], F32)
        vg = ph.sb("vg", [128, 2048], F32)
        P.dma(wsT[:], w["wsT"][:, :, :], "g0", writes=[wsT], q="pool")
        P.dma(bsT[:], w["bsT"][:, :], "g1", writes=[bsT])
        P.dma(vg[:], w["vg"][:, :], "g2", writes=[vg])
        usb = ph.rot("usb", [128, 2048], F32, 1)
        vsb = ph.rot("vsb", [128, 2048], F32, 1)
        szb = ph.rot("szb", [128, 2048], BF16, 1)
        vnb = ph.rot("vnb", [128, 2048], BF16, 1)
        gated = ph.rot("gated", [128, 2048], BF16, 1)
        stv = ph.rot("stv", [128, 16], F32, 2)
        junk = R["junk"]
        order = self.chunks_fwd()
        if last:
            order = [ck for ck in order if ck[0] == "x"]
        for ck in order:
            which = 1 if ck[0] == "c" else 0
            hT, xt = self.front(ph, R, self.src_ap(li, ck), which)
            u_, v_, z_, s_ = usb.next(), vsb.next(), szb.next(), stv.next()
            for n in range(12):
                bank = mm.next()
                for k in range(8):
                    P.pe(lambda e, bank=bank, n=n, k=k, hT=hT: e.matmul(bank[:], lhsT=hT[:, k, :], rhs=Wk[k][:, n * 512:(n + 1) * 512],
                                                                        start=(k == 0), stop=(k == 7)), [hT, Wk[k]], [bank])
                if n < 4:
                    P.act(lambda e, bank=bank, n=n, u_=u_: e.activation(out=u_[:, n * 512:(n + 1) * 512], in_=bank[:], func=AF.Copy), [bank], [u_])
                elif n < 8:
                    P.act(lambda e, bank=bank, n=n, v_=v_, s_=s_: e.activation(out=v_[:, (n - 4) * 512:(n - 3) * 512], in_=bank[:], func=AF.Identity,
                                                                              accum_out=s_[:, n - 4:n - 3]), [bank], [v_, s_])
                else:
                    P.act(lambda e, bank=bank, n=n, z_=z_: e.activation(out=z_[:, (n - 8) * 512:(n - 7) * 512], in_=bank[:], func=AF.Silu), [bank], [z_])
            for hh in range(2):
                P.act(lambda e, v_=v_, s_=s_, hh=hh: e.activation(out=junk[:], in_=v_[:, hh * 1024:(hh + 1) * 1024], func=AF.Square,
                                                                  accum_out=s_[:, 11 + hh:12 + hh]), [v_], [junk, s_])
            P.dve(lambda e, s_=s_: e.tensor_tensor(out=s_[:, 4:5], in0=s_[:, 11:12], in1=s_[:, 12:13], op=ALU.add), [s_], [s_])
            P.dve(lambda e, s_=s_: e.tensor_reduce(out=s_[:, 5:6], in_=s_[:, 0:4], axis=AX.X, op=ALU.add), [s_], [s_])
            P.dve(lambda e, s_=s_: e.tensor_scalar(out=s_[:, 5:6], in0=s_[:, 5:6], scalar1=1.0 / 2048, scalar2=None, op0=ALU.mult), [s_], [s_])
            P.dve(lambda e, s_=s_: e.tensor_tensor(out=s_[:, 6:7], in0=s_[:, 5:6], in1=s_[:, 5:6], op=ALU.mult), [s_], [s_])
            P.dve(lambda e, s_=s_: e.scalar_tensor_tensor(out=s_[:, 7:8], in0=s_[:, 4:5], scalar=1.0 / 2048, in1=s_[:, 6:7], op0=ALU.mult, op1=ALU.subtract),
                  [s_], [s_])
            P.act(lambda e, s_=s_: e.activation(out=s_[:, 8:9], in_=s_[:, 7:8], func=AF.Sqrt, scale=1.0, bias=NORM_EPS), [s_], [s_])
            P.dve(lambda e, s_=s_: e.reciprocal(out=s_[:, 9:10], in_=s_[:, 8:9]), [s_], [s_])
            P.dve(lambda e, s_=s_: e.scalar_tensor_tensor(out=s_[:, 10:11], in0=s_[:, 5:6], scalar=-1.0, in1=s_[:, 9:10], op0=ALU.mult, op1=ALU.mult),
                  [s_], [s_])
            P.act(lambda e, v_=v_, s_=s_: e.activation(out=v_[:], in_=v_[:], func=AF.Identity, scale=s_[:, 9:10], bias=s_[:, 10:11]), [v_, s_], [v_])
            vn_ = vnb.next()
            P.dve(lambda e, v_=v_, vn_=vn_: e.tensor_tensor(out=vn_[:], in0=v_[:], in1=vg[:], op=ALU.mult), [v_, vg], [vn_])
            for g in range(8):
                sb_ = spb.next()
                P.pe(lambda e, g=g, sb_=sb_, vn_=vn_: e.matmul(sb_[:, 0:256], lhsT=wsT[:, g, :], rhs=vn_[:, g * 256:(g + 1) * 256], start=True, stop=True),
                     [wsT, vn_], [sb_])
                P.dve(lambda e, g=g, sb_=sb_, u_=u_: e.scalar_tensor_tensor(out=u_[:, g * 256:(g + 1) * 256], in0=sb_[:, 0:256], scalar=bsT[:, g:g + 1],
                                                                           in1=u_[:, g * 256:(g + 1) * 256], op0=ALU.add, op1=ALU.mult), [sb_, bsT, u_], [u_])
            g_ = gated.next()
            P.pool(lambda e, u_=u_, z_=z_, g_=g_: e.tensor_tensor(out=g_[:], in0=u_[:], in1=z_[:], op=ALU.mult), [u_, z_], [g_])
            self.tail(ph, R, g_, Wo, li, ck, xt, last)
        ph.finish()

    def _rw_inputs(self, w, i, din):
        w["mu"] = din("rw_mu%d" % i, [128, 6, 8])
        w["rkvg"] = din("rw_rkvg%d" % i, [4, D, D])
        w["w1"] = din("rw_w1%d" % i, [2, D, 64])
        w["a1"] = din("rw_a1%d" % i, [2, D, 64])
        w["w2"] = din("rw_w2%d" % i, [2, 64, D])
        w["a2"] = din("rw_a2%d" % i, [2, 64, D])
        w["rows"] = din("rw_rows%d" % i, [8, D])
        w["bc"] = din("rw_bc%d" % i, [128, 5, D])
        w["w_out"] = din("rw_wout%d" % i, [D, D])
        w["masks"] = din("rw_masks%d" % i, [2, 128, 4, 128])
        w["sel8"] = din("rw_sel8%d" % i, [8, 8, 128])
        w["negc"] = din("rw_negc%d" % i, [128, 1])
        w["bmask"] = din("rw_bmask%d" % i, [128, 4, 128])

    def rw_shift(self, P, sh, hc, hp, hn, kind):
        if kind == "x":
            P.pool(lambda e: e.tensor_copy(out=sh[:, 0:2, 1:128], in_=hc[:, 0:2, 0:127]), [hc], [sh])
            P.pool(lambda e: e.memset(sh[:, 0:2, :].rearrange("p k (r c) -> p k r c", c=64)[:, :, :, 0:1], 0.0), [], [sh])
            P.pool(lambda e: e.tensor_copy(out=sh[:, 2:4, 0:127], in_=hc[:, 2:4, 1:128]), [hc], [sh])
            P.pool(lambda e: e.memset(sh[:, 2:4, :].rearrange("p k (r c) -> p k r c", c=64)[:, :, :, 63:64], 0.0), [], [sh])
            P.pool(lambda e: e.tensor_copy(out=sh[:, 4:6, 64:128], in_=hc[:, 4:6, 0:64]), [hc], [sh])
            if hp is not None:
                P.pool(lambda e: e.tensor_copy(out=sh[:, 4:6, 0:64], in_=hp[:, 4:6, 64:128]), [hp], [sh])
            else:
                P.pool(lambda e: e.memset(sh[:, 4:6, 0:64], 0.0), [], [sh])
            P.pool(lambda e: e.tensor_copy(out=sh[:, 6:8, 0:64], in_=hc[:, 6:8, 64:128]), [hc], [sh])
            if hn is not None:
                P.pool(lambda e: e.tensor_copy(out=sh[:, 6:8, 64:128], in_=hn[:, 6:8, 0:64]), [hn], [sh])
            else:
                P.pool(lambda e: e.memset(sh[:, 6:8, 64:128], 0.0), [], [sh])
        else:
            P.pool(lambda e: e.tensor_copy(out=sh[:, 0:4, 1:128], in_=hc[:, 0:4, 0:127]), [hc], [sh])
            if hp is not None:
                P.pool(lambda e: e.tensor_copy(out=sh[:, 0:4, 0:1], in_=hp[:, 0:4, 127:128]), [hp], [sh])
            else:
                P.pool(lambda e: e.memset(sh[:, 0:4, 0:1], 0.0), [], [sh])
            P.pool(lambda e: e.tensor_copy(out=sh[:, 4:8, 0:127], in_=hc[:, 4:8, 1:128]), [hc], [sh])
            if hn is not None:
                P.pool(lambda e: e.tensor_copy(out=sh[:, 4:8, 127:128], in_=hn[:, 4:8, 0:1]), [hn], [sh])
            else:
                P.pool(lambda e: e.memset(sh[:, 4:8, 127:128], 0.0), [], [sh])

    def rw_neighbors(self, ck):
        n = self.NC if ck[0] == "c" else self.NX
        p = (ck[0], ck[1] - 1) if ck[1] > 0 else None
        q = (ck[0], ck[1] + 1) if ck[1] < n - 1 else None
        return p, q

    def rw_hcache(self, ph, R, li):
        cache = []

        def get(ck):
            for c, v in cache:
                if c == ck:
                    return v
            which = 1 if ck[0] == "c" else 0
            v = self.front(ph, R, self.src_ap(li, ck), which)
            cache.append((ck, v))
            if len(cache) > 3:
                cache.pop(0)
            return v
        return get

    def rw_mix(self, P, mixr, tmpr, xx, hc, mu, p):
        tmp, mix = tmpr.next(), mixr.next()
        P.pool(lambda e: e.tensor_tensor(out=tmp[:], in0=xx[:], in1=mu[:, p, :].unsqueeze(2).to_broadcast([128, 8, 128]), op=ALU.mult),
               [xx, mu], [tmp])
        P.dve(lambda e: e.tensor_tensor(out=mix[:], in0=tmp[:], in1=hc[:], op=ALU.add), [tmp, hc], [mix])
        return mix

    def rwkv_layer(self, li, i, last):
        nc, w = self.nc, self.lw[i]
        NCH = self.NX + self.NC
        if not hasattr(self, "scr_o"):
            self.scr_o = nc.dram_tensor("scr_o", [NCH, 128, 2048], F32).ap()
        ph = Phase(nc)
        self.emit_mod(ph, i)
        ph.finish()
        import os as _os
        for d in range(2):
            self.rw_scan_phase(li, i, d)
            if _os.environ.get("RW_STOP_AFTER_F") == "1":
                return
        self.rw_out_phase(li, i, last)

    def rw_scan_phase(self, li, i, d):
        nc, w = self.nc, self.lw[i]
        ph = Phase(nc)
        P = ph.P
        C0 = 0.6065306597126334
        Wr = self.load_w(ph, ph.sb("Wr", [128, 8, 1024], BF16), w["rkvg"][0], 0, 1024, 8)
        Wkk = self.load_w(ph, ph.sb("Wk", [128, 8, 1024], BF16), w["rkvg"][1], 0, 1024, 8)
        Wv = self.load_w(ph, ph.sb("Wv", [128, 8, 1024], BF16), w["rkvg"][2], 0, 1024, 8)
        w1 = self.load_w(ph, ph.sb("w1", [128, 8, 64], BF16), w["w1"][d], 0, 64, 8)
        a1 = self.load_w(ph, ph.sb("a1", [128, 8, 64], BF16), w["a1"][d], 0, 64, 8)
        w2 = ph.sb("w2", [64, 1024], BF16)
        a2 = ph.sb("a2", [64, 1024], BF16)
        P.dma(w2[:], w["w2"][d], "w2", writes=[w2], q="pool")
        P.dma(a2[:], w["a2"][d], "a2", writes=[a2], q="pool")
        rows = ph.sb("rows", [8, 1024], F32)
        sel8 = ph.sb("sel8", [8, 8, 128], F32)
        mu = ph.sb("mu", [128, 6, 8], F32)
        msk = ph.sb("msk", [128, 4, 128], F32)
        negc = ph.sb("negc", [128, 1], F32)
        kk_bc = ph.sb("kk_bc", [128, 1024], F32)
        ka_bc = ph.sb("ka_bc", [128, 1024], F32)
        rk_bc = ph.sb("rk_bc", [128, 1024], F32)
        P.dma(rows[:], w["rows"][:, :], "c0", writes=[rows])
        P.dma(sel8[:], w["sel8"][:, :, :], "c1", writes=[sel8])
        P.dma(mu[:], w["mu"][:, :, :], "c2", writes=[mu])
        P.dma(msk[:], w["masks"][d], "c3", writes=[msk])
        P.dma(negc[:], w["negc"][:, :], "c4", writes=[negc])
        P.dma(kk_bc[:], w["bc"][:, 0, :], "c5", writes=[kk_bc])
        P.dma(ka_bc[:], w["bc"][:, 1, :], "c6", writes=[ka_bc])
        P.dma(rk_bc[:], w["bc"][:, 2, :], "c7", writes=[rk_bc])
        R = self.front_bufs(ph, n_xn=1, n_xt=1, n_hT=4, junk=False)
        get_h = self.rw_hcache(ph, R, li)
        G = Rot([ph.ps("G%d" % j, [128, 512], F32) for j in range(6)])
        psm = ph.ps("psm", [128, 512], F32)
        PT = R["ptrx"]
        sh = ph.sb("sh", [128, 8, 128], BF16)
        xx = ph.sb("xx", [128, 8, 128], F32)
        tmpr = ph.rot("mtmp", [128, 8, 128], F32, 1)
        mixr = ph.rot("mix", [128, 8, 128], BF16, 2)
        jk = Tile(tmpr.tiles[0].t[:].rearrange("p k t -> p (k t)"), "junkalias")
        jk.b = tmpr.tiles[0].b
        R["junk"] = jk
        r_sb = ph.sb("r_sb", [128, 1024], F32)
        k_sb = ph.sb("k_sb", [128, 1024], F32)
        v_bf = ph.sb("v_bf", [128, 1024], BF16)
        Wt = [ph.sb("W%d" % j, [128, 1024], F32) for j in range(4)]
        th_bf = ph.sb("th_bf", [64, 128], BF16)
        la_bf = ph.sb("la_bf", [64, 128], BF16)
        rt_bf = ph.sb("rt_bf", [128, 1024], BF16)
        at_bf = ph.sb("at_bf", [128, 1024], BF16)
        bh_bf = ph.sb("bh_bf", [128, 1024], BF16)
        kh_bf = ph.sb("kh_bf", [128, 1024], BF16)
        AR = ph.sb("AR", [64, 16, 2, 128], BF16)
        BT = ph.sb("BT", [64, 16, 128], BF16)
        KT = ph.sb("KT", [64, 16, 128], BF16)
        WC = ph.sb("WC", [64, 16], F32)
        sm = ph.rot("sm", [128, 64], F32, 2)
        names = ("N", "NT", "NA", "NAT", "NB", "NBT", "O32", "O32T", "O64", "O64T", "O128", "T", "TT", "Aak", "Abr", "Akr")
        bmk = ph.sb("bmk", [128, 4, 128], BF16)
        P.dma(bmk[:], w["bmask"][:, :, :], "c10", writes=[bmk], q="pool")
        U_ = [{n: ph.sb("%s_u%d" % (n, us), [128, 4, 128], BF16) for n in names} for us in range(2)]
        Xb = [ph.sb("Xb%d" % us, [128, 4, 64], BF16) for us in range(2)]
        Ub = [ph.sb("Ub%d" % us, [128, 4, 64], BF16) for us in range(2)]
        S = [ph.sb("S%d" % u, [64, 4, 64], F32) for u in range(4)]
        Sb = [ph.sb("Sb%d" % u, [64, 4, 64], BF16) for u in range(4)]
        for u in range(4):
            P.dve(lambda e, u=u: e.memset(S[u][:], 0.0), [], [S[u]])
            P.pool(lambda e, u=u: e.memset(Sb[u][:], 0.0), [], [Sb[u]])
        ysb = ph.rot("ysb", [128, 1040], F32, 1)
        if d == 1:
            yf = ph.rot("yf", [128, 1040], F32, 1)
        ident = self.ident

        order = self.chunks_fwd() if d == 0 else self.chunks_bwd()
        for ck in order:
            cidx = self.row0(ck) // CH
            pk, nk = self.rw_neighbors(ck)
            hc = get_h(ck)[0]
            hp = get_h(pk)[0] if pk else None
            hn = get_h(nk)[0] if nk else None
            self.rw_shift(P, sh, hc, hp, hn, ck[0])
            P.pool(lambda e, hc=hc: e.tensor_tensor(out=xx[:], in0=sh[:], in1=hc[:], op=ALU.subtract), [sh, hc], [xx])
            def proj(mix, Wl, dst, func=AF.Copy):
                for n in range(2):
                    bank = G.next()
                    for k in range(8):
                        P.pe(lambda e, bank=bank, n=n, k=k: e.matmul(bank[:], lhsT=mix[:, k, :], rhs=Wl[k][:, n * 512:(n + 1) * 512],
                                                                     start=(k == 0), stop=(k == 7)), [mix, Wl[k]], [bank])
                    P.act(lambda e, bank=bank, n=n: e.activation(out=dst[:, n * 512:(n + 1) * 512], in_=bank[:], func=func), [bank], [dst])
            proj(self.rw_mix(P, mixr, tmpr, xx, hc, mu, 0), Wr, r_sb)
            proj(self.rw_mix(P, mixr, tmpr, xx, hc, mu, 2), Wkk, k_sb)
            proj(self.rw_mix(P, mixr, tmpr, xx, hc, mu, 3), Wv, v_bf)
            mix1 = self.rw_mix(P, mixr, tmpr, xx, hc, mu, 1)
            for k in range(8):
                P.pe(lambda e, k=k, mix1=mix1: e.matmul(psm[0:64, 0:128], lhsT=w1[k][:, :], rhs=mix1[:, k, :], start=(k == 0), stop=(k == 7)),
                     [mix1, w1[k]], [psm])
            P.act(lambda e: e.activation(out=th_bf[:], in_=psm[0:64, 0:128], func=AF.Tanh), [psm], [th_bf])
            sig = Wt[0]
            for n in range(2):
                bank = G.next()
                P.pe(lambda e, bank=bank, n=n: e.matmul(bank[:], lhsT=th_bf[:], rhs=w2[:, n * 512:(n + 1) * 512], start=True, stop=False), [th_bf, w2], [bank])
                P.pe(lambda e, bank=bank, n=n: e.matmul(bank[:], lhsT=sel8[:, d, :], rhs=rows[:, n * 512:(n + 1) * 512], start=False, stop=True), [sel8, rows], [bank])
                P.act(lambda e, bank=bank, n=n: e.activation(out=sig[:, n * 512:(n + 1) * 512], in_=bank[:], func=AF.Sigmoid), [bank], [sig])
            mix4 = self.rw_mix(P, mixr, tmpr, xx, hc, mu, 4)
            for k in range(8):
                P.pe(lambda e, k=k, mix4=mix4: e.matmul(psm[0:64, 0:128], lhsT=a1[k][:, :], rhs=mix4[:, k, :], start=(k == 0), stop=(k == 7)),
                     [mix4, a1[k]], [psm])
            P.act(lambda e: e.activation(out=la_bf[:], in_=psm[0:64, 0:128], func=AF.Copy), [psm], [la_bf])
            ep, em, ex = Wt[1], Wt[2], Wt[3]
            for n in range(2):
                bank = G.next()
                sl = slice(n * 512, (n + 1) * 512)
                P.pe(lambda e, bank=bank, sl=sl: e.matmul(bank[:], lhsT=msk[:, 3, :], rhs=sig[:, sl], start=True, stop=True), [msk, sig], [bank])
                P.act(lambda e, bank=bank, sl=sl: e.activation(out=ep[:, sl], in_=bank[:], func=AF.Exp), [bank], [ep])
                P.act(lambda e, bank=bank, sl=sl: e.activation(out=em[:, sl], in_=bank[:], func=AF.Exp, scale=-1.0), [bank], [em])
                P.dve(lambda e, bank=bank, sl=sl: e.scalar_tensor_tensor(out=ex[:, sl], in0=sig[:, sl], scalar=C0, in1=bank[:], op0=ALU.mult, op1=ALU.add),
                      [bank, sig], [ex])
            P.act(lambda e: e.activation(out=ex[:], in_=ex[:], func=AF.Exp), [ex], [ex])
            for h in range(16):
                P.pe(lambda e, h=h: e.matmul(psm[0:64, 256 + h:257 + h], lhsT=sig[:, h * 64:(h + 1) * 64], rhs=negc[:, 0:1], start=True, stop=True),
                     [sig, negc], [psm])
            P.act(lambda e: e.activation(out=WC[:], in_=psm[0:64, 256:272], func=AF.Exp), [psm], [WC])
            P.pool(lambda e: e.tensor_tensor(out=rt_bf[:], in0=r_sb[:], in1=ep[:], op=ALU.mult), [r_sb, ep], [rt_bf])
            kk, sq = Wt[0], Wt[1]
            s_ = sm.next()
            P.dve(lambda e: e.tensor_tensor(out=kk[:], in0=k_sb[:], in1=kk_bc[:], op=ALU.mult), [k_sb, kk_bc, sig], [kk])
            P.pool(lambda e: e.tensor_tensor(out=sq[:], in0=kk[:], in1=kk[:], op=ALU.mult), [kk, rt_bf], [sq])
            P.dve(lambda e, s_=s_: e.tensor_reduce(out=s_[:, 0:16], in_=sq[:].rearrange("p (h c) -> p h c", h=16), axis=AX.X, op=ALU.add), [sq], [s_])
            P.act(lambda e, s_=s_: e.activation(out=s_[:, 16:32], in_=s_[:, 0:16], func=AF.Sqrt), [s_], [s_])
            P.dve(lambda e, s_=s_: e.tensor_scalar(out=s_[:, 16:32], in0=s_[:, 16:32], scalar1=1e-12, scalar2=None, op0=ALU.max), [s_], [s_])
            P.dve(lambda e, s_=s_: e.reciprocal(out=s_[:, 32:48], in_=s_[:, 16:32]), [s_], [s_])
            P.dve(lambda e, s_=s_: e.tensor_tensor(out=kk[:].rearrange("p (h c) -> p h c", h=16), in0=kk[:].rearrange("p (h c) -> p h c", h=16),
                                                   in1=s_[:, 32:48].unsqueeze(2).to_broadcast([128, 16, 64]), op=ALU.mult), [kk, s_], [kk])
            P.dve(lambda e: e.scalar_tensor_tensor(out=at_bf[:], in0=kk[:], scalar=-1.0, in1=ex[:], op0=ALU.mult, op1=ALU.mult), [kk, ex], [at_bf])
            aa = Wt[1]
            for n in range(2):
                bank = G.next()
                P.pe(lambda e, bank=bank, n=n: e.matmul(bank[:], lhsT=la_bf[:], rhs=a2[:, n * 512:(n + 1) * 512], start=True, stop=False), [la_bf, a2], [bank])
                P.pe(lambda e, bank=bank, n=n: e.matmul(bank[:], lhsT=sel8[:, 2 + d, :], rhs=rows[:, n * 512:(n + 1) * 512], start=False, stop=True), [sel8, rows], [bank])
                P.act(lambda e, bank=bank, n=n: e.activation(out=aa[:, n * 512:(n + 1) * 512], in_=bank[:], func=AF.Sigmoid), [bank, sq], [aa])
            be = Wt[3]
            P.dve(lambda e: e.tensor_tensor(out=be[:], in0=kk[:], in1=aa[:], op=ALU.mult), [kk, aa, at_bf], [be])
            P.pool(lambda e: e.tensor_tensor(out=bh_bf[:], in0=be[:], in1=em[:], op=ALU.mult), [be, em], [bh_bf])
            kd = Wt[0]
            P.dve(lambda e: e.scalar_tensor_tensor(out=kd[:], in0=aa[:], scalar=-1.0, in1=ka_bc[:], op0=ALU.add, op1=ALU.mult), [aa, ka_bc, be], [kd])
            P.dve(lambda e: e.scalar_tensor_tensor(out=kd[:], in0=kd[:], scalar=1.0, in1=k_sb[:], op0=ALU.add, op1=ALU.mult), [kd, k_sb], [kd])
            P.pool(lambda e: e.tensor_tensor(out=kh_bf[:], in0=kd[:], in1=em[:], op=ALU.mult), [kd, em], [kh_bf])
            bt = Wt[3]
            P.dve(lambda e: e.tensor_tensor(out=bt[:], in0=kd[:], in1=r_sb[:], op=ALU.mult), [kd, r_sb, bh_bf], [bt])
            P.pool(lambda e: e.tensor_tensor(out=bt[:], in0=bt[:], in1=rk_bc[:], op=ALU.mult), [bt, rk_bc], [bt])
            P.dve(lambda e, s_=s_: e.tensor_reduce(out=s_[:, 48:64], in_=bt[:].rearrange("p (h c) -> p h c", h=16), axis=AX.X, op=ALU.add), [bt], [s_])
            cnt = 0
            for g in range(2):
                for src, dst_fn in ((at_bf, lambda g: AR[:, g * 8:(g + 1) * 8, 0, :]), (rt_bf, lambda g: AR[:, g * 8:(g + 1) * 8, 1, :]),
                                    (bh_bf, lambda g: BT[:, g * 8:(g + 1) * 8, :]), (kh_bf, lambda g: KT[:, g * 8:(g + 1) * 8, :])):
                    for j in range(8):
                        h = g * 8 + j
                        P.pe(lambda e, src=src, h=h, j=j: e.transpose(out=PT[0:64, j, :], in_=src[:, h * 64:(h + 1) * 64], identity=ident[:]),
                             [src, ident], [PT])
                    dtile = AR if src in (at_bf, rt_bf) else (BT if src is bh_bf else KT)
                    dst = dst_fn(g)
                    if cnt % 2 == 0:
                        P.act(lambda e, dst=dst: e.activation(out=dst, in_=PT[0:64, :, :], func=AF.Copy), [PT], [dtile])
                    else:
                        P.dve(lambda e, dst=dst: e.tensor_copy(out=dst, in_=PT[0:64, :, :]), [PT], [dtile])
                    cnt += 1
            y_ = ysb.next()
            if d == 1:
                yf_ = yf.next()
                P.dma(yf_[:], self.scr_o[cidx][:, 0:1040], "lyf%d" % yf.slot, writes=[yf_])
            for g in range(2):
                units = [(2 * g + us, us) for us in range(2)]
                for u, us in units:
                    M = U_[us]
                    h0 = u * 4
                    for pr in range(2):
                        bank = G.next()
                        for j in range(2):
                            h = h0 + pr * 2 + j
                            P.pe(lambda e, bank=bank, j=j, h=h: e.matmul(bank[:, j * 256:(j + 1) * 256], lhsT=BT[:, h, :],
                                                                         rhs=AR[:, h, :, :].rearrange("p a t -> p (a t)"), start=True, stop=True), [BT, AR], [bank])
                        bv = bank[:].rearrange("p (j a t) -> p j a t", j=2, a=2)
                        P.dve(lambda e, bv=bv, M=M, pr=pr: e.tensor_tensor(out=M["N"][:, pr * 2:pr * 2 + 2, :], in0=bv[:, :, 0, :],
                                                                          in1=msk[:, 0, :].unsqueeze(1).to_broadcast([128, 2, 128]), op=ALU.mult), [bank, msk], [M["N"]])
                        P.dve(lambda e, bv=bv, M=M, pr=pr: e.tensor_tensor(out=M["Abr"][:, pr * 2:pr * 2 + 2, :], in0=bv[:, :, 1, :],
                                                                          in1=msk[:, 1, :].unsqueeze(1).to_broadcast([128, 2, 128]), op=ALU.mult), [bank, msk], [M["Abr"]])
                        bank = G.next()
                        for j in range(2):
                            h = h0 + pr * 2 + j
                            P.pe(lambda e, bank=bank, j=j, h=h: e.matmul(bank[:, j * 256:(j + 1) * 256], lhsT=KT[:, h, :],
                                                                         rhs=AR[:, h, :, :].rearrange("p a t -> p (a t)"), start=True, stop=True), [KT, AR], [bank])
                        bv = bank[:].rearrange("p (j a t) -> p j a t", j=2, a=2)
                        P.dve(lambda e, bv=bv, M=M, pr=pr: e.tensor_tensor(out=M["Aak"][:, pr * 2:pr * 2 + 2, :], in0=bv[:, :, 0, :],
                                                                          in1=msk[:, 0, :].unsqueeze(1).to_broadcast([128, 2, 128]), op=ALU.mult), [bank, msk], [M["Aak"]])
                        P.dve(lambda e, bv=bv, M=M, pr=pr: e.tensor_tensor(out=M["Akr"][:, pr * 2:pr * 2 + 2, :], in0=bv[:, :, 1, :],
                                                                          in1=msk[:, 1, :].unsqueeze(1).to_broadcast([128, 2, 128]), op=ALU.mult), [bank, msk], [M["Akr"]])
                    bank = G.next()
                    for j in range(4):
                        h = h0 + j
                        P.pe(lambda e, bank=bank, j=j, h=h: e.matmul(bank[:, j * 128:(j + 1) * 128], lhsT=AR[:, h, 0, :], rhs=BT[:, h, :], start=True, stop=True),
                             [AR, BT], [bank])
                    P.dve(lambda e, bank=bank, M=M: e.tensor_tensor(out=M["NT"][:], in0=bank[:].rearrange("p (j t) -> p j t", j=4),
                                                                    in1=msk[:, 2, :].unsqueeze(1).to_broadcast([128, 4, 128]), op=ALU.mult), [bank, msk], [M["NT"]])
                    for nm, src, mi in (("NA", "N", 0), ("NAT", "NT", 0), ("O32", "N", 1), ("O32T", "NT", 1), ("O64", "N", 2), ("O64T", "NT", 2),
                                        ("O128T", "NT", 3)):
                        dn = "O128" if nm == "O128T" else nm
                        P.pool(lambda e, M=M, dn=dn, src=src, mi=mi: e.tensor_tensor(out=M[dn][:], in0=M[src][:],
                                                                                     in1=bmk[:, mi, :].unsqueeze(1).to_broadcast([128, 4, 128]), op=ALU.mult),
                               [M[src], bmk], [M[dn]])
                    P.pool(lambda e, M=M: e.tensor_tensor(out=M["T"][:], in0=M["NA"][:], in1=ident[:].unsqueeze(1).to_broadcast([128, 4, 128]), op=ALU.add),
                           [M["NA"], ident], [M["T"]])
                    P.pool(lambda e, M=M: e.tensor_tensor(out=M["TT"][:], in0=M["NAT"][:], in1=ident[:].unsqueeze(1).to_broadcast([128, 4, 128]), op=ALU.add),
                           [M["NAT"], ident], [M["TT"]])

                def mm4(bank, M, lt, rt):
                    for j in range(4):
                        P.pe(lambda e, bank=bank, j=j, M=M, lt=lt, rt=rt: e.matmul(bank[:, j * 128:(j + 1) * 128], lhsT=M[lt][:, j, :], rhs=M[rt][:, j, :],
                                                                                  start=True, stop=True), [M[lt], M[rt]], [bank])

                def cp4(bank, M, dn):
                    P.act(lambda e, bank=bank, M=M, dn=dn: e.activation(out=M[dn][:], in_=bank[:].rearrange("p (j t) -> p j t", j=4), func=AF.Copy),
                          [bank], [M[dn]])

                def add4(bank, M, dn):
                    P.dve(lambda e, bank=bank, M=M, dn=dn: e.tensor_tensor(out=M[dn][:], in0=bank[:].rearrange("p (j t) -> p j t", j=4), in1=M[dn][:], op=ALU.add),
                          [bank, M[dn]], [M[dn]])
                cur = {us: ("NA", "NAT") for _, us in units}
                for lvl in range(3):
                    for u, us in units:
                        M = U_[us]
                        nk, nkt = cur[us]
                        nb, nbt = ("NB", "NBT") if nk == "NA" else ("NA", "NAT")
                        b1 = G.next(); mm4(b1, M, nkt, nk); cp4(b1, M, nb)
                        b2 = G.next(); mm4(b2, M, nk, nkt); cp4(b2, M, nbt)
                        b3 = G.next(); mm4(b3, M, nbt, "T")
                        b4 = G.next(); mm4(b4, M, nb, "TT")
                        add4(b3, M, "T"); add4(b4, M, "TT")
                        cur[us] = (nb, nbt)
                for on, lastm in (("O32", False), ("O64", False), ("O128", True)):
                    for u, us in units:
                        M = U_[us]
                        if not lastm:
                            b1 = G.next(); mm4(b1, M, on + "T", "T"); cp4(b1, M, "N")
                            b2 = G.next(); mm4(b2, M, on, "TT"); cp4(b2, M, "NT")
                            b3 = G.next(); mm4(b3, M, "TT", "N")
                            b4 = G.next(); mm4(b4, M, "T", "NT")
                            add4(b3, M, "T"); add4(b4, M, "TT")
                        else:
                            b1 = G.next(); mm4(b1, M, on, "T"); cp4(b1, M, "N")
                            b3 = G.next(); mm4(b3, M, "TT", "N"); add4(b3, M, "T")
                for u, us in units:
                    M = U_[us]
                    h0 = u * 4
                    bank = G.next()
                    for j in range(4):
                        h = h0 + j
                        P.pe(lambda e, bank=bank, j=j, h=h, u=u: e.matmul(bank[:, j * 64:(j + 1) * 64], lhsT=AR[:, h, 0, :], rhs=Sb[u][:, j, :], start=True, stop=False),
                             [AR, Sb[u]], [bank])
                        P.pe(lambda e, bank=bank, j=j, h=h, M=M: e.matmul(bank[:, j * 64:(j + 1) * 64], lhsT=M["Aak"][:, j, :], rhs=v_bf[:, h * 64:(h + 1) * 64],
                                                                          start=False, stop=True), [M["Aak"], v_bf], [bank])
                    P.act(lambda e, bank=bank, us=us: e.activation(out=Xb[us][:], in_=bank[:, 0:256].rearrange("p (j v) -> p j v", j=4), func=AF.Copy),
                          [bank], [Xb[us]])
                    bank = G.next()
                    for j in range(4):
                        P.pe(lambda e, bank=bank, j=j, M=M, us=us: e.matmul(bank[:, j * 64:(j + 1) * 64], lhsT=M["T"][:, j, :], rhs=Xb[us][:, j, :], start=True, stop=True),
                             [M["T"], Xb[us]], [bank])
                    P.dve(lambda e, bank=bank, us=us: e.tensor_copy(out=Ub[us][:], in_=bank[:, 0:256].rearrange("p (j v) -> p j v", j=4)), [bank], [Ub[us]])
                    bank = G.next()
                    for j in range(4):
                        h = h0 + j
                        P.pe(lambda e, bank=bank, j=j, h=h, u=u: e.matmul(bank[:, j * 64:(j + 1) * 64], lhsT=AR[:, h, 1, :], rhs=Sb[u][:, j, :], start=True, stop=False),
                             [AR, Sb[u]], [bank])
                        P.pe(lambda e, bank=bank, j=j, M=M, us=us: e.matmul(bank[:, j * 64:(j + 1) * 64], lhsT=M["Abr"][:, j, :], rhs=Ub[us][:, j, :], start=False, stop=False),
                             [M["Abr"], Ub[us]], [bank])
                        P.pe(lambda e, bank=bank, j=j, h=h, M=M: e.matmul(bank[:, j * 64:(j + 1) * 64], lhsT=M["Akr"][:, j, :], rhs=v_bf[:, h * 64:(h + 1) * 64],
                                                                          start=False, stop=True), [M["Akr"], v_bf], [bank])
                    if d == 0:
                        P.act(lambda e, bank=bank, h0=h0, y_=y_: e.activation(out=y_[:, h0 * 64:(h0 + 4) * 64], in_=bank[:, 0:256], func=AF.Copy), [bank], [y_])
                    else:
                        P.dve(lambda e, bank=bank, h0=h0, y_=y_, yf_=yf_: e.tensor_tensor(out=y_[:, h0 * 64:(h0 + 4) * 64], in0=bank[:, 0:256],
                                                                                         in1=yf_[:, h0 * 64:(h0 + 4) * 64], op=ALU.add), [bank, yf_], [y_])
                    bank = G.next()
                    for j in range(4):
                        h = h0 + j
                        P.pe(lambda e, bank=bank, j=j, h=h, us=us: e.matmul(bank[0:64, j * 64:(j + 1) * 64], lhsT=bh_bf[:, h * 64:(h + 1) * 64], rhs=Ub[us][:, j, :],
                                                                            start=True, stop=False), [bh_bf, Ub[us]], [bank])
                        P.pe(lambda e, bank=bank, j=j, h=h: e.matmul(bank[0:64, j * 64:(j + 1) * 64], lhsT=kh_bf[:, h * 64:(h + 1) * 64], rhs=v_bf[:, h * 64:(h + 1) * 64],
                                                                     start=False, stop=True), [kh_bf, v_bf], [bank])
                    P.dve(lambda e, bank=bank, u=u: e.tensor_tensor(out=S[u][:], in0=bank[0:64, 0:256].rearrange("p (j v) -> p j v", j=4), in1=S[u][:], op=ALU.add),
                          [bank, S[u]], [S[u]])
                    P.dve(lambda e, u=u, h0=h0: e.tensor_tensor(out=S[u][:], in0=S[u][:], in1=WC[:, h0:h0 + 4].unsqueeze(2).to_broadcast([64, 4, 64]), op=ALU.mult),
                          [S[u], WC], [S[u]])
                    P.pool(lambda e, u=u: e.tensor_copy(out=Sb[u][:], in_=S[u][:]), [S[u]], [Sb[u]])
            if d == 0:
                P.dve(lambda e, y_=y_, s_=s_: e.tensor_copy(out=y_[:, 1024:1040], in_=s_[:, 48:64]), [s_], [y_])
                P.dma(self.scr_o[cidx][:, 0:1040], y_[:], "sy%d" % ysb.slot, reads=[y_], q="pool")
            else:
                need_out = True
                yv = y_[:, 0:1024].rearrange("p (h c) -> p h c", h=16)
                t1 = Wt[1]
                t1v = t1[:].rearrange("p (h c) -> p h c", h=16)
                P.dve(lambda e, yv=yv, s_=s_: e.tensor_reduce(out=s_[:, 0:16], in_=yv, axis=AX.X, op=ALU.add), [y_, aa], [s_])
                P.dve(lambda e, s_=s_: e.tensor_scalar(out=s_[:, 0:16], in0=s_[:, 0:16], scalar1=-1.0 / 64, scalar2=None, op0=ALU.mult), [s_], [s_])
                P.dve(lambda e, yv=yv, s_=s_: e.tensor_tensor(out=yv, in0=yv, in1=s_[:, 0:16].unsqueeze(2).to_broadcast([128, 16, 64]), op=ALU.add), [y_, s_], [y_])
                P.pool(lambda e, y_=y_: e.tensor_tensor(out=t1[:], in0=y_[:, 0:1024], in1=y_[:, 0:1024], op=ALU.mult), [y_], [t1])
                P.dve(lambda e, s_=s_: e.tensor_reduce(out=s_[:, 16:32], in_=t1v, axis=AX.X, op=ALU.add), [t1], [s_])
                P.act(lambda e, s_=s_: e.activation(out=s_[:, 16:32], in_=s_[:, 16:32], func=AF.Sqrt, scale=1.0 / 64, bias=64e-5), [s_], [s_])
                P.dve(lambda e, s_=s_: e.reciprocal(out=s_[:, 32:48], in_=s_[:, 16:32]), [s_], [s_])
                P.dve(lambda e, yv=yv, s_=s_: e.tensor_tensor(out=yv, in0=yv, in1=s_[:, 32:48].unsqueeze(2).to_broadcast([128, 16, 64]), op=ALU.mult), [y_, s_], [y_])
                P.dve(lambda e, s_=s_, yf_=yf_: e.tensor_tensor(out=s_[:, 48:64], in0=s_[:, 48:64], in1=yf_[:, 1024:1040], op=ALU.add), [s_, yf_], [s_])
                P.dve(lambda e, s_=s_: e.tensor_tensor(out=t1v, in0=v_bf[:].rearrange("p (h c) -> p h c", h=16),
                                                       in1=s_[:, 48:64].unsqueeze(2).to_broadcast([128, 16, 64]), op=ALU.mult), [v_bf, s_, t1], [t1])
                P.dma(self.scr_o[cidx][:, 0:1024], y_[:, 0:1024], "sy%d" % ysb.slot, reads=[y_], q="pool")
                P.dma(self.scr_o[cidx][:, 1024:2048], t1[:], "sbv", reads=[t1], q="pool")
        ph.finish()

    def rw_out_phase(self, li, i, last):
        nc, w = self.nc, self.lw[i]
        ph = Phase(nc)
        P = ph.P
        Wg = self.load_w(ph, ph.sb("Wg", [128, 8, 1024], BF16), w["rkvg"][3], 0, 1024, 8)
        Wo = self.load_w(ph, ph.sb("Wo", [128, 8, 1024], BF16), w["w_out"], 0, 1024, 8)
        mu = ph.sb("mu", [128, 6, 8], F32)
        P.dma(mu[:], w["mu"][:, :, :], "c2", writes=[mu])
        R = self.front_bufs(ph, n_xn=1, n_xt=4, n_hT=4)
        R.update(self.tail_bufs(ph, last=last))
        get_h = self.rw_hcache(ph, R, li)
        mm = Rot([ph.ps("mm%d" % j, [128, 512], F32) for j in range(2)])
        sh = ph.sb("sh", [128, 8, 128], BF16)
        xx = ph.sb("xx", [128, 8, 128], F32)
        tmpr = ph.rot("mtmp", [128, 8, 128], F32, 2)
        mixr = ph.rot("mix", [128, 8, 128], BF16, 2)
        op = ph.rot("opre", [128, 2048], F32, 2)
        lg_bc = ph.sb("lg_bc", [128, 1024], F32)
        lb_bc = ph.sb("lb_bc", [128, 1024], F32)
        P.dma(lg_bc[:], w["bc"][:, 3, :], "c8", writes=[lg_bc])
        P.dma(lb_bc[:], w["bc"][:, 4, :], "c9", writes=[lb_bc])
        sz = ph.rot("sz", [128, 1024], F32, 2)
        gated = ph.rot("gated", [128, 1024], BF16, 2)
        order = self.chunks_fwd()
        if last:
            order = [ck for ck in order if ck[0] == "x"]
        for ck in order:
            cidx = self.row0(ck) // CH
            pk, nk = self.rw_neighbors(ck)
            hc, xt = get_h(ck)
            hp = get_h(pk)[0] if pk else None
            hn = get_h(nk)[0] if nk else None
            self.rw_shift(P, sh, hc, hp, hn, ck[0])
            P.pool(lambda e, hc=hc: e.tensor_tensor(out=xx[:], in0=sh[:], in1=hc[:], op=ALU.subtract), [sh, hc], [xx])
            mix5 = self.rw_mix(P, mixr, tmpr, xx, hc, mu, 5)
            o_ = op.next()
            P.dma(o_[:], self.scr_o[cidx][:, 0:2048], "lo%d" % op.slot, writes=[o_])
            P.pool(lambda e, o_=o_: e.tensor_tensor(out=o_[:, 0:1024], in0=o_[:, 0:1024], in1=lg_bc[:], op=ALU.mult), [o_, lg_bc], [o_])
            P.pool(lambda e, o_=o_: e.tensor_tensor(out=o_[:, 1024:2048], in0=o_[:, 1024:2048], in1=lb_bc[:], op=ALU.add), [o_, lb_bc], [o_])
            P.pool(lambda e, o_=o_: e.tensor_tensor(out=o_[:, 0:1024], in0=o_[:, 0:1024], in1=o_[:, 1024:2048], op=ALU.add), [o_], [o_])
            sz_ = sz.next()
            for n in range(2):
                bank = mm.next()
                for k in range(8):
                    P.pe(lambda e, bank=bank, n=n, k=k, mix5=mix5: e.matmul(bank[:], lhsT=mix5[:, k, :], rhs=Wg[k][:, n * 512:(n + 1) * 512],
                                                                            start=(k == 0), stop=(k == 7)), [mix5, Wg[k]], [bank])
                P.act(lambda e, bank=bank, n=n, sz_=sz_: e.activation(out=sz_[:, n * 512:(n + 1) * 512], in_=bank[:], func=AF.Silu), [bank], [sz_])
            g_ = gated.next()
            P.dve(lambda e, o_=o_, sz_=sz_, g_=g_: e.tensor_tensor(out=g_[:], in0=o_[:, 0:1024], in1=sz_[:], op=ALU.mult), [o_, sz_], [g_])
            self.tail(ph, R, g_, Wo, li, ck, xt, last, KT=8)
        ph.finish()


def _col(v, k):
    return np.ascontiguousarray(np.asarray(v, np.float32).reshape(k, 128).T)


def _consts(L, CL):
    f32 = np.float32
    t = np.arange(L)
    row, col = (t // 64).astype(f32), (t % 64).astype(f32)
    freqs = (f32(10000.0) ** (-(np.arange(64, dtype=f32)) / f32(64))).astype(f32)
    ang = np.concatenate([row[:, None] * freqs, col[:, None] * freqs], axis=-1).astype(f32)
    rope = np.zeros((CL + L, 2, 128), f32)
    rope[:CL, 0, :] = 1.0
    rope[CL:, 0, :] = np.cos(ang)
    rope[CL:, 1, :] = np.sin(ang)
    jj = np.arange(128)[:, None].astype(f32)
    ii = np.arange(128)[None, :].astype(f32)
    dmat = np.stack([np.maximum(ii - jj, 0), np.maximum(jj - ii, 0), (ii >= jj).astype(f32), (jj >= ii).astype(f32),
                     np.broadcast_to(ii + 1, (128, 128)), np.broadcast_to(128 - ii, (128, 128))], axis=1).astype(f32)
    jcol = np.stack([127 - np.arange(128), np.arange(128)], axis=1).astype(f32)
    sel = np.zeros((2, 2, 128), f32)
    sel[0, 0, :] = 1.0
    sel[1, 1, :] = 1.0
    return {"ident": np.eye(128, dtype=f32), "rope": rope, "dmat": np.ascontiguousarray(dmat), "jcol": jcol, "sel2": sel}


def host_inputs(inp, b, layers, L, CL=256, x_rows=None):
    f32 = np.float32
    m = dict(_consts(L, CL))
    xr = inp["x"][b] if x_rows is None else x_rows
    m["x"] = np.ascontiguousarray(xr[:L], f32)
    m["ctx"] = np.ascontiguousarray(inp["ctx"][b][:CL], f32)
    m["cc"] = np.ascontiguousarray(np.stack([_col(inp["c"][b], 8), _col(inp["c_ctx"], 8)], axis=-1))
    m["final_g_bc"] = np.ascontiguousarray(np.broadcast_to(np.asarray(inp["final_g"], f32), (128, D)))
    for i in layers:
        j = i // 3
        m["ada_w%d" % i] = np.ascontiguousarray(inp["ada_w"][i], f32)
        m["ada_bcol%d" % i] = _col(inp["ada_b"][i], 24)
        m["ada_brow%d" % i] = np.ascontiguousarray(np.broadcast_to(np.asarray(inp["ada_b"][i][2 * D:], f32), (2, D)))
        m["ng_col%d" % i] = _col(inp["norm_g"][i], 8)
        k = KINDS[i]
        if k == 0:
            m["ret_w_in%d" % i] = np.ascontiguousarray(inp["ret_w_in"][j], f32)
            m["ret_w_out%d" % i] = np.ascontiguousarray(inp["ret_w_out"][j], f32)
            dec = np.concatenate([inp["ret_decay"][j][0], inp["ret_decay"][j][1]]).astype(f32)
            m["ret_decay%d" % i] = np.ascontiguousarray(np.broadcast_to(dec, (128, 8)))
        elif k == 1:
            m["gm_w_in%d" % i] = np.ascontiguousarray(inp["gm_w_in"][j], f32)
            m["gm_w_out%d" % i] = np.ascontiguousarray(inp["gm_w_out"][j], f32)
            m["gm_vg%d" % i] = np.ascontiguousarray(np.broadcast_to(np.asarray(inp["gm_vnorm_g"][j], f32), (128, 2048)))
            m["gm_wsT%d" % i] = np.ascontiguousarray(np.transpose(np.asarray(inp["gm_w_s"][j], f32), (2, 0, 1)))
            m["gm_bsT%d" % i] = np.ascontiguousarray(np.asarray(inp["gm_b_s"][j], f32).T)
        else:
            _rw_host(m, inp, i, j)
    return m


def _rw_host(m, inp, i, j):
    f32 = np.float32
    g = lambda k: np.asarray(inp[k][j], f32)
    mu = g("rw_mu")
    m["rw_mu%d" % i] = np.ascontiguousarray(np.stack([_col(mu[p], 8) for p in range(6)], axis=1))
    m["rw_rkvg%d" % i] = np.ascontiguousarray(g("rw_w_rkvg"))
    m["rw_w1%d" % i] = np.ascontiguousarray(g("rw_w1"))
    m["rw_a1%d" % i] = np.ascontiguousarray(g("rw_a1"))
    m["rw_w2%d" % i] = np.ascontiguousarray(g("rw_w2"))
    m["rw_a2%d" % i] = np.ascontiguousarray(g("rw_a2"))
    rows = np.zeros((8, D), f32)
    rows[0:2] = g("rw_w0")
    rows[2:4] = g("rw_a0")
    m["rw_rows%d" % i] = rows
    bc = np.stack([g("rw_k_k"), g("rw_k_a"), g("rw_r_k").reshape(-1), g("rw_lnx_g"), g("rw_lnx_b")], axis=0)
    m["rw_bc%d" % i] = np.ascontiguousarray(np.broadcast_to(bc[None], (128, 5, D)))
    m["rw_wout%d" % i] = np.ascontiguousarray(g("rw_w_out"))
    s_ = np.arange(128)[:, None]
    t_ = np.arange(128)[None, :]
    c0 = f32(-0.6065306597126334)
    fw = np.stack([(s_ < t_), (s_ <= t_), (t_ < s_), (s_ <= t_) * c0], axis=1).astype(f32)
    bw = np.stack([(s_ > t_), (s_ >= t_), (t_ > s_), (s_ >= t_) * c0], axis=1).astype(f32)
    m["rw_masks%d" % i] = np.ascontiguousarray(np.stack([fw, bw], axis=0))
    sel = np.zeros((8, 8, 128), f32)
    for r in range(8):
        sel[r, r, :] = 1.0
    m["rw_sel8%d" % i] = sel
    m["rw_negc%d" % i] = np.full((128, 1), c0, f32)
    blk = lambda n: (s_ // n) == (t_ // n)
    bm = np.stack([blk(16)] + [blk(n) & ~blk(n // 2) for n in (32, 64, 128)], axis=1).astype(f32)
    m["rw_bmask%d" % i] = np.ascontiguousarray(bm)


_MODEL_CACHE = {}


def kernel(**inputs):
    inp = {k: np.asarray(v) for k, v in inputs.items()}
    B, L, _ = inp["x"].shape
    layers = (0, 1, 2, 3)
    key = (L, layers)
    if key not in _MODEL_CACHE:
        _MODEL_CACHE[key] = Model(L, 256, layers)
    model = _MODEL_CACHE[key]
    maps = []
    for core in range(NCORES):
        b = core % B
        hm = host_inputs(inp, b, layers, L)
        maps.append({k: hm[k] for k in model.inputs})
    res = run_bass_kernel_spmd(model.nc, maps, core_ids=list(range(NCORES)))
    out = np.stack([np.asarray(res.results[b]["out"], np.float32) for b in range(B)], axis=0)
    return out
```

```python
from contextlib import ExitStack
import numpy as np
import concourse.bass as bass
import concourse.mybir as mybir
from concourse.bass_utils import run_bass_kernel_spmd

F32 = mybir.dt.float32
BF16 = mybir.dt.bfloat16
AF = mybir.ActivationFunctionType
ALU = mybir.AluOpType
AX = mybir.AxisListType

D = 1024
CH = 128
NCORES = 8
_UID = [0]


class Buf:
    __slots__ = ("name", "last_w", "readers", "excl")

    def __init__(self, name, excl=False):
        self.name = name
        self.last_w = None
        self.readers = []
        self.excl = excl


class Tile:
    def __init__(self, t, name, excl=False):
        self.t = t
        self.b = Buf(name, excl)

    def __getitem__(self, k):
        return self.t[k]


class Rot:
    def __init__(self, tiles):
        self.tiles = tiles
        self.i = 0

    def next(self):
        t = self.tiles[self.i % len(self.tiles)]
        self.slot = self.i % len(self.tiles)
        self.i += 1
        return t


class Op:
    __slots__ = ("eng", "fn", "dma_key", "waits", "signal", "sem", "val", "idx", "prog")


def _b(x):
    return x.b if isinstance(x, Tile) else x


class Prog:
    ENGS = ("pe", "act", "dve", "pool", "sp")

    def __init__(self):
        self.ops = []

    def add(self, eng, fn, reads=(), writes=(), dma_key=None):
        op = Op()
        op.eng, op.fn, op.dma_key = eng, fn, dma_key
        op.signal, op.sem, op.val = False, None, 0
        op.idx, op.prog = len(self.ops), self
        reads = [_b(x) for x in reads]
        writes = [_b(x) for x in writes]
        deps = []
        for b in reads:
            if b.last_w is not None:
                deps.append((b.last_w, True))
            if b.excl:
                deps.extend((r, False) for r in b.readers)
        for b in writes:
            if b.last_w is not None:
                deps.append((b.last_w, False))
            deps.extend((r, False) for r in b.readers)
        need, seen = [], set()
        for d, raw in deps:
            if d.prog is not self:
                continue
            if d.dma_key is None and dma_key is None and d.eng == eng:
                if eng == "pe" or not raw:
                    continue
            if d.idx in seen:
                continue
            seen.add(d.idx)
            d.signal = True
            need.append(d)
        op.waits = need
        for b in reads:
            b.readers.append(op)
        for b in writes:
            b.last_w = op
            b.readers = []
        self.ops.append(op)
        return op

    def pe(self, fn, reads=(), writes=()):
        return self.add("pe", fn, reads, writes)

    def act(self, fn, reads=(), writes=()):
        return self.add("act", fn, reads, writes)

    def dve(self, fn, reads=(), writes=()):
        return self.add("dve", fn, reads, writes)

    def pool(self, fn, reads=(), writes=()):
        return self.add("pool", fn, reads, writes)

    def dma(self, out, in_, key, reads=(), writes=(), q="sp", slow=False):
        if slow:
            return self.add(q, lambda e: e.dma_start(out=out, in_=in_, allow_slow_non_contiguous=True),
                            reads, writes, dma_key=key)
        return self.add(q, lambda e: e.dma_start(out=out, in_=in_), reads, writes, dma_key=key)

    def emit(self, nc, stack):
        def keyof(op):
            return ("dma", op.dma_key) if op.dma_key is not None else ("eng", op.eng)
        last = {}
        for op in self.ops:
            last[keyof(op)] = op
        for op in last.values():
            op.signal = True
        cnt, sems = {}, {}
        for op in self.ops:
            if not op.signal:
                continue
            k = keyof(op)
            cnt[k] = cnt.get(k, 0) + (16 if op.dma_key is not None else 1)
            op.val = cnt[k]
            if k not in sems:
                _UID[0] += 1
                sems[k] = nc.alloc_semaphore(name="s%d" % _UID[0])
            op.sem = sems[k]
        self.n_sems = len(sems)
        per_eng = {e: [] for e in self.ENGS}
        for op in self.ops:
            per_eng[op.eng].append(op)
        finals = [(sems[k], cnt[k]) for k in sems]

        def run(engname, e):
            waited = {}
            for op in per_eng[engname]:
                for d in op.waits:
                    key = id(d.sem)
                    if waited.get(key, 0) >= d.val:
                        continue
                    e.wait_ge(d.sem, d.val)
                    waited[key] = d.val
                ins = op.fn(e)
                if op.signal:
                    ins.then_inc(op.sem, 16 if op.dma_key is not None else 1)
            for s, v in finals:
                if waited.get(id(s), 0) < v:
                    e.wait_ge(s, v)

        with nc.Block() as block:
            block.tensor(lambda e: run("pe", e))
            block.scalar(lambda e: run("act", e))
            block.vector(lambda e: run("dve", e))
            block.gpsimd(lambda e: run("pool", e))
            block.sync(lambda e: run("sp", e))
        nc.clear_and_free_semaphores(list(sems.values()))
        nc.all_engine_barrier()


class Phase:
    def __init__(self, nc):
        self.nc = nc
        self.st = ExitStack()
        self.P = Prog()

    def sb(self, name, shape, dt):
        _UID[0] += 1
        t = self.st.enter_context(self.nc.sbuf_tensor("%s_%d" % (name, _UID[0]), list(shape), dt))
        return Tile(t, name)

    def ps(self, name, shape, dt=F32):
        _UID[0] += 1
        t = self.st.enter_context(self.nc.psum_tensor("%s_%d" % (name, _UID[0]), list(shape), dt))
        return Tile(t, name, excl=True)

    def rot(self, name, shape, dt, n):
        return Rot([self.sb("%s%d" % (name, i), shape, dt) for i in range(n)])

    def finish(self):
        self.P.emit(self.nc, self.st)
        self.st.close()


KINDS = (0, 1, 2, 0)
NORM_EPS = 1e-6


class Model:
    def __init__(self, L, CL=256, layers=(0, 1, 2, 3), final_norm=True):
        self.L, self.CL = L, CL
        self.NX, self.NC = L // CH, CL // CH
        self.layers = tuple(layers)
        self.final_norm = final_norm
        self.nc = nc = bass.Bass("TRN2", target_bir_lowering=False)
        self.inputs = {}
        self.outer = ExitStack()
        NT = L + CL

        def din(name, shape):
            self.inputs[name] = tuple(shape)
            return nc.dram_tensor(name, list(shape), F32, kind="ExternalInput").ap()

        self.x_in = din("x", [L, D])
        self.c_in = din("ctx", [CL, D])
        self.cc = din("cc", [128, 8, 2])
        self.ident_in = din("ident", [128, 128])
        self.rope_in = din("rope", [NT, 2, 128])
        self.dmat_in = din("dmat", [128, 6, 128])
        self.jcol_in = din("jcol", [128, 2])
        self.sel_in = din("sel2", [2, 2, 128])
        self.fg_in = din("final_g_bc", [128, D])
        self.lw = {}
        for i in self.layers:
            w = {}
            w["ada_w"] = din("ada_w%d" % i, [D, 3 * D])
            w["ada_bcol"] = din("ada_bcol%d" % i, [128, 24])
            w["ada_brow"] = din("ada_brow%d" % i, [2, D])
            w["ng_col"] = din("ng_col%d" % i, [128, 8])
            k = KINDS[i]
            if k == 0:
                w["w_in"] = din("ret_w_in%d" % i, [D, 6144])
                w["w_out"] = din("ret_w_out%d" % i, [2048, D])
                w["decay"] = din("ret_decay%d" % i, [128, 8])
            elif k == 1:
                w["w_in"] = din("gm_w_in%d" % i, [D, 6144])
                w["w_out"] = din("gm_w_out%d" % i, [2048, D])
                w["vg"] = din("gm_vg%d" % i, [128, 2048])
                w["wsT"] = din("gm_wsT%d" % i, [128, 8, 128])
                w["bsT"] = din("gm_bsT%d" % i, [128, 8])
            else:
                self._rw_inputs(w, i, din)
            self.lw[i] = w
        self.out = nc.dram_tensor("out", [L, D], F32, kind="ExternalOutput").ap()
        self.xs = [nc.dram_tensor("xs%d" % i, [NT, D], F32).ap() for i in range(2)]
        o = self.outer
        self.ident = Tile(o.enter_context(nc.sbuf_tensor("identb", [128, 128], BF16)), "ident")
        self.ident32 = Tile(o.enter_context(nc.sbuf_tensor("ident32", [128, 128], F32)), "ident32")
        self.mod = Tile(o.enter_context(nc.sbuf_tensor("mod", [128, 2, 2, 8], F32)), "mod")
        self.gate_dram = nc.dram_tensor("gate_dram", [2, 128, D], F32).ap()
        self.sel = Tile(o.enter_context(nc.sbuf_tensor("sel", [2, 2, 128], F32)), "sel")
        self._build()
        self.outer.close()

    def chunks_fwd(self):
        return [("c", i) for i in range(self.NC)] + [("x", i) for i in range(self.NX)]

    def chunks_bwd(self):
        return [("c", i) for i in reversed(range(self.NC))] + [("x", i) for i in reversed(range(self.NX))]

    def row0(self, ck):
        return ck[1] * CH if ck[0] == "c" else self.CL + ck[1] * CH

    def src_ap(self, li, ck):
        r0 = self.row0(ck)
        if li == 0:
            return (self.c_in if ck[0] == "c" else self.x_in)[ck[1] * CH:(ck[1] + 1) * CH, :]
        return self.xs[(li - 1) % 2][r0:r0 + CH, :]

    def dst_ap(self, li, ck):
        r0 = self.row0(ck)
        return self.xs[li % 2][r0:r0 + CH, :]

    def _build(self):
        nc = self.nc
        ph = Phase(nc)
        P = ph.P
        t32 = ph.sb("id32", [128, 128], F32)
        P.dma(self.ident32[:], self.ident_in[:, :], "c0", writes=[self.ident32])
        P.dve(lambda e: e.tensor_copy(out=self.ident[:], in_=self.ident32[:]), [self.ident32], [self.ident])
        P.dma(self.sel[:], self.sel_in[:, :, :], "c1", writes=[self.sel])
        ph.finish()
        for li, i in enumerate(self.layers):
            last = (li == len(self.layers) - 1)
            k = KINDS[i]
            if k == 0:
                self.retention_layer(li, i, last)
            elif k == 1:
                self.gmlp_layer(li, i, last)
            else:
                self.rwkv_layer(li, i, last)

    def emit_mod(self, ph, i):
        P, w = ph.P, self.lw[i]
        cc = ph.sb("cc", [128, 8, 2], F32)
        sg = ph.sb("sg", [128, 8, 2], F32)
        bcol = ph.sb("bcol", [128, 24], F32)
        brow = ph.sb("brow", [2, D], F32)
        ng = ph.sb("ng", [128, 8], F32)
        grow = ph.sb("grow", [2, D], F32)
        gate_t = [ph.sb("gate_t%d" % j, [128, D], F32) for j in range(2)]
        aw = ph.sb("adaw", [128, 8, 3 * D], F32)
        awk = [Tile(aw.t[:, k, :], "adaw%d" % k) for k in range(8)]
        pcol = ph.ps("pcol", [128, 16, 2], F32)
        prow = [ph.ps("prow%d" % n, [128, 512], F32) for n in range(2)]
        P.dma(cc[:], self.cc[:, :, :], "m0", writes=[cc])
        P.dma(bcol[:], w["ada_bcol"][:, :], "m1", writes=[bcol])
        P.dma(brow[:], w["ada_brow"][:, :], "m2", writes=[brow])
        P.dma(ng[:], w["ng_col"][:, :], "m3", writes=[ng])
        for k in range(8):
            P.dma(awk[k][:], w["ada_w"][k * 128:(k + 1) * 128, :], "ada%d" % k, writes=[awk[k]], q=("sp" if k % 2 == 0 else "pool"))
        P.act(lambda e: e.activation(out=sg[:], in_=cc[:], func=AF.Sigmoid), [cc], [sg])
        P.dve(lambda e: e.tensor_tensor(out=sg[:], in0=sg[:], in1=cc[:], op=ALU.mult), [sg, cc], [sg])
        for j in range(16):
            for k in range(8):
                P.pe(lambda e, j=j, k=k: e.matmul(pcol[:, j, :], lhsT=awk[k][:, j * 128:(j + 1) * 128], rhs=sg[:, k, :],
                                                  start=(k == 0), stop=(k == 7)), [awk[k], sg], [pcol])
        for n in range(2):
            for k in range(8):
                P.pe(lambda e, n=n, k=k: e.matmul(prow[n][0:2, :], lhsT=sg[:, k, :], rhs=awk[k][:, 2048 + n * 512:2048 + (n + 1) * 512],
                                                  start=(k == 0), stop=(k == 7)), [awk[k], sg], [prow[n]])
        tmp = ph.sb("modtmp", [128, 16, 2], F32)
        P.dve(lambda e: e.tensor_tensor(out=tmp[:], in0=pcol[:], in1=bcol[:, 0:16].unsqueeze(2).to_broadcast([128, 16, 2]), op=ALU.add),
              [pcol, bcol], [tmp])
        mod = self.mod
        for j in range(2):
            P.dve(lambda e, j=j: e.scalar_tensor_tensor(out=mod[:, 0, j, :], in0=tmp[:, 8:16, j], scalar=1.0, in1=ng[:], op0=ALU.add, op1=ALU.mult),
                  [tmp, ng], [mod])
            P.dve(lambda e, j=j: e.tensor_copy(out=mod[:, 1, j, :], in_=tmp[:, 0:8, j]), [tmp], [mod])
        for n in range(2):
            P.dve(lambda e, n=n: e.tensor_tensor(out=grow[:, n * 512:(n + 1) * 512], in0=prow[n][0:2, :], in1=brow[:, n * 512:(n + 1) * 512], op=ALU.add),
                  [prow[n], brow], [grow])
        for j in range(2):
            for n in range(2):
                P.pe(lambda e, j=j, n=n: e.matmul(prow[n][:, :], lhsT=self.sel[:, j, :], rhs=grow[:, n * 512:(n + 1) * 512], start=True, stop=True),
                     [self.sel, grow], [prow[n]])
                P.act(lambda e, j=j, n=n: e.activation(out=gate_t[j][:, n * 512:(n + 1) * 512], in_=prow[n][:, :], func=AF.Copy),
                      [prow[n]], [gate_t[j]])
        for j in range(2):
            P.dma(self.gate_dram[j], gate_t[j][:], "gst%d" % j, reads=[gate_t[j]], q="pool")

    def load_w(self, ph, Wt, src, col0, ncols, KT):
        P = ph.P
        views = []
        for k in range(KT):
            v = Tile(Wt.t[:, k, :], "%s_k%d" % (Wt.b.name, k))
            views.append(v)
            step = 2048
            for c0 in range(0, ncols, step):
                wdt = min(step, ncols - c0)
                P.dma(v.t[:, c0:c0 + wdt], src[k * 128:(k + 1) * 128, col0 + c0:col0 + c0 + wdt],
                      "w%s%d_%d" % (Wt.b.name, k, c0), writes=[v], q="pool")
        return views

    def front(self, ph, R, src, which, src_reads=()):
        P = ph.P
        mod = self.mod
        xt = R["xt"].next()
        P.dma(xt[:], src, "xt%d" % R["xt"].slot, reads=list(src_reads), writes=[xt])
        st = R["st"].next()
        junk = R["junk"]
        P.act(lambda e: e.activation(out=junk[:], in_=xt[:], func=AF.Square, accum_out=st[:, 0:1]), [xt], [junk, st])
        P.act(lambda e: e.activation(out=st[:, 1:2], in_=st[:, 0:1], func=AF.Sqrt, scale=1.0 / D, bias=NORM_EPS), [st], [st])
        P.dve(lambda e: e.reciprocal(out=st[:, 2:3], in_=st[:, 1:2]), [st], [st])
        xn = R["xn"].next()
        P.dve(lambda e: e.tensor_scalar(out=xn[:], in0=xt[:], scalar1=st[:, 2:3], scalar2=None, op0=ALU.mult), [xt, st], [xn])
        ptr = R["ptrx"]
        for k in range(8):
            P.pe(lambda e, k=k: e.transpose(out=ptr[:, k, :], in_=xn[:, k * 128:(k + 1) * 128], identity=self.ident[:]),
                 [xn, self.ident], [ptr])
        hT = R["hT"].next()
        for k in range(8):
            P.act(lambda e, k=k: e.activation(out=hT[:, k, :], in_=ptr[:, k, :], func=AF.Identity,
                                              scale=mod[:, 0, which, k:k + 1], bias=mod[:, 1, which, k:k + 1]),
                  [ptr, mod], [hT])
        return hT, xt

    def front_bufs(self, ph, n_xn=2, n_xt=2, n_hT=2, junk=True):
        return {
            "xt": ph.rot("xt", [128, D], F32, n_xt),
            "st": ph.rot("st", [128, 4], F32, 4),
            "junk": ph.sb("junk", [128, D], BF16) if junk else None,
            "xn": ph.rot("xn", [128, D], BF16, n_xn),
            "hT": ph.rot("hT", [128, 8, 128], BF16, n_hT),
            "ptrx": ph.ps("ptrx", [128, 8, 128], BF16),
        }

    def tail(self, ph, R, gated, Wo, li, ck, xres, last, KT=16):
        P = ph.P
        which = 1 if ck[0] == "c" else 0
        gT = R["gT"].next()
        for half in range(KT // 8):
            ptg = R["ptg"][half]
            for k in range(8):
                kk = half * 8 + k
                P.pe(lambda e, k=k, kk=kk, ptg=ptg: e.transpose(out=ptg[:, k, :], in_=gated[:, kk * 128:(kk + 1) * 128], identity=self.ident[:]),
                     [gated, self.ident], [ptg])
            if half == 0:
                P.act(lambda e, half=half, ptg=ptg: e.activation(out=gT[:, half * 8:(half + 1) * 8, :], in_=ptg[:], func=AF.Copy), [ptg], [gT])
            else:
                P.dve(lambda e, half=half, ptg=ptg: e.tensor_copy(out=gT[:, half * 8:(half + 1) * 8, :], in_=ptg[:]), [ptg], [gT])
        xo = R["xo"].next()
        for n in range(2):
            po = R["pout"]
            for k in range(KT):
                P.pe(lambda e, n=n, k=k: e.matmul(po[:], lhsT=gT[:, k, :], rhs=Wo[k][:, n * 512:(n + 1) * 512], start=(k == 0), stop=(k == KT - 1)),
                     [gT, Wo[k]], [po])
            P.dve(lambda e, n=n: e.tensor_tensor(out=xo[:, n * 512:(n + 1) * 512], in0=po[:], in1=self.gate[which][:, n * 512:(n + 1) * 512], op=ALU.mult),
                  [po, self.gate[which]], [xo])
        P.pool(lambda e: e.tensor_tensor(out=xo[:], in0=xo[:], in1=xres[:], op=ALU.add), [xo, xres], [xo])
        if last and self.final_norm:
            st = R["st"].next()
            junk = R["junk"]
            P.act(lambda e: e.activation(out=junk[:], in_=xo[:], func=AF.Square, accum_out=st[:, 0:1]), [xo], [junk, st])
            P.act(lambda e: e.activation(out=st[:, 1:2], in_=st[:, 0:1], func=AF.Sqrt, scale=1.0 / D, bias=NORM_EPS), [st], [st])
            P.dve(lambda e: e.reciprocal(out=st[:, 2:3], in_=st[:, 1:2]), [st], [st])
            P.dve(lambda e: e.scalar_tensor_tensor(out=xo[:], in0=xo[:], scalar=st[:, 2:3], in1=self.fg[:], op0=ALU.mult, op1=ALU.mult),
                  [xo, st, self.fg], [xo])
            dst = self.out[ck[1] * CH:(ck[1] + 1) * CH, :]
        elif last:
            dst = self.out[ck[1] * CH:(ck[1] + 1) * CH, :]
        else:
            dst = self.dst_ap(li, ck)
        P.dma(dst, xo[:], "xo%d" % R["xo"].slot, reads=[xo], q="pool")

    def tail_bufs(self, ph, n_gT=2, n_xo=2, last=False):
        if last and self.final_norm:
            self.fg = ph.sb("fg", [128, D], F32)
            ph.P.dma(self.fg[:], self.fg_in[:, :], "fgld", writes=[self.fg])
        self.gate = [ph.sb("gate%d" % j, [128, D], F32) for j in range(2)]
        for j in range(2):
            ph.P.dma(self.gate[j][:], self.gate_dram[j], "gld%d" % j, writes=[self.gate[j]])
        return {
            "gT": ph.rot("gT", [128, 16, 128], BF16, n_gT),
            "ptg": [ph.ps("ptg%d" % h, [128, 8, 128], BF16) for h in range(2)],
            "pout": ph.ps("pout", [128, 512], F32),
            "xo": ph.rot("xo", [128, D], F32, n_xo),
        }

    def retention_layer(self, li, i, last):
        nc, w = self.nc, self.lw[i]
        NCH = self.NX + self.NC
        if not hasattr(self, "scr_q"):
            self.scr_q = nc.dram_tensor("scr_q", [NCH, 128, 1024], BF16).ap()
            self.scr_k = nc.dram_tensor("scr_k", [NCH, 128, 1024], BF16).ap()
            self.scr_v = nc.dram_tensor("scr_v", [NCH, 128, 2048], BF16).ap()
            self.scr_o = nc.dram_tensor("scr_o", [NCH, 128, 2048], F32).ap()
            self.rt_cd = Tile(self.outer.enter_context(nc.sbuf_tensor("rt_cd", [128, 8], F32)), "rt_cd")
        ph = Phase(nc)
        self.emit_mod(ph, i)
        ph.finish()

        ph = Phase(nc)
        P = ph.P
        Wt = ph.sb("Wqkv", [128, 8, 4096], BF16)
        Wk = self.load_w(ph, Wt, w["w_in"], 0, 4096, 8)
        R = self.front_bufs(ph)
        dec = ph.sb("dec", [128, 8], F32)
        lg = ph.sb("lg", [128, 8], F32)
        dm = ph.sb("dm", [128, 6, 128], F32)
        jc = ph.sb("jc", [128, 2], F32)
        tA = ph.sb("tA", [128, 128], F32)
        tB = ph.sb("tB", [128, 128], F32)
        MT = ph.sb("MT", [128, 4, 128], F32)
        QDf = ph.sb("QDf", [128, 8, 128], BF16)
        QDb = ph.sb("QDb", [128, 8, 128], BF16)
        kdec = ph.sb("kdec", [128, 8], F32)
        cd = self.rt_cd
        P.dma(dec[:], w["decay"][:, :], "t0", writes=[dec])
        P.dma(dm[:], self.dmat_in[:, :, :], "t1", writes=[dm])
        P.dma(jc[:], self.jcol_in[:, :], "t2", writes=[jc])
        P.act(lambda e: e.activation(out=lg[:], in_=dec[:], func=AF.Exp, scale=-1.0), [dec], [lg])
        P.act(lambda e: e.activation(out=lg[:], in_=lg[:], func=AF.Ln, bias=1.0), [lg], [lg])
        P.dve(lambda e: e.tensor_scalar(out=lg[:], in0=lg[:], scalar1=-1.0, scalar2=None, op0=ALU.mult), [lg], [lg])
        P.act(lambda e: e.activation(out=cd[:], in_=lg[:], func=AF.Exp, scale=128.0), [lg], [cd])
        P.dve(lambda e: e.tensor_scalar(out=kdec[:, 0:4], in0=lg[:, 0:4], scalar1=jc[:, 0:1], scalar2=None, op0=ALU.mult), [lg, jc], [kdec])
        P.dve(lambda e: e.tensor_scalar(out=kdec[:, 4:8], in0=lg[:, 4:8], scalar1=jc[:, 1:2], scalar2=None, op0=ALU.mult), [lg, jc], [kdec])
        P.act(lambda e: e.activation(out=kdec[:], in_=kdec[:], func=AF.Exp), [kdec], [kdec])
        P.dve(lambda e: e.tensor_scalar(out=kdec[:], in0=kdec[:], scalar1=0.0625, scalar2=None, op0=ALU.mult), [kdec], [kdec])
        for h in range(4):
            P.act(lambda e, h=h: e.activation(out=tA[:], in_=dm[:, 0, :], func=AF.Exp, scale=lg[:, h:h + 1]), [dm, lg, MT], [tA])
            P.act(lambda e, h=h: e.activation(out=tB[:], in_=dm[:, 1, :], func=AF.Exp, scale=lg[:, 4 + h:5 + h]), [dm, lg, MT], [tB])
            P.dve(lambda e: e.tensor_tensor(out=tA[:], in0=tA[:], in1=dm[:, 2, :], op=ALU.mult), [tA, dm], [tA])
            P.dve(lambda e: e.tensor_tensor(out=tB[:], in0=tB[:], in1=dm[:, 3, :], op=ALU.mult), [tB, dm], [tB])
            P.dve(lambda e: e.tensor_tensor(out=tA[:], in0=tA[:], in1=tB[:], op=ALU.add), [tA, tB], [tA])
            P.dve(lambda e, h=h: e.tensor_scalar(out=MT[:, h, :], in0=tA[:], scalar1=0.0625, scalar2=None, op0=ALU.mult), [tA], [MT])
            for r in range(2):
                P.act(lambda e, h=h, r=r: e.activation(out=QDf[:, 2 * h + r, :], in_=dm[:, 4, :], func=AF.Exp, scale=lg[:, h:h + 1]), [dm, lg], [QDf])
                P.act(lambda e, h=h, r=r: e.activation(out=QDb[:, 2 * h + r, :], in_=dm[:, 5, :], func=AF.Exp, scale=lg[:, 4 + h:5 + h]), [dm, lg], [QDb])
        mm = Rot([ph.ps("mm%d" % j, [128, 512], F32) for j in range(2)])
        ptq = ph.ps("ptq", [128, 8, 128], BF16)
        ptk = ph.ps("ptk", [128, 8, 128], BF16)
        psc = ph.ps("psc", [128, 4, 128], F32)
        po = ph.ps("po", [128, 512], F32)
        pst = ph.ps("pst", [128, 512], F32)
        cs = ph.rot("cs", [128, 2, 2, 128], F32, 2)
        rtmp = ph.rot("rtmp", [128, 4, 2, 128], F32, 2)
        q_r = ph.sb("q_r", [128, 1024], BF16)
        k_r = ph.sb("k_r", [128, 1024], BF16)
        qT = ph.rot("qT", [128, 8, 128], BF16, 2)
        kT = ph.rot("kT", [128, 8, 128], BF16, 2)
        qTf = ph.rot("qTf", [128, 8, 128], BF16, 2)
        qTb = ph.rot("qTb", [128, 8, 128], BF16, 2)
        kf = ph.rot("kf", [128, 1024], BF16, 2)
        kb = ph.rot("kb", [128, 1024], BF16, 2)
        vsb = ph.rot("vsb", [128, 2048], BF16, 2)
        sT = ph.rot("sT", [128, 4, 128], BF16, 2)
        osb = ph.rot("osb", [128, 2048], F32, 2)
        Sbf = ph.sb("Sbf", [128, 8, 512], BF16)
        Sbk = [Tile(Sbf.t[:, k, :], "Sb%d" % k) for k in range(8)]
        for k in range(8):
            P.pool(lambda e, k=k: e.memset(Sbk[k][:], 0.0), [], [Sbk[k]])
        cdI = self.ret_cdI(ph)

        for ck in self.chunks_fwd():
            which = 1 if ck[0] == "c" else 0
            r0 = self.row0(ck)
            cidx = r0 // CH
            hT, xt = self.front(ph, R, self.src_ap(li, ck), which)
            c_ = cs.next()
            for r in range(2):
                P.dma(c_[:, :, r, :], self.rope_in[r0:r0 + CH, :, :], "cs%d_%d" % (cs.slot, r), writes=[c_])
            v_ = vsb.next()
            for n in range(8):
                bank = mm.next()
                for k in range(8):
                    P.pe(lambda e, bank=bank, n=n, k=k, hT=hT: e.matmul(bank[:], lhsT=hT[:, k, :], rhs=Wk[k][:, n * 512:(n + 1) * 512],
                                                                        start=(k == 0), stop=(k == 7)), [hT, Wk[k]], [bank])
                if n < 4:
                    dst = q_r if n < 2 else k_r
                    tmp = rtmp.next()
                    pb = bank[:].rearrange("p (h t c) -> p h t c", h=2, t=2)
                    t1, t2 = pb[:, :, 0, :], pb[:, :, 1, :]
                    cos2, sin2 = c_[:, 0, :, :], c_[:, 1, :, :]
                    P.dve(lambda e, tmp=tmp, t1=t1, cos2=cos2: e.tensor_tensor(out=tmp[:, 0], in0=t1, in1=cos2, op=ALU.mult), [bank, c_], [tmp])
                    P.dve(lambda e, tmp=tmp, t2=t2, sin2=sin2: e.tensor_tensor(out=tmp[:, 1], in0=t2, in1=sin2, op=ALU.mult), [bank, c_], [tmp])
                    P.dve(lambda e, tmp=tmp, t1=t1, sin2=sin2: e.tensor_tensor(out=tmp[:, 2], in0=t1, in1=sin2, op=ALU.mult), [bank, c_], [tmp])
                    P.dve(lambda e, tmp=tmp, t2=t2, cos2=cos2: e.tensor_tensor(out=tmp[:, 3], in0=t2, in1=cos2, op=ALU.mult), [bank, c_], [tmp])
                    dv = dst[:].rearrange("p (h t c) -> p h t c", h=4, t=2)
                    h0 = 2 * (n % 2)
                    P.pool(lambda e, tmp=tmp, dv=dv, h0=h0: e.tensor_tensor(out=dv[:, h0:h0 + 2, 0, :], in0=tmp[:, 0], in1=tmp[:, 1], op=ALU.subtract),
                           [tmp], [dst])
                    P.pool(lambda e, tmp=tmp, dv=dv, h0=h0: e.tensor_tensor(out=dv[:, h0:h0 + 2, 1, :], in0=tmp[:, 2], in1=tmp[:, 3], op=ALU.add),
                           [tmp], [dst])
                else:
                    P.act(lambda e, bank=bank, n=n, v_=v_: e.activation(out=v_[:, (n - 4) * 512:(n - 3) * 512], in_=bank[:], func=AF.Copy), [bank], [v_])
            qT_, kT_, qTf_, qTb_ = qT.next(), kT.next(), qTf.next(), qTb.next()
            for k in range(8):
                P.pe(lambda e, k=k: e.transpose(out=ptq[:, k, :], in_=q_r[:, k * 128:(k + 1) * 128], identity=self.ident[:]), [q_r, self.ident], [ptq])
            P.act(lambda e, qT_=qT_: e.activation(out=qT_[:], in_=ptq[:], func=AF.Copy), [ptq], [qT_])
            for k in range(8):
                P.pe(lambda e, k=k: e.transpose(out=ptk[:, k, :], in_=k_r[:, k * 128:(k + 1) * 128], identity=self.ident[:]), [k_r, self.ident], [ptk])
            P.act(lambda e, kT_=kT_: e.activation(out=kT_[:], in_=ptk[:], func=AF.Copy), [ptk], [kT_])
            P.dve(lambda e, qT_=qT_, qTf_=qTf_: e.tensor_tensor(out=qTf_[:], in0=qT_[:], in1=QDf[:], op=ALU.mult), [qT_, QDf], [qTf_])
            P.dve(lambda e, qT_=qT_, qTb_=qTb_: e.tensor_tensor(out=qTb_[:], in0=qT_[:], in1=QDb[:], op=ALU.mult), [qT_, QDb], [qTb_])
            kf_, kb_ = kf.next(), kb.next()
            krv = k_r[:].rearrange("p (h c) -> p h c", h=4)
            P.pool(lambda e, kf_=kf_: e.tensor_tensor(out=kf_[:].rearrange("p (h c) -> p h c", h=4), in0=krv,
                                                      in1=kdec[:, 0:4].unsqueeze(2).to_broadcast([128, 4, 256]), op=ALU.mult), [k_r, kdec], [kf_])
            P.pool(lambda e, kb_=kb_: e.tensor_tensor(out=kb_[:].rearrange("p (h c) -> p h c", h=4), in0=krv,
                                                      in1=kdec[:, 4:8].unsqueeze(2).to_broadcast([128, 4, 256]), op=ALU.mult), [k_r, kdec], [kb_])
            for h in range(4):
                for hf in range(2):
                    P.pe(lambda e, h=h, hf=hf, kT_=kT_, qT_=qT_: e.matmul(psc[:, h, :], lhsT=kT_[:, 2 * h + hf, :], rhs=qT_[:, 2 * h + hf, :],
                                                                          start=(hf == 0), stop=(hf == 1)), [kT_, qT_], [psc])
            sT_ = sT.next()
            P.dve(lambda e, sT_=sT_: e.tensor_tensor(out=sT_[:], in0=psc[:], in1=MT[:], op=ALU.mult), [psc, MT], [sT_])
            o_ = osb.next()
            for h in range(4):
                P.pe(lambda e, h=h, sT_=sT_, v_=v_: e.matmul(po[:], lhsT=sT_[:, h, :], rhs=v_[:, h * 512:(h + 1) * 512], start=True, stop=False), [sT_, v_], [po])
                for hf in range(2):
                    kt = 2 * h + hf
                    P.pe(lambda e, kt=kt, hf=hf, qTf_=qTf_: e.matmul(po[:], lhsT=qTf_[:, kt, :], rhs=Sbk[kt][:], start=False, stop=(hf == 1)),
                         [qTf_, Sbk[kt]], [po])
                P.act(lambda e, h=h, o_=o_: e.activation(out=o_[:, h * 512:(h + 1) * 512], in_=po[:], func=AF.Copy), [po], [o_])
            self.ret_state_update(P, pst, kf_, v_, Sbk, cdI, 0)
            P.dma(self.scr_q[cidx].rearrange("p (k t) -> p k t", k=8), qTb_[:], "sq%d" % qTb.slot, reads=[qTb_], q="pool")
            P.dma(self.scr_k[cidx], kb_[:], "sk%d" % kb.slot, reads=[kb_], q="pool")
            P.dma(self.scr_v[cidx], v_[:], "sv%d" % vsb.slot, reads=[v_], q="pool")
            P.dma(self.scr_o[cidx], o_[:], "so%d" % osb.slot, reads=[o_], q="pool")
        ph.finish()

        ph = Phase(nc)
        P = ph.P
        Wzt = ph.sb("Wz", [128, 8, 2048], BF16)
        Wz = self.load_w(ph, Wzt, w["w_in"], 4096, 2048, 8)
        Wot = ph.sb("Wo", [128, 16, 1024], BF16)
        Wo = self.load_w(ph, Wot, w["w_out"], 0, 1024, 16)
        R = self.front_bufs(ph)
        R.update(self.tail_bufs(ph, last=last))
        mm = Rot([ph.ps("mm%d" % j, [128, 512], F32) for j in range(2)])
        po = ph.ps("po", [128, 512], F32)
        pst = ph.ps("pst", [128, 512], F32)
        qTb = ph.rot("qTb", [128, 8, 128], BF16, 2)
        kb = ph.rot("kb", [128, 1024], BF16, 2)
        vsb = ph.rot("vsb", [128, 2048], BF16, 2)
        osb = ph.rot("osb", [128, 2048], F32, 2)
        sz = ph.rot("sz", [128, 2048], F32, 2)
        gated = ph.rot("gated", [128, 2048], BF16, 2)
        st4 = ph.rot("st4", [128, 12], F32, 2)
        Sbf = ph.sb("Sbf", [128, 8, 512], BF16)
        Sbk = [Tile(Sbf.t[:, k, :], "Sb%d" % k) for k in range(8)]
        for k in range(8):
            P.pool(lambda e, k=k: e.memset(Sbk[k][:], 0.0), [], [Sbk[k]])
        cdI = self.ret_cdI(ph)
        cd = self.rt_cd
        for ck in self.chunks_bwd():
            which = 1 if ck[0] == "c" else 0
            cidx = self.row0(ck) // CH
            need_out = not (last and ck[0] == "c")
            kb_, v_ = kb.next(), vsb.next()
            P.dma(kb_[:], self.scr_k[cidx], "lk%d" % kb.slot, writes=[kb_])
            P.dma(v_[:], self.scr_v[cidx], "lv%d" % vsb.slot, writes=[v_])
            if need_out:
                q_, o_ = qTb.next(), osb.next()
                P.dma(q_[:], self.scr_q[cidx].rearrange("p (k t) -> p k t", k=8), "lq%d" % qTb.slot, writes=[q_])
                P.dma(o_[:], self.scr_o[cidx], "lo%d" % osb.slot, writes=[o_])
                hT, xt = self.front(ph, R, self.src_ap(li, ck), which)
                sz_ = sz.next()
                for n in range(4):
                    bank = mm.next()
                    for k in range(8):
                        P.pe(lambda e, bank=bank, n=n, k=k, hT=hT: e.matmul(bank[:], lhsT=hT[:, k, :], rhs=Wz[k][:, n * 512:(n + 1) * 512],
                                                                            start=(k == 0), stop=(k == 7)), [hT, Wz[k]], [bank])
                    P.act(lambda e, bank=bank, n=n, sz_=sz_: e.activation(out=sz_[:, n * 512:(n + 1) * 512], in_=bank[:], func=AF.Silu), [bank], [sz_])
                s4 = st4.next()
                junk = R["junk"]
                for h in range(4):
                    for hf in range(2):
                        kt = 2 * h + hf
                        P.pe(lambda e, kt=kt, hf=hf, q_=q_: e.matmul(po[:], lhsT=q_[:, kt, :], rhs=Sbk[kt][:], start=(hf == 0), stop=(hf == 1)),
                             [q_, Sbk[kt]], [po])
                    P.dve(lambda e, h=h, o_=o_: e.tensor_tensor(out=o_[:, h * 512:(h + 1) * 512], in0=po[:], in1=o_[:, h * 512:(h + 1) * 512], op=ALU.add),
                          [po, o_], [o_])
                    P.act(lambda e, h=h, o_=o_, s4=s4: e.activation(out=junk[:, 0:512], in_=o_[:, h * 512:(h + 1) * 512], func=AF.Square, accum_out=s4[:, h:h + 1]),
                          [o_], [junk, s4])
                P.act(lambda e, s4=s4: e.activation(out=s4[:, 4:8], in_=s4[:, 0:4], func=AF.Sqrt, scale=1.0 / 512, bias=NORM_EPS), [s4], [s4])
                P.dve(lambda e, s4=s4: e.reciprocal(out=s4[:, 8:12], in_=s4[:, 4:8]), [s4], [s4])
                g_ = gated.next()
                for h in range(4):
                    P.dve(lambda e, h=h, o_=o_, s4=s4, sz_=sz_, g_=g_: e.scalar_tensor_tensor(
                        out=g_[:, h * 512:(h + 1) * 512], in0=o_[:, h * 512:(h + 1) * 512], scalar=s4[:, 8 + h:9 + h],
                        in1=sz_[:, h * 512:(h + 1) * 512], op0=ALU.mult, op1=ALU.mult), [o_, s4, sz_], [g_])
                self.tail(ph, R, g_, Wo, li, ck, xt, last)
            self.ret_state_update(P, pst, kb_, v_, Sbk, cdI, 4)
        ph.finish()

    def ret_state_update(self, P, pst, kd, v_, Sbk, cdI, c0):
        for kt in range(8):
            h = kt // 2
            P.pe(lambda e, kt=kt, h=h: e.matmul(pst[:], lhsT=cdI[:, c0 + h, :], rhs=Sbk[kt][:], start=True, stop=False), [cdI, Sbk[kt]], [pst])
            P.pe(lambda e, kt=kt, h=h: e.matmul(pst[:], lhsT=kd[:, kt * 128:(kt + 1) * 128], rhs=v_[:, h * 512:(h + 1) * 512], start=False, stop=True),
                 [kd, v_], [pst])
            P.act(lambda e, kt=kt: e.activation(out=Sbk[kt][:], in_=pst[:], func=AF.Copy), [pst], [Sbk[kt]])

    def ret_cdI(self, ph):
        cdI = ph.sb("cdI", [128, 8, 128], BF16)
        for j in range(8):
            ph.P.dve(lambda e, j=j: e.tensor_scalar(out=cdI[:, j, :], in0=self.ident32[:], scalar1=self.rt_cd[:, j:j + 1], scalar2=None, op0=ALU.mult),
                     [self.ident32, self.rt_cd], [cdI])
        return cdI

    def gmlp_layer(self, li, i, last):
        nc, w = self.nc, self.lw[i]
        ph = Phase(nc)
        self.emit_mod(ph, i)
        ph.finish()
        ph = Phase(nc)
        P = ph.P
        Wt = ph.sb("Wuvz", [128, 8, 6144], BF16)
        Wk = self.load_w(ph, Wt, w["w_in"], 0, 6144, 8)
        Wot = ph.sb("Wo", [128, 16, 1024], BF16)
        Wo = self.load_w(ph, Wot, w["w_out"], 0, 1024, 16)
        R = self.front_bufs(ph, n_xn=1)
        R.update(self.tail_bufs(ph, n_gT=1, n_xo=1, last=last))
        mm = Rot([ph.ps("mm%d" % j, [128, 512], F32) for j in range(2)])
        spb = Rot([ph.ps("spb%d" % j, [128, 512], F32) for j in range(2)])
        wsT = ph.sb("wsT", [128, 8, 128], BF16)
        bsT = ph.sb("bsT", [128, 8], F32)
        vg = ph.sb("vg", [128, 2048], F32)
        P.dma(wsT[:], w["wsT"][:, :, :], "g0", writes=[wsT], q="pool")
        P.dma(bsT[:], w["bsT"][:, :], "g1", writes=[bsT])
        P.dma(vg[:], w["vg"][:, :], "g2", writes=[vg])
        usb = ph.rot("usb", [128, 2048], F32, 1)
        vsb = ph.rot("vsb", [128, 2048], F32, 1)
        szb = ph.rot("szb", [128, 2048], BF16, 1)
        vnb = ph.rot("vnb", [128, 2048], BF16, 1)
        gated = ph.rot("gated", [128, 2048], BF16, 1)
        stv = ph.rot("stv", [128, 16], F32, 2)
        junk = R["junk"]
        order = self.chunks_fwd()
        if last:
            order = [ck for ck in order if ck[0] == "x"]
        for ck in order:
            which = 1 if ck[0] == "c" else 0
            hT, xt = self.front(ph, R, self.src_ap(li, ck), which)
            u_, v_, z_, s_ = usb.next(), vsb.next(), szb.next(), stv.next()
            for n in range(12):
                bank = mm.next()
                for k in range(8):
                    P.pe(lambda e, bank=bank, n=n, k=k, hT=hT: e.matmul(bank[:], lhsT=hT[:, k, :], rhs=Wk[k][:, n * 512:(n + 1) * 512],
                                                                        start=(k == 0), stop=(k == 7)), [hT, Wk[k]], [bank])
                if n < 4:
                    P.act(lambda e, bank=bank, n=n, u_=u_: e.activation(out=u_[:, n * 512:(n + 1) * 512], in_=bank[:], func=AF.Copy), [bank], [u_])
                elif n < 8:
                    P.act(lambda e, bank=bank, n=n, v_=v_, s_=s_: e.activation(out=v_[:, (n - 4) * 512:(n - 3) * 512], in_=bank[:], func=AF.Identity,
                                                                              accum_out=s_[:, n - 4:n - 3]), [bank], [v_, s_])
                else:
                    P.act(lambda e, bank=bank, n=n, z_=z_: e.activation(out=z_[:, (n - 8) * 512:(n - 7) * 512], in_=bank[:], func=AF.Silu), [bank], [z_])
            for hh in range(2):
                P.act(lambda e, v_=v_, s_=s_, hh=hh: e.activation(out=junk[:], in_=v_[:, hh * 1024:(hh + 1) * 1024], func=AF.Square,
                                                                  accum_out=s_[:, 11 + hh:12 + hh]), [v_], [junk, s_])
            P.dve(lambda e, s_=s_: e.tensor_tensor(out=s_[:, 4:5], in0=s_[:, 11:12], in1=s_[:, 12:13], op=ALU.add), [s_], [s_])
            P.dve(lambda e, s_=s_: e.tensor_reduce(out=s_[:, 5:6], in_=s_[:, 0:4], axis=AX.X, op=ALU.add), [s_], [s_])
            P.dve(lambda e, s_=s_: e.tensor_scalar(out=s_[:, 5:6], in0=s_[:, 5:6], scalar1=1.0 / 2048, scalar2=None, op0=ALU.mult), [s_], [s_])
            P.dve(lambda e, s_=s_: e.tensor_tensor(out=s_[:, 6:7], in0=s_[:, 5:6], in1=s_[:, 5:6], op=ALU.mult), [s_], [s_])
            P.dve(lambda e, s_=s_: e.scalar_tensor_tensor(out=s_[:, 7:8], in0=s_[:, 4:5], scalar=1.0 / 2048, in1=s_[:, 6:7], op0=ALU.mult, op1=ALU.subtract),
                  [s_], [s_])
            P.act(lambda e, s_=s_: e.activation(out=s_[:, 8:9], in_=s_[:, 7:8], func=AF.Sqrt, scale=1.0, bias=NORM_EPS), [s_], [s_])
            P.dve(lambda e, s_=s_: e.reciprocal(out=s_[:, 9:10], in_=s_[:, 8:9]), [s_], [s_])
            P.dve(lambda e, s_=s_: e.scalar_tensor_tensor(out=s_[:, 10:11], in0=s_[:, 5:6], scalar=-1.0, in1=s_[:, 9:10], op0=ALU.mult, op1=ALU.mult),
                  [s_], [s_])
            P.act(lambda e, v_=v_, s_=s_: e.activation(out=v_[:], in_=v_[:], func=AF.Identity, scale=s_[:, 9:10], bias=s_[:, 10:11]), [v_, s_], [v_])
            vn_ = vnb.next()
            P.dve(lambda e, v_=v_, vn_=vn_: e.tensor_tensor(out=vn_[:], in0=v_[:], in1=vg[:], op=ALU.mult), [v_, vg], [vn_])
            for g in range(8):
                sb_ = spb.next()
                P.pe(lambda e, g=g, sb_=sb_, vn_=vn_: e.matmul(sb_[:, 0:256], lhsT=wsT[:, g, :], rhs=vn_[:, g * 256:(g + 1) * 256], start=True, stop=True),
                     [wsT, vn_], [sb_])
                P.dve(lambda e, g=g, sb_=sb_, u_=u_: e.scalar_tensor_tensor(out=u_[:, g * 256:(g + 1) * 256], in0=sb_[:, 0:256], scalar=bsT[:, g:g + 1],
                                                                           in1=u_[:, g * 256:(g + 1) * 256], op0=ALU.add, op1=ALU.mult), [sb_, bsT, u_], [u_])
            g_ = gated.next()
            P.pool(lambda e, u_=u_, z_=z_, g_=g_: e.tensor_tensor(out=g_[:], in0=u_[:], in1=z_[:], op=ALU.mult), [u_, z_], [g_])
            self.tail(ph, R, g_, Wo, li, ck, xt, last)
        ph.finish()

    def _rw_inputs(self, w, i, din):
        w["mu"] = din("rw_mu%d" % i, [128, 6, 8])
        w["rkvg"] = din("rw_rkvg%d" % i, [4, D, D])
        w["w1"] = din("rw_w1%d" % i, [2, D, 64])
        w["a1"] = din("rw_a1%d" % i, [2, D, 64])
        w["w2"] = din("rw_w2%d" % i, [2, 64, D])
        w["a2"] = din("rw_a2%d" % i, [2, 64, D])
        w["rows"] = din("rw_rows%d" % i, [8, D])
        w["bc"] = din("rw_bc%d" % i, [128, 5, D])
        w["w_out"] = din("rw_wout%d" % i, [D, D])
        w["masks"] = din("rw_masks%d" % i, [2, 128, 4, 128])
        w["sel8"] = din("rw_sel8%d" % i, [8, 8, 128])
        w["negc"] = din("rw_negc%d" % i, [128, 1])
        w["bmask"] = din("rw_bmask%d" % i, [128, 4, 128])
        w["cmask"] = din("rw_cmask%d" % i, [2, 128, 7, 128])

    def rw_shift(self, P, sh, hc, hp, hn, kind):
        if kind == "x":
            P.act(lambda e: e.activation(out=sh[:, 0:2, 1:128], in_=hc[:, 0:2, 0:127], func=AF.Copy), [hc], [sh])
            P.pool(lambda e: e.memset(sh[:, 0:2, :].rearrange("p k (r c) -> p k r c", c=64)[:, :, :, 0:1], 0.0), [], [sh])
            P.act(lambda e: e.activation(out=sh[:, 2:4, 0:127], in_=hc[:, 2:4, 1:128], func=AF.Copy), [hc], [sh])
            P.pool(lambda e: e.memset(sh[:, 2:4, :].rearrange("p k (r c) -> p k r c", c=64)[:, :, :, 63:64], 0.0), [], [sh])
            P.act(lambda e: e.activation(out=sh[:, 4:6, 64:128], in_=hc[:, 4:6, 0:64], func=AF.Copy), [hc], [sh])
            if hp is not None:
                P.act(lambda e: e.activation(out=sh[:, 4:6, 0:64], in_=hp[:, 4:6, 64:128], func=AF.Copy), [hp], [sh])
            else:
                P.pool(lambda e: e.memset(sh[:, 4:6, 0:64], 0.0), [], [sh])
            P.act(lambda e: e.activation(out=sh[:, 6:8, 0:64], in_=hc[:, 6:8, 64:128], func=AF.Copy), [hc], [sh])
            if hn is not None:
                P.act(lambda e: e.activation(out=sh[:, 6:8, 64:128], in_=hn[:, 6:8, 0:64], func=AF.Copy), [hn], [sh])
            else:
                P.pool(lambda e: e.memset(sh[:, 6:8, 64:128], 0.0), [], [sh])
        else:
            P.act(lambda e: e.activation(out=sh[:, 0:4, 1:128], in_=hc[:, 0:4, 0:127], func=AF.Copy), [hc], [sh])
            if hp is not None:
                P.act(lambda e: e.activation(out=sh[:, 0:4, 0:1], in_=hp[:, 0:4, 127:128], func=AF.Copy), [hp], [sh])
            else:
                P.pool(lambda e: e.memset(sh[:, 0:4, 0:1], 0.0), [], [sh])
            P.act(lambda e: e.activation(out=sh[:, 4:8, 0:127], in_=hc[:, 4:8, 1:128], func=AF.Copy), [hc], [sh])
            if hn is not None:
                P.act(lambda e: e.activation(out=sh[:, 4:8, 127:128], in_=hn[:, 4:8, 0:1], func=AF.Copy), [hn], [sh])
            else:
                P.pool(lambda e: e.memset(sh[:, 4:8, 127:128], 0.0), [], [sh])

    def rw_neighbors(self, ck):
        n = self.NC if ck[0] == "c" else self.NX
        p = (ck[0], ck[1] - 1) if ck[1] > 0 else None
        q = (ck[0], ck[1] + 1) if ck[1] < n - 1 else None
        return p, q

    def rw_hcache(self, ph, R, li):
        cache = []

        def get(ck):
            for c, v in cache:
                if c == ck:
                    return v
            which = 1 if ck[0] == "c" else 0
            v = self.front(ph, R, self.src_ap(li, ck), which)
            cache.append((ck, v))
            if len(cache) > 3:
                cache.pop(0)
            return v
        return get

    def rw_mix(self, P, mixr, tmpr, xx, hc, mu, p):
        tmp, mix = tmpr.next(), mixr.next()
        P.pool(lambda e: e.tensor_tensor(out=tmp[:], in0=xx[:], in1=mu[:, p, :].unsqueeze(2).to_broadcast([128, 8, 128]), op=ALU.mult),
               [xx, mu], [tmp])
        P.dve(lambda e: e.tensor_tensor(out=mix[:], in0=tmp[:], in1=hc[:], op=ALU.add), [tmp, hc], [mix])
        return mix

    def rwkv_layer(self, li, i, last):
        nc, w = self.nc, self.lw[i]
        NCH = self.NX + self.NC
        if not hasattr(self, "scr_o"):
            self.scr_o = nc.dram_tensor("scr_o", [NCH, 128, 2048], F32).ap()
        if not hasattr(self, "rw_scr"):
            self.rw_scr = nc.dram_tensor("rw_scr", [NCH, 128, 3072], F32).ap()
            self.rw_scr_v = nc.dram_tensor("rw_scr_v", [NCH, 128, 1024], BF16).ap()
        ph = Phase(nc)
        self.emit_mod(ph, i)
        ph.finish()
        self.rw_prep_phase(li, i)
        import os as _os
        for d in range(2):
            self.rw_scan_phase(li, i, d)
            if _os.environ.get("RW_STOP_AFTER_F") == "1":
                return
        self.rw_out_phase(li, i, last)

    def rw_prep_phase(self, li, i):
        nc, w = self.nc, self.lw[i]
        ph = Phase(nc)
        P = ph.P
        Wr = self.load_w(ph, ph.sb("Wr", [128, 8, 1024], BF16), w["rkvg"][0], 0, 1024, 8)
        Wkk = self.load_w(ph, ph.sb("Wk", [128, 8, 1024], BF16), w["rkvg"][1], 0, 1024, 8)
        Wv = self.load_w(ph, ph.sb("Wv", [128, 8, 1024], BF16), w["rkvg"][2], 0, 1024, 8)
        mu = ph.sb("mu", [128, 6, 8], F32)
        kk_bc = ph.sb("kk_bc", [128, 1024], F32)
        P.dma(mu[:], w["mu"][:, :, :], "c2", writes=[mu])
        P.dma(kk_bc[:], w["bc"][:, 0, :], "c5", writes=[kk_bc])
        R = self.front_bufs(ph, n_xn=2, n_xt=2, n_hT=4)
        get_h = self.rw_hcache(ph, R, li)
        G = Rot([ph.ps("G%d" % j, [128, 512], F32) for j in range(4)])
        sh = ph.rot("sh", [128, 8, 128], BF16, 2)
        xx = ph.rot("xx", [128, 8, 128], F32, 2)
        tmpr = ph.rot("mtmp", [128, 8, 128], F32, 2)
        mixr = ph.rot("mix", [128, 8, 128], BF16, 3)
        r_sb = ph.rot("r_sb", [128, 1024], F32, 2)
        k_sb = ph.rot("k_sb", [128, 1024], F32, 2)
        kkr = ph.rot("kkr", [128, 1024], F32, 2)
        sqr = ph.rot("sqr", [128, 1024], F32, 1)
        v_bf = ph.rot("v_bf", [128, 1024], BF16, 2)
        sm = ph.rot("sm", [128, 48], F32, 2)
        for ck in self.chunks_fwd():
            cidx = self.row0(ck) // CH
            pk, nk = self.rw_neighbors(ck)
            hc = get_h(ck)[0]
            hp = get_h(pk)[0] if pk else None
            hn = get_h(nk)[0] if nk else None
            sh_, xx_ = sh.next(), xx.next()
            self.rw_shift(P, sh_, hc, hp, hn, ck[0])
            P.pool(lambda e, hc=hc, sh_=sh_, xx_=xx_: e.tensor_tensor(out=xx_[:], in0=sh_[:], in1=hc[:], op=ALU.subtract), [sh_, hc], [xx_])
            r_, k_, v_, kk, sq, s_ = r_sb.next(), k_sb.next(), v_bf.next(), kkr.next(), sqr.next(), sm.next()
            for p, Wl, dst in ((0, Wr, r_), (2, Wkk, k_), (3, Wv, v_)):
                mix = self.rw_mix(P, mixr, tmpr, xx_, hc, mu, p)
                for n in range(2):
                    bank = G.next()
                    for k in range(8):
                        P.pe(lambda e, bank=bank, n=n, k=k, mix=mix, Wl=Wl: e.matmul(bank[:], lhsT=mix[:, k, :], rhs=Wl[k][:, n * 512:(n + 1) * 512],
                                                                                    start=(k == 0), stop=(k == 7)), [mix, Wl[k]], [bank])
                    P.act(lambda e, bank=bank, n=n, dst=dst: e.activation(out=dst[:, n * 512:(n + 1) * 512], in_=bank[:], func=AF.Copy), [bank], [dst])
            P.dve(lambda e, kk=kk, k_=k_: e.tensor_tensor(out=kk[:], in0=k_[:], in1=kk_bc[:], op=ALU.mult), [k_, kk_bc], [kk])
            P.pool(lambda e, kk=kk, sq=sq: e.tensor_tensor(out=sq[:], in0=kk[:], in1=kk[:], op=ALU.mult), [kk], [sq])
            P.dve(lambda e, s_=s_, sq=sq: e.tensor_reduce(out=s_[:, 0:16], in_=sq[:].rearrange("p (h c) -> p h c", h=16), axis=AX.X, op=ALU.add), [sq], [s_])
            P.act(lambda e, s_=s_: e.activation(out=s_[:, 16:32], in_=s_[:, 0:16], func=AF.Sqrt), [s_], [s_])
            P.dve(lambda e, s_=s_: e.tensor_scalar(out=s_[:, 16:32], in0=s_[:, 16:32], scalar1=1e-12, scalar2=None, op0=ALU.max), [s_], [s_])
            P.dve(lambda e, s_=s_: e.reciprocal(out=s_[:, 32:48], in_=s_[:, 16:32]), [s_], [s_])
            P.dve(lambda e, s_=s_, kk=kk: e.tensor_tensor(out=kk[:].rearrange("p (h c) -> p h c", h=16), in0=kk[:].rearrange("p (h c) -> p h c", h=16),
                                                          in1=s_[:, 32:48].unsqueeze(2).to_broadcast([128, 16, 64]), op=ALU.mult), [kk, s_], [kk])
            P.dma(self.rw_scr[cidx][:, 0:1024], r_[:], "pr%d" % r_sb.slot, reads=[r_], q="pool")
            P.dma(self.rw_scr[cidx][:, 1024:2048], k_[:], "pk%d" % k_sb.slot, reads=[k_], q="pool")
            P.dma(self.rw_scr[cidx][:, 2048:3072], kk[:], "pkk%d" % kkr.slot, reads=[kk], q="pool")
            P.dma(self.rw_scr_v[cidx], v_[:], "pv%d" % v_bf.slot, reads=[v_], q="pool")
        ph.finish()

    def rw_scan_phase(self, li, i, d):
        nc, w = self.nc, self.lw[i]
        ph = Phase(nc)
        P = ph.P
        C0 = 0.6065306597126334
        w1 = self.load_w(ph, ph.sb("w1", [128, 8, 64], BF16), w["w1"][d], 0, 64, 8)
        a1 = self.load_w(ph, ph.sb("a1", [128, 8, 64], BF16), w["a1"][d], 0, 64, 8)
        w2 = ph.sb("w2", [64, 1024], BF16)
        a2 = ph.sb("a2", [64, 1024], BF16)
        P.dma(w2[:], w["w2"][d], "w2", writes=[w2], q="pool")
        P.dma(a2[:], w["a2"][d], "a2", writes=[a2], q="pool")
        rows = ph.sb("rows", [8, 1024], F32)
        sel8 = ph.sb("sel8", [8, 8, 128], F32)
        mu = ph.sb("mu", [128, 6, 8], F32)
        msk = ph.sb("msk", [128, 4, 128], F32)
        negc = ph.sb("negc", [128, 1], F32)
        ka_bc = ph.sb("ka_bc", [128, 1024], F32)
        rk_bc = ph.sb("rk_bc", [128, 1024], F32)
        P.dma(rows[:], w["rows"][:, :], "c0", writes=[rows])
        P.dma(sel8[:], w["sel8"][:, :, :], "c1", writes=[sel8])
        P.dma(mu[:], w["mu"][:, :, :], "c2", writes=[mu])
        P.dma(msk[:], w["masks"][d], "c3", writes=[msk])
        P.dma(negc[:], w["negc"][:, :], "c4", writes=[negc])
        P.dma(ka_bc[:], w["bc"][:, 1, :], "c6", writes=[ka_bc])
        P.dma(rk_bc[:], w["bc"][:, 2, :], "c7", writes=[rk_bc])
        R = self.front_bufs(ph, n_xn=1, n_xt=1, n_hT=4, junk=False)
        get_h = self.rw_hcache(ph, R, li)
        G = Rot([ph.ps("G%d" % j, [128, 512], F32) for j in range(6)])
        psm = ph.ps("psm", [128, 512], F32)
        PT = R["ptrx"]
        sh = ph.sb("sh", [128, 8, 128], BF16)
        xx = ph.sb("xx", [128, 8, 128], F32)
        tmpr = ph.rot("mtmp", [128, 8, 128], F32, 1)
        mixr = ph.rot("mix", [128, 8, 128], BF16, 2)
        jk = Tile(tmpr.tiles[0].t[:].rearrange("p k t -> p (k t)"), "junkalias")
        jk.b = tmpr.tiles[0].b
        R["junk"] = jk
        r_rot = ph.rot("r_sb", [128, 1024], F32, 2)
        k_rot = ph.rot("k_sb", [128, 1024], F32, 2)
        kk_rot = ph.rot("kk_sb", [128, 1024], F32, 2)
        v_rot = ph.rot("v_bf", [128, 1024], BF16, 2)
        Wt = [ph.sb("W%d" % j, [128, 1024], F32) for j in range(4)]
        th_bf = ph.sb("th_bf", [64, 128], BF16)
        la_bf = ph.sb("la_bf", [64, 128], BF16)
        rt_bf = ph.sb("rt_bf", [128, 1024], BF16)
        at_bf = ph.sb("at_bf", [128, 1024], BF16)
        bh_bf = ph.sb("bh_bf", [128, 1024], BF16)
        kh_bf = ph.sb("kh_bf", [128, 1024], BF16)
        AR = ph.sb("AR", [64, 16, 2, 128], BF16)
        BT = ph.sb("BT", [64, 16, 128], BF16)
        KT = ph.sb("KT", [64, 16, 128], BF16)
        WC = ph.sb("WC", [64, 16], F32)
        sm = ph.rot("sm", [128, 64], F32, 2)
        names = ("N", "NT", "NA", "NAT", "NB", "NBT", "O32", "O32T", "O64", "O64T", "O128", "T", "TT", "Aak", "Abr", "Akr")
        cmk = ph.sb("cmk", [128, 7, 128], BF16)
        P.dma(cmk[:], w["cmask"][d], "c10", writes=[cmk], q="pool")
        NU = 4
        U_ = [{n: ph.sb("%s_u%d" % (n, us), [128, 4, 128], BF16) for n in names} for us in range(NU)]
        Xb = [ph.sb("Xb%d" % us, [128, 4, 64], BF16) for us in range(NU)]
        Ub = [ph.sb("Ub%d" % us, [128, 4, 64], BF16) for us in range(NU)]
        S = [ph.sb("S%d" % u, [64, 4, 64], F32) for u in range(4)]
        Sb = [ph.sb("Sb%d" % u, [64, 4, 64], BF16) for u in range(4)]
        for u in range(4):
            P.dve(lambda e, u=u: e.memset(S[u][:], 0.0), [], [S[u]])
            P.pool(lambda e, u=u: e.memset(Sb[u][:], 0.0), [], [Sb[u]])
        ysb = ph.rot("ysb", [128, 1040], F32, 1)
        if d == 1:
            yf = ph.rot("yf", [128, 1040], F32, 1)
        ident = self.ident

        order = self.chunks_fwd() if d == 0 else self.chunks_bwd()

        def chunk_body(ck):
            cidx = self.row0(ck) // CH
            pk, nk = self.rw_neighbors(ck)
            hc = get_h(ck)[0]
            hp = get_h(pk)[0] if pk else None
            hn = get_h(nk)[0] if nk else None
            self.rw_shift(P, sh, hc, hp, hn, ck[0])
            P.pool(lambda e, hc=hc: e.tensor_tensor(out=xx[:], in0=sh[:], in1=hc[:], op=ALU.subtract), [sh, hc], [xx])
            r_sb, k_sb, kk, v_bf = r_rot.next(), k_rot.next(), kk_rot.next(), v_rot.next()
            P.dma(r_sb[:], self.rw_scr[cidx][:, 0:1024], "lr%d" % r_rot.slot, writes=[r_sb])
            P.dma(k_sb[:], self.rw_scr[cidx][:, 1024:2048], "lk%d" % k_rot.slot, writes=[k_sb])
            P.dma(kk[:], self.rw_scr[cidx][:, 2048:3072], "lkk%d" % kk_rot.slot, writes=[kk])
            P.dma(v_bf[:], self.rw_scr_v[cidx], "lv%d" % v_rot.slot, writes=[v_bf])
            mix1 = self.rw_mix(P, mixr, tmpr, xx, hc, mu, 1)
            for k in range(8):
                P.pe(lambda e, k=k, mix1=mix1: e.matmul(psm[0:64, 0:128], lhsT=w1[k][:, :], rhs=mix1[:, k, :], start=(k == 0), stop=(k == 7)),
                     [mix1, w1[k]], [psm])
            P.act(lambda e: e.activation(out=th_bf[:], in_=psm[0:64, 0:128], func=AF.Tanh), [psm], [th_bf])
            sig = Wt[0]
            for n in range(2):
                bank = G.next()
                P.pe(lambda e, bank=bank, n=n: e.matmul(bank[:], lhsT=th_bf[:], rhs=w2[:, n * 512:(n + 1) * 512], start=True, stop=False), [th_bf, w2], [bank])
                P.pe(lambda e, bank=bank, n=n: e.matmul(bank[:], lhsT=sel8[:, d, :], rhs=rows[:, n * 512:(n + 1) * 512], start=False, stop=True), [sel8, rows], [bank])
                P.act(lambda e, bank=bank, n=n: e.activation(out=sig[:, n * 512:(n + 1) * 512], in_=bank[:], func=AF.Sigmoid), [bank], [sig])
            mix4 = self.rw_mix(P, mixr, tmpr, xx, hc, mu, 4)
            for k in range(8):
                P.pe(lambda e, k=k, mix4=mix4: e.matmul(psm[0:64, 0:128], lhsT=a1[k][:, :], rhs=mix4[:, k, :], start=(k == 0), stop=(k == 7)),
                     [mix4, a1[k]], [psm])
            P.act(lambda e: e.activation(out=la_bf[:], in_=psm[0:64, 0:128], func=AF.Copy), [psm], [la_bf])
            ep, em, ex = Wt[1], Wt[2], Wt[3]
            for n in range(2):
                bank = G.next()
                sl = slice(n * 512, (n + 1) * 512)
                P.pe(lambda e, bank=bank, sl=sl: e.matmul(bank[:], lhsT=msk[:, 3, :], rhs=sig[:, sl], start=True, stop=True), [msk, sig], [bank])
                P.act(lambda e, bank=bank, sl=sl: e.activation(out=ep[:, sl], in_=bank[:], func=AF.Exp), [bank], [ep])
                P.act(lambda e, bank=bank, sl=sl: e.activation(out=em[:, sl], in_=bank[:], func=AF.Exp, scale=-1.0), [bank], [em])
                P.dve(lambda e, bank=bank, sl=sl: e.scalar_tensor_tensor(out=ex[:, sl], in0=sig[:, sl], scalar=C0, in1=bank[:], op0=ALU.mult, op1=ALU.add),
                      [bank, sig], [ex])
            P.act(lambda e: e.activation(out=ex[:], in_=ex[:], func=AF.Exp), [ex], [ex])
            for h in range(16):
                P.pe(lambda e, h=h: e.matmul(psm[0:64, 256 + h:257 + h], lhsT=sig[:, h * 64:(h + 1) * 64], rhs=negc[:, 0:1], start=True, stop=True),
                     [sig, negc], [psm])
            P.act(lambda e: e.activation(out=WC[:], in_=psm[0:64, 256:272], func=AF.Exp), [psm], [WC])
            P.pool(lambda e: e.tensor_tensor(out=rt_bf[:], in0=r_sb[:], in1=ep[:], op=ALU.mult), [r_sb, ep], [rt_bf])
            s_ = sm.next()
            P.dve(lambda e: e.scalar_tensor_tensor(out=at_bf[:], in0=kk[:], scalar=-1.0, in1=ex[:], op0=ALU.mult, op1=ALU.mult), [kk, ex], [at_bf])
            aa = Wt[1]
            for n in range(2):
                bank = G.next()
                P.pe(lambda e, bank=bank, n=n: e.matmul(bank[:], lhsT=la_bf[:], rhs=a2[:, n * 512:(n + 1) * 512], start=True, stop=False), [la_bf, a2], [bank])
                P.pe(lambda e, bank=bank, n=n: e.matmul(bank[:], lhsT=sel8[:, 2 + d, :], rhs=rows[:, n * 512:(n + 1) * 512], start=False, stop=True), [sel8, rows], [bank])
                P.act(lambda e, bank=bank, n=n: e.activation(out=aa[:, n * 512:(n + 1) * 512], in_=bank[:], func=AF.Sigmoid), [bank], [aa])
            be = Wt[3]
            P.dve(lambda e: e.tensor_tensor(out=be[:], in0=kk[:], in1=aa[:], op=ALU.mult), [kk, aa, at_bf], [be])
            P.pool(lambda e: e.tensor_tensor(out=bh_bf[:], in0=be[:], in1=em[:], op=ALU.mult), [be, em], [bh_bf])
            kd = Wt[0]
            P.dve(lambda e: e.scalar_tensor_tensor(out=kd[:], in0=aa[:], scalar=-1.0, in1=ka_bc[:], op0=ALU.add, op1=ALU.mult), [aa, ka_bc, be], [kd])
            P.dve(lambda e: e.scalar_tensor_tensor(out=kd[:], in0=kd[:], scalar=1.0, in1=k_sb[:], op0=ALU.add, op1=ALU.mult), [kd, k_sb], [kd])
            P.pool(lambda e: e.tensor_tensor(out=kh_bf[:], in0=kd[:], in1=em[:], op=ALU.mult), [kd, em], [kh_bf])
            bt = Wt[3]
            P.dve(lambda e: e.tensor_tensor(out=bt[:], in0=kd[:], in1=r_sb[:], op=ALU.mult), [kd, r_sb, bh_bf], [bt])
            P.pool(lambda e: e.tensor_tensor(out=bt[:], in0=bt[:], in1=rk_bc[:], op=ALU.mult), [bt, rk_bc], [bt])
            P.dve(lambda e, s_=s_: e.tensor_reduce(out=s_[:, 48:64], in_=bt[:].rearrange("p (h c) -> p h c", h=16), axis=AX.X, op=ALU.add), [bt], [s_])
            cnt = 0
            for g in range(2):
                for src, dst_fn in ((at_bf, lambda g: AR[:, g * 8:(g + 1) * 8, 0, :]), (rt_bf, lambda g: AR[:, g * 8:(g + 1) * 8, 1, :]),
                                    (bh_bf, lambda g: BT[:, g * 8:(g + 1) * 8, :]), (kh_bf, lambda g: KT[:, g * 8:(g + 1) * 8, :])):
                    for j in range(8):
                        h = g * 8 + j
                        P.pe(lambda e, src=src, h=h, j=j: e.transpose(out=PT[0:64, j, :], in_=src[:, h * 64:(h + 1) * 64], identity=ident[:]),
                             [src, ident], [PT])
                    dtile = AR if src in (at_bf, rt_bf) else (BT if src is bh_bf else KT)
                    dst = dst_fn(g)
                    if cnt % 2 == 0:
                        P.act(lambda e, dst=dst: e.activation(out=dst, in_=PT[0:64, :, :], func=AF.Copy), [PT], [dtile])
                    else:
                        P.dve(lambda e, dst=dst: e.tensor_copy(out=dst, in_=PT[0:64, :, :]), [PT], [dtile])
                    cnt += 1
            y_ = ysb.next()
            if d == 1:
                yf_ = yf.next()
                P.dma(yf_[:], self.scr_o[cidx][:, 0:1040], "lyf%d" % yf.slot, writes=[yf_])
            for g in range(1):
                units = [(us, us) for us in range(4)]
                for u, us in units:
                    M = U_[us]
                    h0 = u * 4
                    for pr in range(2):
                        bank = G.next()
                        for j in range(2):
                            h = h0 + pr * 2 + j
                            P.pe(lambda e, bank=bank, j=j, h=h: e.matmul(bank[:, j * 256:(j + 1) * 256], lhsT=BT[:, h, :],
                                                                         rhs=AR[:, h, :, :].rearrange("p a t -> p (a t)"), start=True, stop=True), [BT, AR], [bank])
                        bv = bank[:].rearrange("p (j a t) -> p j a t", j=2, a=2)
                        for dn, mi in (("NA", 0), ("O32", 1), ("O64", 2)):
                            P.dve(lambda e, bv=bv, M=M, pr=pr, dn=dn, mi=mi: e.tensor_tensor(out=M[dn][:, pr * 2:pr * 2 + 2, :], in0=bv[:, :, 0, :],
                                                                                            in1=cmk[:, mi, :].unsqueeze(1).to_broadcast([128, 2, 128]), op=ALU.mult),
                                  [bank, cmk], [M[dn]])
                        P.dve(lambda e, bv=bv, M=M, pr=pr: e.tensor_tensor(out=M["Abr"][:, pr * 2:pr * 2 + 2, :], in0=bv[:, :, 1, :],
                                                                          in1=msk[:, 1, :].unsqueeze(1).to_broadcast([128, 2, 128]), op=ALU.mult), [bank, msk], [M["Abr"]])
                        bank = G.next()
                        for j in range(2):
                            h = h0 + pr * 2 + j
                            P.pe(lambda e, bank=bank, j=j, h=h: e.matmul(bank[:, j * 256:(j + 1) * 256], lhsT=KT[:, h, :],
                                                                         rhs=AR[:, h, :, :].rearrange("p a t -> p (a t)"), start=True, stop=True), [KT, AR], [bank])
                        bv = bank[:].rearrange("p (j a t) -> p j a t", j=2, a=2)
                        P.dve(lambda e, bv=bv, M=M, pr=pr: e.tensor_tensor(out=M["Aak"][:, pr * 2:pr * 2 + 2, :], in0=bv[:, :, 0, :],
                                                                          in1=msk[:, 0, :].unsqueeze(1).to_broadcast([128, 2, 128]), op=ALU.mult), [bank, msk], [M["Aak"]])
                        P.dve(lambda e, bv=bv, M=M, pr=pr: e.tensor_tensor(out=M["Akr"][:, pr * 2:pr * 2 + 2, :], in0=bv[:, :, 1, :],
                                                                          in1=msk[:, 1, :].unsqueeze(1).to_broadcast([128, 2, 128]), op=ALU.mult), [bank, msk], [M["Akr"]])
                    bank = G.next()
                    for j in range(4):
                        h = h0 + j
                        P.pe(lambda e, bank=bank, j=j, h=h: e.matmul(bank[:, j * 128:(j + 1) * 128], lhsT=AR[:, h, 0, :], rhs=BT[:, h, :], start=True, stop=True),
                             [AR, BT], [bank])
                    for dn, mi in (("NAT", 3), ("O32T", 4), ("O64T", 5), ("O128", 6)):
                        P.dve(lambda e, bank=bank, M=M, dn=dn, mi=mi: e.tensor_tensor(out=M[dn][:], in0=bank[:].rearrange("p (j t) -> p j t", j=4),
                                                                                      in1=cmk[:, mi, :].unsqueeze(1).to_broadcast([128, 4, 128]), op=ALU.mult),
                              [bank, cmk], [M[dn]])
                    P.pool(lambda e, M=M: e.tensor_tensor(out=M["T"][:], in0=M["NA"][:], in1=ident[:].unsqueeze(1).to_broadcast([128, 4, 128]), op=ALU.add),
                           [M["NA"], ident], [M["T"]])
                    P.pool(lambda e, M=M: e.tensor_tensor(out=M["TT"][:], in0=M["NAT"][:], in1=ident[:].unsqueeze(1).to_broadcast([128, 4, 128]), op=ALU.add),
                           [M["NAT"], ident], [M["TT"]])

                def mm4(bank, M, lt, rt):
                    for j in range(4):
                        P.pe(lambda e, bank=bank, j=j, M=M, lt=lt, rt=rt: e.matmul(bank[:, j * 128:(j + 1) * 128], lhsT=M[lt][:, j, :], rhs=M[rt][:, j, :],
                                                                                  start=True, stop=True), [M[lt], M[rt]], [bank])

                def cp4(bank, M, dn):
                    P.act(lambda e, bank=bank, M=M, dn=dn: e.activation(out=M[dn][:], in_=bank[:].rearrange("p (j t) -> p j t", j=4), func=AF.Copy),
                          [bank], [M[dn]])

                def add4(bank, M, dn):
                    P.dve(lambda e, bank=bank, M=M, dn=dn: e.tensor_tensor(out=M[dn][:], in0=bank[:].rearrange("p (j t) -> p j t", j=4), in1=M[dn][:], op=ALU.add),
                          [bank, M[dn]], [M[dn]])
                cur = {us: ("NA", "NAT") for _, us in units}
                for lvl in range(3):
                    nxt = {}
                    for u, us in units:
                        M = U_[us]
                        nk, nkt = cur[us]
                        nb, nbt = ("NB", "NBT") if nk == "NA" else ("NA", "NAT")
                        b1 = G.next(); mm4(b1, M, nkt, nk); cp4(b1, M, nb)
                        b2 = G.next(); mm4(b2, M, nk, nkt); cp4(b2, M, nbt)
                        nxt[us] = (nb, nbt)
                    for u, us in units:
                        M = U_[us]
                        nb, nbt = nxt[us]
                        b3 = G.next(); mm4(b3, M, nbt, "T"); add4(b3, M, "T")
                    for u, us in units:
                        M = U_[us]
                        nb, nbt = nxt[us]
                        b4 = G.next(); mm4(b4, M, nb, "TT"); add4(b4, M, "TT")
                    cur = nxt
                for on, lastm in (("O32", False), ("O64", False), ("O128", True)):
                    if not lastm:
                        for u, us in units:
                            M = U_[us]
                            b1 = G.next(); mm4(b1, M, on + "T", "T"); cp4(b1, M, "N")
                        for u, us in units:
                            M = U_[us]
                            b2 = G.next(); mm4(b2, M, on, "TT"); cp4(b2, M, "NT")
                        bb = {}
                        for u, us in units:
                            M = U_[us]
                            b3 = G.next(); mm4(b3, M, "TT", "N"); bb[us] = b3
                            if us % 2 == 1:
                                pass
                        b4s = {}
                        for u, us in units[:2]:
                            M = U_[us]
                            b4 = G.next(); mm4(b4, M, "T", "NT"); b4s[us] = b4
                        for u, us in units[:2]:
                            add4(bb[us], U_[us], "T"); add4(b4s[us], U_[us], "TT")
                        for u, us in units[2:]:
                            M = U_[us]
                            b4 = G.next(); mm4(b4, M, "T", "NT"); b4s[us] = b4
                        for u, us in units[2:]:
                            add4(bb[us], U_[us], "T"); add4(b4s[us], U_[us], "TT")
                    else:
                        for u, us in units:
                            M = U_[us]
                            b1 = G.next(); mm4(b1, M, on, "T"); cp4(b1, M, "N")
                        for u, us in units:
                            M = U_[us]
                            b3 = G.next(); mm4(b3, M, "TT", "N"); add4(b3, M, "T")
                for u, us in units:
                    M = U_[us]
                    h0 = u * 4
                    bank = G.next()
                    for j in range(4):
                        h = h0 + j
                        P.pe(lambda e, bank=bank, j=j, h=h, u=u: e.matmul(bank[:, j * 64:(j + 1) * 64], lhsT=AR[:, h, 0, :], rhs=Sb[u][:, j, :], start=True, stop=False),
                             [AR, Sb[u]], [bank])
                        P.pe(lambda e, bank=bank, j=j, h=h, M=M: e.matmul(bank[:, j * 64:(j + 1) * 64], lhsT=M["Aak"][:, j, :], rhs=v_bf[:, h * 64:(h + 1) * 64],
                                                                          start=False, stop=True), [M["Aak"], v_bf], [bank])
                    P.act(lambda e, bank=bank, us=us: e.activation(out=Xb[us][:], in_=bank[:, 0:256].rearrange("p (j v) -> p j v", j=4), func=AF.Copy),
                          [bank], [Xb[us]])
                for u, us in units:
                    M = U_[us]
                    h0 = u * 4
                    bank = G.next()
                    for j in range(4):
                        P.pe(lambda e, bank=bank, j=j, M=M, us=us: e.matmul(bank[:, j * 64:(j + 1) * 64], lhsT=M["T"][:, j, :], rhs=Xb[us][:, j, :], start=True, stop=True),
                             [M["T"], Xb[us]], [bank])
                    P.dve(lambda e, bank=bank, us=us: e.tensor_copy(out=Ub[us][:], in_=bank[:, 0:256].rearrange("p (j v) -> p j v", j=4)), [bank], [Ub[us]])
                for u, us in units:
                    M = U_[us]
                    h0 = u * 4
                    bank = G.next()
                    for j in range(4):
                        h = h0 + j
                        P.pe(lambda e, bank=bank, j=j, h=h, u=u: e.matmul(bank[:, j * 64:(j + 1) * 64], lhsT=AR[:, h, 1, :], rhs=Sb[u][:, j, :], start=True, stop=False),
                             [AR, Sb[u]], [bank])
                        P.pe(lambda e, bank=bank, j=j, M=M, us=us: e.matmul(bank[:, j * 64:(j + 1) * 64], lhsT=M["Abr"][:, j, :], rhs=Ub[us][:, j, :], start=False, stop=False),
                             [M["Abr"], Ub[us]], [bank])
                        P.pe(lambda e, bank=bank, j=j, h=h, M=M: e.matmul(bank[:, j * 64:(j + 1) * 64], lhsT=M["Akr"][:, j, :], rhs=v_bf[:, h * 64:(h + 1) * 64],
                                                                          start=False, stop=True), [M["Akr"], v_bf], [bank])
                    if d == 0:
                        P.act(lambda e, bank=bank, h0=h0, y_=y_: e.activation(out=y_[:, h0 * 64:(h0 + 4) * 64], in_=bank[:, 0:256], func=AF.Copy), [bank], [y_])
                    else:
                        P.dve(lambda e, bank=bank, h0=h0, y_=y_, yf_=yf_: e.tensor_tensor(out=y_[:, h0 * 64:(h0 + 4) * 64], in0=bank[:, 0:256],
                                                                                         in1=yf_[:, h0 * 64:(h0 + 4) * 64], op=ALU.add), [bank, yf_], [y_])
                for u, us in units:
                    M = U_[us]
                    h0 = u * 4
                    bank = G.next()
                    for j in range(4):
                        h = h0 + j
                        P.pe(lambda e, bank=bank, j=j, h=h, us=us: e.matmul(bank[0:64, j * 64:(j + 1) * 64], lhsT=bh_bf[:, h * 64:(h + 1) * 64], rhs=Ub[us][:, j, :],
                                                                            start=True, stop=False), [bh_bf, Ub[us]], [bank])
                        P.pe(lambda e, bank=bank, j=j, h=h: e.matmul(bank[0:64, j * 64:(j + 1) * 64], lhsT=kh_bf[:, h * 64:(h + 1) * 64], rhs=v_bf[:, h * 64:(h + 1) * 64],
                                                                     start=False, stop=True), [kh_bf, v_bf], [bank])
                    P.dve(lambda e, bank=bank, u=u: e.tensor_tensor(out=S[u][:], in0=bank[0:64, 0:256].rearrange("p (j v) -> p j v", j=4), in1=S[u][:], op=ALU.add),
                          [bank, S[u]], [S[u]])
                    P.dve(lambda e, u=u, h0=h0: e.tensor_tensor(out=S[u][:], in0=S[u][:], in1=WC[:, h0:h0 + 4].unsqueeze(2).to_broadcast([64, 4, 64]), op=ALU.mult),
                          [S[u], WC], [S[u]])
                    P.pool(lambda e, u=u: e.tensor_copy(out=Sb[u][:], in_=S[u][:]), [S[u]], [Sb[u]])
            if d == 0:
                P.dve(lambda e, y_=y_, s_=s_: e.tensor_copy(out=y_[:, 1024:1040], in_=s_[:, 48:64]), [s_], [y_])
                P.dma(self.scr_o[cidx][:, 0:1040], y_[:], "sy%d" % ysb.slot, reads=[y_], q="pool")
            else:
                need_out = True
                yv = y_[:, 0:1024].rearrange("p (h c) -> p h c", h=16)
                t1 = Wt[1]
                t1v = t1[:].rearrange("p (h c) -> p h c", h=16)
                P.dve(lambda e, yv=yv, s_=s_: e.tensor_reduce(out=s_[:, 0:16], in_=yv, axis=AX.X, op=ALU.add), [y_, aa], [s_])
                P.dve(lambda e, s_=s_: e.tensor_scalar(out=s_[:, 0:16], in0=s_[:, 0:16], scalar1=-1.0 / 64, scalar2=None, op0=ALU.mult), [s_], [s_])
                P.dve(lambda e, yv=yv, s_=s_: e.tensor_tensor(out=yv, in0=yv, in1=s_[:, 0:16].unsqueeze(2).to_broadcast([128, 16, 64]), op=ALU.add), [y_, s_], [y_])
                P.pool(lambda e, y_=y_: e.tensor_tensor(out=t1[:], in0=y_[:, 0:1024], in1=y_[:, 0:1024], op=ALU.mult), [y_], [t1])
                P.dve(lambda e, s_=s_: e.tensor_reduce(out=s_[:, 16:32], in_=t1v, axis=AX.X, op=ALU.add), [t1], [s_])
                P.act(lambda e, s_=s_: e.activation(out=s_[:, 16:32], in_=s_[:, 16:32], func=AF.Sqrt, scale=1.0 / 64, bias=64e-5), [s_], [s_])
                P.dve(lambda e, s_=s_: e.reciprocal(out=s_[:, 32:48], in_=s_[:, 16:32]), [s_], [s_])
                P.dve(lambda e, yv=yv, s_=s_: e.tensor_tensor(out=yv, in0=yv, in1=s_[:, 32:48].unsqueeze(2).to_broadcast([128, 16, 64]), op=ALU.mult), [y_, s_], [y_])
                P.dve(lambda e, s_=s_, yf_=yf_: e.tensor_tensor(out=s_[:, 48:64], in0=s_[:, 48:64], in1=yf_[:, 1024:1040], op=ALU.add), [s_, yf_], [s_])
                P.dve(lambda e, s_=s_: e.tensor_tensor(out=t1v, in0=v_bf[:].rearrange("p (h c) -> p h c", h=16),
                                                       in1=s_[:, 48:64].unsqueeze(2).to_broadcast([128, 16, 64]), op=ALU.mult), [v_bf, s_, t1], [t1])
                P.dma(self.scr_o[cidx][:, 0:1024], y_[:, 0:1024], "sy%d" % ysb.slot, reads=[y_], q="pool")
                P.dma(self.scr_o[cidx][:, 1024:2048], t1[:], "sbv", reads=[t1], q="pool")
        for ck in order:
            chunk_body(ck)
        ph.finish()

    def rw_out_phase(self, li, i, last):
        nc, w = self.nc, self.lw[i]
        ph = Phase(nc)
        P = ph.P
        Wg = self.load_w(ph, ph.sb("Wg", [128, 8, 1024], BF16), w["rkvg"][3], 0, 1024, 8)
        Wo = self.load_w(ph, ph.sb("Wo", [128, 8, 1024], BF16), w["w_out"], 0, 1024, 8)
        mu = ph.sb("mu", [128, 6, 8], F32)
        P.dma(mu[:], w["mu"][:, :, :], "c2", writes=[mu])
        R = self.front_bufs(ph, n_xn=1, n_xt=4, n_hT=4)
        R.update(self.tail_bufs(ph, last=last))
        get_h = self.rw_hcache(ph, R, li)
        mm = Rot([ph.ps("mm%d" % j, [128, 512], F32) for j in range(2)])
        sh = ph.sb("sh", [128, 8, 128], BF16)
        xx = ph.sb("xx", [128, 8, 128], F32)
        tmpr = ph.rot("mtmp", [128, 8, 128], F32, 2)
        mixr = ph.rot("mix", [128, 8, 128], BF16, 2)
        op = ph.rot("opre", [128, 2048], F32, 2)
        lg_bc = ph.sb("lg_bc", [128, 1024], F32)
        lb_bc = ph.sb("lb_bc", [128, 1024], F32)
        P.dma(lg_bc[:], w["bc"][:, 3, :], "c8", writes=[lg_bc])
        P.dma(lb_bc[:], w["bc"][:, 4, :], "c9", writes=[lb_bc])
        sz = ph.rot("sz", [128, 1024], F32, 2)
        gated = ph.rot("gated", [128, 1024], BF16, 2)
        order = self.chunks_fwd()
        if last:
            order = [ck for ck in order if ck[0] == "x"]
        for ck in order:
            cidx = self.row0(ck) // CH
            pk, nk = self.rw_neighbors(ck)
            hc, xt = get_h(ck)
            hp = get_h(pk)[0] if pk else None
            hn = get_h(nk)[0] if nk else None
            self.rw_shift(P, sh, hc, hp, hn, ck[0])
            P.pool(lambda e, hc=hc: e.tensor_tensor(out=xx[:], in0=sh[:], in1=hc[:], op=ALU.subtract), [sh, hc], [xx])
            mix5 = self.rw_mix(P, mixr, tmpr, xx, hc, mu, 5)
            o_ = op.next()
            P.dma(o_[:], self.scr_o[cidx][:, 0:2048], "lo%d" % op.slot, writes=[o_])
            P.pool(lambda e, o_=o_: e.tensor_tensor(out=o_[:, 0:1024], in0=o_[:, 0:1024], in1=lg_bc[:], op=ALU.mult), [o_, lg_bc], [o_])
            P.pool(lambda e, o_=o_: e.tensor_tensor(out=o_[:, 1024:2048], in0=o_[:, 1024:2048], in1=lb_bc[:], op=ALU.add), [o_, lb_bc], [o_])
            P.pool(lambda e, o_=o_: e.tensor_tensor(out=o_[:, 0:1024], in0=o_[:, 0:1024], in1=o_[:, 1024:2048], op=ALU.add), [o_], [o_])
            sz_ = sz.next()
            for n in range(2):
                bank = mm.next()
                for k in range(8):
                    P.pe(lambda e, bank=bank, n=n, k=k, mix5=mix5: e.matmul(bank[:], lhsT=mix5[:, k, :], rhs=Wg[k][:, n * 512:(n + 1) * 512],
                                                                            start=(k == 0), stop=(k == 7)), [mix5, Wg[k]], [bank])
                P.act(lambda e, bank=bank, n=n, sz_=sz_: e.activation(out=sz_[:, n * 512:(n + 1) * 512], in_=bank[:], func=AF.Silu), [bank], [sz_])
            g_ = gated.next()
            P.dve(lambda e, o_=o_, sz_=sz_, g_=g_: e.tensor_tensor(out=g_[:], in0=o_[:, 0:1024], in1=sz_[:], op=ALU.mult), [o_, sz_], [g_])
            self.tail(ph, R, g_, Wo, li, ck, xt, last, KT=8)
        ph.finish()


def _col(v, k):
    return np.ascontiguousarray(np.asarray(v, np.float32).reshape(k, 128).T)


def _consts(L, CL):
    f32 = np.float32
    t = np.arange(L)
    row, col = (t // 64).astype(f32), (t % 64).astype(f32)
    freqs = (f32(10000.0) ** (-(np.arange(64, dtype=f32)) / f32(64))).astype(f32)
    ang = np.concatenate([row[:, None] * freqs, col[:, None] * freqs], axis=-1).astype(f32)
    rope = np.zeros((CL + L, 2, 128), f32)
    rope[:CL, 0, :] = 1.0
    rope[CL:, 0, :] = np.cos(ang)
    rope[CL:, 1, :] = np.sin(ang)
    jj = np.arange(128)[:, None].astype(f32)
    ii = np.arange(128)[None, :].astype(f32)
    dmat = np.stack([np.maximum(ii - jj, 0), np.maximum(jj - ii, 0), (ii >= jj).astype(f32), (jj >= ii).astype(f32),
                     np.broadcast_to(ii + 1, (128, 128)), np.broadcast_to(128 - ii, (128, 128))], axis=1).astype(f32)
    jcol = np.stack([127 - np.arange(128), np.arange(128)], axis=1).astype(f32)
    sel = np.zeros((2, 2, 128), f32)
    sel[0, 0, :] = 1.0
    sel[1, 1, :] = 1.0
    return {"ident": np.eye(128, dtype=f32), "rope": rope, "dmat": np.ascontiguousarray(dmat), "jcol": jcol, "sel2": sel}


def host_inputs(inp, b, layers, L, CL=256, x_rows=None):
    f32 = np.float32
    m = dict(_consts(L, CL))
    xr = inp["x"][b] if x_rows is None else x_rows
    m["x"] = np.ascontiguousarray(xr[:L], f32)
    m["ctx"] = np.ascontiguousarray(inp["ctx"][b][:CL], f32)
    m["cc"] = np.ascontiguousarray(np.stack([_col(inp["c"][b], 8), _col(inp["c_ctx"], 8)], axis=-1))
    m["final_g_bc"] = np.ascontiguousarray(np.broadcast_to(np.asarray(inp["final_g"], f32), (128, D)))
    for i in layers:
        j = i // 3
        m["ada_w%d" % i] = np.ascontiguousarray(inp["ada_w"][i], f32)
        m["ada_bcol%d" % i] = _col(inp["ada_b"][i], 24)
        m["ada_brow%d" % i] = np.ascontiguousarray(np.broadcast_to(np.asarray(inp["ada_b"][i][2 * D:], f32), (2, D)))
        m["ng_col%d" % i] = _col(inp["norm_g"][i], 8)
        k = KINDS[i]
        if k == 0:
            m["ret_w_in%d" % i] = np.ascontiguousarray(inp["ret_w_in"][j], f32)
            m["ret_w_out%d" % i] = np.ascontiguousarray(inp["ret_w_out"][j], f32)
            dec = np.concatenate([inp["ret_decay"][j][0], inp["ret_decay"][j][1]]).astype(f32)
            m["ret_decay%d" % i] = np.ascontiguousarray(np.broadcast_to(dec, (128, 8)))
        elif k == 1:
            m["gm_w_in%d" % i] = np.ascontiguousarray(inp["gm_w_in"][j], f32)
            m["gm_w_out%d" % i] = np.ascontiguousarray(inp["gm_w_out"][j], f32)
            m["gm_vg%d" % i] = np.ascontiguousarray(np.broadcast_to(np.asarray(inp["gm_vnorm_g"][j], f32), (128, 2048)))
            m["gm_wsT%d" % i] = np.ascontiguousarray(np.transpose(np.asarray(inp["gm_w_s"][j], f32), (2, 0, 1)))
            m["gm_bsT%d" % i] = np.ascontiguousarray(np.asarray(inp["gm_b_s"][j], f32).T)
        else:
            _rw_host(m, inp, i, j)
    return m


def _rw_host(m, inp, i, j):
    f32 = np.float32
    g = lambda k: np.asarray(inp[k][j], f32)
    mu = g("rw_mu")
    m["rw_mu%d" % i] = np.ascontiguousarray(np.stack([_col(mu[p], 8) for p in range(6)], axis=1))
    m["rw_rkvg%d" % i] = np.ascontiguousarray(g("rw_w_rkvg"))
    m["rw_w1%d" % i] = np.ascontiguousarray(g("rw_w1"))
    m["rw_a1%d" % i] = np.ascontiguousarray(g("rw_a1"))
    m["rw_w2%d" % i] = np.ascontiguousarray(g("rw_w2"))
    m["rw_a2%d" % i] = np.ascontiguousarray(g("rw_a2"))
    rows = np.zeros((8, D), f32)
    rows[0:2] = g("rw_w0")
    rows[2:4] = g("rw_a0")
    m["rw_rows%d" % i] = rows
    bc = np.stack([g("rw_k_k"), g("rw_k_a"), g("rw_r_k").reshape(-1), g("rw_lnx_g"), g("rw_lnx_b")], axis=0)
    m["rw_bc%d" % i] = np.ascontiguousarray(np.broadcast_to(bc[None], (128, 5, D)))
    m["rw_wout%d" % i] = np.ascontiguousarray(g("rw_w_out"))
    s_ = np.arange(128)[:, None]
    t_ = np.arange(128)[None, :]
    c0 = f32(-0.6065306597126334)
    fw = np.stack([(s_ < t_), (s_ <= t_), (t_ < s_), (s_ <= t_) * c0], axis=1).astype(f32)
    bw = np.stack([(s_ > t_), (s_ >= t_), (t_ > s_), (s_ >= t_) * c0], axis=1).astype(f32)
    m["rw_masks%d" % i] = np.ascontiguousarray(np.stack([fw, bw], axis=0))
    sel = np.zeros((8, 8, 128), f32)
    for r in range(8):
        sel[r, r, :] = 1.0
    m["rw_sel8%d" % i] = sel
    m["rw_negc%d" % i] = np.full((128, 1), c0, f32)
    blk = lambda n: (s_ // n) == (t_ // n)
    bm = np.stack([blk(16)] + [blk(n) & ~blk(n // 2) for n in (32, 64, 128)], axis=1).astype(f32)
    m["rw_bmask%d" % i] = np.ascontiguousarray(bm)
    cms = []
    for dd in (fw, bw):
        st, stT = dd[:, 0, :], dd[:, 2, :]
        cms.append(np.stack([st * bm[:, 0], st * bm[:, 1], st * bm[:, 2], stT * bm[:, 0], stT * bm[:, 1], stT * bm[:, 2], stT * bm[:, 3]], axis=1))
    m["rw_cmask%d" % i] = np.ascontiguousarray(np.stack(cms, axis=0).astype(f32))


_MODEL_CACHE = {}


def kernel(**inputs):
    inp = {k: np.asarray(v) for k, v in inputs.items()}
    B, L, _ = inp["x"].shape
    layers = (0, 1, 2, 3)
    key = (L, layers)
    if key not in _MODEL_CACHE:
        _MODEL_CACHE[key] = Model(L, 256, layers)
    model = _MODEL_CACHE[key]
    maps = []
    for core in range(NCORES):
        b = core % B
        hm = host_inputs(inp, b, layers, L)
        maps.append({k: hm[k] for k in model.inputs})
    res = run_bass_kernel_spmd(model.nc, maps, core_ids=list(range(NCORES)))
    out = np.stack([np.asarray(res.results[b]["out"], np.float32) for b in range(B)], axis=0)
    return out
```

```python
from contextlib import ExitStack
import numpy as np
import concourse.bass as bass
import concourse.mybir as mybir
from concourse.bass_utils import run_bass_kernel_spmd

F32 = mybir.dt.float32
BF16 = mybir.dt.bfloat16
AF = mybir.ActivationFunctionType
ALU = mybir.AluOpType
AX = mybir.AxisListType

D = 1024
CH = 128
NCORES = 8
_UID = [0]


class Buf:
    __slots__ = ("name", "last_w", "readers", "excl")

    def __init__(self, name, excl=False):
        self.name = name
        self.last_w = None
        self.readers = []
        self.excl = excl


class Tile:
    def __init__(self, t, name, excl=False):
        self.t = t
        self.b = Buf(name, excl)

    def __getitem__(self, k):
        return self.t[k]


class Rot:
    def __init__(self, tiles):
        self.tiles = tiles
        self.i = 0

    def next(self):
        t = self.tiles[self.i % len(self.tiles)]
        self.slot = self.i % len(self.tiles)
        self.i += 1
        return t


class Op:
    __slots__ = ("eng", "fn", "dma_key", "waits", "signal", "sem", "val", "idx", "prog")


def _b(x):
    return x.b if isinstance(x, Tile) else x


class Prog:
    ENGS = ("pe", "act", "dve", "pool", "sp")

    def __init__(self):
        self.ops = []

    def add(self, eng, fn, reads=(), writes=(), dma_key=None):
        op = Op()
        op.eng, op.fn, op.dma_key = eng, fn, dma_key
        op.signal, op.sem, op.val = False, None, 0
        op.idx, op.prog = len(self.ops), self
        reads = [_b(x) for x in reads]
        writes = [_b(x) for x in writes]
        deps = []
        for b in reads:
            if b.last_w is not None:
                deps.append((b.last_w, True))
            if b.excl:
                deps.extend((r, False) for r in b.readers)
        for b in writes:
            if b.last_w is not None:
                deps.append((b.last_w, False))
            deps.extend((r, False) for r in b.readers)
        need, seen = [], set()
        for d, raw in deps:
            if d.prog is not self:
                continue
            if d.dma_key is None and dma_key is None and d.eng == eng:
                if eng == "pe" or not raw:
                    continue
            if d.idx in seen:
                continue
            seen.add(d.idx)
            d.signal = True
            need.append(d)
        op.waits = need
        for b in reads:
            b.readers.append(op)
        for b in writes:
            b.last_w = op
            b.readers = []
        self.ops.append(op)
        return op

    def pe(self, fn, reads=(), writes=()):
        return self.add("pe", fn, reads, writes)

    def act(self, fn, reads=(), writes=()):
        return self.add("act", fn, reads, writes)

    def dve(self, fn, reads=(), writes=()):
        return self.add("dve", fn, reads, writes)

    def pool(self, fn, reads=(), writes=()):
        return self.add("pool", fn, reads, writes)

    def dma(self, out, in_, key, reads=(), writes=(), q="sp", slow=False):
        if slow:
            return self.add(q, lambda e: e.dma_start(out=out, in_=in_, allow_slow_non_contiguous=True),
                            reads, writes, dma_key=key)
        return self.add(q, lambda e: e.dma_start(out=out, in_=in_), reads, writes, dma_key=key)

    def emit(self, nc, stack):
        def keyof(op):
            return ("dma", op.dma_key) if op.dma_key is not None else ("eng", op.eng)
        last = {}
        for op in self.ops:
            last[keyof(op)] = op
        for op in last.values():
            op.signal = True
        cnt, sems = {}, {}
        for op in self.ops:
            if not op.signal:
                continue
            k = keyof(op)
            cnt[k] = cnt.get(k, 0) + (16 if op.dma_key is not None else 1)
            op.val = cnt[k]
            if k not in sems:
                _UID[0] += 1
                sems[k] = nc.alloc_semaphore(name="s%d" % _UID[0])
            op.sem = sems[k]
        self.n_sems = len(sems)
        per_eng = {e: [] for e in self.ENGS}
        for op in self.ops:
            per_eng[op.eng].append(op)
        finals = [(sems[k], cnt[k]) for k in sems]

        def run(engname, e):
            waited = {}
            for op in per_eng[engname]:
                for d in op.waits:
                    key = id(d.sem)
                    if waited.get(key, 0) >= d.val:
                        continue
                    e.wait_ge(d.sem, d.val)
                    waited[key] = d.val
                ins = op.fn(e)
                if op.signal:
                    ins.then_inc(op.sem, 16 if op.dma_key is not None else 1)
            for s, v in finals:
                if waited.get(id(s), 0) < v:
                    e.wait_ge(s, v)

        with nc.Block() as block:
            block.tensor(lambda e: run("pe", e))
            block.scalar(lambda e: run("act", e))
            block.vector(lambda e: run("dve", e))
            block.gpsimd(lambda e: run("pool", e))
            block.sync(lambda e: run("sp", e))
        nc.clear_and_free_semaphores(list(sems.values()))
        nc.all_engine_barrier()


class Phase:
    def __init__(self, nc):
        self.nc = nc
        self.st = ExitStack()
        self.P = Prog()

    def sb(self, name, shape, dt):
        _UID[0] += 1
        t = self.st.enter_context(self.nc.sbuf_tensor("%s_%d" % (name, _UID[0]), list(shape), dt))
        return Tile(t, name)

    def ps(self, name, shape, dt=F32):
        _UID[0] += 1
        t = self.st.enter_context(self.nc.psum_tensor("%s_%d" % (name, _UID[0]), list(shape), dt))
        return Tile(t, name, excl=True)

    def rot(self, name, shape, dt, n):
        return Rot([self.sb("%s%d" % (name, i), shape, dt) for i in range(n)])

    def finish(self):
        self.P.emit(self.nc, self.st)
        self.st.close()


KINDS = (0, 1, 2, 0)
NORM_EPS = 1e-6


class Model:
    def __init__(self, L, CL=256, layers=(0, 1, 2, 3), final_norm=True):
        self.L, self.CL = L, CL
        self.NX, self.NC = L // CH, CL // CH
        self.layers = tuple(layers)
        self.final_norm = final_norm
        self.nc = nc = bass.Bass("TRN2", target_bir_lowering=False)
        self.inputs = {}
        self.outer = ExitStack()
        NT = L + CL

        def din(name, shape):
            self.inputs[name] = tuple(shape)
            return nc.dram_tensor(name, list(shape), F32, kind="ExternalInput").ap()

        self.x_in = din("x", [L, D])
        self.c_in = din("ctx", [CL, D])
        self.cc = din("cc", [128, 8, 2])
        self.ident_in = din("ident", [128, 128])
        self.rope_in = din("rope", [NT, 2, 128])
        self.dmat_in = din("dmat", [128, 6, 128])
        self.jcol_in = din("jcol", [128, 2])
        self.sel_in = din("sel2", [2, 2, 128])
        self.fg_in = din("final_g_bc", [128, D])
        self.lw = {}
        for i in self.layers:
            w = {}
            w["ada_w"] = din("ada_w%d" % i, [D, 3 * D])
            w["ada_bcol"] = din("ada_bcol%d" % i, [128, 24])
            w["ada_brow"] = din("ada_brow%d" % i, [2, D])
            w["ng_col"] = din("ng_col%d" % i, [128, 8])
            k = KINDS[i]
            if k == 0:
                w["w_in"] = din("ret_w_in%d" % i, [D, 6144])
                w["w_out"] = din("ret_w_out%d" % i, [2048, D])
                w["decay"] = din("ret_decay%d" % i, [128, 8])
            elif k == 1:
                w["w_in"] = din("gm_w_in%d" % i, [D, 6144])
                w["w_out"] = din("gm_w_out%d" % i, [2048, D])
                w["vg"] = din("gm_vg%d" % i, [128, 2048])
                w["wsT"] = din("gm_wsT%d" % i, [128, 8, 128])
                w["bsT"] = din("gm_bsT%d" % i, [128, 8])
            else:
                self._rw_inputs(w, i, din)
            self.lw[i] = w
        self.out = nc.dram_tensor("out", [L, D], F32, kind="ExternalOutput").ap()
        self.xs = [nc.dram_tensor("xs%d" % i, [NT, D], F32).ap() for i in range(2)]
        o = self.outer
        self.ident = Tile(o.enter_context(nc.sbuf_tensor("identb", [128, 128], BF16)), "ident")
        self.ident32 = Tile(o.enter_context(nc.sbuf_tensor("ident32", [128, 128], F32)), "ident32")
        self.mod = Tile(o.enter_context(nc.sbuf_tensor("mod", [128, 2, 2, 8], F32)), "mod")
        self.gate_dram = nc.dram_tensor("gate_dram", [2, 128, D], F32).ap()
        self.sel = Tile(o.enter_context(nc.sbuf_tensor("sel", [2, 2, 128], F32)), "sel")
        self._build()
        self.outer.close()

    def chunks_fwd(self):
        return [("c", i) for i in range(self.NC)] + [("x", i) for i in range(self.NX)]

    def chunks_bwd(self):
        return [("c", i) for i in reversed(range(self.NC))] + [("x", i) for i in reversed(range(self.NX))]

    def row0(self, ck):
        return ck[1] * CH if ck[0] == "c" else self.CL + ck[1] * CH

    def src_ap(self, li, ck):
        r0 = self.row0(ck)
        if li == 0:
            return (self.c_in if ck[0] == "c" else self.x_in)[ck[1] * CH:(ck[1] + 1) * CH, :]
        return self.xs[(li - 1) % 2][r0:r0 + CH, :]

    def dst_ap(self, li, ck):
        r0 = self.row0(ck)
        return self.xs[li % 2][r0:r0 + CH, :]

    def _build(self):
        nc = self.nc
        ph = Phase(nc)
        P = ph.P
        t32 = ph.sb("id32", [128, 128], F32)
        P.dma(self.ident32[:], self.ident_in[:, :], "c0", writes=[self.ident32])
        P.dve(lambda e: e.tensor_copy(out=self.ident[:], in_=self.ident32[:]), [self.ident32], [self.ident])
        P.dma(self.sel[:], self.sel_in[:, :, :], "c1", writes=[self.sel])
        ph.finish()
        for li, i in enumerate(self.layers):
            last = (li == len(self.layers) - 1)
            k = KINDS[i]
            if k == 0:
                self.retention_layer(li, i, last)
            elif k == 1:
                self.gmlp_layer(li, i, last)
            else:
                self.rwkv_layer(li, i, last)

    def emit_mod(self, ph, i):
        P, w = ph.P, self.lw[i]
        cc = ph.sb("cc", [128, 8, 2], F32)
        sg = ph.sb("sg", [128, 8, 2], F32)
        bcol = ph.sb("bcol", [128, 24], F32)
        brow = ph.sb("brow", [2, D], F32)
        ng = ph.sb("ng", [128, 8], F32)
        grow = ph.sb("grow", [2, D], F32)
        gate_t = [ph.sb("gate_t%d" % j, [128, D], F32) for j in range(2)]
        aw = ph.sb("adaw", [128, 8, 3 * D], F32)
        awk = [Tile(aw.t[:, k, :], "adaw%d" % k) for k in range(8)]
        pcol = ph.ps("pcol", [128, 16, 2], F32)
        prow = [ph.ps("prow%d" % n, [128, 512], F32) for n in range(2)]
        P.dma(cc[:], self.cc[:, :, :], "m0", writes=[cc])
        P.dma(bcol[:], w["ada_bcol"][:, :], "m1", writes=[bcol])
        P.dma(brow[:], w["ada_brow"][:, :], "m2", writes=[brow])
        P.dma(ng[:], w["ng_col"][:, :], "m3", writes=[ng])
        for k in range(8):
            P.dma(awk[k][:], w["ada_w"][k * 128:(k + 1) * 128, :], "ada%d" % k, writes=[awk[k]], q=("sp" if k % 2 == 0 else "pool"))
        P.act(lambda e: e.activation(out=sg[:], in_=cc[:], func=AF.Sigmoid), [cc], [sg])
        P.dve(lambda e: e.tensor_tensor(out=sg[:], in0=sg[:], in1=cc[:], op=ALU.mult), [sg, cc], [sg])
        for j in range(16):
            for k in range(8):
                P.pe(lambda e, j=j, k=k: e.matmul(pcol[:, j, :], lhsT=awk[k][:, j * 128:(j + 1) * 128], rhs=sg[:, k, :],
                                                  start=(k == 0), stop=(k == 7)), [awk[k], sg], [pcol])
        for n in range(2):
            for k in range(8):
                P.pe(lambda e, n=n, k=k: e.matmul(prow[n][0:2, :], lhsT=sg[:, k, :], rhs=awk[k][:, 2048 + n * 512:2048 + (n + 1) * 512],
                                                  start=(k == 0), stop=(k == 7)), [awk[k], sg], [prow[n]])
        tmp = ph.sb("modtmp", [128, 16, 2], F32)
        P.dve(lambda e: e.tensor_tensor(out=tmp[:], in0=pcol[:], in1=bcol[:, 0:16].unsqueeze(2).to_broadcast([128, 16, 2]), op=ALU.add),
              [pcol, bcol], [tmp])
        mod = self.mod
        for j in range(2):
            P.dve(lambda e, j=j: e.scalar_tensor_tensor(out=mod[:, 0, j, :], in0=tmp[:, 8:16, j], scalar=1.0, in1=ng[:], op0=ALU.add, op1=ALU.mult),
                  [tmp, ng], [mod])
            P.dve(lambda e, j=j: e.tensor_copy(out=mod[:, 1, j, :], in_=tmp[:, 0:8, j]), [tmp], [mod])
        for n in range(2):
            P.dve(lambda e, n=n: e.tensor_tensor(out=grow[:, n * 512:(n + 1) * 512], in0=prow[n][0:2, :], in1=brow[:, n * 512:(n + 1) * 512], op=ALU.add),
                  [prow[n], brow], [grow])
        for j in range(2):
            for n in range(2):
                P.pe(lambda e, j=j, n=n: e.matmul(prow[n][:, :], lhsT=self.sel[:, j, :], rhs=grow[:, n * 512:(n + 1) * 512], start=True, stop=True),
                     [self.sel, grow], [prow[n]])
                P.act(lambda e, j=j, n=n: e.activation(out=gate_t[j][:, n * 512:(n + 1) * 512], in_=prow[n][:, :], func=AF.Copy),
                      [prow[n]], [gate_t[j]])
        for j in range(2):
            P.dma(self.gate_dram[j], gate_t[j][:], "gst%d" % j, reads=[gate_t[j]], q="pool")

    def load_w(self, ph, Wt, src, col0, ncols, KT):
        P = ph.P
        views = []
        for k in range(KT):
            v = Tile(Wt.t[:, k, :], "%s_k%d" % (Wt.b.name, k))
            views.append(v)
            step = 2048
            for c0 in range(0, ncols, step):
                wdt = min(step, ncols - c0)
                P.dma(v.t[:, c0:c0 + wdt], src[k * 128:(k + 1) * 128, col0 + c0:col0 + c0 + wdt],
                      "w%s%d_%d" % (Wt.b.name, k, c0), writes=[v], q="pool")
        return views

    def front(self, ph, R, src, which, src_reads=()):
        P = ph.P
        mod = self.mod
        xt = R["xt"].next()
        P.dma(xt[:], src, "xt%d" % R["xt"].slot, reads=list(src_reads), writes=[xt])
        st = R["st"].next()
        junk = R["junk"]
        P.act(lambda e: e.activation(out=junk[:], in_=xt[:], func=AF.Square, accum_out=st[:, 0:1]), [xt], [junk, st])
        P.act(lambda e: e.activation(out=st[:, 1:2], in_=st[:, 0:1], func=AF.Sqrt, scale=1.0 / D, bias=NORM_EPS), [st], [st])
        P.dve(lambda e: e.reciprocal(out=st[:, 2:3], in_=st[:, 1:2]), [st], [st])
        xn = R["xn"].next()
        P.dve(lambda e: e.tensor_scalar(out=xn[:], in0=xt[:], scalar1=st[:, 2:3], scalar2=None, op0=ALU.mult), [xt, st], [xn])
        ptr = R["ptrx"]
        for k in range(8):
            P.pe(lambda e, k=k: e.transpose(out=ptr[:, k, :], in_=xn[:, k * 128:(k + 1) * 128], identity=self.ident[:]),
                 [xn, self.ident], [ptr])
        hT = R["hT"].next()
        for k in range(8):
            P.act(lambda e, k=k: e.activation(out=hT[:, k, :], in_=ptr[:, k, :], func=AF.Identity,
                                              scale=mod[:, 0, which, k:k + 1], bias=mod[:, 1, which, k:k + 1]),
                  [ptr, mod], [hT])
        return hT, xt

    def front_bufs(self, ph, n_xn=2, n_xt=2, n_hT=2, junk=True):
        return {
            "xt": ph.rot("xt", [128, D], F32, n_xt),
            "st": ph.rot("st", [128, 4], F32, 4),
            "junk": ph.sb("junk", [128, D], BF16) if junk else None,
            "xn": ph.rot("xn", [128, D], BF16, n_xn),
            "hT": ph.rot("hT", [128, 8, 128], BF16, n_hT),
            "ptrx": ph.ps("ptrx", [128, 8, 128], BF16),
        }

    def tail(self, ph, R, gated, Wo, li, ck, xres, last, KT=16):
        P = ph.P
        which = 1 if ck[0] == "c" else 0
        gT = R["gT"].next()
        for half in range(KT // 8):
            ptg = R["ptg"][half]
            for k in range(8):
                kk = half * 8 + k
                P.pe(lambda e, k=k, kk=kk, ptg=ptg: e.transpose(out=ptg[:, k, :], in_=gated[:, kk * 128:(kk + 1) * 128], identity=self.ident[:]),
                     [gated, self.ident], [ptg])
            if half == 0:
                P.act(lambda e, half=half, ptg=ptg: e.activation(out=gT[:, half * 8:(half + 1) * 8, :], in_=ptg[:], func=AF.Copy), [ptg], [gT])
            else:
                P.dve(lambda e, half=half, ptg=ptg: e.tensor_copy(out=gT[:, half * 8:(half + 1) * 8, :], in_=ptg[:]), [ptg], [gT])
        xo = R["xo"].next()
        for n in range(2):
            po = R["pout"].next() if isinstance(R["pout"], Rot) else R["pout"]
            for k in range(KT):
                P.pe(lambda e, n=n, k=k, po=po: e.matmul(po[:], lhsT=gT[:, k, :], rhs=Wo[k][:, n * 512:(n + 1) * 512], start=(k == 0), stop=(k == KT - 1)),
                     [gT, Wo[k]], [po])
            P.dve(lambda e, n=n, po=po: e.tensor_tensor(out=xo[:, n * 512:(n + 1) * 512], in0=po[:], in1=self.gate[which][:, n * 512:(n + 1) * 512], op=ALU.mult),
                  [po, self.gate[which]], [xo])
        P.pool(lambda e: e.tensor_tensor(out=xo[:], in0=xo[:], in1=xres[:], op=ALU.add), [xo, xres], [xo])
        if last and self.final_norm:
            st = R["st"].next()
            junk = R["junk"]
            P.act(lambda e: e.activation(out=junk[:], in_=xo[:], func=AF.Square, accum_out=st[:, 0:1]), [xo], [junk, st])
            P.act(lambda e: e.activation(out=st[:, 1:2], in_=st[:, 0:1], func=AF.Sqrt, scale=1.0 / D, bias=NORM_EPS), [st], [st])
            P.dve(lambda e: e.reciprocal(out=st[:, 2:3], in_=st[:, 1:2]), [st], [st])
            P.dve(lambda e: e.scalar_tensor_tensor(out=xo[:], in0=xo[:], scalar=st[:, 2:3], in1=self.fg[:], op0=ALU.mult, op1=ALU.mult),
                  [xo, st, self.fg], [xo])
            dst = self.out[ck[1] * CH:(ck[1] + 1) * CH, :]
        elif last:
            dst = self.out[ck[1] * CH:(ck[1] + 1) * CH, :]
        else:
            dst = self.dst_ap(li, ck)
        P.dma(dst, xo[:], "xo%d" % R["xo"].slot, reads=[xo], q="pool")

    def tail_bufs(self, ph, n_gT=2, n_xo=2, last=False, pout=True):
        if last and self.final_norm:
            self.fg = ph.sb("fg", [128, D], F32)
            ph.P.dma(self.fg[:], self.fg_in[:, :], "fgld", writes=[self.fg])
        self.gate = [ph.sb("gate%d" % j, [128, D], F32) for j in range(2)]
        for j in range(2):
            ph.P.dma(self.gate[j][:], self.gate_dram[j], "gld%d" % j, writes=[self.gate[j]])
        return {
            "gT": ph.rot("gT", [128, 16, 128], BF16, n_gT),
            "ptg": [ph.ps("ptg%d" % h, [128, 8, 128], BF16) for h in range(2)],
            "pout": ph.ps("pout", [128, 512], F32) if pout else None,
            "xo": ph.rot("xo", [128, D], F32, n_xo),
        }

    def retention_layer(self, li, i, last):
        nc, w = self.nc, self.lw[i]
        NCH = self.NX + self.NC
        if not hasattr(self, "scr_q"):
            self.scr_q = nc.dram_tensor("scr_q", [NCH, 128, 1024], BF16).ap()
            self.scr_k = nc.dram_tensor("scr_k", [NCH, 128, 1024], BF16).ap()
            self.scr_v = nc.dram_tensor("scr_v", [NCH, 128, 2048], BF16).ap()
            self.scr_o = nc.dram_tensor("scr_o", [NCH, 128, 2048], F32).ap()
            self.rt_cd = Tile(self.outer.enter_context(nc.sbuf_tensor("rt_cd", [128, 8], F32)), "rt_cd")
        ph = Phase(nc)
        self.emit_mod(ph, i)
        ph.finish()

        ph = Phase(nc)
        P = ph.P
        Wt = ph.sb("Wqkv", [128, 8, 4096], BF16)
        Wk = self.load_w(ph, Wt, w["w_in"], 0, 4096, 8)
        R = self.front_bufs(ph)
        dec = ph.sb("dec", [128, 8], F32)
        lg = ph.sb("lg", [128, 8], F32)
        dm = ph.sb("dm", [128, 6, 128], F32)
        jc = ph.sb("jc", [128, 2], F32)
        tA = ph.sb("tA", [128, 128], F32)
        tB = ph.sb("tB", [128, 128], F32)
        MT = ph.sb("MT", [128, 4, 128], F32)
        QDf = ph.sb("QDf", [128, 8, 128], BF16)
        QDb = ph.sb("QDb", [128, 8, 128], BF16)
        kdec = ph.sb("kdec", [128, 8], F32)
        cd = self.rt_cd
        P.dma(dec[:], w["decay"][:, :], "t0", writes=[dec])
        P.dma(dm[:], self.dmat_in[:, :, :], "t1", writes=[dm])
        P.dma(jc[:], self.jcol_in[:, :], "t2", writes=[jc])
        P.act(lambda e: e.activation(out=lg[:], in_=dec[:], func=AF.Exp, scale=-1.0), [dec], [lg])
        P.act(lambda e: e.activation(out=lg[:], in_=lg[:], func=AF.Ln, bias=1.0), [lg], [lg])
        P.dve(lambda e: e.tensor_scalar(out=lg[:], in0=lg[:], scalar1=-1.0, scalar2=None, op0=ALU.mult), [lg], [lg])
        P.act(lambda e: e.activation(out=cd[:], in_=lg[:], func=AF.Exp, scale=128.0), [lg], [cd])
        P.dve(lambda e: e.tensor_scalar(out=kdec[:, 0:4], in0=lg[:, 0:4], scalar1=jc[:, 0:1], scalar2=None, op0=ALU.mult), [lg, jc], [kdec])
        P.dve(lambda e: e.tensor_scalar(out=kdec[:, 4:8], in0=lg[:, 4:8], scalar1=jc[:, 1:2], scalar2=None, op0=ALU.mult), [lg, jc], [kdec])
        P.act(lambda e: e.activation(out=kdec[:], in_=kdec[:], func=AF.Exp), [kdec], [kdec])
        P.dve(lambda e: e.tensor_scalar(out=kdec[:], in0=kdec[:], scalar1=0.0625, scalar2=None, op0=ALU.mult), [kdec], [kdec])
        for h in range(4):
            P.act(lambda e, h=h: e.activation(out=tA[:], in_=dm[:, 0, :], func=AF.Exp, scale=lg[:, h:h + 1]), [dm, lg, MT], [tA])
            P.act(lambda e, h=h: e.activation(out=tB[:], in_=dm[:, 1, :], func=AF.Exp, scale=lg[:, 4 + h:5 + h]), [dm, lg, MT], [tB])
            P.dve(lambda e: e.tensor_tensor(out=tA[:], in0=tA[:], in1=dm[:, 2, :], op=ALU.mult), [tA, dm], [tA])
            P.dve(lambda e: e.tensor_tensor(out=tB[:], in0=tB[:], in1=dm[:, 3, :], op=ALU.mult), [tB, dm], [tB])
            P.dve(lambda e: e.tensor_tensor(out=tA[:], in0=tA[:], in1=tB[:], op=ALU.add), [tA, tB], [tA])
            P.dve(lambda e, h=h: e.tensor_scalar(out=MT[:, h, :], in0=tA[:], scalar1=0.0625, scalar2=None, op0=ALU.mult), [tA], [MT])
            for r in range(2):
                P.act(lambda e, h=h, r=r: e.activation(out=QDf[:, 2 * h + r, :], in_=dm[:, 4, :], func=AF.Exp, scale=lg[:, h:h + 1]), [dm, lg], [QDf])
                P.act(lambda e, h=h, r=r: e.activation(out=QDb[:, 2 * h + r, :], in_=dm[:, 5, :], func=AF.Exp, scale=lg[:, 4 + h:5 + h]), [dm, lg], [QDb])
        mm = Rot([ph.ps("mm%d" % j, [128, 512], F32) for j in range(2)])
        ptq = ph.ps("ptq", [128, 8, 128], BF16)
        ptk = ph.ps("ptk", [128, 8, 128], BF16)
        Gp = Rot([ph.ps("Gp%d" % j, [128, 512], F32) for j in range(3)])
        cs = ph.rot("cs", [128, 2, 2, 128], F32, 2)
        rtmp = ph.rot("rtmp", [128, 4, 2, 128], F32, 2)
        q_r = ph.sb("q_r", [128, 1024], BF16)
        k_r = ph.sb("k_r", [128, 1024], BF16)
        qT = ph.rot("qT", [128, 8, 128], BF16, 2)
        kT = ph.rot("kT", [128, 8, 128], BF16, 2)
        qTf = ph.rot("qTf", [128, 8, 128], BF16, 2)
        qTb = ph.rot("qTb", [128, 8, 128], BF16, 2)
        kf = ph.rot("kf", [128, 1024], BF16, 2)
        kb = ph.rot("kb", [128, 1024], BF16, 2)
        vsb = ph.rot("vsb", [128, 2048], BF16, 2)
        sT = ph.rot("sT", [128, 4, 128], BF16, 2)
        osb = ph.rot("osb", [128, 2048], F32, 2)
        Sbf = ph.sb("Sbf", [128, 8, 512], BF16)
        Sbk = [Tile(Sbf.t[:, k, :], "Sb%d" % k) for k in range(8)]
        for k in range(8):
            P.pool(lambda e, k=k: e.memset(Sbk[k][:], 0.0), [], [Sbk[k]])
        cdI = self.ret_cdI(ph)

        for ck in self.chunks_fwd():
            which = 1 if ck[0] == "c" else 0
            r0 = self.row0(ck)
            cidx = r0 // CH
            hT, xt = self.front(ph, R, self.src_ap(li, ck), which)
            c_ = cs.next()
            for r in range(2):
                P.dma(c_[:, :, r, :], self.rope_in[r0:r0 + CH, :, :], "cs%d_%d" % (cs.slot, r), writes=[c_])
            v_ = vsb.next()
            for n in range(8):
                bank = mm.next()
                for k in range(8):
                    P.pe(lambda e, bank=bank, n=n, k=k, hT=hT: e.matmul(bank[:], lhsT=hT[:, k, :], rhs=Wk[k][:, n * 512:(n + 1) * 512],
                                                                        start=(k == 0), stop=(k == 7)), [hT, Wk[k]], [bank])
                if n < 4:
                    dst = q_r if n < 2 else k_r
                    tmp = rtmp.next()
                    pb = bank[:].rearrange("p (h t c) -> p h t c", h=2, t=2)
                    t1, t2 = pb[:, :, 0, :], pb[:, :, 1, :]
                    cos2, sin2 = c_[:, 0, :, :], c_[:, 1, :, :]
                    P.dve(lambda e, tmp=tmp, t1=t1, cos2=cos2: e.tensor_tensor(out=tmp[:, 0], in0=t1, in1=cos2, op=ALU.mult), [bank, c_], [tmp])
                    P.dve(lambda e, tmp=tmp, t2=t2, sin2=sin2: e.tensor_tensor(out=tmp[:, 1], in0=t2, in1=sin2, op=ALU.mult), [bank, c_], [tmp])
                    P.dve(lambda e, tmp=tmp, t1=t1, sin2=sin2: e.tensor_tensor(out=tmp[:, 2], in0=t1, in1=sin2, op=ALU.mult), [bank, c_], [tmp])
                    P.dve(lambda e, tmp=tmp, t2=t2, cos2=cos2: e.tensor_tensor(out=tmp[:, 3], in0=t2, in1=cos2, op=ALU.mult), [bank, c_], [tmp])
                    dv = dst[:].rearrange("p (h t c) -> p h t c", h=4, t=2)
                    h0 = 2 * (n % 2)
                    P.pool(lambda e, tmp=tmp, dv=dv, h0=h0: e.tensor_tensor(out=dv[:, h0:h0 + 2, 0, :], in0=tmp[:, 0], in1=tmp[:, 1], op=ALU.subtract),
                           [tmp], [dst])
                    P.pool(lambda e, tmp=tmp, dv=dv, h0=h0: e.tensor_tensor(out=dv[:, h0:h0 + 2, 1, :], in0=tmp[:, 2], in1=tmp[:, 3], op=ALU.add),
                           [tmp], [dst])
                else:
                    P.act(lambda e, bank=bank, n=n, v_=v_: e.activation(out=v_[:, (n - 4) * 512:(n - 3) * 512], in_=bank[:], func=AF.Copy), [bank], [v_])
            qT_, kT_, qTf_, qTb_ = qT.next(), kT.next(), qTf.next(), qTb.next()
            for k in range(8):
                P.pe(lambda e, k=k: e.transpose(out=ptq[:, k, :], in_=q_r[:, k * 128:(k + 1) * 128], identity=self.ident[:]), [q_r, self.ident], [ptq])
            P.act(lambda e, qT_=qT_: e.activation(out=qT_[:], in_=ptq[:], func=AF.Copy), [ptq], [qT_])
            for k in range(8):
                P.pe(lambda e, k=k: e.transpose(out=ptk[:, k, :], in_=k_r[:, k * 128:(k + 1) * 128], identity=self.ident[:]), [k_r, self.ident], [ptk])
            P.act(lambda e, kT_=kT_: e.activation(out=kT_[:], in_=ptk[:], func=AF.Copy), [ptk], [kT_])
            P.dve(lambda e, qT_=qT_, qTf_=qTf_: e.tensor_tensor(out=qTf_[:], in0=qT_[:], in1=QDf[:], op=ALU.mult), [qT_, QDf], [qTf_])
            P.dve(lambda e, qT_=qT_, qTb_=qTb_: e.tensor_tensor(out=qTb_[:], in0=qT_[:], in1=QDb[:], op=ALU.mult), [qT_, QDb], [qTb_])
            kf_, kb_ = kf.next(), kb.next()
            krv = k_r[:].rearrange("p (h c) -> p h c", h=4)
            P.pool(lambda e, kf_=kf_: e.tensor_tensor(out=kf_[:].rearrange("p (h c) -> p h c", h=4), in0=krv,
                                                      in1=kdec[:, 0:4].unsqueeze(2).to_broadcast([128, 4, 256]), op=ALU.mult), [k_r, kdec], [kf_])
            P.pool(lambda e, kb_=kb_: e.tensor_tensor(out=kb_[:].rearrange("p (h c) -> p h c", h=4), in0=krv,
                                                      in1=kdec[:, 4:8].unsqueeze(2).to_broadcast([128, 4, 256]), op=ALU.mult), [k_r, kdec], [kb_])
            psc = Gp.next()
            for h in range(4):
                for hf in range(2):
                    P.pe(lambda e, h=h, hf=hf, kT_=kT_, qT_=qT_, psc=psc: e.matmul(psc[:, h * 128:(h + 1) * 128], lhsT=kT_[:, 2 * h + hf, :], rhs=qT_[:, 2 * h + hf, :],
                                                                                   start=(hf == 0), stop=(hf == 1)), [kT_, qT_], [psc])
            sT_ = sT.next()
            P.dve(lambda e, sT_=sT_, psc=psc: e.tensor_tensor(out=sT_[:], in0=psc[:].rearrange("p (h t) -> p h t", h=4), in1=MT[:], op=ALU.mult), [psc, MT], [sT_])
            o_ = osb.next()
            for h in range(4):
                po = Gp.next()
                P.pe(lambda e, h=h, sT_=sT_, v_=v_, po=po: e.matmul(po[:], lhsT=sT_[:, h, :], rhs=v_[:, h * 512:(h + 1) * 512], start=True, stop=False), [sT_, v_], [po])
                for hf in range(2):
                    kt = 2 * h + hf
                    P.pe(lambda e, kt=kt, hf=hf, qTf_=qTf_, po=po: e.matmul(po[:], lhsT=qTf_[:, kt, :], rhs=Sbk[kt][:], start=False, stop=(hf == 1)),
                         [qTf_, Sbk[kt]], [po])
                P.act(lambda e, h=h, o_=o_, po=po: e.activation(out=o_[:, h * 512:(h + 1) * 512], in_=po[:], func=AF.Copy), [po], [o_])
            self.ret_state_update(P, Gp, kf_, v_, Sbk, cdI, 0)
            P.dma(self.scr_q[cidx].rearrange("p (k t) -> p k t", k=8), qTb_[:], "sq%d" % qTb.slot, reads=[qTb_], q="pool")
            P.dma(self.scr_k[cidx], kb_[:], "sk%d" % kb.slot, reads=[kb_], q="pool")
            P.dma(self.scr_v[cidx], v_[:], "sv%d" % vsb.slot, reads=[v_], q="pool")
            P.dma(self.scr_o[cidx], o_[:], "so%d" % osb.slot, reads=[o_], q="pool")
        ph.finish()

        ph = Phase(nc)
        P = ph.P
        Wzt = ph.sb("Wz", [128, 8, 2048], BF16)
        Wz = self.load_w(ph, Wzt, w["w_in"], 4096, 2048, 8)
        Wot = ph.sb("Wo", [128, 16, 1024], BF16)
        Wo = self.load_w(ph, Wot, w["w_out"], 0, 1024, 16)
        R = self.front_bufs(ph)
        R.update(self.tail_bufs(ph, last=last, pout=False))
        mm = Rot([ph.ps("mm%d" % j, [128, 512], F32) for j in range(2)])
        Gp = Rot([ph.ps("Gp%d" % j, [128, 512], F32) for j in range(3)])
        R["pout"] = Gp
        qTb = ph.rot("qTb", [128, 8, 128], BF16, 2)
        kb = ph.rot("kb", [128, 1024], BF16, 2)
        vsb = ph.rot("vsb", [128, 2048], BF16, 2)
        osb = ph.rot("osb", [128, 2048], F32, 2)
        sz = ph.rot("sz", [128, 2048], F32, 2)
        gated = ph.rot("gated", [128, 2048], BF16, 2)
        st4 = ph.rot("st4", [128, 12], F32, 2)
        Sbf = ph.sb("Sbf", [128, 8, 512], BF16)
        Sbk = [Tile(Sbf.t[:, k, :], "Sb%d" % k) for k in range(8)]
        for k in range(8):
            P.pool(lambda e, k=k: e.memset(Sbk[k][:], 0.0), [], [Sbk[k]])
        cdI = self.ret_cdI(ph)
        cd = self.rt_cd
        for ck in self.chunks_bwd():
            which = 1 if ck[0] == "c" else 0
            cidx = self.row0(ck) // CH
            need_out = not (last and ck[0] == "c")
            kb_, v_ = kb.next(), vsb.next()
            P.dma(kb_[:], self.scr_k[cidx], "lk%d" % kb.slot, writes=[kb_])
            P.dma(v_[:], self.scr_v[cidx], "lv%d" % vsb.slot, writes=[v_])
            if need_out:
                q_, o_ = qTb.next(), osb.next()
                P.dma(q_[:], self.scr_q[cidx].rearrange("p (k t) -> p k t", k=8), "lq%d" % qTb.slot, writes=[q_])
                P.dma(o_[:], self.scr_o[cidx], "lo%d" % osb.slot, writes=[o_])
                hT, xt = self.front(ph, R, self.src_ap(li, ck), which)
                sz_ = sz.next()
                for n in range(4):
                    bank = mm.next()
                    for k in range(8):
                        P.pe(lambda e, bank=bank, n=n, k=k, hT=hT: e.matmul(bank[:], lhsT=hT[:, k, :], rhs=Wz[k][:, n * 512:(n + 1) * 512],
                                                                            start=(k == 0), stop=(k == 7)), [hT, Wz[k]], [bank])
                    P.act(lambda e, bank=bank, n=n, sz_=sz_: e.activation(out=sz_[:, n * 512:(n + 1) * 512], in_=bank[:], func=AF.Silu), [bank], [sz_])
                s4 = st4.next()
                junk = R["junk"]
                for h in range(4):
                    po = Gp.next()
                    for hf in range(2):
                        kt = 2 * h + hf
                        P.pe(lambda e, kt=kt, hf=hf, q_=q_, po=po: e.matmul(po[:], lhsT=q_[:, kt, :], rhs=Sbk[kt][:], start=(hf == 0), stop=(hf == 1)),
                             [q_, Sbk[kt]], [po])
                    P.dve(lambda e, h=h, o_=o_, po=po: e.tensor_tensor(out=o_[:, h * 512:(h + 1) * 512], in0=po[:], in1=o_[:, h * 512:(h + 1) * 512], op=ALU.add),
                          [po, o_], [o_])
                    P.act(lambda e, h=h, o_=o_, s4=s4: e.activation(out=junk[:, 0:512], in_=o_[:, h * 512:(h + 1) * 512], func=AF.Square, accum_out=s4[:, h:h + 1]),
                          [o_], [junk, s4])
                P.act(lambda e, s4=s4: e.activation(out=s4[:, 4:8], in_=s4[:, 0:4], func=AF.Sqrt, scale=1.0 / 512, bias=NORM_EPS), [s4], [s4])
                P.dve(lambda e, s4=s4: e.reciprocal(out=s4[:, 8:12], in_=s4[:, 4:8]), [s4], [s4])
                g_ = gated.next()
                for h in range(4):
                    P.dve(lambda e, h=h, o_=o_, s4=s4, sz_=sz_, g_=g_: e.scalar_tensor_tensor(
                        out=g_[:, h * 512:(h + 1) * 512], in0=o_[:, h * 512:(h + 1) * 512], scalar=s4[:, 8 + h:9 + h],
                        in1=sz_[:, h * 512:(h + 1) * 512], op0=ALU.mult, op1=ALU.mult), [o_, s4, sz_], [g_])
                self.tail(ph, R, g_, Wo, li, ck, xt, last)
            self.ret_state_update(P, Gp, kb_, v_, Sbk, cdI, 4)
        ph.finish()

    def ret_state_update(self, P, pst_rot, kd, v_, Sbk, cdI, c0):
        for kt in range(8):
            h = kt // 2
            pst = pst_rot.next()
            P.pe(lambda e, kt=kt, h=h, pst=pst: e.matmul(pst[:], lhsT=cdI[:, c0 + h, :], rhs=Sbk[kt][:], start=True, stop=False), [cdI, Sbk[kt]], [pst])
            P.pe(lambda e, kt=kt, h=h, pst=pst: e.matmul(pst[:], lhsT=kd[:, kt * 128:(kt + 1) * 128], rhs=v_[:, h * 512:(h + 1) * 512], start=False, stop=True),
                 [kd, v_], [pst])
            if kt % 2 == 0:
                P.act(lambda e, kt=kt, pst=pst: e.activation(out=Sbk[kt][:], in_=pst[:], func=AF.Copy), [pst], [Sbk[kt]])
            else:
                P.dve(lambda e, kt=kt, pst=pst: e.tensor_copy(out=Sbk[kt][:], in_=pst[:]), [pst], [Sbk[kt]])

    def ret_cdI(self, ph):
        cdI = ph.sb("cdI", [128, 8, 128], BF16)
        for j in range(8):
            ph.P.dve(lambda e, j=j: e.tensor_scalar(out=cdI[:, j, :], in0=self.ident32[:], scalar1=self.rt_cd[:, j:j + 1], scalar2=None, op0=ALU.mult),
                     [self.ident32, self.rt_cd], [cdI])
        return cdI

    def gmlp_layer(self, li, i, last):
        nc, w = self.nc, self.lw[i]
        ph = Phase(nc)
        self.emit_mod(ph, i)
        ph.finish()
        ph = Phase(nc)
        P = ph.P
        Wt = ph.sb("Wuvz", [128, 8, 6144], BF16)
        Wk = self.load_w(ph, Wt, w["w_in"], 0, 6144, 8)
        Wot = ph.sb("Wo", [128, 16, 1024], BF16)
        Wo = self.load_w(ph, Wot, w["w_out"], 0, 1024, 16)
        R = self.front_bufs(ph, n_xn=1)
        R.update(self.tail_bufs(ph, n_gT=1, n_xo=1, last=last))
        mm = Rot([ph.ps("mm%d" % j, [128, 512], F32) for j in range(2)])
        spb = Rot([ph.ps("spb%d" % j, [128, 512], F32) for j in range(2)])
        wsT = ph.sb("wsT", [128, 8, 128], BF16)
        bsT = ph.sb("bsT", [128, 8], F32)
        vg = ph.sb("vg", [128, 2048], F32)
        P.dma(wsT[:], w["wsT"][:, :, :], "g0", writes=[wsT], q="pool")
        P.dma(bsT[:], w["bsT"][:, :], "g1", writes=[bsT])
        P.dma(vg[:], w["vg"][:, :], "g2", writes=[vg])
        usb = ph.rot("usb", [128, 2048], F32, 1)
        vsb = ph.rot("vsb", [128, 2048], F32, 1)
        szb = ph.rot("szb", [128, 2048], BF16, 1)
        vnb = ph.rot("vnb", [128, 2048], BF16, 1)
        gated = ph.rot("gated", [128, 2048], BF16, 1)
        stv = ph.rot("stv", [128, 16], F32, 2)
        junk = R["junk"]
        order = self.chunks_fwd()
        if last:
            order = [ck for ck in order if ck[0] == "x"]
        for ck in order:
            which = 1 if ck[0] == "c" else 0
            hT, xt = self.front(ph, R, self.src_ap(li, ck), which)
            u_, v_, z_, s_ = usb.next(), vsb.next(), szb.next(), stv.next()
            for n in range(12):
                bank = mm.next()
                for k in range(8):
                    P.pe(lambda e, bank=bank, n=n, k=k, hT=hT: e.matmul(bank[:], lhsT=hT[:, k, :], rhs=Wk[k][:, n * 512:(n + 1) * 512],
                                                                        start=(k == 0), stop=(k == 7)), [hT, Wk[k]], [bank])
                if n < 4:
                    P.act(lambda e, bank=bank, n=n, u_=u_: e.activation(out=u_[:, n * 512:(n + 1) * 512], in_=bank[:], func=AF.Copy), [bank], [u_])
                elif n < 8:
                    P.act(lambda e, bank=bank, n=n, v_=v_, s_=s_: e.activation(out=v_[:, (n - 4) * 512:(n - 3) * 512], in_=bank[:], func=AF.Identity,
                                                                              accum_out=s_[:, n - 4:n - 3]), [bank], [v_, s_])
                else:
                    P.act(lambda e, bank=bank, n=n, z_=z_: e.activation(out=z_[:, (n - 8) * 512:(n - 7) * 512], in_=bank[:], func=AF.Silu), [bank], [z_])
            for hh in range(2):
                P.act(lambda e, v_=v_, s_=s_, hh=hh: e.activation(out=junk[:], in_=v_[:, hh * 1024:(hh + 1) * 1024], func=AF.Square,
                                                                  accum_out=s_[:, 11 + hh:12 + hh]), [v_], [junk, s_])
            P.dve(lambda e, s_=s_: e.tensor_tensor(out=s_[:, 4:5], in0=s_[:, 11:12], in1=s_[:, 12:13], op=ALU.add), [s_], [s_])
            P.dve(lambda e, s_=s_: e.tensor_reduce(out=s_[:, 5:6], in_=s_[:, 0:4], axis=AX.X, op=ALU.add), [s_], [s_])
            P.dve(lambda e, s_=s_: e.tensor_scalar(out=s_[:, 5:6], in0=s_[:, 5:6], scalar1=1.0 / 2048, scalar2=None, op0=ALU.mult), [s_], [s_])
            P.dve(lambda e, s_=s_: e.tensor_tensor(out=s_[:, 6:7], in0=s_[:, 5:6], in1=s_[:, 5:6], op=ALU.mult), [s_], [s_])
            P.dve(lambda e, s_=s_: e.scalar_tensor_tensor(out=s_[:, 7:8], in0=s_[:, 4:5], scalar=1.0 / 2048, in1=s_[:, 6:7], op0=ALU.mult, op1=ALU.subtract),
                  [s_], [s_])
            P.act(lambda e, s_=s_: e.activation(out=s_[:, 8:9], in_=s_[:, 7:8], func=AF.Sqrt, scale=1.0, bias=NORM_EPS), [s_], [s_])
            P.dve(lambda e, s_=s_: e.reciprocal(out=s_[:, 9:10], in_=s_[:, 8:9]), [s_], [s_])
            P.dve(lambda e, s_=s_: e.scalar_tensor_tensor(out=s_[:, 10:11], in0=s_[:, 5:6], scalar=-1.0, in1=s_[:, 9:10], op0=ALU.mult, op1=ALU.mult),
                  [s_], [s_])
            P.act(lambda e, v_=v_, s_=s_: e.activation(out=v_[:], in_=v_[:], func=AF.Identity, scale=s_[:, 9:10], bias=s_[:, 10:11]), [v_, s_], [v_])
            vn_ = vnb.next()
            P.dve(lambda e, v_=v_, vn_=vn_: e.tensor_tensor(out=vn_[:], in0=v_[:], in1=vg[:], op=ALU.mult), [v_, vg], [vn_])
            for g in range(8):
                sb_ = spb.next()
                P.pe(lambda e, g=g, sb_=sb_, vn_=vn_: e.matmul(sb_[:, 0:256], lhsT=wsT[:, g, :], rhs=vn_[:, g * 256:(g + 1) * 256], start=True, stop=True),
                     [wsT, vn_], [sb_])
                P.dve(lambda e, g=g, sb_=sb_, u_=u_: e.scalar_tensor_tensor(out=u_[:, g * 256:(g + 1) * 256], in0=sb_[:, 0:256], scalar=bsT[:, g:g + 1],
                                                                           in1=u_[:, g * 256:(g + 1) * 256], op0=ALU.add, op1=ALU.mult), [sb_, bsT, u_], [u_])
            g_ = gated.next()
            P.pool(lambda e, u_=u_, z_=z_, g_=g_: e.tensor_tensor(out=g_[:], in0=u_[:], in1=z_[:], op=ALU.mult), [u_, z_], [g_])
            self.tail(ph, R, g_, Wo, li, ck, xt, last)
        ph.finish()

    def _rw_inputs(self, w, i, din):
        w["mu"] = din("rw_mu%d" % i, [128, 6, 8])
        w["rkvg"] = din("rw_rkvg%d" % i, [4, D, D])
        w["w1"] = din("rw_w1%d" % i, [2, D, 64])
        w["a1"] = din("rw_a1%d" % i, [2, D, 64])
        w["w2"] = din("rw_w2%d" % i, [2, 64, D])
        w["a2"] = din("rw_a2%d" % i, [2, 64, D])
        w["rows"] = din("rw_rows%d" % i, [8, D])
        w["bc"] = din("rw_bc%d" % i, [128, 5, D])
        w["w_out"] = din("rw_wout%d" % i, [D, D])
        w["masks"] = din("rw_masks%d" % i, [2, 128, 4, 128])
        w["sel8"] = din("rw_sel8%d" % i, [8, 8, 128])
        w["negc"] = din("rw_negc%d" % i, [128, 1])
        w["bmask"] = din("rw_bmask%d" % i, [128, 4, 128])
        w["cmask"] = din("rw_cmask%d" % i, [2, 128, 7, 128])

    def rw_shift(self, P, sh, hc, hp, hn, kind):
        if kind == "x":
            P.act(lambda e: e.activation(out=sh[:, 0:2, 1:128], in_=hc[:, 0:2, 0:127], func=AF.Copy), [hc], [sh])
            P.pool(lambda e: e.memset(sh[:, 0:2, :].rearrange("p k (r c) -> p k r c", c=64)[:, :, :, 0:1], 0.0), [], [sh])
            P.act(lambda e: e.activation(out=sh[:, 2:4, 0:127], in_=hc[:, 2:4, 1:128], func=AF.Copy), [hc], [sh])
            P.pool(lambda e: e.memset(sh[:, 2:4, :].rearrange("p k (r c) -> p k r c", c=64)[:, :, :, 63:64], 0.0), [], [sh])
            P.act(lambda e: e.activation(out=sh[:, 4:6, 64:128], in_=hc[:, 4:6, 0:64], func=AF.Copy), [hc], [sh])
            if hp is not None:
                P.act(lambda e: e.activation(out=sh[:, 4:6, 0:64], in_=hp[:, 4:6, 64:128], func=AF.Copy), [hp], [sh])
            else:
                P.pool(lambda e: e.memset(sh[:, 4:6, 0:64], 0.0), [], [sh])
            P.act(lambda e: e.activation(out=sh[:, 6:8, 0:64], in_=hc[:, 6:8, 64:128], func=AF.Copy), [hc], [sh])
            if hn is not None:
                P.act(lambda e: e.activation(out=sh[:, 6:8, 64:128], in_=hn[:, 6:8, 0:64], func=AF.Copy), [hn], [sh])
            else:
                P.pool(lambda e: e.memset(sh[:, 6:8, 64:128], 0.0), [], [sh])
        else:
            P.act(lambda e: e.activation(out=sh[:, 0:4, 1:128], in_=hc[:, 0:4, 0:127], func=AF.Copy), [hc], [sh])
            if hp is not None:
                P.act(lambda e: e.activation(out=sh[:, 0:4, 0:1], in_=hp[:, 0:4, 127:128], func=AF.Copy), [hp], [sh])
            else:
                P.pool(lambda e: e.memset(sh[:, 0:4, 0:1], 0.0), [], [sh])
            P.act(lambda e: e.activation(out=sh[:, 4:8, 0:127], in_=hc[:, 4:8, 1:128], func=AF.Copy), [hc], [sh])
            if hn is not None:
                P.act(lambda e: e.activation(out=sh[:, 4:8, 127:128], in_=hn[:, 4:8, 0:1], func=AF.Copy), [hn], [sh])
            else:
                P.pool(lambda e: e.memset(sh[:, 4:8, 127:128], 0.0), [], [sh])

    def rw_neighbors(self, ck):
        n = self.NC if ck[0] == "c" else self.NX
        p = (ck[0], ck[1] - 1) if ck[1] > 0 else None
        q = (ck[0], ck[1] + 1) if ck[1] < n - 1 else None
        return p, q

    def rw_hcache(self, ph, R, li):
        cache = []

        def get(ck):
            for c, v in cache:
                if c == ck:
                    return v
            which = 1 if ck[0] == "c" else 0
            v = self.front(ph, R, self.src_ap(li, ck), which)
            cache.append((ck, v))
            if len(cache) > 3:
                cache.pop(0)
            return v
        return get

    def rw_mix(self, P, mixr, tmpr, xx, hc, mu, p):
        mix = mixr.next()
        for k in range(8):
            P.dve(lambda e, k=k: e.scalar_tensor_tensor(out=mix[:, k, :], in0=xx[:, k, :], scalar=mu[:, p, k:k + 1], in1=hc[:, k, :],
                                                        op0=ALU.mult, op1=ALU.add), [xx, mu, hc], [mix])
        return mix

    def rwkv_layer(self, li, i, last):
        nc, w = self.nc, self.lw[i]
        NCH = self.NX + self.NC
        if not hasattr(self, "scr_o"):
            self.scr_o = nc.dram_tensor("scr_o", [NCH, 128, 2048], F32).ap()
        if not hasattr(self, "rw_scr"):
            self.rw_scr = nc.dram_tensor("rw_scr", [NCH, 128, 3072], F32).ap()
            self.rw_scr_v = nc.dram_tensor("rw_scr_v", [NCH, 128, 1024], BF16).ap()
        ph = Phase(nc)
        self.emit_mod(ph, i)
        ph.finish()
        self.rw_prep_phase(li, i)
        import os as _os
        for d in range(2):
            self.rw_scan_phase(li, i, d)
            if _os.environ.get("RW_STOP_AFTER_F") == "1":
                return
        self.rw_out_phase(li, i, last)

    def rw_prep_phase(self, li, i):
        nc, w = self.nc, self.lw[i]
        ph = Phase(nc)
        P = ph.P
        Wr = self.load_w(ph, ph.sb("Wr", [128, 8, 1024], BF16), w["rkvg"][0], 0, 1024, 8)
        Wkk = self.load_w(ph, ph.sb("Wk", [128, 8, 1024], BF16), w["rkvg"][1], 0, 1024, 8)
        Wv = self.load_w(ph, ph.sb("Wv", [128, 8, 1024], BF16), w["rkvg"][2], 0, 1024, 8)
        mu = ph.sb("mu", [128, 6, 8], F32)
        kk_bc = ph.sb("kk_bc", [128, 1024], F32)
        P.dma(mu[:], w["mu"][:, :, :], "c2", writes=[mu])
        P.dma(kk_bc[:], w["bc"][:, 0, :], "c5", writes=[kk_bc])
        R = self.front_bufs(ph, n_xn=2, n_xt=2, n_hT=4)
        get_h = self.rw_hcache(ph, R, li)
        G = Rot([ph.ps("G%d" % j, [128, 512], F32) for j in range(4)])
        sh = ph.rot("sh", [128, 8, 128], BF16, 2)
        xx = ph.rot("xx", [128, 8, 128], F32, 2)
        tmpr = ph.rot("mtmp", [128, 8, 128], F32, 2)
        mixr = ph.rot("mix", [128, 8, 128], BF16, 3)
        r_sb = ph.rot("r_sb", [128, 1024], F32, 2)
        k_sb = ph.rot("k_sb", [128, 1024], F32, 2)
        kkr = ph.rot("kkr", [128, 1024], F32, 2)
        sqr = ph.rot("sqr", [128, 1024], F32, 1)
        v_bf = ph.rot("v_bf", [128, 1024], BF16, 2)
        sm = ph.rot("sm", [128, 48], F32, 2)
        for ck in self.chunks_fwd():
            cidx = self.row0(ck) // CH
            pk, nk = self.rw_neighbors(ck)
            hc = get_h(ck)[0]
            hp = get_h(pk)[0] if pk else None
            hn = get_h(nk)[0] if nk else None
            sh_, xx_ = sh.next(), xx.next()
            self.rw_shift(P, sh_, hc, hp, hn, ck[0])
            P.pool(lambda e, hc=hc, sh_=sh_, xx_=xx_: e.tensor_tensor(out=xx_[:], in0=sh_[:], in1=hc[:], op=ALU.subtract), [sh_, hc], [xx_])
            r_, k_, v_, kk, sq, s_ = r_sb.next(), k_sb.next(), v_bf.next(), kkr.next(), sqr.next(), sm.next()
            for p, Wl, dst in ((0, Wr, r_), (2, Wkk, k_), (3, Wv, v_)):
                mix = self.rw_mix(P, mixr, tmpr, xx_, hc, mu, p)
                for n in range(2):
                    bank = G.next()
                    for k in range(8):
                        P.pe(lambda e, bank=bank, n=n, k=k, mix=mix, Wl=Wl: e.matmul(bank[:], lhsT=mix[:, k, :], rhs=Wl[k][:, n * 512:(n + 1) * 512],
                                                                                    start=(k == 0), stop=(k == 7)), [mix, Wl[k]], [bank])
                    P.act(lambda e, bank=bank, n=n, dst=dst: e.activation(out=dst[:, n * 512:(n + 1) * 512], in_=bank[:], func=AF.Copy), [bank], [dst])
            P.dve(lambda e, kk=kk, k_=k_: e.tensor_tensor(out=kk[:], in0=k_[:], in1=kk_bc[:], op=ALU.mult), [k_, kk_bc], [kk])
            P.pool(lambda e, kk=kk, sq=sq: e.tensor_tensor(out=sq[:], in0=kk[:], in1=kk[:], op=ALU.mult), [kk], [sq])
            P.dve(lambda e, s_=s_, sq=sq: e.tensor_reduce(out=s_[:, 0:16], in_=sq[:].rearrange("p (h c) -> p h c", h=16), axis=AX.X, op=ALU.add), [sq], [s_])
            P.act(lambda e, s_=s_: e.activation(out=s_[:, 16:32], in_=s_[:, 0:16], func=AF.Sqrt), [s_], [s_])
            P.dve(lambda e, s_=s_: e.tensor_scalar(out=s_[:, 16:32], in0=s_[:, 16:32], scalar1=1e-12, scalar2=None, op0=ALU.max), [s_], [s_])
            P.dve(lambda e, s_=s_: e.reciprocal(out=s_[:, 32:48], in_=s_[:, 16:32]), [s_], [s_])
            P.dve(lambda e, s_=s_, kk=kk: e.tensor_tensor(out=kk[:].rearrange("p (h c) -> p h c", h=16), in0=kk[:].rearrange("p (h c) -> p h c", h=16),
                                                          in1=s_[:, 32:48].unsqueeze(2).to_broadcast([128, 16, 64]), op=ALU.mult), [kk, s_], [kk])
            P.dma(self.rw_scr[cidx][:, 0:1024], r_[:], "pr%d" % r_sb.slot, reads=[r_], q="pool")
            P.dma(self.rw_scr[cidx][:, 1024:2048], k_[:], "pk%d" % k_sb.slot, reads=[k_], q="pool")
            P.dma(self.rw_scr[cidx][:, 2048:3072], kk[:], "pkk%d" % kkr.slot, reads=[kk], q="pool")
            P.dma(self.rw_scr_v[cidx], v_[:], "pv%d" % v_bf.slot, reads=[v_], q="pool")
        ph.finish()

    def rw_scan_phase(self, li, i, d):
        nc, w = self.nc, self.lw[i]
        ph = Phase(nc)
        P = ph.P
        C0 = 0.6065306597126334
        w1 = self.load_w(ph, ph.sb("w1", [128, 8, 64], BF16), w["w1"][d], 0, 64, 8)
        a1 = self.load_w(ph, ph.sb("a1", [128, 8, 64], BF16), w["a1"][d], 0, 64, 8)
        w2 = ph.sb("w2", [64, 1024], BF16)
        a2 = ph.sb("a2", [64, 1024], BF16)
        P.dma(w2[:], w["w2"][d], "w2", writes=[w2], q="pool")
        P.dma(a2[:], w["a2"][d], "a2", writes=[a2], q="pool")
        rows = ph.sb("rows", [8, 1024], F32)
        sel8 = ph.sb("sel8", [8, 8, 128], F32)
        mu = ph.sb("mu", [128, 6, 8], F32)
        msk = ph.sb("msk", [128, 4, 128], F32)
        negc = ph.sb("negc", [128, 1], F32)
        ka_bc = ph.sb("ka_bc", [128, 1024], F32)
        rk_bc = ph.sb("rk_bc", [128, 1024], F32)
        P.dma(rows[:], w["rows"][:, :], "c0", writes=[rows])
        P.dma(sel8[:], w["sel8"][:, :, :], "c1", writes=[sel8])
        P.dma(mu[:], w["mu"][:, :, :], "c2", writes=[mu])
        P.dma(msk[:], w["masks"][d], "c3", writes=[msk])
        P.dma(negc[:], w["negc"][:, :], "c4", writes=[negc])
        P.dma(ka_bc[:], w["bc"][:, 1, :], "c6", writes=[ka_bc])
        P.dma(rk_bc[:], w["bc"][:, 2, :], "c7", writes=[rk_bc])
        R = self.front_bufs(ph, n_xn=1, n_xt=1, n_hT=4, junk=False)
        get_h = self.rw_hcache(ph, R, li)
        G = Rot([ph.ps("G%d" % j, [128, 512], F32) for j in range(6)])
        psm = ph.ps("psm", [128, 512], F32)
        PT = R["ptrx"]
        sh = ph.sb("sh", [128, 8, 128], BF16)
        xx = ph.sb("xx", [128, 8, 128], F32)
        tmpr = ph.rot("mtmp", [128, 8, 128], F32, 1)
        mixr = ph.rot("mix", [128, 8, 128], BF16, 2)
        jk = Tile(tmpr.tiles[0].t[:].rearrange("p k t -> p (k t)"), "junkalias")
        jk.b = tmpr.tiles[0].b
        R["junk"] = jk
        r_rot = ph.rot("r_sb", [128, 1024], F32, 2)
        k_rot = ph.rot("k_sb", [128, 1024], F32, 2)
        kk_rot = ph.rot("kk_sb", [128, 1024], F32, 2)
        v_rot = ph.rot("v_bf", [128, 1024], BF16, 2)
        Wt = [ph.sb("W%d" % j, [128, 1024], F32) for j in range(4)]
        th_bf = ph.sb("th_bf", [64, 128], BF16)
        la_bf = ph.sb("la_bf", [64, 128], BF16)
        rt_bf = ph.sb("rt_bf", [128, 1024], BF16)
        at_bf = ph.sb("at_bf", [128, 1024], BF16)
        bh_bf = ph.sb("bh_bf", [128, 1024], BF16)
        kh_bf = ph.sb("kh_bf", [128, 1024], BF16)
        AR = ph.sb("AR", [64, 16, 2, 128], BF16)
        BT = ph.sb("BT", [64, 16, 128], BF16)
        KT = ph.sb("KT", [64, 16, 128], BF16)
        WC = ph.sb("WC", [64, 16], F32)
        sm = ph.rot("sm", [128, 64], F32, 2)
        names = ("N", "NT", "NA", "NAT", "NB", "NBT", "O32", "O32T", "O64", "O64T", "O128", "T", "TT", "Aak", "Abr", "Akr")
        cmk = ph.sb("cmk", [128, 7, 128], BF16)
        P.dma(cmk[:], w["cmask"][d], "c10", writes=[cmk], q="pool")
        NU = 4
        U_ = [{n: ph.sb("%s_u%d" % (n, us), [128, 4, 128], BF16) for n in names} for us in range(NU)]
        Xb = [ph.sb("Xb%d" % us, [128, 4, 64], BF16) for us in range(NU)]
        Ub = [ph.sb("Ub%d" % us, [128, 4, 64], BF16) for us in range(NU)]
        S = [ph.sb("S%d" % u, [64, 4, 64], F32) for u in range(4)]
        Sb = [ph.sb("Sb%d" % u, [64, 4, 64], BF16) for u in range(4)]
        for u in range(4):
            P.dve(lambda e, u=u: e.memset(S[u][:], 0.0), [], [S[u]])
            P.pool(lambda e, u=u: e.memset(Sb[u][:], 0.0), [], [Sb[u]])
        ysb = ph.rot("ysb", [128, 1040], F32, 1)
        if d == 1:
            yf = ph.rot("yf", [128, 1040], F32, 1)
        ident = self.ident

        order = self.chunks_fwd() if d == 0 else self.chunks_bwd()

        def chunk_body(ck):
            cidx = self.row0(ck) // CH
            pk, nk = self.rw_neighbors(ck)
            hc = get_h(ck)[0]
            hp = get_h(pk)[0] if pk else None
            hn = get_h(nk)[0] if nk else None
            self.rw_shift(P, sh, hc, hp, hn, ck[0])
            P.pool(lambda e, hc=hc: e.tensor_tensor(out=xx[:], in0=sh[:], in1=hc[:], op=ALU.subtract), [sh, hc], [xx])
            r_sb, k_sb, kk, v_bf = r_rot.next(), k_rot.next(), kk_rot.next(), v_rot.next()
            P.dma(r_sb[:], self.rw_scr[cidx][:, 0:1024], "lr%d" % r_rot.slot, writes=[r_sb])
            P.dma(k_sb[:], self.rw_scr[cidx][:, 1024:2048], "lk%d" % k_rot.slot, writes=[k_sb])
            P.dma(kk[:], self.rw_scr[cidx][:, 2048:3072], "lkk%d" % kk_rot.slot, writes=[kk])
            P.dma(v_bf[:], self.rw_scr_v[cidx], "lv%d" % v_rot.slot, writes=[v_bf])
            mix1 = self.rw_mix(P, mixr, tmpr, xx, hc, mu, 1)
            for k in range(8):
                P.pe(lambda e, k=k, mix1=mix1: e.matmul(psm[0:64, 0:128], lhsT=w1[k][:, :], rhs=mix1[:, k, :], start=(k == 0), stop=(k == 7)),
                     [mix1, w1[k]], [psm])
            P.act(lambda e: e.activation(out=th_bf[:], in_=psm[0:64, 0:128], func=AF.Tanh), [psm], [th_bf])
            sig = Wt[0]
            for n in range(2):
                bank = G.next()
                P.pe(lambda e, bank=bank, n=n: e.matmul(bank[:], lhsT=th_bf[:], rhs=w2[:, n * 512:(n + 1) * 512], start=True, stop=False), [th_bf, w2], [bank])
                P.pe(lambda e, bank=bank, n=n: e.matmul(bank[:], lhsT=sel8[:, d, :], rhs=rows[:, n * 512:(n + 1) * 512], start=False, stop=True), [sel8, rows], [bank])
                P.act(lambda e, bank=bank, n=n: e.activation(out=sig[:, n * 512:(n + 1) * 512], in_=bank[:], func=AF.Sigmoid), [bank], [sig])
            mix4 = self.rw_mix(P, mixr, tmpr, xx, hc, mu, 4)
            for k in range(8):
                P.pe(lambda e, k=k, mix4=mix4: e.matmul(psm[0:64, 0:128], lhsT=a1[k][:, :], rhs=mix4[:, k, :], start=(k == 0), stop=(k == 7)),
                     [mix4, a1[k]], [psm])
            P.act(lambda e: e.activation(out=la_bf[:], in_=psm[0:64, 0:128], func=AF.Copy), [psm], [la_bf])
            ep, em, ex = Wt[1], Wt[2], Wt[3]
            for n in range(2):
                bank = G.next()
                sl = slice(n * 512, (n + 1) * 512)
                P.pe(lambda e, bank=bank, sl=sl: e.matmul(bank[:], lhsT=msk[:, 3, :], rhs=sig[:, sl], start=True, stop=True), [msk, sig], [bank])
                P.act(lambda e, bank=bank, sl=sl: e.activation(out=ep[:, sl], in_=bank[:], func=AF.Exp), [bank], [ep])
                P.act(lambda e, bank=bank, sl=sl: e.activation(out=em[:, sl], in_=bank[:], func=AF.Exp, scale=-1.0), [bank], [em])
                P.dve(lambda e, bank=bank, sl=sl: e.scalar_tensor_tensor(out=ex[:, sl], in0=sig[:, sl], scalar=C0, in1=bank[:], op0=ALU.mult, op1=ALU.add),
                      [bank, sig], [ex])
            P.act(lambda e: e.activation(out=ex[:], in_=ex[:], func=AF.Exp), [ex], [ex])
            for h in range(16):
                P.pe(lambda e, h=h: e.matmul(psm[0:64, 256 + h:257 + h], lhsT=sig[:, h * 64:(h + 1) * 64], rhs=negc[:, 0:1], start=True, stop=True),
                     [sig, negc], [psm])
            P.act(lambda e: e.activation(out=WC[:], in_=psm[0:64, 256:272], func=AF.Exp), [psm], [WC])
            P.pool(lambda e: e.tensor_tensor(out=rt_bf[:], in0=r_sb[:], in1=ep[:], op=ALU.mult), [r_sb, ep], [rt_bf])
            s_ = sm.next()
            P.dve(lambda e: e.scalar_tensor_tensor(out=at_bf[:], in0=kk[:], scalar=-1.0, in1=ex[:], op0=ALU.mult, op1=ALU.mult), [kk, ex], [at_bf])
            aa = Wt[1]
            for n in range(2):
                bank = G.next()
                P.pe(lambda e, bank=bank, n=n: e.matmul(bank[:], lhsT=la_bf[:], rhs=a2[:, n * 512:(n + 1) * 512], start=True, stop=False), [la_bf, a2], [bank])
                P.pe(lambda e, bank=bank, n=n: e.matmul(bank[:], lhsT=sel8[:, 2 + d, :], rhs=rows[:, n * 512:(n + 1) * 512], start=False, stop=True), [sel8, rows], [bank])
                P.act(lambda e, bank=bank, n=n: e.activation(out=aa[:, n * 512:(n + 1) * 512], in_=bank[:], func=AF.Sigmoid), [bank], [aa])
            be = Wt[3]
            P.dve(lambda e: e.tensor_tensor(out=be[:], in0=kk[:], in1=aa[:], op=ALU.mult), [kk, aa, at_bf], [be])
            P.pool(lambda e: e.tensor_tensor(out=bh_bf[:], in0=be[:], in1=em[:], op=ALU.mult), [be, em], [bh_bf])
            kd = Wt[0]
            P.dve(lambda e: e.scalar_tensor_tensor(out=kd[:], in0=aa[:], scalar=-1.0, in1=ka_bc[:], op0=ALU.add, op1=ALU.mult), [aa, ka_bc, be], [kd])
            P.dve(lambda e: e.scalar_tensor_tensor(out=kd[:], in0=kd[:], scalar=1.0, in1=k_sb[:], op0=ALU.add, op1=ALU.mult), [kd, k_sb], [kd])
            P.pool(lambda e: e.tensor_tensor(out=kh_bf[:], in0=kd[:], in1=em[:], op=ALU.mult), [kd, em], [kh_bf])
            bt = Wt[3]
            P.dve(lambda e: e.tensor_tensor(out=bt[:], in0=kd[:], in1=r_sb[:], op=ALU.mult), [kd, r_sb, bh_bf], [bt])
            P.pool(lambda e: e.tensor_tensor(out=bt[:], in0=bt[:], in1=rk_bc[:], op=ALU.mult), [bt, rk_bc], [bt])
            P.dve(lambda e, s_=s_: e.tensor_reduce(out=s_[:, 48:64], in_=bt[:].rearrange("p (h c) -> p h c", h=16), axis=AX.X, op=ALU.add), [bt], [s_])
            cnt = 0
            for g in range(2):
                for src, dst_fn in ((at_bf, lambda g: AR[:, g * 8:(g + 1) * 8, 0, :]), (rt_bf, lambda g: AR[:, g * 8:(g + 1) * 8, 1, :]),
                                    (bh_bf, lambda g: BT[:, g * 8:(g + 1) * 8, :]), (kh_bf, lambda g: KT[:, g * 8:(g + 1) * 8, :])):
                    for j in range(8):
                        h = g * 8 + j
                        P.pe(lambda e, src=src, h=h, j=j: e.transpose(out=PT[0:64, j, :], in_=src[:, h * 64:(h + 1) * 64], identity=ident[:]),
                             [src, ident], [PT])
                    dtile = AR if src in (at_bf, rt_bf) else (BT if src is bh_bf else KT)
                    dst = dst_fn(g)
                    if cnt % 2 == 0:
                        P.act(lambda e, dst=dst: e.activation(out=dst, in_=PT[0:64, :, :], func=AF.Copy), [PT], [dtile])
                    else:
                        P.dve(lambda e, dst=dst: e.tensor_copy(out=dst, in_=PT[0:64, :, :]), [PT], [dtile])
                    cnt += 1
            y_ = ysb.next()
            if d == 1:
                yf_ = yf.next()
                P.dma(yf_[:], self.scr_o[cidx][:, 0:1040], "lyf%d" % yf.slot, writes=[yf_])
            for g in range(1):
                units = [(us, us) for us in range(4)]
                for u, us in units:
                    M = U_[us]
                    h0 = u * 4
                    for pr in range(2):
                        bank = G.next()
                        for j in range(2):
                            h = h0 + pr * 2 + j
                            P.pe(lambda e, bank=bank, j=j, h=h: e.matmul(bank[:, j * 256:(j + 1) * 256], lhsT=BT[:, h, :],
                                                                         rhs=AR[:, h, :, :].rearrange("p a t -> p (a t)"), start=True, stop=True), [BT, AR], [bank])
                        bv = bank[:].rearrange("p (j a t) -> p j a t", j=2, a=2)
                        for dn, mi in (("NA", 0), ("O32", 1), ("O64", 2)):
                            P.dve(lambda e, bv=bv, M=M, pr=pr, dn=dn, mi=mi: e.tensor_tensor(out=M[dn][:, pr * 2:pr * 2 + 2, :], in0=bv[:, :, 0, :],
                                                                                            in1=cmk[:, mi, :].unsqueeze(1).to_broadcast([128, 2, 128]), op=ALU.mult),
                                  [bank, cmk], [M[dn]])
                        P.dve(lambda e, bv=bv, M=M, pr=pr: e.tensor_tensor(out=M["Abr"][:, pr * 2:pr * 2 + 2, :], in0=bv[:, :, 1, :],
                                                                          in1=msk[:, 1, :].unsqueeze(1).to_broadcast([128, 2, 128]), op=ALU.mult), [bank, msk], [M["Abr"]])
                        bank = G.next()
                        for j in range(2):
                            h = h0 + pr * 2 + j
                            P.pe(lambda e, bank=bank, j=j, h=h: e.matmul(bank[:, j * 256:(j + 1) * 256], lhsT=KT[:, h, :],
                                                                         rhs=AR[:, h, :, :].rearrange("p a t -> p (a t)"), start=True, stop=True), [KT, AR], [bank])
                        bv = bank[:].rearrange("p (j a t) -> p j a t", j=2, a=2)
                        P.dve(lambda e, bv=bv, M=M, pr=pr: e.tensor_tensor(out=M["Aak"][:, pr * 2:pr * 2 + 2, :], in0=bv[:, :, 0, :],
                                                                          in1=msk[:, 0, :].unsqueeze(1).to_broadcast([128, 2, 128]), op=ALU.mult), [bank, msk], [M["Aak"]])
                        P.dve(lambda e, bv=bv, M=M, pr=pr: e.tensor_tensor(out=M["Akr"][:, pr * 2:pr * 2 + 2, :], in0=bv[:, :, 1, :],
                                                                          in1=msk[:, 1, :].unsqueeze(1).to_broadcast([128, 2, 128]), op=ALU.mult), [bank, msk], [M["Akr"]])
                    bank = G.next()
                    for j in range(4):
                        h = h0 + j
                        P.pe(lambda e, bank=bank, j=j, h=h: e.matmul(bank[:, j * 128:(j + 1) * 128], lhsT=AR[:, h, 0, :], rhs=BT[:, h, :], start=True, stop=True),
                             [AR, BT], [bank])
                    for dn, mi in (("NAT", 3), ("O32T", 4), ("O64T", 5), ("O128", 6)):
                        P.dve(lambda e, bank=bank, M=M, dn=dn, mi=mi: e.tensor_tensor(out=M[dn][:], in0=bank[:].rearrange("p (j t) -> p j t", j=4),
                                                                                      in1=cmk[:, mi, :].unsqueeze(1).to_broadcast([128, 4, 128]), op=ALU.mult),
                              [bank, cmk], [M[dn]])
                    P.pool(lambda e, M=M: e.tensor_tensor(out=M["T"][:], in0=M["NA"][:], in1=ident[:].unsqueeze(1).to_broadcast([128, 4, 128]), op=ALU.add),
                           [M["NA"], ident], [M["T"]])
                    P.pool(lambda e, M=M: e.tensor_tensor(out=M["TT"][:], in0=M["NAT"][:], in1=ident[:].unsqueeze(1).to_broadcast([128, 4, 128]), op=ALU.add),
                           [M["NAT"], ident], [M["TT"]])

                def mm4(bank, M, lt, rt):
                    for j in range(4):
                        P.pe(lambda e, bank=bank, j=j, M=M, lt=lt, rt=rt: e.matmul(bank[:, j * 128:(j + 1) * 128], lhsT=M[lt][:, j, :], rhs=M[rt][:, j, :],
                                                                                  start=True, stop=True), [M[lt], M[rt]], [bank])

                def cp4(bank, M, dn):
                    P.act(lambda e, bank=bank, M=M, dn=dn: e.activation(out=M[dn][:], in_=bank[:].rearrange("p (j t) -> p j t", j=4), func=AF.Copy),
                          [bank], [M[dn]])

                def add4(bank, M, dn):
                    P.dve(lambda e, bank=bank, M=M, dn=dn: e.tensor_tensor(out=M[dn][:], in0=bank[:].rearrange("p (j t) -> p j t", j=4), in1=M[dn][:], op=ALU.add),
                          [bank, M[dn]], [M[dn]])
                cur = {us: ("NA", "NAT") for _, us in units}
                for lvl in range(3):
                    nxt = {}
                    for u, us in units:
                        M = U_[us]
                        nk, nkt = cur[us]
                        nb, nbt = ("NB", "NBT") if nk == "NA" else ("NA", "NAT")
                        b1 = G.next(); mm4(b1, M, nkt, nk); cp4(b1, M, nb)
                        b2 = G.next(); mm4(b2, M, nk, nkt); cp4(b2, M, nbt)
                        nxt[us] = (nb, nbt)
                    for u, us in units:
                        M = U_[us]
                        nb, nbt = nxt[us]
                        b3 = G.next(); mm4(b3, M, nbt, "T"); add4(b3, M, "T")
                    for u, us in units:
                        M = U_[us]
                        nb, nbt = nxt[us]
                        b4 = G.next(); mm4(b4, M, nb, "TT"); add4(b4, M, "TT")
                    cur = nxt
                for on, lastm in (("O32", False), ("O64", False), ("O128", True)):
                    if not lastm:
                        for u, us in units:
                            M = U_[us]
                            b1 = G.next(); mm4(b1, M, on + "T", "T"); cp4(b1, M, "N")
                        for u, us in units:
                            M = U_[us]
                            b2 = G.next(); mm4(b2, M, on, "TT"); cp4(b2, M, "NT")
                        bb = {}
                        for u, us in units:
                            M = U_[us]
                            b3 = G.next(); mm4(b3, M, "TT", "N"); bb[us] = b3
                            if us % 2 == 1:
                                pass
                        b4s = {}
                        for u, us in units[:2]:
                            M = U_[us]
                            b4 = G.next(); mm4(b4, M, "T", "NT"); b4s[us] = b4
                        for u, us in units[:2]:
                            add4(bb[us], U_[us], "T"); add4(b4s[us], U_[us], "TT")
                        for u, us in units[2:]:
                            M = U_[us]
                            b4 = G.next(); mm4(b4, M, "T", "NT"); b4s[us] = b4
                        for u, us in units[2:]:
                            add4(bb[us], U_[us], "T"); add4(b4s[us], U_[us], "TT")
                    else:
                        for u, us in units:
                            M = U_[us]
                            b1 = G.next(); mm4(b1, M, on, "T"); cp4(b1, M, "N")
                        for u, us in units:
                            M = U_[us]
                            b3 = G.next(); mm4(b3, M, "TT", "N"); add4(b3, M, "T")
                for u, us in units:
                    M = U_[us]
                    h0 = u * 4
                    bank = G.next()
                    for j in range(4):
                        h = h0 + j
                        P.pe(lambda e, bank=bank, j=j, h=h, u=u: e.matmul(bank[:, j * 64:(j + 1) * 64], lhsT=AR[:, h, 0, :], rhs=Sb[u][:, j, :], start=True, stop=False),
                             [AR, Sb[u]], [bank])
                        P.pe(lambda e, bank=bank, j=j, h=h, M=M: e.matmul(bank[:, j * 64:(j + 1) * 64], lhsT=M["Aak"][:, j, :], rhs=v_bf[:, h * 64:(h + 1) * 64],
                                                                          start=False, stop=True), [M["Aak"], v_bf], [bank])
                    P.act(lambda e, bank=bank, us=us: e.activation(out=Xb[us][:], in_=bank[:, 0:256].rearrange("p (j v) -> p j v", j=4), func=AF.Copy),
                          [bank], [Xb[us]])
                for u, us in units:
                    M = U_[us]
                    h0 = u * 4
                    bank = G.next()
                    for j in range(4):
                        P.pe(lambda e, bank=bank, j=j, M=M, us=us: e.matmul(bank[:, j * 64:(j + 1) * 64], lhsT=M["T"][:, j, :], rhs=Xb[us][:, j, :], start=True, stop=True),
                             [M["T"], Xb[us]], [bank])
                    P.dve(lambda e, bank=bank, us=us: e.tensor_copy(out=Ub[us][:], in_=bank[:, 0:256].rearrange("p (j v) -> p j v", j=4)), [bank], [Ub[us]])
                for u, us in units:
                    M = U_[us]
                    h0 = u * 4
                    bank = G.next()
                    for j in range(4):
                        h = h0 + j
                        P.pe(lambda e, bank=bank, j=j, h=h, u=u: e.matmul(bank[:, j * 64:(j + 1) * 64], lhsT=AR[:, h, 1, :], rhs=Sb[u][:, j, :], start=True, stop=False),
                             [AR, Sb[u]], [bank])
                        P.pe(lambda e, bank=bank, j=j, M=M, us=us: e.matmul(bank[:, j * 64:(j + 1) * 64], lhsT=M["Abr"][:, j, :], rhs=Ub[us][:, j, :], start=False, stop=False),
                             [M["Abr"], Ub[us]], [bank])
                        P.pe(lambda e, bank=bank, j=j, h=h, M=M: e.matmul(bank[:, j * 64:(j + 1) * 64], lhsT=M["Akr"][:, j, :], rhs=v_bf[:, h * 64:(h + 1) * 64],
                                                                          start=False, stop=True), [M["Akr"], v_bf], [bank])
                    if d == 0:
                        P.act(lambda e, bank=bank, h0=h0, y_=y_: e.activation(out=y_[:, h0 * 64:(h0 + 4) * 64], in_=bank[:, 0:256], func=AF.Copy), [bank], [y_])
                    else:
                        P.dve(lambda e, bank=bank, h0=h0, y_=y_, yf_=yf_: e.tensor_tensor(out=y_[:, h0 * 64:(h0 + 4) * 64], in0=bank[:, 0:256],
                                                                                         in1=yf_[:, h0 * 64:(h0 + 4) * 64], op=ALU.add), [bank, yf_], [y_])
                for u, us in units:
                    M = U_[us]
                    h0 = u * 4
                    bank = G.next()
                    for j in range(4):
                        h = h0 + j
                        P.pe(lambda e, bank=bank, j=j, h=h, us=us: e.matmul(bank[0:64, j * 64:(j + 1) * 64], lhsT=bh_bf[:, h * 64:(h + 1) * 64], rhs=Ub[us][:, j, :],
                                                                            start=True, stop=False), [bh_bf, Ub[us]], [bank])
                        P.pe(lambda e, bank=bank, j=j, h=h: e.matmul(bank[0:64, j * 64:(j + 1) * 64], lhsT=kh_bf[:, h * 64:(h + 1) * 64], rhs=v_bf[:, h * 64:(h + 1) * 64],
                                                                     start=False, stop=True), [kh_bf, v_bf], [bank])
                    P.dve(lambda e, bank=bank, u=u: e.tensor_tensor(out=S[u][:], in0=bank[0:64, 0:256].rearrange("p (j v) -> p j v", j=4), in1=S[u][:], op=ALU.add),
                          [bank, S[u]], [S[u]])
                    P.dve(lambda e, u=u, h0=h0: e.tensor_tensor(out=S[u][:], in0=S[u][:], in1=WC[:, h0:h0 + 4].unsqueeze(2).to_broadcast([64, 4, 64]), op=ALU.mult),
                          [S[u], WC], [S[u]])
                    P.pool(lambda e, u=u: e.tensor_copy(out=Sb[u][:], in_=S[u][:]), [S[u]], [Sb[u]])
            if d == 0:
                P.dve(lambda e, y_=y_, s_=s_: e.tensor_copy(out=y_[:, 1024:1040], in_=s_[:, 48:64]), [s_], [y_])
                P.dma(self.scr_o[cidx][:, 0:1040], y_[:], "sy%d" % ysb.slot, reads=[y_], q="pool")
            else:
                need_out = True
                yv = y_[:, 0:1024].rearrange("p (h c) -> p h c", h=16)
                t1 = Wt[1]
                t1v = t1[:].rearrange("p (h c) -> p h c", h=16)
                P.dve(lambda e, yv=yv, s_=s_: e.tensor_reduce(out=s_[:, 0:16], in_=yv, axis=AX.X, op=ALU.add), [y_, aa], [s_])
                P.dve(lambda e, s_=s_: e.tensor_scalar(out=s_[:, 0:16], in0=s_[:, 0:16], scalar1=-1.0 / 64, scalar2=None, op0=ALU.mult), [s_], [s_])
                P.dve(lambda e, yv=yv, s_=s_: e.tensor_tensor(out=yv, in0=yv, in1=s_[:, 0:16].unsqueeze(2).to_broadcast([128, 16, 64]), op=ALU.add), [y_, s_], [y_])
                P.pool(lambda e, y_=y_: e.tensor_tensor(out=t1[:], in0=y_[:, 0:1024], in1=y_[:, 0:1024], op=ALU.mult), [y_], [t1])
                P.dve(lambda e, s_=s_: e.tensor_reduce(out=s_[:, 16:32], in_=t1v, axis=AX.X, op=ALU.add), [t1], [s_])
                P.act(lambda e, s_=s_: e.activation(out=s_[:, 16:32], in_=s_[:, 16:32], func=AF.Sqrt, scale=1.0 / 64, bias=64e-5), [s_], [s_])
                P.dve(lambda e, s_=s_: e.reciprocal(out=s_[:, 32:48], in_=s_[:, 16:32]), [s_], [s_])
                P.dve(lambda e, yv=yv, s_=s_: e.tensor_tensor(out=yv, in0=yv, in1=s_[:, 32:48].unsqueeze(2).to_broadcast([128, 16, 64]), op=ALU.mult), [y_, s_], [y_])
                P.dve(lambda e, s_=s_, yf_=yf_: e.tensor_tensor(out=s_[:, 48:64], in0=s_[:, 48:64], in1=yf_[:, 1024:1040], op=ALU.add), [s_, yf_], [s_])
                P.dve(lambda e, s_=s_: e.tensor_tensor(out=t1v, in0=v_bf[:].rearrange("p (h c) -> p h c", h=16),
                                                       in1=s_[:, 48:64].unsqueeze(2).to_broadcast([128, 16, 64]), op=ALU.mult), [v_bf, s_, t1], [t1])
                P.dma(self.scr_o[cidx][:, 0:1024], y_[:, 0:1024], "sy%d" % ysb.slot, reads=[y_], q="pool")
                P.dma(self.scr_o[cidx][:, 1024:2048], t1[:], "sbv", reads=[t1], q="pool")
        for ck in order:
            chunk_body(ck)
        ph.finish()

    def rw_out_phase(self, li, i, last):
        nc, w = self.nc, self.lw[i]
        ph = Phase(nc)
        P = ph.P
        Wg = self.load_w(ph, ph.sb("Wg", [128, 8, 1024], BF16), w["rkvg"][3], 0, 1024, 8)
        Wo = self.load_w(ph, ph.sb("Wo", [128, 8, 1024], BF16), w["w_out"], 0, 1024, 8)
        mu = ph.sb("mu", [128, 6, 8], F32)
        P.dma(mu[:], w["mu"][:, :, :], "c2", writes=[mu])
        R = self.front_bufs(ph, n_xn=1, n_xt=4, n_hT=4)
        R.update(self.tail_bufs(ph, last=last))
        get_h = self.rw_hcache(ph, R, li)
        mm = Rot([ph.ps("mm%d" % j, [128, 512], F32) for j in range(2)])
        sh = ph.sb("sh", [128, 8, 128], BF16)
        xx = ph.sb("xx", [128, 8, 128], F32)
        tmpr = ph.rot("mtmp", [128, 8, 128], F32, 2)
        mixr = ph.rot("mix", [128, 8, 128], BF16, 2)
        op = ph.rot("opre", [128, 2048], F32, 2)
        lg_bc = ph.sb("lg_bc", [128, 1024], F32)
        lb_bc = ph.sb("lb_bc", [128, 1024], F32)
        P.dma(lg_bc[:], w["bc"][:, 3, :], "c8", writes=[lg_bc])
        P.dma(lb_bc[:], w["bc"][:, 4, :], "c9", writes=[lb_bc])
        sz = ph.rot("sz", [128, 1024], F32, 2)
        gated = ph.rot("gated", [128, 1024], BF16, 2)
        order = self.chunks_fwd()
        if last:
            order = [ck for ck in order if ck[0] == "x"]
        for ck in order:
            cidx = self.row0(ck) // CH
            pk, nk = self.rw_neighbors(ck)
            hc, xt = get_h(ck)
            hp = get_h(pk)[0] if pk else None
            hn = get_h(nk)[0] if nk else None
            self.rw_shift(P, sh, hc, hp, hn, ck[0])
            P.pool(lambda e, hc=hc: e.tensor_tensor(out=xx[:], in0=sh[:], in1=hc[:], op=ALU.subtract), [sh, hc], [xx])
            mix5 = self.rw_mix(P, mixr, tmpr, xx, hc, mu, 5)
            o_ = op.next()
            P.dma(o_[:], self.scr_o[cidx][:, 0:2048], "lo%d" % op.slot, writes=[o_])
            P.dve(lambda e, o_=o_: e.tensor_tensor(out=o_[:, 0:1024], in0=o_[:, 0:1024], in1=lg_bc[:], op=ALU.mult), [o_, lg_bc], [o_])
            P.pool(lambda e, o_=o_: e.tensor_tensor(out=o_[:, 1024:2048], in0=o_[:, 1024:2048], in1=lb_bc[:], op=ALU.add), [o_, lb_bc], [o_])
            P.dve(lambda e, o_=o_: e.tensor_tensor(out=o_[:, 0:1024], in0=o_[:, 0:1024], in1=o_[:, 1024:2048], op=ALU.add), [o_], [o_])
            sz_ = sz.next()
            for n in range(2):
                bank = mm.next()
                for k in range(8):
                    P.pe(lambda e, bank=bank, n=n, k=k, mix5=mix5: e.matmul(bank[:], lhsT=mix5[:, k, :], rhs=Wg[k][:, n * 512:(n + 1) * 512],
                                                                            start=(k == 0), stop=(k == 7)), [mix5, Wg[k]], [bank])
                P.act(lambda e, bank=bank, n=n, sz_=sz_: e.activation(out=sz_[:, n * 512:(n + 1) * 512], in_=bank[:], func=AF.Silu), [bank], [sz_])
            g_ = gated.next()
            P.dve(lambda e, o_=o_, sz_=sz_, g_=g_: e.tensor_tensor(out=g_[:], in0=o_[:, 0:1024], in1=sz_[:], op=ALU.mult), [o_, sz_], [g_])
            self.tail(ph, R, g_, Wo, li, ck, xt, last, KT=8)
        ph.finish()


def _col(v, k):
    return np.ascontiguousarray(np.asarray(v, np.float32).reshape(k, 128).T)


def _consts(L, CL):
    f32 = np.float32
    t = np.arange(L)
    row, col = (t // 64).astype(f32), (t % 64).astype(f32)
    freqs = (f32(10000.0) ** (-(np.arange(64, dtype=f32)) / f32(64))).astype(f32)
    ang = np.concatenate([row[:, None] * freqs, col[:, None] * freqs], axis=-1).astype(f32)
    rope = np.zeros((CL + L, 2, 128), f32)
    rope[:CL, 0, :] = 1.0
    rope[CL:, 0, :] = np.cos(ang)
    rope[CL:, 1, :] = np.sin(ang)
    jj = np.arange(128)[:, None].astype(f32)
    ii = np.arange(128)[None, :].astype(f32)
    dmat = np.stack([np.maximum(ii - jj, 0), np.maximum(jj - ii, 0), (ii >= jj).astype(f32), (jj >= ii).astype(f32),
                     np.broadcast_to(ii + 1, (128, 128)), np.broadcast_to(128 - ii, (128, 128))], axis=1).astype(f32)
    jcol = np.stack([127 - np.arange(128), np.arange(128)], axis=1).astype(f32)
    sel = np.zeros((2, 2, 128), f32)
    sel[0, 0, :] = 1.0
    sel[1, 1, :] = 1.0
    return {"ident": np.eye(128, dtype=f32), "rope": rope, "dmat": np.ascontiguousarray(dmat), "jcol": jcol, "sel2": sel}


def host_inputs(inp, b, layers, L, CL=256, x_rows=None):
    f32 = np.float32
    m = dict(_consts(L, CL))
    xr = inp["x"][b] if x_rows is None else x_rows
    m["x"] = np.ascontiguousarray(xr[:L], f32)
    m["ctx"] = np.ascontiguousarray(inp["ctx"][b][:CL], f32)
    m["cc"] = np.ascontiguousarray(np.stack([_col(inp["c"][b], 8), _col(inp["c_ctx"], 8)], axis=-1))
    m["final_g_bc"] = np.ascontiguousarray(np.broadcast_to(np.asarray(inp["final_g"], f32), (128, D)))
    for i in layers:
        j = i // 3
        m["ada_w%d" % i] = np.ascontiguousarray(inp["ada_w"][i], f32)
        m["ada_bcol%d" % i] = _col(inp["ada_b"][i], 24)
        m["ada_brow%d" % i] = np.ascontiguousarray(np.broadcast_to(np.asarray(inp["ada_b"][i][2 * D:], f32), (2, D)))
        m["ng_col%d" % i] = _col(inp["norm_g"][i], 8)
        k = KINDS[i]
        if k == 0:
            m["ret_w_in%d" % i] = np.ascontiguousarray(inp["ret_w_in"][j], f32)
            m["ret_w_out%d" % i] = np.ascontiguousarray(inp["ret_w_out"][j], f32)
            dec = np.concatenate([inp["ret_decay"][j][0], inp["ret_decay"][j][1]]).astype(f32)
            m["ret_decay%d" % i] = np.ascontiguousarray(np.broadcast_to(dec, (128, 8)))
        elif k == 1:
            m["gm_w_in%d" % i] = np.ascontiguousarray(inp["gm_w_in"][j], f32)
            m["gm_w_out%d" % i] = np.ascontiguousarray(inp["gm_w_out"][j], f32)
            m["gm_vg%d" % i] = np.ascontiguousarray(np.broadcast_to(np.asarray(inp["gm_vnorm_g"][j], f32), (128, 2048)))
            m["gm_wsT%d" % i] = np.ascontiguousarray(np.transpose(np.asarray(inp["gm_w_s"][j], f32), (2, 0, 1)))
            m["gm_bsT%d" % i] = np.ascontiguousarray(np.asarray(inp["gm_b_s"][j], f32).T)
        else:
            _rw_host(m, inp, i, j)
    return m


def _rw_host(m, inp, i, j):
    f32 = np.float32
    g = lambda k: np.asarray(inp[k][j], f32)
    mu = g("rw_mu")
    m["rw_mu%d" % i] = np.ascontiguousarray(np.stack([_col(mu[p], 8) for p in range(6)], axis=1))
    m["rw_rkvg%d" % i] = np.ascontiguousarray(g("rw_w_rkvg"))
    m["rw_w1%d" % i] = np.ascontiguousarray(g("rw_w1"))
    m["rw_a1%d" % i] = np.ascontiguousarray(g("rw_a1"))
    m["rw_w2%d" % i] = np.ascontiguousarray(g("rw_w2"))
    m["rw_a2%d" % i] = np.ascontiguousarray(g("rw_a2"))
    rows = np.zeros((8, D), f32)
    rows[0:2] = g("rw_w0")
    rows[2:4] = g("rw_a0")
    m["rw_rows%d" % i] = rows
    bc = np.stack([g("rw_k_k"), g("rw_k_a"), g("rw_r_k").reshape(-1), g("rw_lnx_g"), g("rw_lnx_b")], axis=0)
    m["rw_bc%d" % i] = np.ascontiguousarray(np.broadcast_to(bc[None], (128, 5, D)))
    m["rw_wout%d" % i] = np.ascontiguousarray(g("rw_w_out"))
    s_ = np.arange(128)[:, None]
    t_ = np.arange(128)[None, :]
    c0 = f32(-0.6065306597126334)
    fw = np.stack([(s_ < t_), (s_ <= t_), (t_ < s_), (s_ <= t_) * c0], axis=1).astype(f32)
    bw = np.stack([(s_ > t_), (s_ >= t_), (t_ > s_), (s_ >= t_) * c0], axis=1).astype(f32)
    m["rw_masks%d" % i] = np.ascontiguousarray(np.stack([fw, bw], axis=0))
    sel = np.zeros((8, 8, 128), f32)
    for r in range(8):
        sel[r, r, :] = 1.0
    m["rw_sel8%d" % i] = sel
    m["rw_negc%d" % i] = np.full((128, 1), c0, f32)
    blk = lambda n: (s_ // n) == (t_ // n)
    bm = np.stack([blk(16)] + [blk(n) & ~blk(n // 2) for n in (32, 64, 128)], axis=1).astype(f32)
    m["rw_bmask%d" % i] = np.ascontiguousarray(bm)
    cms = []
    for dd in (fw, bw):
        st, stT = dd[:, 0, :], dd[:, 2, :]
        cms.append(np.stack([st * bm[:, 0], st * bm[:, 1], st * bm[:, 2], stT * bm[:, 0], stT * bm[:, 1], stT * bm[:, 2], stT * bm[:, 3]], axis=1))
    m["rw_cmask%d" % i] = np.ascontiguousarray(np.stack(cms, axis=0).astype(f32))


_MODEL_CACHE = {}


def kernel(**inputs):
    inp = {k: np.asarray(v) for k, v in inputs.items()}
    B, L, _ = inp["x"].shape
    layers = (0, 1, 2, 3)
    key = (L, layers)
    if key not in _MODEL_CACHE:
        _MODEL_CACHE[key] = Model(L, 256, layers)
    model = _MODEL_CACHE[key]
    maps = []
    for core in range(NCORES):
        b = core % B
        hm = host_inputs(inp, b, layers, L)
        maps.append({k: hm[k] for k in model.inputs})
    res = run_bass_kernel_spmd(model.nc, maps, core_ids=list(range(NCORES)))
    out = np.stack([np.asarray(res.results[b]["out"], np.float32) for b in range(B)], axis=0)
    return out
```

```python
from contextlib import ExitStack
import numpy as np
import concourse.bass as bass
import concourse.mybir as mybir
from concourse.bass_utils import run_bass_kernel_spmd

F32 = mybir.dt.float32
BF16 = mybir.dt.bfloat16
AF = mybir.ActivationFunctionType
ALU = mybir.AluOpType
AX = mybir.AxisListType

D = 1024
CH = 128
NCORES = 8
_UID = [0]


class Buf:
    __slots__ = ("name", "last_w", "readers", "excl")

    def __init__(self, name, excl=False):
        self.name = name
        self.last_w = None
        self.readers = []
        self.excl = excl


class Tile:
    def __init__(self, t, name, excl=False):
        self.t = t
        self.b = Buf(name, excl)

    def __getitem__(self, k):
        return self.t[k]


class Rot:
    def __init__(self, tiles):
        self.tiles = tiles
        self.i = 0

    def next(self):
        t = self.tiles[self.i % len(self.tiles)]
        self.slot = self.i % len(self.tiles)
        self.i += 1
        return t


class Op:
    __slots__ = ("eng", "fn", "dma_key", "waits", "signal", "sem", "val", "idx", "prog")


def _b(x):
    return x.b if isinstance(x, Tile) else x


class Prog:
    ENGS = ("pe", "act", "dve", "pool", "sp")

    def __init__(self):
        self.ops = []

    def add(self, eng, fn, reads=(), writes=(), dma_key=None):
        op = Op()
        op.eng, op.fn, op.dma_key = eng, fn, dma_key
        op.signal, op.sem, op.val = False, None, 0
        op.idx, op.prog = len(self.ops), self
        reads = [_b(x) for x in reads]
        writes = [_b(x) for x in writes]
        deps = []
        for b in reads:
            if b.last_w is not None:
                deps.append((b.last_w, True))
            if b.excl:
                deps.extend((r, False) for r in b.readers)
        for b in writes:
            if b.last_w is not None:
                deps.append((b.last_w, False))
            deps.extend((r, False) for r in b.readers)
        need, seen = [], set()
        for d, raw in deps:
            if d.prog is not self:
                continue
            if d.dma_key is None and dma_key is None and d.eng == eng:
                if eng == "pe" or not raw:
                    continue
            if d.idx in seen:
                continue
            seen.add(d.idx)
            d.signal = True
            need.append(d)
        op.waits = need
        for b in reads:
            b.readers.append(op)
        for b in writes:
            b.last_w = op
            b.readers = []
        self.ops.append(op)
        return op

    def pe(self, fn, reads=(), writes=()):
        return self.add("pe", fn, reads, writes)

    def act(self, fn, reads=(), writes=()):
        return self.add("act", fn, reads, writes)

    def dve(self, fn, reads=(), writes=()):
        return self.add("dve", fn, reads, writes)

    def pool(self, fn, reads=(), writes=()):
        return self.add("pool", fn, reads, writes)

    def dma(self, out, in_, key, reads=(), writes=(), q="sp", slow=False):
        if slow:
            return self.add(q, lambda e: e.dma_start(out=out, in_=in_, allow_slow_non_contiguous=True),
                            reads, writes, dma_key=key)
        return self.add(q, lambda e: e.dma_start(out=out, in_=in_), reads, writes, dma_key=key)

    def emit(self, nc, stack):
        def keyof(op):
            return ("dma", op.dma_key) if op.dma_key is not None else ("eng", op.eng)
        last = {}
        for op in self.ops:
            last[keyof(op)] = op
        for op in last.values():
            op.signal = True
        cnt, sems = {}, {}
        for op in self.ops:
            if not op.signal:
                continue
            k = keyof(op)
            cnt[k] = cnt.get(k, 0) + (16 if op.dma_key is not None else 1)
            op.val = cnt[k]
            if k not in sems:
                _UID[0] += 1
                sems[k] = nc.alloc_semaphore(name="s%d" % _UID[0])
            op.sem = sems[k]
        self.n_sems = len(sems)
        per_eng = {e: [] for e in self.ENGS}
        for op in self.ops:
            per_eng[op.eng].append(op)
        finals = [(sems[k], cnt[k]) for k in sems]

        def run(engname, e):
            waited = {}
            for op in per_eng[engname]:
                for d in op.waits:
                    key = id(d.sem)
                    if waited.get(key, 0) >= d.val:
                        continue
                    e.wait_ge(d.sem, d.val)
                    waited[key] = d.val
                ins = op.fn(e)
                if op.signal:
                    ins.then_inc(op.sem, 16 if op.dma_key is not None else 1)
            for s, v in finals:
                if waited.get(id(s), 0) < v:
                    e.wait_ge(s, v)

        with nc.Block() as block:
            block.tensor(lambda e: run("pe", e))
            block.scalar(lambda e: run("act", e))
            block.vector(lambda e: run("dve", e))
            block.gpsimd(lambda e: run("pool", e))
            block.sync(lambda e: run("sp", e))
        nc.clear_and_free_semaphores(list(sems.values()))
        nc.all_engine_barrier()


class Phase:
    def __init__(self, nc):
        self.nc = nc
        self.st = ExitStack()
        self.P = Prog()

    def sb(self, name, shape, dt):
        _UID[0] += 1
        t = self.st.enter_context(self.nc.sbuf_tensor("%s_%d" % (name, _UID[0]), list(shape), dt))
        return Tile(t, name)

    def ps(self, name, shape, dt=F32):
        _UID[0] += 1
        t = self.st.enter_context(self.nc.psum_tensor("%s_%d" % (name, _UID[0]), list(shape), dt))
        return Tile(t, name, excl=True)

    def rot(self, name, shape, dt, n):
        return Rot([self.sb("%s%d" % (name, i), shape, dt) for i in range(n)])

    def finish(self):
        self.P.emit(self.nc, self.st)
        self.st.close()


KINDS = (0, 1, 2, 0)
NORM_EPS = 1e-6


class Model:
    def __init__(self, L, CL=256, layers=(0, 1, 2, 3), final_norm=True):
        self.L, self.CL = L, CL
        self.NX, self.NC = L // CH, CL // CH
        self.layers = tuple(layers)
        self.final_norm = final_norm
        self.nc = nc = bass.Bass("TRN2", target_bir_lowering=False)
        self.inputs = {}
        self.outer = ExitStack()
        NT = L + CL

        def din(name, shape):
            self.inputs[name] = tuple(shape)
            return nc.dram_tensor(name, list(shape), F32, kind="ExternalInput").ap()

        self.x_in = din("x", [L, D])
        self.c_in = din("ctx", [CL, D])
        self.cc = din("cc", [128, 8, 2])
        self.ident_in = din("ident", [128, 128])
        self.rope_in = din("rope", [NT, 2, 128])
        self.dmat_in = din("dmat", [128, 6, 128])
        self.jcol_in = din("jcol", [128, 2])
        self.sel_in = din("sel2", [2, 2, 128])
        self.fg_in = din("final_g_bc", [128, D])
        self.lw = {}
        for i in self.layers:
            w = {}
            w["ada_w"] = din("ada_w%d" % i, [D, 3 * D])
            w["ada_bcol"] = din("ada_bcol%d" % i, [128, 24])
            w["ada_brow"] = din("ada_brow%d" % i, [2, D])
            w["ng_col"] = din("ng_col%d" % i, [128, 8])
            k = KINDS[i]
            if k == 0:
                w["w_in"] = din("ret_w_in%d" % i, [D, 6144])
                w["w_out"] = din("ret_w_out%d" % i, [2048, D])
                w["decay"] = din("ret_decay%d" % i, [128, 8])
            elif k == 1:
                w["w_in"] = din("gm_w_in%d" % i, [D, 6144])
                w["w_out"] = din("gm_w_out%d" % i, [2048, D])
                w["vg"] = din("gm_vg%d" % i, [128, 2048])
                w["wsT"] = din("gm_wsT%d" % i, [128, 8, 128])
                w["bsT"] = din("gm_bsT%d" % i, [128, 8])
            else:
                self._rw_inputs(w, i, din)
            self.lw[i] = w
        self.out = nc.dram_tensor("out", [L, D], F32, kind="ExternalOutput").ap()
        self.xs = [nc.dram_tensor("xs%d" % i, [NT, D], F32).ap() for i in range(2)]
        o = self.outer
        self.ident = Tile(o.enter_context(nc.sbuf_tensor("identb", [128, 128], BF16)), "ident")
        self.ident32 = Tile(o.enter_context(nc.sbuf_tensor("ident32", [128, 128], F32)), "ident32")
        self.mod = Tile(o.enter_context(nc.sbuf_tensor("mod", [128, 2, 2, 8], F32)), "mod")
        self.gate_dram = nc.dram_tensor("gate_dram", [2, 128, D], F32).ap()
        self.sel = Tile(o.enter_context(nc.sbuf_tensor("sel", [2, 2, 128], F32)), "sel")
        self._build()
        self.outer.close()

    def chunks_fwd(self):
        return [("c", i) for i in range(self.NC)] + [("x", i) for i in range(self.NX)]

    def chunks_bwd(self):
        return [("c", i) for i in reversed(range(self.NC))] + [("x", i) for i in reversed(range(self.NX))]

    def row0(self, ck):
        return ck[1] * CH if ck[0] == "c" else self.CL + ck[1] * CH

    def src_ap(self, li, ck):
        r0 = self.row0(ck)
        if li == 0:
            return (self.c_in if ck[0] == "c" else self.x_in)[ck[1] * CH:(ck[1] + 1) * CH, :]
        return self.xs[(li - 1) % 2][r0:r0 + CH, :]

    def dst_ap(self, li, ck):
        r0 = self.row0(ck)
        return self.xs[li % 2][r0:r0 + CH, :]

    def _build(self):
        nc = self.nc
        ph = Phase(nc)
        P = ph.P
        t32 = ph.sb("id32", [128, 128], F32)
        P.dma(self.ident32[:], self.ident_in[:, :], "c0", writes=[self.ident32])
        P.dve(lambda e: e.tensor_copy(out=self.ident[:], in_=self.ident32[:]), [self.ident32], [self.ident])
        P.dma(self.sel[:], self.sel_in[:, :, :], "c1", writes=[self.sel])
        ph.finish()
        for li, i in enumerate(self.layers):
            last = (li == len(self.layers) - 1)
            k = KINDS[i]
            if k == 0:
                self.retention_layer(li, i, last)
            elif k == 1:
                self.gmlp_layer(li, i, last)
            else:
                self.rwkv_layer(li, i, last)

    def emit_mod(self, ph, i):
        P, w = ph.P, self.lw[i]
        cc = ph.sb("cc", [128, 8, 2], F32)
        sg = ph.sb("sg", [128, 8, 2], F32)
        bcol = ph.sb("bcol", [128, 24], F32)
        brow = ph.sb("brow", [2, D], F32)
        ng = ph.sb("ng", [128, 8], F32)
        grow = ph.sb("grow", [2, D], F32)
        gate_t = [ph.sb("gate_t%d" % j, [128, D], F32) for j in range(2)]
        aw = ph.sb("adaw", [128, 8, 3 * D], F32)
        awk = [Tile(aw.t[:, k, :], "adaw%d" % k) for k in range(8)]
        pcol = ph.ps("pcol", [128, 16, 2], F32)
        prow = [ph.ps("prow%d" % n, [128, 512], F32) for n in range(2)]
        P.dma(cc[:], self.cc[:, :, :], "m0", writes=[cc])
        P.dma(bcol[:], w["ada_bcol"][:, :], "m1", writes=[bcol])
        P.dma(brow[:], w["ada_brow"][:, :], "m2", writes=[brow])
        P.dma(ng[:], w["ng_col"][:, :], "m3", writes=[ng])
        for k in range(8):
            P.dma(awk[k][:], w["ada_w"][k * 128:(k + 1) * 128, :], "ada%d" % k, writes=[awk[k]], q=("sp" if k % 2 == 0 else "pool"))
        P.act(lambda e: e.activation(out=sg[:], in_=cc[:], func=AF.Sigmoid), [cc], [sg])
        P.dve(lambda e: e.tensor_tensor(out=sg[:], in0=sg[:], in1=cc[:], op=ALU.mult), [sg, cc], [sg])
        for j in range(16):
            for k in range(8):
                P.pe(lambda e, j=j, k=k: e.matmul(pcol[:, j, :], lhsT=awk[k][:, j * 128:(j + 1) * 128], rhs=sg[:, k, :],
                                                  start=(k == 0), stop=(k == 7)), [awk[k], sg], [pcol])
        for n in range(2):
            for k in range(8):
                P.pe(lambda e, n=n, k=k: e.matmul(prow[n][0:2, :], lhsT=sg[:, k, :], rhs=awk[k][:, 2048 + n * 512:2048 + (n + 1) * 512],
                                                  start=(k == 0), stop=(k == 7)), [awk[k], sg], [prow[n]])
        tmp = ph.sb("modtmp", [128, 16, 2], F32)
        P.dve(lambda e: e.tensor_tensor(out=tmp[:], in0=pcol[:], in1=bcol[:, 0:16].unsqueeze(2).to_broadcast([128, 16, 2]), op=ALU.add),
              [pcol, bcol], [tmp])
        mod = self.mod
        for j in range(2):
            P.dve(lambda e, j=j: e.scalar_tensor_tensor(out=mod[:, 0, j, :], in0=tmp[:, 8:16, j], scalar=1.0, in1=ng[:], op0=ALU.add, op1=ALU.mult),
                  [tmp, ng], [mod])
            P.dve(lambda e, j=j: e.tensor_copy(out=mod[:, 1, j, :], in_=tmp[:, 0:8, j]), [tmp], [mod])
        for n in range(2):
            P.dve(lambda e, n=n: e.tensor_tensor(out=grow[:, n * 512:(n + 1) * 512], in0=prow[n][0:2, :], in1=brow[:, n * 512:(n + 1) * 512], op=ALU.add),
                  [prow[n], brow], [grow])
        for j in range(2):
            for n in range(2):
                P.pe(lambda e, j=j, n=n: e.matmul(prow[n][:, :], lhsT=self.sel[:, j, :], rhs=grow[:, n * 512:(n + 1) * 512], start=True, stop=True),
                     [self.sel, grow], [prow[n]])
                P.act(lambda e, j=j, n=n: e.activation(out=gate_t[j][:, n * 512:(n + 1) * 512], in_=prow[n][:, :], func=AF.Copy),
                      [prow[n]], [gate_t[j]])
        for j in range(2):
            P.dma(self.gate_dram[j], gate_t[j][:], "gst%d" % j, reads=[gate_t[j]], q="pool")

    def load_w(self, ph, Wt, src, col0, ncols, KT):
        P = ph.P
        views = []
        for k in range(KT):
            v = Tile(Wt.t[:, k, :], "%s_k%d" % (Wt.b.name, k))
            views.append(v)
            step = 2048
            for c0 in range(0, ncols, step):
                wdt = min(step, ncols - c0)
                P.dma(v.t[:, c0:c0 + wdt], src[k * 128:(k + 1) * 128, col0 + c0:col0 + c0 + wdt],
                      "w%s%d_%d" % (Wt.b.name, k, c0), writes=[v], q="pool")
        return views

    def front(self, ph, R, src, which, src_reads=()):
        P = ph.P
        mod = self.mod
        xt = R["xt"].next()
        P.dma(xt[:], src, "xt%d" % R["xt"].slot, reads=list(src_reads), writes=[xt])
        st = R["st"].next()
        junk = R["junk"]
        P.act(lambda e: e.activation(out=junk[:], in_=xt[:], func=AF.Square, accum_out=st[:, 0:1]), [xt], [junk, st])
        P.act(lambda e: e.activation(out=st[:, 1:2], in_=st[:, 0:1], func=AF.Sqrt, scale=1.0 / D, bias=NORM_EPS), [st], [st])
        P.dve(lambda e: e.reciprocal(out=st[:, 2:3], in_=st[:, 1:2]), [st], [st])
        xn = R["xn"].next()
        P.dve(lambda e: e.tensor_scalar(out=xn[:], in0=xt[:], scalar1=st[:, 2:3], scalar2=None, op0=ALU.mult), [xt, st], [xn])
        ptr = R["ptrx"]
        for k in range(8):
            P.pe(lambda e, k=k: e.transpose(out=ptr[:, k, :], in_=xn[:, k * 128:(k + 1) * 128], identity=self.ident[:]),
                 [xn, self.ident], [ptr])
        hT = R["hT"].next()
        for k in range(8):
            P.act(lambda e, k=k: e.activation(out=hT[:, k, :], in_=ptr[:, k, :], func=AF.Identity,
                                              scale=mod[:, 0, which, k:k + 1], bias=mod[:, 1, which, k:k + 1]),
                  [ptr, mod], [hT])
        return hT, xt

    def front_bufs(self, ph, n_xn=2, n_xt=2, n_hT=2, junk=True):
        return {
            "xt": ph.rot("xt", [128, D], F32, n_xt),
            "st": ph.rot("st", [128, 4], F32, 4),
            "junk": ph.sb("junk", [128, D], BF16) if junk else None,
            "xn": ph.rot("xn", [128, D], BF16, n_xn),
            "hT": ph.rot("hT", [128, 8, 128], BF16, n_hT),
            "ptrx": ph.ps("ptrx", [128, 8, 128], BF16),
        }

    def tail(self, ph, R, gated, Wo, li, ck, xres, last, KT=16):
        P = ph.P
        which = 1 if ck[0] == "c" else 0
        gT = R["gT"].next()
        for half in range(KT // 8):
            ptg = R["ptg"][half]
            for k in range(8):
                kk = half * 8 + k
                P.pe(lambda e, k=k, kk=kk, ptg=ptg: e.transpose(out=ptg[:, k, :], in_=gated[:, kk * 128:(kk + 1) * 128], identity=self.ident[:]),
                     [gated, self.ident], [ptg])
            if half == 0:
                P.act(lambda e, half=half, ptg=ptg: e.activation(out=gT[:, half * 8:(half + 1) * 8, :], in_=ptg[:], func=AF.Copy), [ptg], [gT])
            else:
                P.dve(lambda e, half=half, ptg=ptg: e.tensor_copy(out=gT[:, half * 8:(half + 1) * 8, :], in_=ptg[:]), [ptg], [gT])
        xo = R["xo"].next()
        for n in range(2):
            po = R["pout"].next() if isinstance(R["pout"], Rot) else R["pout"]
            for k in range(KT):
                P.pe(lambda e, n=n, k=k, po=po: e.matmul(po[:], lhsT=gT[:, k, :], rhs=Wo[k][:, n * 512:(n + 1) * 512], start=(k == 0), stop=(k == KT - 1)),
                     [gT, Wo[k]], [po])
            P.dve(lambda e, n=n, po=po: e.tensor_tensor(out=xo[:, n * 512:(n + 1) * 512], in0=po[:], in1=self.gate[which][:, n * 512:(n + 1) * 512], op=ALU.mult),
                  [po, self.gate[which]], [xo])
        P.pool(lambda e: e.tensor_tensor(out=xo[:], in0=xo[:], in1=xres[:], op=ALU.add), [xo, xres], [xo])
        if last and self.final_norm:
            st = R["st"].next()
            junk = R["junk"]
            P.act(lambda e: e.activation(out=junk[:], in_=xo[:], func=AF.Square, accum_out=st[:, 0:1]), [xo], [junk, st])
            P.act(lambda e: e.activation(out=st[:, 1:2], in_=st[:, 0:1], func=AF.Sqrt, scale=1.0 / D, bias=NORM_EPS), [st], [st])
            P.dve(lambda e: e.reciprocal(out=st[:, 2:3], in_=st[:, 1:2]), [st], [st])
            P.dve(lambda e: e.scalar_tensor_tensor(out=xo[:], in0=xo[:], scalar=st[:, 2:3], in1=self.fg[:], op0=ALU.mult, op1=ALU.mult),
                  [xo, st, self.fg], [xo])
            dst = self.out[ck[1] * CH:(ck[1] + 1) * CH, :]
        elif last:
            dst = self.out[ck[1] * CH:(ck[1] + 1) * CH, :]
        else:
            dst = self.dst_ap(li, ck)
        P.dma(dst, xo[:], "xo%d" % R["xo"].slot, reads=[xo], q="pool")

    def tail_bufs(self, ph, n_gT=2, n_xo=2, last=False, pout=True):
        if last and self.final_norm:
            self.fg = ph.sb("fg", [128, D], F32)
            ph.P.dma(self.fg[:], self.fg_in[:, :], "fgld", writes=[self.fg])
        self.gate = [ph.sb("gate%d" % j, [128, D], F32) for j in range(2)]
        for j in range(2):
            ph.P.dma(self.gate[j][:], self.gate_dram[j], "gld%d" % j, writes=[self.gate[j]])
        return {
            "gT": ph.rot("gT", [128, 16, 128], BF16, n_gT),
            "ptg": [ph.ps("ptg%d" % h, [128, 8, 128], BF16) for h in range(2)],
            "pout": ph.ps("pout", [128, 512], F32) if pout else None,
            "xo": ph.rot("xo", [128, D], F32, n_xo),
        }

    def retention_layer(self, li, i, last):
        nc, w = self.nc, self.lw[i]
        NCH = self.NX + self.NC
        if not hasattr(self, "scr_q"):
            self.scr_q = nc.dram_tensor("scr_q", [NCH, 128, 1024], BF16).ap()
            self.scr_k = nc.dram_tensor("scr_k", [NCH, 128, 1024], BF16).ap()
            self.scr_v = nc.dram_tensor("scr_v", [NCH, 128, 2048], BF16).ap()
            self.scr_o = nc.dram_tensor("scr_o", [NCH, 128, 2048], F32).ap()
            self.rt_cd = Tile(self.outer.enter_context(nc.sbuf_tensor("rt_cd", [128, 8], F32)), "rt_cd")
        ph = Phase(nc)
        self.emit_mod(ph, i)
        ph.finish()

        ph = Phase(nc)
        P = ph.P
        Wt = ph.sb("Wqkv", [128, 8, 4096], BF16)
        Wk = self.load_w(ph, Wt, w["w_in"], 0, 4096, 8)
        R = self.front_bufs(ph)
        dec = ph.sb("dec", [128, 8], F32)
        lg = ph.sb("lg", [128, 8], F32)
        dm = ph.sb("dm", [128, 6, 128], F32)
        jc = ph.sb("jc", [128, 2], F32)
        tA = ph.sb("tA", [128, 128], F32)
        tB = ph.sb("tB", [128, 128], F32)
        MT = ph.sb("MT", [128, 4, 128], F32)
        QDf = ph.sb("QDf", [128, 8, 128], BF16)
        QDb = ph.sb("QDb", [128, 8, 128], BF16)
        kdec = ph.sb("kdec", [128, 8], F32)
        cd = self.rt_cd
        P.dma(dec[:], w["decay"][:, :], "t0", writes=[dec])
        P.dma(dm[:], self.dmat_in[:, :, :], "t1", writes=[dm])
        P.dma(jc[:], self.jcol_in[:, :], "t2", writes=[jc])
        P.act(lambda e: e.activation(out=lg[:], in_=dec[:], func=AF.Exp, scale=-1.0), [dec], [lg])
        P.act(lambda e: e.activation(out=lg[:], in_=lg[:], func=AF.Ln, bias=1.0), [lg], [lg])
        P.dve(lambda e: e.tensor_scalar(out=lg[:], in0=lg[:], scalar1=-1.0, scalar2=None, op0=ALU.mult), [lg], [lg])
        P.act(lambda e: e.activation(out=cd[:], in_=lg[:], func=AF.Exp, scale=128.0), [lg], [cd])
        P.dve(lambda e: e.tensor_scalar(out=kdec[:, 0:4], in0=lg[:, 0:4], scalar1=jc[:, 0:1], scalar2=None, op0=ALU.mult), [lg, jc], [kdec])
        P.dve(lambda e: e.tensor_scalar(out=kdec[:, 4:8], in0=lg[:, 4:8], scalar1=jc[:, 1:2], scalar2=None, op0=ALU.mult), [lg, jc], [kdec])
        P.act(lambda e: e.activation(out=kdec[:], in_=kdec[:], func=AF.Exp), [kdec], [kdec])
        P.dve(lambda e: e.tensor_scalar(out=kdec[:], in0=kdec[:], scalar1=0.0625, scalar2=None, op0=ALU.mult), [kdec], [kdec])
        for h in range(4):
            P.act(lambda e, h=h: e.activation(out=tA[:], in_=dm[:, 0, :], func=AF.Exp, scale=lg[:, h:h + 1]), [dm, lg, MT], [tA])
            P.act(lambda e, h=h: e.activation(out=tB[:], in_=dm[:, 1, :], func=AF.Exp, scale=lg[:, 4 + h:5 + h]), [dm, lg, MT], [tB])
            P.dve(lambda e: e.tensor_tensor(out=tA[:], in0=tA[:], in1=dm[:, 2, :], op=ALU.mult), [tA, dm], [tA])
            P.dve(lambda e: e.tensor_tensor(out=tB[:], in0=tB[:], in1=dm[:, 3, :], op=ALU.mult), [tB, dm], [tB])
            P.dve(lambda e: e.tensor_tensor(out=tA[:], in0=tA[:], in1=tB[:], op=ALU.add), [tA, tB], [tA])
            P.dve(lambda e, h=h: e.tensor_scalar(out=MT[:, h, :], in0=tA[:], scalar1=0.0625, scalar2=None, op0=ALU.mult), [tA], [MT])
            for r in range(2):
                P.act(lambda e, h=h, r=r: e.activation(out=QDf[:, 2 * h + r, :], in_=dm[:, 4, :], func=AF.Exp, scale=lg[:, h:h + 1]), [dm, lg], [QDf])
                P.act(lambda e, h=h, r=r: e.activation(out=QDb[:, 2 * h + r, :], in_=dm[:, 5, :], func=AF.Exp, scale=lg[:, 4 + h:5 + h]), [dm, lg], [QDb])
        mm = Rot([ph.ps("mm%d" % j, [128, 512], F32) for j in range(2)])
        ptq = ph.ps("ptq", [128, 8, 128], BF16)
        ptk = ph.ps("ptk", [128, 8, 128], BF16)
        Gp = Rot([ph.ps("Gp%d" % j, [128, 512], F32) for j in range(3)])
        cs = ph.rot("cs", [128, 2, 2, 128], F32, 2)
        rtmp = ph.rot("rtmp", [128, 4, 2, 128], F32, 2)
        q_r = ph.sb("q_r", [128, 1024], BF16)
        k_r = ph.sb("k_r", [128, 1024], BF16)
        qT = ph.rot("qT", [128, 8, 128], BF16, 2)
        kT = ph.rot("kT", [128, 8, 128], BF16, 2)
        qTf = ph.rot("qTf", [128, 8, 128], BF16, 2)
        qTb = ph.rot("qTb", [128, 8, 128], BF16, 2)
        kf = ph.rot("kf", [128, 1024], BF16, 2)
        kb = ph.rot("kb", [128, 1024], BF16, 2)
        vsb = ph.rot("vsb", [128, 2048], BF16, 2)
        sT = ph.rot("sT", [128, 4, 128], BF16, 2)
        osb = ph.rot("osb", [128, 2048], F32, 2)
        Sbf = ph.sb("Sbf", [128, 8, 512], BF16)
        Sbk = [Tile(Sbf.t[:, k, :], "Sb%d" % k) for k in range(8)]
        for k in range(8):
            P.pool(lambda e, k=k: e.memset(Sbk[k][:], 0.0), [], [Sbk[k]])
        cdI = self.ret_cdI(ph)

        for ck in self.chunks_fwd():
            which = 1 if ck[0] == "c" else 0
            r0 = self.row0(ck)
            cidx = r0 // CH
            hT, xt = self.front(ph, R, self.src_ap(li, ck), which)
            c_ = cs.next()
            for r in range(2):
                P.dma(c_[:, :, r, :], self.rope_in[r0:r0 + CH, :, :], "cs%d_%d" % (cs.slot, r), writes=[c_])
            v_ = vsb.next()
            for n in range(8):
                bank = mm.next()
                for k in range(8):
                    P.pe(lambda e, bank=bank, n=n, k=k, hT=hT: e.matmul(bank[:], lhsT=hT[:, k, :], rhs=Wk[k][:, n * 512:(n + 1) * 512],
                                                                        start=(k == 0), stop=(k == 7)), [hT, Wk[k]], [bank])
                if n < 4:
                    dst = q_r if n < 2 else k_r
                    tmp = rtmp.next()
                    pb = bank[:].rearrange("p (h t c) -> p h t c", h=2, t=2)
                    t1, t2 = pb[:, :, 0, :], pb[:, :, 1, :]
                    cos2, sin2 = c_[:, 0, :, :], c_[:, 1, :, :]
                    P.dve(lambda e, tmp=tmp, t1=t1, cos2=cos2: e.tensor_tensor(out=tmp[:, 0], in0=t1, in1=cos2, op=ALU.mult), [bank, c_], [tmp])
                    P.dve(lambda e, tmp=tmp, t2=t2, sin2=sin2: e.tensor_tensor(out=tmp[:, 1], in0=t2, in1=sin2, op=ALU.mult), [bank, c_], [tmp])
                    P.dve(lambda e, tmp=tmp, t1=t1, sin2=sin2: e.tensor_tensor(out=tmp[:, 2], in0=t1, in1=sin2, op=ALU.mult), [bank, c_], [tmp])
                    P.dve(lambda e, tmp=tmp, t2=t2, cos2=cos2: e.tensor_tensor(out=tmp[:, 3], in0=t2, in1=cos2, op=ALU.mult), [bank, c_], [tmp])
                    dv = dst[:].rearrange("p (h t c) -> p h t c", h=4, t=2)
                    h0 = 2 * (n % 2)
                    P.pool(lambda e, tmp=tmp, dv=dv, h0=h0: e.tensor_tensor(out=dv[:, h0:h0 + 2, 0, :], in0=tmp[:, 0], in1=tmp[:, 1], op=ALU.subtract),
                           [tmp], [dst])
                    P.pool(lambda e, tmp=tmp, dv=dv, h0=h0: e.tensor_tensor(out=dv[:, h0:h0 + 2, 1, :], in0=tmp[:, 2], in1=tmp[:, 3], op=ALU.add),
                           [tmp], [dst])
                else:
                    P.act(lambda e, bank=bank, n=n, v_=v_: e.activation(out=v_[:, (n - 4) * 512:(n - 3) * 512], in_=bank[:], func=AF.Copy), [bank], [v_])
            qT_, kT_, qTf_, qTb_ = qT.next(), kT.next(), qTf.next(), qTb.next()
            for k in range(8):
                P.pe(lambda e, k=k: e.transpose(out=ptq[:, k, :], in_=q_r[:, k * 128:(k + 1) * 128], identity=self.ident[:]), [q_r, self.ident], [ptq])
            P.act(lambda e, qT_=qT_: e.activation(out=qT_[:], in_=ptq[:], func=AF.Copy), [ptq], [qT_])
            for k in range(8):
                P.pe(lambda e, k=k: e.transpose(out=ptk[:, k, :], in_=k_r[:, k * 128:(k + 1) * 128], identity=self.ident[:]), [k_r, self.ident], [ptk])
            P.act(lambda e, kT_=kT_: e.activation(out=kT_[:], in_=ptk[:], func=AF.Copy), [ptk], [kT_])
            P.dve(lambda e, qT_=qT_, qTf_=qTf_: e.tensor_tensor(out=qTf_[:], in0=qT_[:], in1=QDf[:], op=ALU.mult), [qT_, QDf], [qTf_])
            P.dve(lambda e, qT_=qT_, qTb_=qTb_: e.tensor_tensor(out=qTb_[:], in0=qT_[:], in1=QDb[:], op=ALU.mult), [qT_, QDb], [qTb_])
            kf_, kb_ = kf.next(), kb.next()
            krv = k_r[:].rearrange("p (h c) -> p h c", h=4)
            P.pool(lambda e, kf_=kf_: e.tensor_tensor(out=kf_[:].rearrange("p (h c) -> p h c", h=4), in0=krv,
                                                      in1=kdec[:, 0:4].unsqueeze(2).to_broadcast([128, 4, 256]), op=ALU.mult), [k_r, kdec], [kf_])
            P.pool(lambda e, kb_=kb_: e.tensor_tensor(out=kb_[:].rearrange("p (h c) -> p h c", h=4), in0=krv,
                                                      in1=kdec[:, 4:8].unsqueeze(2).to_broadcast([128, 4, 256]), op=ALU.mult), [k_r, kdec], [kb_])
            psc = Gp.next()
            for h in range(4):
                for hf in range(2):
                    P.pe(lambda e, h=h, hf=hf, kT_=kT_, qT_=qT_, psc=psc: e.matmul(psc[:, h * 128:(h + 1) * 128], lhsT=kT_[:, 2 * h + hf, :], rhs=qT_[:, 2 * h + hf, :],
                                                                                   start=(hf == 0), stop=(hf == 1)), [kT_, qT_], [psc])
            sT_ = sT.next()
            P.dve(lambda e, sT_=sT_, psc=psc: e.tensor_tensor(out=sT_[:], in0=psc[:].rearrange("p (h t) -> p h t", h=4), in1=MT[:], op=ALU.mult), [psc, MT], [sT_])
            o_ = osb.next()
            for h in range(4):
                po = Gp.next()
                P.pe(lambda e, h=h, sT_=sT_, v_=v_, po=po: e.matmul(po[:], lhsT=sT_[:, h, :], rhs=v_[:, h * 512:(h + 1) * 512], start=True, stop=False), [sT_, v_], [po])
                for hf in range(2):
                    kt = 2 * h + hf
                    P.pe(lambda e, kt=kt, hf=hf, qTf_=qTf_, po=po: e.matmul(po[:], lhsT=qTf_[:, kt, :], rhs=Sbk[kt][:], start=False, stop=(hf == 1)),
                         [qTf_, Sbk[kt]], [po])
                P.act(lambda e, h=h, o_=o_, po=po: e.activation(out=o_[:, h * 512:(h + 1) * 512], in_=po[:], func=AF.Copy), [po], [o_])
            self.ret_state_update(P, Gp, kf_, v_, Sbk, cdI, 0)
            P.dma(self.scr_q[cidx].rearrange("p (k t) -> p k t", k=8), qTb_[:], "sq%d" % qTb.slot, reads=[qTb_], q="pool")
            P.dma(self.scr_k[cidx], kb_[:], "sk%d" % kb.slot, reads=[kb_], q="pool")
            P.dma(self.scr_v[cidx], v_[:], "sv%d" % vsb.slot, reads=[v_], q="pool")
            P.dma(self.scr_o[cidx], o_[:], "so%d" % osb.slot, reads=[o_], q="pool")
        ph.finish()

        ph = Phase(nc)
        P = ph.P
        Wzt = ph.sb("Wz", [128, 8, 2048], BF16)
        Wz = self.load_w(ph, Wzt, w["w_in"], 4096, 2048, 8)
        Wot = ph.sb("Wo", [128, 16, 1024], BF16)
        Wo = self.load_w(ph, Wot, w["w_out"], 0, 1024, 16)
        R = self.front_bufs(ph)
        R.update(self.tail_bufs(ph, last=last, pout=False))
        mm = Rot([ph.ps("mm%d" % j, [128, 512], F32) for j in range(2)])
        Gp = Rot([ph.ps("Gp%d" % j, [128, 512], F32) for j in range(3)])
        R["pout"] = Gp
        qTb = ph.rot("qTb", [128, 8, 128], BF16, 2)
        kb = ph.rot("kb", [128, 1024], BF16, 2)
        vsb = ph.rot("vsb", [128, 2048], BF16, 2)
        osb = ph.rot("osb", [128, 2048], F32, 2)
        sz = ph.rot("sz", [128, 2048], F32, 2)
        gated = ph.rot("gated", [128, 2048], BF16, 2)
        st4 = ph.rot("st4", [128, 12], F32, 2)
        Sbf = ph.sb("Sbf", [128, 8, 512], BF16)
        Sbk = [Tile(Sbf.t[:, k, :], "Sb%d" % k) for k in range(8)]
        for k in range(8):
            P.pool(lambda e, k=k: e.memset(Sbk[k][:], 0.0), [], [Sbk[k]])
        cdI = self.ret_cdI(ph)
        cd = self.rt_cd
        for ck in self.chunks_bwd():
            which = 1 if ck[0] == "c" else 0
            cidx = self.row0(ck) // CH
            need_out = not (last and ck[0] == "c")
            kb_, v_ = kb.next(), vsb.next()
            P.dma(kb_[:], self.scr_k[cidx], "lk%d" % kb.slot, writes=[kb_])
            P.dma(v_[:], self.scr_v[cidx], "lv%d" % vsb.slot, writes=[v_])
            if need_out:
                q_, o_ = qTb.next(), osb.next()
                P.dma(q_[:], self.scr_q[cidx].rearrange("p (k t) -> p k t", k=8), "lq%d" % qTb.slot, writes=[q_])
                P.dma(o_[:], self.scr_o[cidx], "lo%d" % osb.slot, writes=[o_])
                hT, xt = self.front(ph, R, self.src_ap(li, ck), which)
                sz_ = sz.next()
                for n in range(4):
                    bank = mm.next()
                    for k in range(8):
                        P.pe(lambda e, bank=bank, n=n, k=k, hT=hT: e.matmul(bank[:], lhsT=hT[:, k, :], rhs=Wz[k][:, n * 512:(n + 1) * 512],
                                                                            start=(k == 0), stop=(k == 7)), [hT, Wz[k]], [bank])
                    P.act(lambda e, bank=bank, n=n, sz_=sz_: e.activation(out=sz_[:, n * 512:(n + 1) * 512], in_=bank[:], func=AF.Silu), [bank], [sz_])
                s4 = st4.next()
                junk = R["junk"]
                for h in range(4):
                    po = Gp.next()
                    for hf in range(2):
                        kt = 2 * h + hf
                        P.pe(lambda e, kt=kt, hf=hf, q_=q_, po=po: e.matmul(po[:], lhsT=q_[:, kt, :], rhs=Sbk[kt][:], start=(hf == 0), stop=(hf == 1)),
                             [q_, Sbk[kt]], [po])
                    P.dve(lambda e, h=h, o_=o_, po=po: e.tensor_tensor(out=o_[:, h * 512:(h + 1) * 512], in0=po[:], in1=o_[:, h * 512:(h + 1) * 512], op=ALU.add),
                          [po, o_], [o_])
                    P.act(lambda e, h=h, o_=o_, s4=s4: e.activation(out=junk[:, 0:512], in_=o_[:, h * 512:(h + 1) * 512], func=AF.Square, accum_out=s4[:, h:h + 1]),
                          [o_], [junk, s4])
                P.act(lambda e, s4=s4: e.activation(out=s4[:, 4:8], in_=s4[:, 0:4], func=AF.Sqrt, scale=1.0 / 512, bias=NORM_EPS), [s4], [s4])
                P.dve(lambda e, s4=s4: e.reciprocal(out=s4[:, 8:12], in_=s4[:, 4:8]), [s4], [s4])
                g_ = gated.next()
                for h in range(4):
                    P.dve(lambda e, h=h, o_=o_, s4=s4, sz_=sz_, g_=g_: e.scalar_tensor_tensor(
                        out=g_[:, h * 512:(h + 1) * 512], in0=o_[:, h * 512:(h + 1) * 512], scalar=s4[:, 8 + h:9 + h],
                        in1=sz_[:, h * 512:(h + 1) * 512], op0=ALU.mult, op1=ALU.mult), [o_, s4, sz_], [g_])
                self.tail(ph, R, g_, Wo, li, ck, xt, last)
            self.ret_state_update(P, Gp, kb_, v_, Sbk, cdI, 4)
        ph.finish()

    def ret_state_update(self, P, pst_rot, kd, v_, Sbk, cdI, c0):
        for kt in range(8):
            h = kt // 2
            pst = pst_rot.next()
            P.pe(lambda e, kt=kt, h=h, pst=pst: e.matmul(pst[:], lhsT=cdI[:, c0 + h, :], rhs=Sbk[kt][:], start=True, stop=False), [cdI, Sbk[kt]], [pst])
            P.pe(lambda e, kt=kt, h=h, pst=pst: e.matmul(pst[:], lhsT=kd[:, kt * 128:(kt + 1) * 128], rhs=v_[:, h * 512:(h + 1) * 512], start=False, stop=True),
                 [kd, v_], [pst])
            if kt % 2 == 0:
                P.act(lambda e, kt=kt, pst=pst: e.activation(out=Sbk[kt][:], in_=pst[:], func=AF.Copy), [pst], [Sbk[kt]])
            else:
                P.dve(lambda e, kt=kt, pst=pst: e.tensor_copy(out=Sbk[kt][:], in_=pst[:]), [pst], [Sbk[kt]])

    def ret_cdI(self, ph):
        cdI = ph.sb("cdI", [128, 8, 128], BF16)
        for j in range(8):
            ph.P.dve(lambda e, j=j: e.tensor_scalar(out=cdI[:, j, :], in0=self.ident32[:], scalar1=self.rt_cd[:, j:j + 1], scalar2=None, op0=ALU.mult),
                     [self.ident32, self.rt_cd], [cdI])
        return cdI

    def gmlp_layer(self, li, i, last):
        nc, w = self.nc, self.lw[i]
        ph = Phase(nc)
        self.emit_mod(ph, i)
        ph.finish()
        ph = Phase(nc)
        P = ph.P
        Wt = ph.sb("Wuvz", [128, 8, 6144], BF16)
        Wk = self.load_w(ph, Wt, w["w_in"], 0, 6144, 8)
        Wot = ph.sb("Wo", [128, 16, 1024], BF16)
        Wo = self.load_w(ph, Wot, w["w_out"], 0, 1024, 16)
        R = self.front_bufs(ph, n_xn=1)
        R.update(self.tail_bufs(ph, n_gT=1, n_xo=1, last=last))
        mm = Rot([ph.ps("mm%d" % j, [128, 512], F32) for j in range(2)])
        spb = Rot([ph.ps("spb%d" % j, [128, 512], F32) for j in range(2)])
        wsT = ph.sb("wsT", [128, 8, 128], BF16)
        bsT = ph.sb("bsT", [128, 8], F32)
        vg = ph.sb("vg", [128, 2048], F32)
        P.dma(wsT[:], w["wsT"][:, :, :], "g0", writes=[wsT], q="pool")
        P.dma(bsT[:], w["bsT"][:, :], "g1", writes=[bsT])
        P.dma(vg[:], w["vg"][:, :], "g2", writes=[vg])
        usb = ph.rot("usb", [128, 2048], F32, 1)
        vsb = ph.rot("vsb", [128, 2048], F32, 1)
        szb = ph.rot("szb", [128, 2048], BF16, 1)
        vnb = ph.rot("vnb", [128, 2048], BF16, 1)
        gated = ph.rot("gated", [128, 2048], BF16, 1)
        stv = ph.rot("stv", [128, 16], F32, 2)
        junk = R["junk"]
        order = self.chunks_fwd()
        if last:
            order = [ck for ck in order if ck[0] == "x"]
        for ck in order:
            which = 1 if ck[0] == "c" else 0
            hT, xt = self.front(ph, R, self.src_ap(li, ck), which)
            u_, v_, z_, s_ = usb.next(), vsb.next(), szb.next(), stv.next()
            for n in range(12):
                bank = mm.next()
                for k in range(8):
                    P.pe(lambda e, bank=bank, n=n, k=k, hT=hT: e.matmul(bank[:], lhsT=hT[:, k, :], rhs=Wk[k][:, n * 512:(n + 1) * 512],
                                                                        start=(k == 0), stop=(k == 7)), [hT, Wk[k]], [bank])
                if n < 4:
                    P.act(lambda e, bank=bank, n=n, u_=u_: e.activation(out=u_[:, n * 512:(n + 1) * 512], in_=bank[:], func=AF.Copy), [bank], [u_])
                elif n < 8:
                    P.act(lambda e, bank=bank, n=n, v_=v_, s_=s_: e.activation(out=v_[:, (n - 4) * 512:(n - 3) * 512], in_=bank[:], func=AF.Identity,
                                                                              accum_out=s_[:, n - 4:n - 3]), [bank], [v_, s_])
                else:
                    P.act(lambda e, bank=bank, n=n, z_=z_: e.activation(out=z_[:, (n - 8) * 512:(n - 7) * 512], in_=bank[:], func=AF.Silu), [bank], [z_])
            for hh in range(2):
                P.act(lambda e, v_=v_, s_=s_, hh=hh: e.activation(out=junk[:], in_=v_[:, hh * 1024:(hh + 1) * 1024], func=AF.Square,
                                                                  accum_out=s_[:, 11 + hh:12 + hh]), [v_], [junk, s_])
            P.dve(lambda e, s_=s_: e.tensor_tensor(out=s_[:, 4:5], in0=s_[:, 11:12], in1=s_[:, 12:13], op=ALU.add), [s_], [s_])
            P.dve(lambda e, s_=s_: e.tensor_reduce(out=s_[:, 5:6], in_=s_[:, 0:4], axis=AX.X, op=ALU.add), [s_], [s_])
            P.dve(lambda e, s_=s_: e.tensor_scalar(out=s_[:, 5:6], in0=s_[:, 5:6], scalar1=1.0 / 2048, scalar2=None, op0=ALU.mult), [s_], [s_])
            P.dve(lambda e, s_=s_: e.tensor_tensor(out=s_[:, 6:7], in0=s_[:, 5:6], in1=s_[:, 5:6], op=ALU.mult), [s_], [s_])
            P.dve(lambda e, s_=s_: e.scalar_tensor_tensor(out=s_[:, 7:8], in0=s_[:, 4:5], scalar=1.0 / 2048, in1=s_[:, 6:7], op0=ALU.mult, op1=ALU.subtract),
                  [s_], [s_])
            P.act(lambda e, s_=s_: e.activation(out=s_[:, 8:9], in_=s_[:, 7:8], func=AF.Sqrt, scale=1.0, bias=NORM_EPS), [s_], [s_])
            P.dve(lambda e, s_=s_: e.reciprocal(out=s_[:, 9:10], in_=s_[:, 8:9]), [s_], [s_])
            P.dve(lambda e, s_=s_: e.scalar_tensor_tensor(out=s_[:, 10:11], in0=s_[:, 5:6], scalar=-1.0, in1=s_[:, 9:10], op0=ALU.mult, op1=ALU.mult),
                  [s_], [s_])
            P.act(lambda e, v_=v_, s_=s_: e.activation(out=v_[:], in_=v_[:], func=AF.Identity, scale=s_[:, 9:10], bias=s_[:, 10:11]), [v_, s_], [v_])
            vn_ = vnb.next()
            P.dve(lambda e, v_=v_, vn_=vn_: e.tensor_tensor(out=vn_[:], in0=v_[:], in1=vg[:], op=ALU.mult), [v_, vg], [vn_])
            for g in range(8):
                sb_ = spb.next()
                P.pe(lambda e, g=g, sb_=sb_, vn_=vn_: e.matmul(sb_[:, 0:256], lhsT=wsT[:, g, :], rhs=vn_[:, g * 256:(g + 1) * 256], start=True, stop=True),
                     [wsT, vn_], [sb_])
                P.dve(lambda e, g=g, sb_=sb_, u_=u_: e.scalar_tensor_tensor(out=u_[:, g * 256:(g + 1) * 256], in0=sb_[:, 0:256], scalar=bsT[:, g:g + 1],
                                                                           in1=u_[:, g * 256:(g + 1) * 256], op0=ALU.add, op1=ALU.mult), [sb_, bsT, u_], [u_])
            g_ = gated.next()
            P.pool(lambda e, u_=u_, z_=z_, g_=g_: e.tensor_tensor(out=g_[:], in0=u_[:], in1=z_[:], op=ALU.mult), [u_, z_], [g_])
            self.tail(ph, R, g_, Wo, li, ck, xt, last)
        ph.finish()

    def _rw_inputs(self, w, i, din):
        w["mu"] = din("rw_mu%d" % i, [128, 6, 8])
        w["rkvg"] = din("rw_rkvg%d" % i, [4, D, D])
        w["w1"] = din("rw_w1%d" % i, [2, D, 64])
        w["a1"] = din("rw_a1%d" % i, [2, D, 64])
        w["w2"] = din("rw_w2%d" % i, [2, 64, D])
        w["a2"] = din("rw_a2%d" % i, [2, 64, D])
        w["rows"] = din("rw_rows%d" % i, [8, D])
        w["bc"] = din("rw_bc%d" % i, [128, 5, D])
        w["w_out"] = din("rw_wout%d" % i, [D, D])
        w["masks"] = din("rw_masks%d" % i, [2, 128, 4, 128])
        w["sel8"] = din("rw_sel8%d" % i, [8, 8, 128])
        w["negc"] = din("rw_negc%d" % i, [128, 1])
        w["bmask"] = din("rw_bmask%d" % i, [128, 4, 128])
        w["cmask"] = din("rw_cmask%d" % i, [2, 128, 7, 128])

    def rw_shift(self, P, sh, hc, hp, hn, kind):
        if kind == "x":
            P.act(lambda e: e.activation(out=sh[:, 0:2, 1:128], in_=hc[:, 0:2, 0:127], func=AF.Copy), [hc], [sh])
            P.pool(lambda e: e.memset(sh[:, 0:2, :].rearrange("p k (r c) -> p k r c", c=64)[:, :, :, 0:1], 0.0), [], [sh])
            P.act(lambda e: e.activation(out=sh[:, 2:4, 0:127], in_=hc[:, 2:4, 1:128], func=AF.Copy), [hc], [sh])
            P.pool(lambda e: e.memset(sh[:, 2:4, :].rearrange("p k (r c) -> p k r c", c=64)[:, :, :, 63:64], 0.0), [], [sh])
            P.act(lambda e: e.activation(out=sh[:, 4:6, 64:128], in_=hc[:, 4:6, 0:64], func=AF.Copy), [hc], [sh])
            if hp is not None:
                P.act(lambda e: e.activation(out=sh[:, 4:6, 0:64], in_=hp[:, 4:6, 64:128], func=AF.Copy), [hp], [sh])
            else:
                P.pool(lambda e: e.memset(sh[:, 4:6, 0:64], 0.0), [], [sh])
            P.act(lambda e: e.activation(out=sh[:, 6:8, 0:64], in_=hc[:, 6:8, 64:128], func=AF.Copy), [hc], [sh])
            if hn is not None:
                P.act(lambda e: e.activation(out=sh[:, 6:8, 64:128], in_=hn[:, 6:8, 0:64], func=AF.Copy), [hn], [sh])
            else:
                P.pool(lambda e: e.memset(sh[:, 6:8, 64:128], 0.0), [], [sh])
        else:
            P.act(lambda e: e.activation(out=sh[:, 0:4, 1:128], in_=hc[:, 0:4, 0:127], func=AF.Copy), [hc], [sh])
            if hp is not None:
                P.act(lambda e: e.activation(out=sh[:, 0:4, 0:1], in_=hp[:, 0:4, 127:128], func=AF.Copy), [hp], [sh])
            else:
                P.pool(lambda e: e.memset(sh[:, 0:4, 0:1], 0.0), [], [sh])
            P.act(lambda e: e.activation(out=sh[:, 4:8, 0:127], in_=hc[:, 4:8, 1:128], func=AF.Copy), [hc], [sh])
            if hn is not None:
                P.act(lambda e: e.activation(out=sh[:, 4:8, 127:128], in_=hn[:, 4:8, 0:1], func=AF.Copy), [hn], [sh])
            else:
                P.pool(lambda e: e.memset(sh[:, 4:8, 127:128], 0.0), [], [sh])

    def rw_neighbors(self, ck):
        n = self.NC if ck[0] == "c" else self.NX
        p = (ck[0], ck[1] - 1) if ck[1] > 0 else None
        q = (ck[0], ck[1] + 1) if ck[1] < n - 1 else None
        return p, q

    def rw_hcache(self, ph, R, li):
        cache = []

        def get(ck):
            for c, v in cache:
                if c == ck:
                    return v
            which = 1 if ck[0] == "c" else 0
            v = self.front(ph, R, self.src_ap(li, ck), which)
            cache.append((ck, v))
            if len(cache) > 3:
                cache.pop(0)
            return v
        return get

    def rw_mix(self, P, mixr, tmpr, xx, hc, mu, p):
        mix = mixr.next()
        for k in range(8):
            P.dve(lambda e, k=k: e.scalar_tensor_tensor(out=mix[:, k, :], in0=xx[:, k, :], scalar=mu[:, p, k:k + 1], in1=hc[:, k, :],
                                                        op0=ALU.mult, op1=ALU.add), [xx, mu, hc], [mix])
        return mix

    def rwkv_layer(self, li, i, last):
        nc, w = self.nc, self.lw[i]
        NCH = self.NX + self.NC
        if not hasattr(self, "scr_o"):
            self.scr_o = nc.dram_tensor("scr_o", [NCH, 128, 2048], F32).ap()
        if not hasattr(self, "rw_scr_v"):
            self.rw_scr_v = nc.dram_tensor("rw_scr_v", [NCH, 128, 1024], BF16).ap()
            self.rw_dir = nc.dram_tensor("rw_dir", [2, NCH, 128, 4096], BF16).ap()
            self.rw_small = nc.dram_tensor("rw_small", [2, NCH, 128, 32], F32).ap()
        ph = Phase(nc)
        self.emit_mod(ph, i)
        ph.finish()
        self.rw_prep_phase(li, i)
        import os as _os
        for d in range(2):
            self.rw_scan_phase(li, i, d)
            if _os.environ.get("RW_STOP_AFTER_F") == "1":
                return
        self.rw_out_phase(li, i, last)

    def rw_prep_phase(self, li, i):
        nc, w = self.nc, self.lw[i]
        ph = Phase(nc)
        P = ph.P
        C0 = 0.6065306597126334
        Wr = self.load_w(ph, ph.sb("Wr", [128, 8, 1024], BF16), w["rkvg"][0], 0, 1024, 8)
        Wkk = self.load_w(ph, ph.sb("Wk", [128, 8, 1024], BF16), w["rkvg"][1], 0, 1024, 8)
        Wv = self.load_w(ph, ph.sb("Wv", [128, 8, 1024], BF16), w["rkvg"][2], 0, 1024, 8)
        w1 = [self.load_w(ph, ph.sb("w1_%d" % d, [128, 8, 64], BF16), w["w1"][d], 0, 64, 8) for d in range(2)]
        a1 = [self.load_w(ph, ph.sb("a1_%d" % d, [128, 8, 64], BF16), w["a1"][d], 0, 64, 8) for d in range(2)]
        w2 = [ph.sb("w2_%d" % d, [64, 1024], BF16) for d in range(2)]
        a2 = [ph.sb("a2_%d" % d, [64, 1024], BF16) for d in range(2)]
        tri = [ph.sb("tri%d" % d, [128, 128], F32) for d in range(2)]
        for d in range(2):
            P.dma(w2[d][:], w["w2"][d], "w2%d" % d, writes=[w2[d]], q="pool")
            P.dma(a2[d][:], w["a2"][d], "a2%d" % d, writes=[a2[d]], q="pool")
            P.dma(tri[d][:], w["masks"][d][:, 3, :], "tri%d" % d, writes=[tri[d]])
        rows = ph.sb("rows", [8, 1024], F32)
        sel8 = ph.sb("sel8", [8, 8, 128], F32)
        mu = ph.sb("mu", [128, 6, 8], F32)
        negc = ph.sb("negc", [128, 1], F32)
        kk_bc = ph.sb("kk_bc", [128, 1024], F32)
        ka_bc = ph.sb("ka_bc", [128, 1024], F32)
        rk_bc = ph.sb("rk_bc", [128, 1024], F32)
        P.dma(rows[:], w["rows"][:, :], "c0", writes=[rows])
        P.dma(sel8[:], w["sel8"][:, :, :], "c1", writes=[sel8])
        P.dma(mu[:], w["mu"][:, :, :], "c2", writes=[mu])
        P.dma(negc[:], w["negc"][:, :], "c4", writes=[negc])
        P.dma(kk_bc[:], w["bc"][:, 0, :], "c5", writes=[kk_bc])
        P.dma(ka_bc[:], w["bc"][:, 1, :], "c6", writes=[ka_bc])
        P.dma(rk_bc[:], w["bc"][:, 2, :], "c7", writes=[rk_bc])
        R = self.front_bufs(ph, n_xn=2, n_xt=2, n_hT=4)
        get_h = self.rw_hcache(ph, R, li)
        G = Rot([ph.ps("G%d" % j, [128, 512], F32) for j in range(5)])
        psm = [ph.ps("psm%d" % d, [128, 512], F32) for d in range(2)]
        sh = ph.rot("sh", [128, 8, 128], BF16, 2)
        xx = ph.rot("xx", [128, 8, 128], F32, 2)
        mixr = ph.rot("mix", [128, 8, 128], BF16, 4)
        r_sb = ph.sb("r_sb", [128, 1024], F32)
        k_sb = ph.sb("k_sb", [128, 1024], F32)
        kk = ph.sb("kk_sb", [128, 1024], F32)
        v_bf = ph.rot("v_bf", [128, 1024], BF16, 2)
        Wt = [[ph.sb("W%d_%d" % (d, j), [128, 1024], F32) for j in range(4)] for d in range(2)]
        th_bf = [ph.sb("th_bf%d" % d, [64, 128], BF16) for d in range(2)]
        la_bf = [ph.sb("la_bf%d" % d, [64, 128], BF16) for d in range(2)]
        outb = [ph.sb("outb%d" % d, [128, 4096], BF16) for d in range(2)]
        small = [ph.rot("small%d" % d, [128, 32], F32, 2) for d in range(2)]
        sm = ph.rot("sm", [128, 48], F32, 2)
        for d in range(2):
            for t_ in small[d].tiles:
                P.dve(lambda e, t_=t_: e.memset(t_[:], 0.0), [], [t_])

        def chunk_body(ck):
            cidx = self.row0(ck) // CH
            pk, nk = self.rw_neighbors(ck)
            hc = get_h(ck)[0]
            hp = get_h(pk)[0] if pk else None
            hn = get_h(nk)[0] if nk else None
            sh_, xx_ = sh.next(), xx.next()
            self.rw_shift(P, sh_, hc, hp, hn, ck[0])
            P.pool(lambda e: e.tensor_tensor(out=xx_[:], in0=sh_[:], in1=hc[:], op=ALU.subtract), [sh_, hc], [xx_])
            v_, s_ = v_bf.next(), sm.next()
            for p, Wl, dst in ((0, Wr, r_sb), (2, Wkk, k_sb), (3, Wv, v_)):
                mix = self.rw_mix(P, mixr, None, xx_, hc, mu, p)
                for n in range(2):
                    bank = G.next()
                    for k in range(8):
                        P.pe(lambda e, bank=bank, n=n, k=k, mix=mix, Wl=Wl: e.matmul(bank[:], lhsT=mix[:, k, :], rhs=Wl[k][:, n * 512:(n + 1) * 512],
                                                                                    start=(k == 0), stop=(k == 7)), [mix, Wl[k]], [bank])
                    P.act(lambda e, bank=bank, n=n, dst=dst: e.activation(out=dst[:, n * 512:(n + 1) * 512], in_=bank[:], func=AF.Copy), [bank], [dst])
            P.dma(self.rw_scr_v[cidx], v_[:], "pv%d" % v_bf.slot, reads=[v_], q="pool")
            sq = Wt[0][1]
            P.dve(lambda e: e.tensor_tensor(out=kk[:], in0=k_sb[:], in1=kk_bc[:], op=ALU.mult), [k_sb, kk_bc], [kk])
            P.pool(lambda e: e.tensor_tensor(out=sq[:], in0=kk[:], in1=kk[:], op=ALU.mult), [kk], [sq])
            P.dve(lambda e: e.tensor_reduce(out=s_[:, 0:16], in_=sq[:].rearrange("p (h c) -> p h c", h=16), axis=AX.X, op=ALU.add), [sq], [s_])
            P.act(lambda e: e.activation(out=s_[:, 16:32], in_=s_[:, 0:16], func=AF.Sqrt), [s_], [s_])
            P.dve(lambda e: e.tensor_scalar(out=s_[:, 16:32], in0=s_[:, 16:32], scalar1=1e-12, scalar2=None, op0=ALU.max), [s_], [s_])
            P.dve(lambda e: e.reciprocal(out=s_[:, 32:48], in_=s_[:, 16:32]), [s_], [s_])
            P.dve(lambda e: e.tensor_tensor(out=kk[:].rearrange("p (h c) -> p h c", h=16), in0=kk[:].rearrange("p (h c) -> p h c", h=16),
                                            in1=s_[:, 32:48].unsqueeze(2).to_broadcast([128, 16, 64]), op=ALU.mult), [kk, s_], [kk])
            mix1 = self.rw_mix(P, mixr, None, xx_, hc, mu, 1)
            mix4 = self.rw_mix(P, mixr, None, xx_, hc, mu, 4)
            sml = [small[d].next() for d in range(2)]
            for d in range(2):
                for k in range(8):
                    P.pe(lambda e, k=k, d=d: e.matmul(psm[d][0:64, 0:128], lhsT=w1[d][k][:, :], rhs=mix1[:, k, :], start=(k == 0), stop=(k == 7)),
                         [mix1, w1[d][k]], [psm[d]])
                P.act(lambda e, d=d: e.activation(out=th_bf[d][:], in_=psm[d][0:64, 0:128], func=AF.Tanh), [psm[d]], [th_bf[d]])
            for d in range(2):
                for k in range(8):
                    P.pe(lambda e, k=k, d=d: e.matmul(psm[d][0:64, 0:128], lhsT=a1[d][k][:, :], rhs=mix4[:, k, :], start=(k == 0), stop=(k == 7)),
                         [mix4, a1[d][k]], [psm[d]])
                P.act(lambda e, d=d: e.activation(out=la_bf[d][:], in_=psm[d][0:64, 0:128], func=AF.Copy), [psm[d]], [la_bf[d]])
            for d in range(2):
                sig = Wt[d][0]
                for n in range(2):
                    bank = G.next()
                    P.pe(lambda e, bank=bank, n=n, d=d: e.matmul(bank[:], lhsT=th_bf[d][:], rhs=w2[d][:, n * 512:(n + 1) * 512], start=True, stop=False),
                         [th_bf[d], w2[d]], [bank])
                    P.pe(lambda e, bank=bank, n=n, d=d: e.matmul(bank[:], lhsT=sel8[:, d, :], rhs=rows[:, n * 512:(n + 1) * 512], start=False, stop=True),
                         [sel8, rows], [bank])
                    P.act(lambda e, bank=bank, n=n, sig=sig: e.activation(out=sig[:, n * 512:(n + 1) * 512], in_=bank[:], func=AF.Sigmoid), [bank], [sig])
            for d in range(2):
                sig, ep, em, ex = Wt[d]
                for n in range(2):
                    bank = G.next()
                    sl = slice(n * 512, (n + 1) * 512)
                    P.pe(lambda e, bank=bank, sl=sl, d=d, sig=sig: e.matmul(bank[:], lhsT=tri[d][:], rhs=sig[:, sl], start=True, stop=True), [tri[d], sig], [bank])
                    P.act(lambda e, bank=bank, sl=sl, ep=ep: e.activation(out=ep[:, sl], in_=bank[:], func=AF.Exp), [bank], [ep])
                    P.act(lambda e, bank=bank, sl=sl, em=em: e.activation(out=em[:, sl], in_=bank[:], func=AF.Exp, scale=-1.0), [bank], [em])
                    P.dve(lambda e, bank=bank, sl=sl, ex=ex, sig=sig: e.scalar_tensor_tensor(out=ex[:, sl], in0=sig[:, sl], scalar=C0, in1=bank[:], op0=ALU.mult, op1=ALU.add),
                          [bank, sig], [ex])
                P.act(lambda e, ex=ex: e.activation(out=ex[:], in_=ex[:], func=AF.Exp), [ex], [ex])
            for d in range(2):
                sig = Wt[d][0]
                for h in range(16):
                    P.pe(lambda e, h=h, d=d, sig=sig: e.matmul(psm[d][0:64, 256 + h:257 + h], lhsT=sig[:, h * 64:(h + 1) * 64], rhs=negc[:, 0:1], start=True, stop=True),
                         [sig, negc], [psm[d]])
                P.act(lambda e, d=d: e.activation(out=sml[d][0:64, 0:16], in_=psm[d][0:64, 256:272], func=AF.Exp), [psm[d]], [sml[d]])
            for d in range(2):
                sig, ep, em, ex = Wt[d]
                P.pool(lambda e, d=d, ep=ep: e.tensor_tensor(out=outb[d][:, 0:1024], in0=r_sb[:], in1=ep[:], op=ALU.mult), [r_sb, ep], [outb[d]])
                P.dve(lambda e, d=d, ex=ex: e.scalar_tensor_tensor(out=outb[d][:, 1024:2048], in0=kk[:], scalar=-1.0, in1=ex[:], op0=ALU.mult, op1=ALU.mult),
                      [kk, ex], [outb[d]])
            for d in range(2):
                aa = Wt[d][1]
                for n in range(2):
                    bank = G.next()
                    P.pe(lambda e, bank=bank, n=n, d=d: e.matmul(bank[:], lhsT=la_bf[d][:], rhs=a2[d][:, n * 512:(n + 1) * 512], start=True, stop=False),
                         [la_bf[d], a2[d]], [bank])
                    P.pe(lambda e, bank=bank, n=n, d=d: e.matmul(bank[:], lhsT=sel8[:, 2 + d, :], rhs=rows[:, n * 512:(n + 1) * 512], start=False, stop=True),
                         [sel8, rows], [bank])
                    P.act(lambda e, bank=bank, n=n, aa=aa: e.activation(out=aa[:, n * 512:(n + 1) * 512], in_=bank[:], func=AF.Sigmoid), [bank], [aa])
            for d in range(2):
                aa, em, be = Wt[d][1], Wt[d][2], Wt[d][3]
                P.dve(lambda e, aa=aa, be=be: e.tensor_tensor(out=be[:], in0=kk[:], in1=aa[:], op=ALU.mult), [kk, aa], [be])
                P.pool(lambda e, d=d, be=be, em=em: e.tensor_tensor(out=outb[d][:, 2048:3072], in0=be[:], in1=em[:], op=ALU.mult), [be, em], [outb[d]])
            for d in range(2):
                kd, aa, em = Wt[d][0], Wt[d][1], Wt[d][2]
                P.dve(lambda e, kd=kd, aa=aa: e.scalar_tensor_tensor(out=kd[:], in0=aa[:], scalar=-1.0, in1=ka_bc[:], op0=ALU.add, op1=ALU.mult), [aa, ka_bc], [kd])
                P.dve(lambda e, kd=kd: e.scalar_tensor_tensor(out=kd[:], in0=kd[:], scalar=1.0, in1=k_sb[:], op0=ALU.add, op1=ALU.mult), [kd, k_sb], [kd])
                P.pool(lambda e, d=d, kd=kd, em=em: e.tensor_tensor(out=outb[d][:, 3072:4096], in0=kd[:], in1=em[:], op=ALU.mult), [kd, em], [outb[d]])
            for d in range(2):
                kd, bt = Wt[d][0], Wt[d][3]
                P.dve(lambda e, kd=kd, bt=bt: e.tensor_tensor(out=bt[:], in0=kd[:], in1=r_sb[:], op=ALU.mult), [kd, r_sb], [bt])
                P.pool(lambda e, bt=bt: e.tensor_tensor(out=bt[:], in0=bt[:], in1=rk_bc[:], op=ALU.mult), [bt, rk_bc], [bt])
                P.dve(lambda e, d=d, bt=bt: e.tensor_reduce(out=sml[d][:, 16:32], in_=bt[:].rearrange("p (h c) -> p h c", h=16), axis=AX.X, op=ALU.add), [bt], [sml[d]])
            for d in range(2):
                P.dma(self.rw_dir[d][cidx], outb[d][:], "po%d" % d, reads=[outb[d]], q="pool")
                P.dma(self.rw_small[d][cidx], sml[d][:], "ps%d_%d" % (d, small[d].slot), reads=[sml[d]], q="pool")

        for ck in self.chunks_fwd():
            chunk_body(ck)
        ph.finish()

    def rw_scan_phase(self, li, i, d):
        nc, w = self.nc, self.lw[i]
        ph = Phase(nc)
        P = ph.P
        msk = ph.sb("msk", [128, 4, 128], F32)
        cmk = ph.sb("cmk", [128, 7, 128], BF16)
        P.dma(msk[:], w["masks"][d], "c3", writes=[msk])
        P.dma(cmk[:], w["cmask"][d], "c10", writes=[cmk], q="pool")
        G = Rot([ph.ps("G%d" % j, [128, 512], F32) for j in range(6)])
        PTr = Rot([ph.ps("PT%d" % j, [128, 8, 128], BF16) for j in range(2)])
        inb = ph.rot("inb", [128, 4096], BF16, 2)
        v_rot = ph.rot("v_bf", [128, 1024], BF16, 2)
        smr = ph.rot("smr", [128, 32], F32, 2)
        if d == 1:
            smf = ph.rot("smf", [128, 32], F32, 2)
            t1 = ph.sb("t1", [128, 1024], F32)
            sm = ph.rot("sm", [128, 48], F32, 2)
        ARr = ph.rot("AR", [64, 16, 2, 128], BF16, 2)
        BTr = ph.rot("BT", [64, 16, 128], BF16, 2)
        KTr = ph.rot("KT", [64, 16, 128], BF16, 2)
        names = ("N", "NT", "NA", "NAT", "NB", "NBT", "O32", "O32T", "O64", "O64T", "O128", "T", "TT", "Aak", "Abr", "Akr")
        NU = 4
        U_ = [{n: ph.sb("%s_u%d" % (n, us), [128, 4, 128], BF16) for n in names} for us in range(NU)]
        Xb = [ph.sb("Xb%d" % us, [128, 4, 64], BF16) for us in range(NU)]
        Ub = [ph.sb("Ub%d" % us, [128, 4, 64], BF16) for us in range(NU)]
        S = [ph.sb("S%d" % u, [64, 4, 64], F32) for u in range(4)]
        Sb = [ph.sb("Sb%d" % u, [64, 4, 64], BF16) for u in range(4)]
        for u in range(4):
            P.dve(lambda e, u=u: e.memset(S[u][:], 0.0), [], [S[u]])
            P.pool(lambda e, u=u: e.memset(Sb[u][:], 0.0), [], [Sb[u]])
        ysb = ph.rot("ysb", [128, 1040], F32, 2)
        if d == 1:
            yf = ph.rot("yf", [128, 1040], F32, 2)
        ident = self.ident
        order = self.chunks_fwd() if d == 0 else self.chunks_bwd()

        def chunk_body(ck):
            cidx = self.row0(ck) // CH
            in_, v_bf, s_ = inb.next(), v_rot.next(), smr.next()
            P.dma(in_[:], self.rw_dir[d][cidx], "li%d" % inb.slot, writes=[in_])
            P.dma(v_bf[:], self.rw_scr_v[cidx], "lv%d" % v_rot.slot, writes=[v_bf])
            P.dma(s_[:], self.rw_small[d][cidx], "ls%d" % smr.slot, writes=[s_])
            WC = Tile(s_.t[0:64, 0:16], "WCview")
            WC.b = s_.b
            bh_bf = Tile(in_.t[:, 2048:3072], "bhv")
            bh_bf.b = in_.b
            kh_bf = Tile(in_.t[:, 3072:4096], "khv")
            kh_bf.b = in_.b
            AR, BT, KT = ARr.next(), BTr.next(), KTr.next()
            cnt = 0
            for g in range(2):
                for c0, dtile, dst in ((1024, AR, AR[:, g * 8:(g + 1) * 8, 0, :]), (0, AR, AR[:, g * 8:(g + 1) * 8, 1, :]),
                                       (2048, BT, BT[:, g * 8:(g + 1) * 8, :]), (3072, KT, KT[:, g * 8:(g + 1) * 8, :])):
                    PT = PTr.next()
                    for j in range(8):
                        h = g * 8 + j
                        P.pe(lambda e, PT=PT, c0=c0, h=h, j=j: e.transpose(out=PT[0:64, j, :], in_=in_[:, c0 + h * 64:c0 + (h + 1) * 64], identity=ident[:]),
                             [in_, ident], [PT])
                    if cnt % 2 == 0:
                        P.act(lambda e, PT=PT, dst=dst: e.activation(out=dst, in_=PT[0:64, :, :], func=AF.Copy), [PT], [dtile])
                    else:
                        P.dve(lambda e, PT=PT, dst=dst: e.tensor_copy(out=dst, in_=PT[0:64, :, :]), [PT], [dtile])
                    cnt += 1
            y_ = ysb.next()
            if d == 1:
                yf_ = yf.next()
                P.dma(yf_[:, 0:1024], self.scr_o[cidx][:, 0:1024], "lyf%d" % yf.slot, writes=[yf_])
                sf_ = smf.next()
                P.dma(sf_[:], self.rw_small[0][cidx], "lsf%d" % smf.slot, writes=[sf_])
            for g in range(1):
                units = [(us, us) for us in range(4)]
                for u, us in units:
                    M = U_[us]
                    h0 = u * 4
                    for pr in range(2):
                        bank = G.next()
                        for j in range(2):
                            h = h0 + pr * 2 + j
                            P.pe(lambda e, bank=bank, j=j, h=h: e.matmul(bank[:, j * 256:(j + 1) * 256], lhsT=BT[:, h, :],
                                                                         rhs=AR[:, h, :, :].rearrange("p a t -> p (a t)"), start=True, stop=True), [BT, AR], [bank])
                        bv = bank[:].rearrange("p (j a t) -> p j a t", j=2, a=2)
                        for dn, mi in (("NA", 0), ("O32", 1), ("O64", 2)):
                            P.dve(lambda e, bv=bv, M=M, pr=pr, dn=dn, mi=mi: e.tensor_tensor(out=M[dn][:, pr * 2:pr * 2 + 2, :], in0=bv[:, :, 0, :],
                                                                                            in1=cmk[:, mi, :].unsqueeze(1).to_broadcast([128, 2, 128]), op=ALU.mult),
                                  [bank, cmk], [M[dn]])
                        P.dve(lambda e, bv=bv, M=M, pr=pr: e.tensor_tensor(out=M["Abr"][:, pr * 2:pr * 2 + 2, :], in0=bv[:, :, 1, :],
                                                                          in1=msk[:, 1, :].unsqueeze(1).to_broadcast([128, 2, 128]), op=ALU.mult), [bank, msk], [M["Abr"]])
                        bank = G.next()
                        for j in range(2):
                            h = h0 + pr * 2 + j
                            P.pe(lambda e, bank=bank, j=j, h=h: e.matmul(bank[:, j * 256:(j + 1) * 256], lhsT=KT[:, h, :],
                                                                         rhs=AR[:, h, :, :].rearrange("p a t -> p (a t)"), start=True, stop=True), [KT, AR], [bank])
                        bv = bank[:].rearrange("p (j a t) -> p j a t", j=2, a=2)
                        P.dve(lambda e, bv=bv, M=M, pr=pr: e.tensor_tensor(out=M["Aak"][:, pr * 2:pr * 2 + 2, :], in0=bv[:, :, 0, :],
                                                                          in1=msk[:, 0, :].unsqueeze(1).to_broadcast([128, 2, 128]), op=ALU.mult), [bank, msk], [M["Aak"]])
                        P.dve(lambda e, bv=bv, M=M, pr=pr: e.tensor_tensor(out=M["Akr"][:, pr * 2:pr * 2 + 2, :], in0=bv[:, :, 1, :],
                                                                          in1=msk[:, 1, :].unsqueeze(1).to_broadcast([128, 2, 128]), op=ALU.mult), [bank, msk], [M["Akr"]])
                    bank = G.next()
                    for j in range(4):
                        h = h0 + j
                        P.pe(lambda e, bank=bank, j=j, h=h: e.matmul(bank[:, j * 128:(j + 1) * 128], lhsT=AR[:, h, 0, :], rhs=BT[:, h, :], start=True, stop=True),
                             [AR, BT], [bank])
                    for dn, mi in (("NAT", 3), ("O32T", 4), ("O64T", 5), ("O128", 6)):
                        P.dve(lambda e, bank=bank, M=M, dn=dn, mi=mi: e.tensor_tensor(out=M[dn][:], in0=bank[:].rearrange("p (j t) -> p j t", j=4),
                                                                                      in1=cmk[:, mi, :].unsqueeze(1).to_broadcast([128, 4, 128]), op=ALU.mult),
                              [bank, cmk], [M[dn]])
                    P.pool(lambda e, M=M: e.tensor_tensor(out=M["T"][:], in0=M["NA"][:], in1=ident[:].unsqueeze(1).to_broadcast([128, 4, 128]), op=ALU.add),
                           [M["NA"], ident], [M["T"]])
                    P.pool(lambda e, M=M: e.tensor_tensor(out=M["TT"][:], in0=M["NAT"][:], in1=ident[:].unsqueeze(1).to_broadcast([128, 4, 128]), op=ALU.add),
                           [M["NAT"], ident], [M["TT"]])

                def mm4(bank, M, lt, rt):
                    for j in range(4):
                        P.pe(lambda e, bank=bank, j=j, M=M, lt=lt, rt=rt: e.matmul(bank[:, j * 128:(j + 1) * 128], lhsT=M[lt][:, j, :], rhs=M[rt][:, j, :],
                                                                                  start=True, stop=True), [M[lt], M[rt]], [bank])

                def cp4(bank, M, dn):
                    P.act(lambda e, bank=bank, M=M, dn=dn: e.activation(out=M[dn][:], in_=bank[:].rearrange("p (j t) -> p j t", j=4), func=AF.Copy),
                          [bank], [M[dn]])

                def add4(bank, M, dn):
                    P.dve(lambda e, bank=bank, M=M, dn=dn: e.tensor_tensor(out=M[dn][:], in0=bank[:].rearrange("p (j t) -> p j t", j=4), in1=M[dn][:], op=ALU.add),
                          [bank, M[dn]], [M[dn]])
                cur = {us: ("NA", "NAT") for _, us in units}
                for lvl in range(3):
                    nxt = {}
                    for u, us in units:
                        M = U_[us]
                        nk, nkt = cur[us]
                        nb, nbt = ("NB", "NBT") if nk == "NA" else ("NA", "NAT")
                        b1 = G.next(); mm4(b1, M, nkt, nk); cp4(b1, M, nb)
                        b2 = G.next(); mm4(b2, M, nk, nkt); cp4(b2, M, nbt)
                        nxt[us] = (nb, nbt)
                    for u, us in units:
                        M = U_[us]
                        nb, nbt = nxt[us]
                        b3 = G.next(); mm4(b3, M, nbt, "T"); add4(b3, M, "T")
                    for u, us in units:
                        M = U_[us]
                        nb, nbt = nxt[us]
                        b4 = G.next(); mm4(b4, M, nb, "TT"); add4(b4, M, "TT")
                    cur = nxt
                for on, lastm in (("O32", False), ("O64", False), ("O128", True)):
                    if not lastm:
                        for u, us in units:
                            M = U_[us]
                            b1 = G.next(); mm4(b1, M, on + "T", "T"); cp4(b1, M, "N")
                        for u, us in units:
                            M = U_[us]
                            b2 = G.next(); mm4(b2, M, on, "TT"); cp4(b2, M, "NT")
                        bb = {}
                        for u, us in units:
                            M = U_[us]
                            b3 = G.next(); mm4(b3, M, "TT", "N"); bb[us] = b3
                            if us % 2 == 1:
                                pass
                        b4s = {}
                        for u, us in units[:2]:
                            M = U_[us]
                            b4 = G.next(); mm4(b4, M, "T", "NT"); b4s[us] = b4
                        for u, us in units[:2]:
                            add4(bb[us], U_[us], "T"); add4(b4s[us], U_[us], "TT")
                        for u, us in units[2:]:
                            M = U_[us]
                            b4 = G.next(); mm4(b4, M, "T", "NT"); b4s[us] = b4
                        for u, us in units[2:]:
                            add4(bb[us], U_[us], "T"); add4(b4s[us], U_[us], "TT")
                    else:
                        for u, us in units:
                            M = U_[us]
                            b1 = G.next(); mm4(b1, M, on, "T"); cp4(b1, M, "N")
                        for u, us in units:
                            M = U_[us]
                            b3 = G.next(); mm4(b3, M, "TT", "N"); add4(b3, M, "T")
                for u, us in units:
                    M = U_[us]
                    h0 = u * 4
                    bank = G.next()
                    for j in range(4):
                        h = h0 + j
                        P.pe(lambda e, bank=bank, j=j, h=h, u=u: e.matmul(bank[:, j * 64:(j + 1) * 64], lhsT=AR[:, h, 0, :], rhs=Sb[u][:, j, :], start=True, stop=False),
                             [AR, Sb[u]], [bank])
                        P.pe(lambda e, bank=bank, j=j, h=h, M=M: e.matmul(bank[:, j * 64:(j + 1) * 64], lhsT=M["Aak"][:, j, :], rhs=v_bf[:, h * 64:(h + 1) * 64],
                                                                          start=False, stop=True), [M["Aak"], v_bf], [bank])
                    P.act(lambda e, bank=bank, us=us: e.activation(out=Xb[us][:], in_=bank[:, 0:256].rearrange("p (j v) -> p j v", j=4), func=AF.Copy),
                          [bank], [Xb[us]])
                for u, us in units:
                    M = U_[us]
                    h0 = u * 4
                    bank = G.next()
                    for j in range(4):
                        P.pe(lambda e, bank=bank, j=j, M=M, us=us: e.matmul(bank[:, j * 64:(j + 1) * 64], lhsT=M["T"][:, j, :], rhs=Xb[us][:, j, :], start=True, stop=True),
                             [M["T"], Xb[us]], [bank])
                    P.dve(lambda e, bank=bank, us=us: e.tensor_copy(out=Ub[us][:], in_=bank[:, 0:256].rearrange("p (j v) -> p j v", j=4)), [bank], [Ub[us]])
                for u, us in units:
                    M = U_[us]
                    h0 = u * 4
                    bank = G.next()
                    for j in range(4):
                        h = h0 + j
                        P.pe(lambda e, bank=bank, j=j, h=h, u=u: e.matmul(bank[:, j * 64:(j + 1) * 64], lhsT=AR[:, h, 1, :], rhs=Sb[u][:, j, :], start=True, stop=False),
                             [AR, Sb[u]], [bank])
                        P.pe(lambda e, bank=bank, j=j, M=M, us=us: e.matmul(bank[:, j * 64:(j + 1) * 64], lhsT=M["Abr"][:, j, :], rhs=Ub[us][:, j, :], start=False, stop=False),
                             [M["Abr"], Ub[us]], [bank])
                        P.pe(lambda e, bank=bank, j=j, h=h, M=M: e.matmul(bank[:, j * 64:(j + 1) * 64], lhsT=M["Akr"][:, j, :], rhs=v_bf[:, h * 64:(h + 1) * 64],
                                                                          start=False, stop=True), [M["Akr"], v_bf], [bank])
                    if d == 0:
                        P.act(lambda e, bank=bank, h0=h0, y_=y_: e.activation(out=y_[:, h0 * 64:(h0 + 4) * 64], in_=bank[:, 0:256], func=AF.Copy), [bank], [y_])
                    else:
                        P.dve(lambda e, bank=bank, h0=h0, y_=y_, yf_=yf_: e.tensor_tensor(out=y_[:, h0 * 64:(h0 + 4) * 64], in0=bank[:, 0:256],
                                                                                         in1=yf_[:, h0 * 64:(h0 + 4) * 64], op=ALU.add), [bank, yf_], [y_])
                for u, us in units:
                    M = U_[us]
                    h0 = u * 4
                    bank = G.next()
                    for j in range(4):
                        h = h0 + j
                        P.pe(lambda e, bank=bank, j=j, h=h, us=us: e.matmul(bank[0:64, j * 64:(j + 1) * 64], lhsT=bh_bf[:, h * 64:(h + 1) * 64], rhs=Ub[us][:, j, :],
                                                                            start=True, stop=False), [bh_bf, Ub[us]], [bank])
                        P.pe(lambda e, bank=bank, j=j, h=h: e.matmul(bank[0:64, j * 64:(j + 1) * 64], lhsT=kh_bf[:, h * 64:(h + 1) * 64], rhs=v_bf[:, h * 64:(h + 1) * 64],
                                                                     start=False, stop=True), [kh_bf, v_bf], [bank])
                    P.dve(lambda e, bank=bank, u=u: e.tensor_tensor(out=S[u][:], in0=bank[0:64, 0:256].rearrange("p (j v) -> p j v", j=4), in1=S[u][:], op=ALU.add),
                          [bank, S[u]], [S[u]])
                    P.dve(lambda e, u=u, h0=h0: e.tensor_tensor(out=S[u][:], in0=S[u][:], in1=WC[:, h0:h0 + 4].unsqueeze(2).to_broadcast([64, 4, 64]), op=ALU.mult),
                          [S[u], WC], [S[u]])
                    P.pool(lambda e, u=u: e.tensor_copy(out=Sb[u][:], in_=S[u][:]), [S[u]], [Sb[u]])
            if d == 0:
                P.dma(self.scr_o[cidx][:, 0:1024], y_[:, 0:1024], "sy%d" % ysb.slot, reads=[y_], q="pool")
            else:
                need_out = True
                yv = y_[:, 0:1024].rearrange("p (h c) -> p h c", h=16)
                t1v = t1[:].rearrange("p (h c) -> p h c", h=16)
                st_ = sm.next()
                P.dve(lambda e, yv=yv: e.tensor_reduce(out=st_[:, 0:16], in_=yv, axis=AX.X, op=ALU.add), [y_], [st_])
                P.dve(lambda e: e.tensor_scalar(out=st_[:, 0:16], in0=st_[:, 0:16], scalar1=-1.0 / 64, scalar2=None, op0=ALU.mult), [st_], [st_])
                P.dve(lambda e, yv=yv: e.tensor_tensor(out=yv, in0=yv, in1=st_[:, 0:16].unsqueeze(2).to_broadcast([128, 16, 64]), op=ALU.add), [y_, st_], [y_])
                P.pool(lambda e, y_=y_: e.tensor_tensor(out=t1[:], in0=y_[:, 0:1024], in1=y_[:, 0:1024], op=ALU.mult), [y_], [t1])
                P.dve(lambda e: e.tensor_reduce(out=st_[:, 16:32], in_=t1v, axis=AX.X, op=ALU.add), [t1], [st_])
                P.act(lambda e: e.activation(out=st_[:, 16:32], in_=st_[:, 16:32], func=AF.Sqrt, scale=1.0 / 64, bias=64e-5), [st_], [st_])
                P.dve(lambda e: e.reciprocal(out=st_[:, 32:48], in_=st_[:, 16:32]), [st_], [st_])
                P.dve(lambda e, yv=yv: e.tensor_tensor(out=yv, in0=yv, in1=st_[:, 32:48].unsqueeze(2).to_broadcast([128, 16, 64]), op=ALU.mult), [y_, st_], [y_])
                P.dve(lambda e: e.tensor_tensor(out=st_[:, 0:16], in0=s_[:, 16:32], in1=sf_[:, 16:32], op=ALU.add), [s_, sf_], [st_])
                P.dve(lambda e: e.tensor_tensor(out=t1v, in0=v_bf[:].rearrange("p (h c) -> p h c", h=16),
                                                in1=st_[:, 0:16].unsqueeze(2).to_broadcast([128, 16, 64]), op=ALU.mult), [v_bf, st_, t1], [t1])
                P.dma(self.scr_o[cidx][:, 0:1024], y_[:, 0:1024], "sy%d" % ysb.slot, reads=[y_], q="pool")
                P.dma(self.scr_o[cidx][:, 1024:2048], t1[:], "sbv", reads=[t1], q="pool")
        for ck in order:
            chunk_body(ck)
        ph.finish()

    def rw_out_phase(self, li, i, last):
        nc, w = self.nc, self.lw[i]
        ph = Phase(nc)
        P = ph.P
        Wg = self.load_w(ph, ph.sb("Wg", [128, 8, 1024], BF16), w["rkvg"][3], 0, 1024, 8)
        Wo = self.load_w(ph, ph.sb("Wo", [128, 8, 1024], BF16), w["w_out"], 0, 1024, 8)
        mu = ph.sb("mu", [128, 6, 8], F32)
        P.dma(mu[:], w["mu"][:, :, :], "c2", writes=[mu])
        R = self.front_bufs(ph, n_xn=1, n_xt=4, n_hT=4)
        R.update(self.tail_bufs(ph, last=last))
        get_h = self.rw_hcache(ph, R, li)
        mm = Rot([ph.ps("mm%d" % j, [128, 512], F32) for j in range(2)])
        sh = ph.sb("sh", [128, 8, 128], BF16)
        xx = ph.sb("xx", [128, 8, 128], F32)
        tmpr = ph.rot("mtmp", [128, 8, 128], F32, 2)
        mixr = ph.rot("mix", [128, 8, 128], BF16, 2)
        op = ph.rot("opre", [128, 2048], F32, 2)
        lg_bc = ph.sb("lg_bc", [128, 1024], F32)
        lb_bc = ph.sb("lb_bc", [128, 1024], F32)
        P.dma(lg_bc[:], w["bc"][:, 3, :], "c8", writes=[lg_bc])
        P.dma(lb_bc[:], w["bc"][:, 4, :], "c9", writes=[lb_bc])
        sz = ph.rot("sz", [128, 1024], F32, 2)
        gated = ph.rot("gated", [128, 1024], BF16, 2)
        order = self.chunks_fwd()
        if last:
            order = [ck for ck in order if ck[0] == "x"]
        for ck in order:
            cidx = self.row0(ck) // CH
            pk, nk = self.rw_neighbors(ck)
            hc, xt = get_h(ck)
            hp = get_h(pk)[0] if pk else None
            hn = get_h(nk)[0] if nk else None
            self.rw_shift(P, sh, hc, hp, hn, ck[0])
            P.pool(lambda e, hc=hc: e.tensor_tensor(out=xx[:], in0=sh[:], in1=hc[:], op=ALU.subtract), [sh, hc], [xx])
            mix5 = self.rw_mix(P, mixr, tmpr, xx, hc, mu, 5)
            o_ = op.next()
            P.dma(o_[:], self.scr_o[cidx][:, 0:2048], "lo%d" % op.slot, writes=[o_])
            P.dve(lambda e, o_=o_: e.tensor_tensor(out=o_[:, 0:1024], in0=o_[:, 0:1024], in1=lg_bc[:], op=ALU.mult), [o_, lg_bc], [o_])
            P.pool(lambda e, o_=o_: e.tensor_tensor(out=o_[:, 1024:2048], in0=o_[:, 1024:2048], in1=lb_bc[:], op=ALU.add), [o_, lb_bc], [o_])
            P.dve(lambda e, o_=o_: e.tensor_tensor(out=o_[:, 0:1024], in0=o_[:, 0:1024], in1=o_[:, 1024:2048], op=ALU.add), [o_], [o_])
            sz_ = sz.next()
            for n in range(2):
                bank = mm.next()
                for k in range(8):
                    P.pe(lambda e, bank=bank, n=n, k=k, mix5=mix5: e.matmul(bank[:], lhsT=mix5[:, k, :], rhs=Wg[k][:, n * 512:(n + 1) * 512],
                                                                            start=(k == 0), stop=(k == 7)), [mix5, Wg[k]], [bank])
                P.act(lambda e, bank=bank, n=n, sz_=sz_: e.activation(out=sz_[:, n * 512:(n + 1) * 512], in_=bank[:], func=AF.Silu), [bank], [sz_])
            g_ = gated.next()
            P.dve(lambda e, o_=o_, sz_=sz_, g_=g_: e.tensor_tensor(out=g_[:], in0=o_[:, 0:1024], in1=sz_[:], op=ALU.mult), [o_, sz_], [g_])
            self.tail(ph, R, g_, Wo, li, ck, xt, last, KT=8)
        ph.finish()


def _col(v, k):
    return np.ascontiguousarray(np.asarray(v, np.float32).reshape(k, 128).T)


def _consts(L, CL):
    f32 = np.float32
    t = np.arange(L)
    row, col = (t // 64).astype(f32), (t % 64).astype(f32)
    freqs = (f32(10000.0) ** (-(np.arange(64, dtype=f32)) / f32(64))).astype(f32)
    ang = np.concatenate([row[:, None] * freqs, col[:, None] * freqs], axis=-1).astype(f32)
    rope = np.zeros((CL + L, 2, 128), f32)
    rope[:CL, 0, :] = 1.0
    rope[CL:, 0, :] = np.cos(ang)
    rope[CL:, 1, :] = np.sin(ang)
    jj = np.arange(128)[:, None].astype(f32)
    ii = np.arange(128)[None, :].astype(f32)
    dmat = np.stack([np.maximum(ii - jj, 0), np.maximum(jj - ii, 0), (ii >= jj).astype(f32), (jj >= ii).astype(f32),
                     np.broadcast_to(ii + 1, (128, 128)), np.broadcast_to(128 - ii, (128, 128))], axis=1).astype(f32)
    jcol = np.stack([127 - np.arange(128), np.arange(128)], axis=1).astype(f32)
    sel = np.zeros((2, 2, 128), f32)
    sel[0, 0, :] = 1.0
    sel[1, 1, :] = 1.0
    return {"ident": np.eye(128, dtype=f32), "rope": rope, "dmat": np.ascontiguousarray(dmat), "jcol": jcol, "sel2": sel}


def host_inputs(inp, b, layers, L, CL=256, x_rows=None):
    f32 = np.float32
    m = dict(_consts(L, CL))
    xr = inp["x"][b] if x_rows is None else x_rows
    m["x"] = np.ascontiguousarray(xr[:L], f32)
    m["ctx"] = np.ascontiguousarray(inp["ctx"][b][:CL], f32)
    m["cc"] = np.ascontiguousarray(np.stack([_col(inp["c"][b], 8), _col(inp["c_ctx"], 8)], axis=-1))
    m["final_g_bc"] = np.ascontiguousarray(np.broadcast_to(np.asarray(inp["final_g"], f32), (128, D)))
    for i in layers:
        j = i // 3
        m["ada_w%d" % i] = np.ascontiguousarray(inp["ada_w"][i], f32)
        m["ada_bcol%d" % i] = _col(inp["ada_b"][i], 24)
        m["ada_brow%d" % i] = np.ascontiguousarray(np.broadcast_to(np.asarray(inp["ada_b"][i][2 * D:], f32), (2, D)))
        m["ng_col%d" % i] = _col(inp["norm_g"][i], 8)
        k = KINDS[i]
        if k == 0:
            m["ret_w_in%d" % i] = np.ascontiguousarray(inp["ret_w_in"][j], f32)
            m["ret_w_out%d" % i] = np.ascontiguousarray(inp["ret_w_out"][j], f32)
            dec = np.concatenate([inp["ret_decay"][j][0], inp["ret_decay"][j][1]]).astype(f32)
            m["ret_decay%d" % i] = np.ascontiguousarray(np.broadcast_to(dec, (128, 8)))
        elif k == 1:
            m["gm_w_in%d" % i] = np.ascontiguousarray(inp["gm_w_in"][j], f32)
            m["gm_w_out%d" % i] = np.ascontiguousarray(inp["gm_w_out"][j], f32)
            m["gm_vg%d" % i] = np.ascontiguousarray(np.broadcast_to(np.asarray(inp["gm_vnorm_g"][j], f32), (128, 2048)))
            m["gm_wsT%d" % i] = np.ascontiguousarray(np.transpose(np.asarray(inp["gm_w_s"][j], f32), (2, 0, 1)))
            m["gm_bsT%d" % i] = np.ascontiguousarray(np.asarray(inp["gm_b_s"][j], f32).T)
        else:
            _rw_host(m, inp, i, j)
    return m


def _rw_host(m, inp, i, j):
    f32 = np.float32
    g = lambda k: np.asarray(inp[k][j], f32)
    mu = g("rw_mu")
    m["rw_mu%d" % i] = np.ascontiguousarray(np.stack([_col(mu[p], 8) for p in range(6)], axis=1))
    m["rw_rkvg%d" % i] = np.ascontiguousarray(g("rw_w_rkvg"))
    m["rw_w1%d" % i] = np.ascontiguousarray(g("rw_w1"))
    m["rw_a1%d" % i] = np.ascontiguousarray(g("rw_a1"))
    m["rw_w2%d" % i] = np.ascontiguousarray(g("rw_w2"))
    m["rw_a2%d" % i] = np.ascontiguousarray(g("rw_a2"))
    rows = np.zeros((8, D), f32)
    rows[0:2] = g("rw_w0")
    rows[2:4] = g("rw_a0")
    m["rw_rows%d" % i] = rows
    bc = np.stack([g("rw_k_k"), g("rw_k_a"), g("rw_r_k").reshape(-1), g("rw_lnx_g"), g("rw_lnx_b")], axis=0)
    m["rw_bc%d" % i] = np.ascontiguousarray(np.broadcast_to(bc[None], (128, 5, D)))
    m["rw_wout%d" % i] = np.ascontiguousarray(g("rw_w_out"))
    s_ = np.arange(128)[:, None]
    t_ = np.arange(128)[None, :]
    c0 = f32(-0.6065306597126334)
    fw = np.stack([(s_ < t_), (s_ <= t_), (t_ < s_), (s_ <= t_) * c0], axis=1).astype(f32)
    bw = np.stack([(s_ > t_), (s_ >= t_), (t_ > s_), (s_ >= t_) * c0], axis=1).astype(f32)
    m["rw_masks%d" % i] = np.ascontiguousarray(np.stack([fw, bw], axis=0))
    sel = np.zeros((8, 8, 128), f32)
    for r in range(8):
        sel[r, r, :] = 1.0
    m["rw_sel8%d" % i] = sel
    m["rw_negc%d" % i] = np.full((128, 1), c0, f32)
    blk = lambda n: (s_ // n) == (t_ // n)
    bm = np.stack([blk(16)] + [blk(n) & ~blk(n // 2) for n in (32, 64, 128)], axis=1).astype(f32)
    m["rw_bmask%d" % i] = np.ascontiguousarray(bm)
    cms = []
    for dd in (fw, bw):
        st, stT = dd[:, 0, :], dd[:, 2, :]
        cms.append(np.stack([st * bm[:, 0], st * bm[:, 1], st * bm[:, 2], stT * bm[:, 0], stT * bm[:, 1], stT * bm[:, 2], stT * bm[:, 3]], axis=1))
    m["rw_cmask%d" % i] = np.ascontiguousarray(np.stack(cms, axis=0).astype(f32))


_MODEL_CACHE = {}


def kernel(**inputs):
    inp = {k: np.asarray(v) for k, v in inputs.items()}
    B, L, _ = inp["x"].shape
    layers = (0, 1, 2, 3)
    key = (L, layers)
    if key not in _MODEL_CACHE:
        _MODEL_CACHE[key] = Model(L, 256, layers)
    model = _MODEL_CACHE[key]
    maps = []
    for core in range(NCORES):
        b = core % B
        hm = host_inputs(inp, b, layers, L)
        maps.append({k: hm[k] for k in model.inputs})
    res = run_bass_kernel_spmd(model.nc, maps, core_ids=list(range(NCORES)))
    out = np.stack([np.asarray(res.results[b]["out"], np.float32) for b in range(B)], axis=0)
    return out
```

```python
from contextlib import ExitStack
import numpy as np
import concourse.bass as bass
import concourse.mybir as mybir
from concourse.bass_utils import run_bass_kernel_spmd

F32 = mybir.dt.float32
BF16 = mybir.dt.bfloat16
AF = mybir.ActivationFunctionType
ALU = mybir.AluOpType
AX = mybir.AxisListType

D = 1024
CH = 128
NCORES = 8
_UID = [0]


class Buf:
    __slots__ = ("name", "last_w", "readers", "excl")

    def __init__(self, name, excl=False):
        self.name = name
        self.last_w = None
        self.readers = []
        self.excl = excl


class Tile:
    def __init__(self, t, name, excl=False):
        self.t = t
        self.b = Buf(name, excl)

    def __getitem__(self, k):
        return self.t[k]


class Rot:
    def __init__(self, tiles):
        self.tiles = tiles
        self.i = 0

    def next(self):
        t = self.tiles[self.i % len(self.tiles)]
        self.slot = self.i % len(self.tiles)
        self.i += 1
        return t


class Op:
    __slots__ = ("eng", "fn", "dma_key", "waits", "signal", "sem", "val", "idx", "prog")


def _b(x):
    return x.b if isinstance(x, Tile) else x


class Prog:
    ENGS = ("pe", "act", "dve", "pool", "sp")

    def __init__(self):
        self.ops = []

    def add(self, eng, fn, reads=(), writes=(), dma_key=None):
        op = Op()
        op.eng, op.fn, op.dma_key = eng, fn, dma_key
        op.signal, op.sem, op.val = False, None, 0
        op.idx, op.prog = len(self.ops), self
        reads = [_b(x) for x in reads]
        writes = [_b(x) for x in writes]
        deps = []
        for b in reads:
            if b.last_w is not None:
                deps.append((b.last_w, True))
            if b.excl:
                deps.extend((r, False) for r in b.readers)
        for b in writes:
            if b.last_w is not None:
                deps.append((b.last_w, False))
            deps.extend((r, False) for r in b.readers)
        need, seen = [], set()
        for d, raw in deps:
            if d.prog is not self:
                continue
            if d.dma_key is None and dma_key is None and d.eng == eng:
                if eng == "pe" or not raw:
                    continue
            if d.idx in seen:
                continue
            seen.add(d.idx)
            d.signal = True
            need.append(d)
        op.waits = need
        for b in reads:
            b.readers.append(op)
        for b in writes:
            b.last_w = op
            b.readers = []
        self.ops.append(op)
        return op

    def pe(self, fn, reads=(), writes=()):
        return self.add("pe", fn, reads, writes)

    def act(self, fn, reads=(), writes=()):
        return self.add("act", fn, reads, writes)

    def dve(self, fn, reads=(), writes=()):
        return self.add("dve", fn, reads, writes)

    def pool(self, fn, reads=(), writes=()):
        return self.add("pool", fn, reads, writes)

    def dma(self, out, in_, key, reads=(), writes=(), q="sp", slow=False):
        if slow:
            return self.add(q, lambda e: e.dma_start(out=out, in_=in_, allow_slow_non_contiguous=True),
                            reads, writes, dma_key=key)
        return self.add(q, lambda e: e.dma_start(out=out, in_=in_), reads, writes, dma_key=key)

    def emit(self, nc, stack):
        def keyof(op):
            return ("dma", op.dma_key) if op.dma_key is not None else ("eng", op.eng)
        last = {}
        for op in self.ops:
            last[keyof(op)] = op
        for op in last.values():
            op.signal = True
        cnt, sems = {}, {}
        for op in self.ops:
            if not op.signal:
                continue
            k = keyof(op)
            cnt[k] = cnt.get(k, 0) + (16 if op.dma_key is not None else 1)
            op.val = cnt[k]
            if k not in sems:
                _UID[0] += 1
                sems[k] = nc.alloc_semaphore(name="s%d" % _UID[0])
            op.sem = sems[k]
        self.n_sems = len(sems)
        per_eng = {e: [] for e in self.ENGS}
        for op in self.ops:
            per_eng[op.eng].append(op)
        finals = [(sems[k], cnt[k]) for k in sems]

        def run(engname, e):
            waited = {}
            for op in per_eng[engname]:
                for d in op.waits:
                    key = id(d.sem)
                    if waited.get(key, 0) >= d.val:
                        continue
                    e.wait_ge(d.sem, d.val)
                    waited[key] = d.val
                ins = op.fn(e)
                if op.signal:
                    ins.then_inc(op.sem, 16 if op.dma_key is not None else 1)
            for s, v in finals:
                if waited.get(id(s), 0) < v:
                    e.wait_ge(s, v)

        with nc.Block() as block:
            block.tensor(lambda e: run("pe", e))
            block.scalar(lambda e: run("act", e))
            block.vector(lambda e: run("dve", e))
            block.gpsimd(lambda e: run("pool", e))
            block.sync(lambda e: run("sp", e))
        nc.clear_and_free_semaphores(list(sems.values()))
        nc.all_engine_barrier()


class Phase:
    def __init__(self, nc):
        self.nc = nc
        self.st = ExitStack()
        self.P = Prog()

    def sb(self, name, shape, dt):
        _UID[0] += 1
        t = self.st.enter_context(self.nc.sbuf_tensor("%s_%d" % (name, _UID[0]), list(shape), dt))
        return Tile(t, name)

    def ps(self, name, shape, dt=F32):
        _UID[0] += 1
        t = self.st.enter_context(self.nc.psum_tensor("%s_%d" % (name, _UID[0]), list(shape), dt))
        return Tile(t, name, excl=True)

    def rot(self, name, shape, dt, n):
        return Rot([self.sb("%s%d" % (name, i), shape, dt) for i in range(n)])

    def finish(self):
        self.P.emit(self.nc, self.st)
        self.st.close()


KINDS = (0, 1, 2, 0)
NORM_EPS = 1e-6


class Model:
    def __init__(self, L, CL=256, layers=(0, 1, 2, 3), final_norm=True):
        self.L, self.CL = L, CL
        self.NX, self.NC = L // CH, CL // CH
        self.layers = tuple(layers)
        self.final_norm = final_norm
        self.nc = nc = bass.Bass("TRN2", target_bir_lowering=False)
        self.inputs = {}
        self.outer = ExitStack()
        NT = L + CL

        def din(name, shape):
            self.inputs[name] = tuple(shape)
            return nc.dram_tensor(name, list(shape), F32, kind="ExternalInput").ap()

        self.x_in = din("x", [L, D])
        self.c_in = din("ctx", [CL, D])
        self.cc = din("cc", [128, 8, 2])
        self.ident_in = din("ident", [128, 128])
        self.rope_in = din("rope", [NT, 2, 128])
        self.dmat_in = din("dmat", [128, 6, 128])
        self.jcol_in = din("jcol", [128, 2])
        self.sel_in = din("sel2", [2, 2, 128])
        self.fg_in = din("final_g_bc", [128, D])
        self.lw = {}
        for i in self.layers:
            w = {}
            w["ada_w"] = din("ada_w%d" % i, [D, 3 * D])
            w["ada_bcol"] = din("ada_bcol%d" % i, [128, 24])
            w["ada_brow"] = din("ada_brow%d" % i, [2, D])
            w["ng_col"] = din("ng_col%d" % i, [128, 8])
            k = KINDS[i]
            if k == 0:
                w["w_in"] = din("ret_w_in%d" % i, [D, 6144])
                w["w_out"] = din("ret_w_out%d" % i, [2048, D])
                w["decay"] = din("ret_decay%d" % i, [128, 8])
            elif k == 1:
                w["w_in"] = din("gm_w_in%d" % i, [D, 6144])
                w["w_out"] = din("gm_w_out%d" % i, [2048, D])
                w["vg"] = din("gm_vg%d" % i, [128, 2048])
                w["wsT"] = din("gm_wsT%d" % i, [128, 8, 128])
                w["bsT"] = din("gm_bsT%d" % i, [128, 8])
            else:
                self._rw_inputs(w, i, din)
            self.lw[i] = w
        self.out = nc.dram_tensor("out", [L, D], F32, kind="ExternalOutput").ap()
        self.xs = [nc.dram_tensor("xs%d" % i, [NT, D], F32).ap() for i in range(2)]
        o = self.outer
        self.ident = Tile(o.enter_context(nc.sbuf_tensor("identb", [128, 128], BF16)), "ident")
        self.ident32 = Tile(o.enter_context(nc.sbuf_tensor("ident32", [128, 128], F32)), "ident32")
        self.mod = Tile(o.enter_context(nc.sbuf_tensor("mod", [128, 2, 2, 8], F32)), "mod")
        self.gate_dram = nc.dram_tensor("gate_dram", [2, 128, D], F32).ap()
        self.sel = Tile(o.enter_context(nc.sbuf_tensor("sel", [2, 2, 128], F32)), "sel")
        self._build()
        self.outer.close()

    def chunks_fwd(self):
        return [("c", i) for i in range(self.NC)] + [("x", i) for i in range(self.NX)]

    def chunks_bwd(self):
        return [("c", i) for i in reversed(range(self.NC))] + [("x", i) for i in reversed(range(self.NX))]

    def row0(self, ck):
        return ck[1] * CH if ck[0] == "c" else self.CL + ck[1] * CH

    def src_ap(self, li, ck):
        r0 = self.row0(ck)
        if li == 0:
            return (self.c_in if ck[0] == "c" else self.x_in)[ck[1] * CH:(ck[1] + 1) * CH, :]
        return self.xs[(li - 1) % 2][r0:r0 + CH, :]

    def dst_ap(self, li, ck):
        r0 = self.row0(ck)
        return self.xs[li % 2][r0:r0 + CH, :]

    def _build(self):
        nc = self.nc
        ph = Phase(nc)
        P = ph.P
        t32 = ph.sb("id32", [128, 128], F32)
        P.dma(self.ident32[:], self.ident_in[:, :], "c0", writes=[self.ident32])
        P.dve(lambda e: e.tensor_copy(out=self.ident[:], in_=self.ident32[:]), [self.ident32], [self.ident])
        P.dma(self.sel[:], self.sel_in[:, :, :], "c1", writes=[self.sel])
        ph.finish()
        for li, i in enumerate(self.layers):
            last = (li == len(self.layers) - 1)
            k = KINDS[i]
            if k == 0:
                self.retention_layer(li, i, last)
            elif k == 1:
                self.gmlp_layer(li, i, last)
            else:
                self.rwkv_layer(li, i, last)

    def emit_mod(self, ph, i):
        P, w = ph.P, self.lw[i]
        cc = ph.sb("cc", [128, 8, 2], F32)
        sg = ph.sb("sg", [128, 8, 2], F32)
        bcol = ph.sb("bcol", [128, 24], F32)
        brow = ph.sb("brow", [2, D], F32)
        ng = ph.sb("ng", [128, 8], F32)
        grow = ph.sb("grow", [2, D], F32)
        gate_t = [ph.sb("gate_t%d" % j, [128, D], F32) for j in range(2)]
        aw = ph.sb("adaw", [128, 8, 3 * D], F32)
        awk = [Tile(aw.t[:, k, :], "adaw%d" % k) for k in range(8)]
        pcol = ph.ps("pcol", [128, 16, 2], F32)
        prow = [ph.ps("prow%d" % n, [128, 512], F32) for n in range(2)]
        P.dma(cc[:], self.cc[:, :, :], "m0", writes=[cc])
        P.dma(bcol[:], w["ada_bcol"][:, :], "m1", writes=[bcol])
        P.dma(brow[:], w["ada_brow"][:, :], "m2", writes=[brow])
        P.dma(ng[:], w["ng_col"][:, :], "m3", writes=[ng])
        for k in range(8):
            P.dma(awk[k][:], w["ada_w"][k * 128:(k + 1) * 128, :], "ada%d" % k, writes=[awk[k]], q=("sp" if k % 2 == 0 else "pool"))
        P.act(lambda e: e.activation(out=sg[:], in_=cc[:], func=AF.Sigmoid), [cc], [sg])
        P.dve(lambda e: e.tensor_tensor(out=sg[:], in0=sg[:], in1=cc[:], op=ALU.mult), [sg, cc], [sg])
        for j in range(16):
            for k in range(8):
                P.pe(lambda e, j=j, k=k: e.matmul(pcol[:, j, :], lhsT=awk[k][:, j * 128:(j + 1) * 128], rhs=sg[:, k, :],
                                                  start=(k == 0), stop=(k == 7)), [awk[k], sg], [pcol])
        for n in range(2):
            for k in range(8):
                P.pe(lambda e, n=n, k=k: e.matmul(prow[n][0:2, :], lhsT=sg[:, k, :], rhs=awk[k][:, 2048 + n * 512:2048 + (n + 1) * 512],
                                                  start=(k == 0), stop=(k == 7)), [awk[k], sg], [prow[n]])
        tmp = ph.sb("modtmp", [128, 16, 2], F32)
        P.dve(lambda e: e.tensor_tensor(out=tmp[:], in0=pcol[:], in1=bcol[:, 0:16].unsqueeze(2).to_broadcast([128, 16, 2]), op=ALU.add),
              [pcol, bcol], [tmp])
        mod = self.mod
        for j in range(2):
            P.dve(lambda e, j=j: e.scalar_tensor_tensor(out=mod[:, 0, j, :], in0=tmp[:, 8:16, j], scalar=1.0, in1=ng[:], op0=ALU.add, op1=ALU.mult),
                  [tmp, ng], [mod])
            P.dve(lambda e, j=j: e.tensor_copy(out=mod[:, 1, j, :], in_=tmp[:, 0:8, j]), [tmp], [mod])
        for n in range(2):
            P.dve(lambda e, n=n: e.tensor_tensor(out=grow[:, n * 512:(n + 1) * 512], in0=prow[n][0:2, :], in1=brow[:, n * 512:(n + 1) * 512], op=ALU.add),
                  [prow[n], brow], [grow])
        for j in range(2):
            for n in range(2):
                P.pe(lambda e, j=j, n=n: e.matmul(prow[n][:, :], lhsT=self.sel[:, j, :], rhs=grow[:, n * 512:(n + 1) * 512], start=True, stop=True),
                     [self.sel, grow], [prow[n]])
                P.act(lambda e, j=j, n=n: e.activation(out=gate_t[j][:, n * 512:(n + 1) * 512], in_=prow[n][:, :], func=AF.Copy),
                      [prow[n]], [gate_t[j]])
        for j in range(2):
            P.dma(self.gate_dram[j], gate_t[j][:], "gst%d" % j, reads=[gate_t[j]], q="pool")

    def load_w(self, ph, Wt, src, col0, ncols, KT):
        P = ph.P
        views = []
        for k in range(KT):
            v = Tile(Wt.t[:, k, :], "%s_k%d" % (Wt.b.name, k))
            views.append(v)
            step = 2048
            for c0 in range(0, ncols, step):
                wdt = min(step, ncols - c0)
                P.dma(v.t[:, c0:c0 + wdt], src[k * 128:(k + 1) * 128, col0 + c0:col0 + c0 + wdt],
                      "w%s%d_%d" % (Wt.b.name, k, c0), writes=[v], q="pool")
        return views

    def front(self, ph, R, src, which, src_reads=()):
        P = ph.P
        mod = self.mod
        xt = R["xt"].next()
        P.dma(xt[:], src, "xt%d" % R["xt"].slot, reads=list(src_reads), writes=[xt])
        st = R["st"].next()
        junk = R["junk"]
        P.act(lambda e: e.activation(out=junk[:], in_=xt[:], func=AF.Square, accum_out=st[:, 0:1]), [xt], [junk, st])
        P.act(lambda e: e.activation(out=st[:, 1:2], in_=st[:, 0:1], func=AF.Sqrt, scale=1.0 / D, bias=NORM_EPS), [st], [st])
        P.dve(lambda e: e.reciprocal(out=st[:, 2:3], in_=st[:, 1:2]), [st], [st])
        xn = R["xn"].next()
        P.dve(lambda e: e.tensor_scalar(out=xn[:], in0=xt[:], scalar1=st[:, 2:3], scalar2=None, op0=ALU.mult), [xt, st], [xn])
        ptr = R["ptrx"]
        for k in range(8):
            P.pe(lambda e, k=k: e.transpose(out=ptr[:, k, :], in_=xn[:, k * 128:(k + 1) * 128], identity=self.ident[:]),
                 [xn, self.ident], [ptr])
        hT = R["hT"].next()
        for k in range(8):
            P.act(lambda e, k=k: e.activation(out=hT[:, k, :], in_=ptr[:, k, :], func=AF.Identity,
                                              scale=mod[:, 0, which, k:k + 1], bias=mod[:, 1, which, k:k + 1]),
                  [ptr, mod], [hT])
        return hT, xt

    def front_bufs(self, ph, n_xn=2, n_xt=2, n_hT=2, junk=True):
        return {
            "xt": ph.rot("xt", [128, D], F32, n_xt),
            "st": ph.rot("st", [128, 4], F32, 4),
            "junk": ph.sb("junk", [128, D], BF16) if junk else None,
            "xn": ph.rot("xn", [128, D], BF16, n_xn),
            "hT": ph.rot("hT", [128, 8, 128], BF16, n_hT),
            "ptrx": ph.ps("ptrx", [128, 8, 128], BF16),
        }

    def tail(self, ph, R, gated, Wo, li, ck, xres, last, KT=16):
        P = ph.P
        which = 1 if ck[0] == "c" else 0
        gT = R["gT"].next()
        for half in range(KT // 8):
            ptg = R["ptg"][half]
            for k in range(8):
                kk = half * 8 + k
                P.pe(lambda e, k=k, kk=kk, ptg=ptg: e.transpose(out=ptg[:, k, :], in_=gated[:, kk * 128:(kk + 1) * 128], identity=self.ident[:]),
                     [gated, self.ident], [ptg])
            if half == 0:
                P.act(lambda e, half=half, ptg=ptg: e.activation(out=gT[:, half * 8:(half + 1) * 8, :], in_=ptg[:], func=AF.Copy), [ptg], [gT])
            else:
                P.dve(lambda e, half=half, ptg=ptg: e.tensor_copy(out=gT[:, half * 8:(half + 1) * 8, :], in_=ptg[:]), [ptg], [gT])
        xo = R["xo"].next()
        for n in range(2):
            po = R["pout"].next() if isinstance(R["pout"], Rot) else R["pout"]
            for k in range(KT):
                P.pe(lambda e, n=n, k=k, po=po: e.matmul(po[:], lhsT=gT[:, k, :], rhs=Wo[k][:, n * 512:(n + 1) * 512], start=(k == 0), stop=(k == KT - 1)),
                     [gT, Wo[k]], [po])
            P.dve(lambda e, n=n, po=po: e.tensor_tensor(out=xo[:, n * 512:(n + 1) * 512], in0=po[:], in1=self.gate[which][:, n * 512:(n + 1) * 512], op=ALU.mult),
                  [po, self.gate[which]], [xo])
        P.pool(lambda e: e.tensor_tensor(out=xo[:], in0=xo[:], in1=xres[:], op=ALU.add), [xo, xres], [xo])
        if last and self.final_norm:
            st = R["st"].next()
            junk = R["junk"]
            P.act(lambda e: e.activation(out=junk[:], in_=xo[:], func=AF.Square, accum_out=st[:, 0:1]), [xo], [junk, st])
            P.act(lambda e: e.activation(out=st[:, 1:2], in_=st[:, 0:1], func=AF.Sqrt, scale=1.0 / D, bias=NORM_EPS), [st], [st])
            P.dve(lambda e: e.reciprocal(out=st[:, 2:3], in_=st[:, 1:2]), [st], [st])
            P.dve(lambda e: e.scalar_tensor_tensor(out=xo[:], in0=xo[:], scalar=st[:, 2:3], in1=self.fg[:], op0=ALU.mult, op1=ALU.mult),
                  [xo, st, self.fg], [xo])
            dst = self.out[ck[1] * CH:(ck[1] + 1) * CH, :]
        elif last:
            dst = self.out[ck[1] * CH:(ck[1] + 1) * CH, :]
        else:
            dst = self.dst_ap(li, ck)
        P.dma(dst, xo[:], "xo%d" % R["xo"].slot, reads=[xo], q="pool")

    def tail_bufs(self, ph, n_gT=2, n_xo=2, last=False, pout=True):
        if last and self.final_norm:
            self.fg = ph.sb("fg", [128, D], F32)
            ph.P.dma(self.fg[:], self.fg_in[:, :], "fgld", writes=[self.fg])
        self.gate = [ph.sb("gate%d" % j, [128, D], F32) for j in range(2)]
        for j in range(2):
            ph.P.dma(self.gate[j][:], self.gate_dram[j], "gld%d" % j, writes=[self.gate[j]])
        return {
            "gT": ph.rot("gT", [128, 16, 128], BF16, n_gT),
            "ptg": [ph.ps("ptg%d" % h, [128, 8, 128], BF16) for h in range(2)],
            "pout": ph.ps("pout", [128, 512], F32) if pout else None,
            "xo": ph.rot("xo", [128, D], F32, n_xo),
        }

    def retention_layer(self, li, i, last):
        nc, w = self.nc, self.lw[i]
        NCH = self.NX + self.NC
        if not hasattr(self, "scr_q"):
            self.scr_q = nc.dram_tensor("scr_q", [NCH, 128, 1024], BF16).ap()
            self.scr_k = nc.dram_tensor("scr_k", [NCH, 128, 1024], BF16).ap()
            self.scr_v = nc.dram_tensor("scr_v", [NCH, 128, 2048], BF16).ap()
            self.scr_o = nc.dram_tensor("scr_o", [NCH, 128, 2048], F32).ap()
            self.rt_cd = Tile(self.outer.enter_context(nc.sbuf_tensor("rt_cd", [128, 8], F32)), "rt_cd")
        ph = Phase(nc)
        self.emit_mod(ph, i)
        ph.finish()

        ph = Phase(nc)
        P = ph.P
        Wt = ph.sb("Wqkv", [128, 8, 4096], BF16)
        Wk = self.load_w(ph, Wt, w["w_in"], 0, 4096, 8)
        R = self.front_bufs(ph)
        dec = ph.sb("dec", [128, 8], F32)
        lg = ph.sb("lg", [128, 8], F32)
        dm = ph.sb("dm", [128, 6, 128], F32)
        jc = ph.sb("jc", [128, 2], F32)
        tA = ph.sb("tA", [128, 128], F32)
        tB = ph.sb("tB", [128, 128], F32)
        MT = ph.sb("MT", [128, 4, 128], F32)
        QDf = ph.sb("QDf", [128, 8, 128], BF16)
        QDb = ph.sb("QDb", [128, 8, 128], BF16)
        kdec = ph.sb("kdec", [128, 8], F32)
        cd = self.rt_cd
        P.dma(dec[:], w["decay"][:, :], "t0", writes=[dec])
        P.dma(dm[:], self.dmat_in[:, :, :], "t1", writes=[dm])
        P.dma(jc[:], self.jcol_in[:, :], "t2", writes=[jc])
        P.act(lambda e: e.activation(out=lg[:], in_=dec[:], func=AF.Exp, scale=-1.0), [dec], [lg])
        P.act(lambda e: e.activation(out=lg[:], in_=lg[:], func=AF.Ln, bias=1.0), [lg], [lg])
        P.dve(lambda e: e.tensor_scalar(out=lg[:], in0=lg[:], scalar1=-1.0, scalar2=None, op0=ALU.mult), [lg], [lg])
        P.act(lambda e: e.activation(out=cd[:], in_=lg[:], func=AF.Exp, scale=128.0), [lg], [cd])
        P.dve(lambda e: e.tensor_scalar(out=kdec[:, 0:4], in0=lg[:, 0:4], scalar1=jc[:, 0:1], scalar2=None, op0=ALU.mult), [lg, jc], [kdec])
        P.dve(lambda e: e.tensor_scalar(out=kdec[:, 4:8], in0=lg[:, 4:8], scalar1=jc[:, 1:2], scalar2=None, op0=ALU.mult), [lg, jc], [kdec])
        P.act(lambda e: e.activation(out=kdec[:], in_=kdec[:], func=AF.Exp), [kdec], [kdec])
        P.dve(lambda e: e.tensor_scalar(out=kdec[:], in0=kdec[:], scalar1=0.0625, scalar2=None, op0=ALU.mult), [kdec], [kdec])
        for h in range(4):
            P.act(lambda e, h=h: e.activation(out=tA[:], in_=dm[:, 0, :], func=AF.Exp, scale=lg[:, h:h + 1]), [dm, lg, MT], [tA])
            P.act(lambda e, h=h: e.activation(out=tB[:], in_=dm[:, 1, :], func=AF.Exp, scale=lg[:, 4 + h:5 + h]), [dm, lg, MT], [tB])
            P.dve(lambda e: e.tensor_tensor(out=tA[:], in0=tA[:], in1=dm[:, 2, :], op=ALU.mult), [tA, dm], [tA])
            P.dve(lambda e: e.tensor_tensor(out=tB[:], in0=tB[:], in1=dm[:, 3, :], op=ALU.mult), [tB, dm], [tB])
            P.dve(lambda e: e.tensor_tensor(out=tA[:], in0=tA[:], in1=tB[:], op=ALU.add), [tA, tB], [tA])
            P.dve(lambda e, h=h: e.tensor_scalar(out=MT[:, h, :], in0=tA[:], scalar1=0.0625, scalar2=None, op0=ALU.mult), [tA], [MT])
            for r in range(2):
                P.act(lambda e, h=h, r=r: e.activation(out=QDf[:, 2 * h + r, :], in_=dm[:, 4, :], func=AF.Exp, scale=lg[:, h:h + 1]), [dm, lg], [QDf])
                P.act(lambda e, h=h, r=r: e.activation(out=QDb[:, 2 * h + r, :], in_=dm[:, 5, :], func=AF.Exp, scale=lg[:, 4 + h:5 + h]), [dm, lg], [QDb])
        mm = Rot([ph.ps("mm%d" % j, [128, 512], F32) for j in range(2)])
        ptq = ph.ps("ptq", [128, 8, 128], BF16)
        ptk = ph.ps("ptk", [128, 8, 128], BF16)
        Gp = Rot([ph.ps("Gp%d" % j, [128, 512], F32) for j in range(3)])
        cs = ph.rot("cs", [128, 2, 2, 128], F32, 2)
        rtmp = ph.rot("rtmp", [128, 4, 2, 128], F32, 2)
        q_r = ph.sb("q_r", [128, 1024], BF16)
        k_r = ph.sb("k_r", [128, 1024], BF16)
        qT = ph.rot("qT", [128, 8, 128], BF16, 2)
        kT = ph.rot("kT", [128, 8, 128], BF16, 2)
        qTf = ph.rot("qTf", [128, 8, 128], BF16, 2)
        qTb = ph.rot("qTb", [128, 8, 128], BF16, 2)
        kf = ph.rot("kf", [128, 1024], BF16, 2)
        kb = ph.rot("kb", [128, 1024], BF16, 2)
        vsb = ph.rot("vsb", [128, 2048], BF16, 2)
        sT = ph.rot("sT", [128, 4, 128], BF16, 2)
        osb = ph.rot("osb", [128, 2048], F32, 2)
        Sbf = ph.sb("Sbf", [128, 8, 512], BF16)
        Sbk = [Tile(Sbf.t[:, k, :], "Sb%d" % k) for k in range(8)]
        for k in range(8):
            P.pool(lambda e, k=k: e.memset(Sbk[k][:], 0.0), [], [Sbk[k]])
        cdI = self.ret_cdI(ph)

        for ck in self.chunks_fwd():
            which = 1 if ck[0] == "c" else 0
            r0 = self.row0(ck)
            cidx = r0 // CH
            hT, xt = self.front(ph, R, self.src_ap(li, ck), which)
            c_ = cs.next()
            for r in range(2):
                P.dma(c_[:, :, r, :], self.rope_in[r0:r0 + CH, :, :], "cs%d_%d" % (cs.slot, r), writes=[c_])
            v_ = vsb.next()
            for n in range(8):
                bank = mm.next()
                for k in range(8):
                    P.pe(lambda e, bank=bank, n=n, k=k, hT=hT: e.matmul(bank[:], lhsT=hT[:, k, :], rhs=Wk[k][:, n * 512:(n + 1) * 512],
                                                                        start=(k == 0), stop=(k == 7)), [hT, Wk[k]], [bank])
                if n < 4:
                    dst = q_r if n < 2 else k_r
                    tmp = rtmp.next()
                    pb = bank[:].rearrange("p (h t c) -> p h t c", h=2, t=2)
                    t1, t2 = pb[:, :, 0, :], pb[:, :, 1, :]
                    cos2, sin2 = c_[:, 0, :, :], c_[:, 1, :, :]
                    P.dve(lambda e, tmp=tmp, t1=t1, cos2=cos2: e.tensor_tensor(out=tmp[:, 0], in0=t1, in1=cos2, op=ALU.mult), [bank, c_], [tmp])
                    P.dve(lambda e, tmp=tmp, t2=t2, sin2=sin2: e.tensor_tensor(out=tmp[:, 1], in0=t2, in1=sin2, op=ALU.mult), [bank, c_], [tmp])
                    P.dve(lambda e, tmp=tmp, t1=t1, sin2=sin2: e.tensor_tensor(out=tmp[:, 2], in0=t1, in1=sin2, op=ALU.mult), [bank, c_], [tmp])
                    P.dve(lambda e, tmp=tmp, t2=t2, cos2=cos2: e.tensor_tensor(out=tmp[:, 3], in0=t2, in1=cos2, op=ALU.mult), [bank, c_], [tmp])
                    dv = dst[:].rearrange("p (h t c) -> p h t c", h=4, t=2)
                    h0 = 2 * (n % 2)
                    P.pool(lambda e, tmp=tmp, dv=dv, h0=h0: e.tensor_tensor(out=dv[:, h0:h0 + 2, 0, :], in0=tmp[:, 0], in1=tmp[:, 1], op=ALU.subtract),
                           [tmp], [dst])
                    P.pool(lambda e, tmp=tmp, dv=dv, h0=h0: e.tensor_tensor(out=dv[:, h0:h0 + 2, 1, :], in0=tmp[:, 2], in1=tmp[:, 3], op=ALU.add),
                           [tmp], [dst])
                else:
                    P.act(lambda e, bank=bank, n=n, v_=v_: e.activation(out=v_[:, (n - 4) * 512:(n - 3) * 512], in_=bank[:], func=AF.Copy), [bank], [v_])
            qT_, kT_, qTf_, qTb_ = qT.next(), kT.next(), qTf.next(), qTb.next()
            for k in range(8):
                P.pe(lambda e, k=k: e.transpose(out=ptq[:, k, :], in_=q_r[:, k * 128:(k + 1) * 128], identity=self.ident[:]), [q_r, self.ident], [ptq])
            P.act(lambda e, qT_=qT_: e.activation(out=qT_[:], in_=ptq[:], func=AF.Copy), [ptq], [qT_])
            for k in range(8):
                P.pe(lambda e, k=k: e.transpose(out=ptk[:, k, :], in_=k_r[:, k * 128:(k + 1) * 128], identity=self.ident[:]), [k_r, self.ident], [ptk])
            P.act(lambda e, kT_=kT_: e.activation(out=kT_[:], in_=ptk[:], func=AF.Copy), [ptk], [kT_])
            P.dve(lambda e, qT_=qT_, qTf_=qTf_: e.tensor_tensor(out=qTf_[:], in0=qT_[:], in1=QDf[:], op=ALU.mult), [qT_, QDf], [qTf_])
            P.dve(lambda e, qT_=qT_, qTb_=qTb_: e.tensor_tensor(out=qTb_[:], in0=qT_[:], in1=QDb[:], op=ALU.mult), [qT_, QDb], [qTb_])
            kf_, kb_ = kf.next(), kb.next()
            krv = k_r[:].rearrange("p (h c) -> p h c", h=4)
            P.pool(lambda e, kf_=kf_: e.tensor_tensor(out=kf_[:].rearrange("p (h c) -> p h c", h=4), in0=krv,
                                                      in1=kdec[:, 0:4].unsqueeze(2).to_broadcast([128, 4, 256]), op=ALU.mult), [k_r, kdec], [kf_])
            P.pool(lambda e, kb_=kb_: e.tensor_tensor(out=kb_[:].rearrange("p (h c) -> p h c", h=4), in0=krv,
                                                      in1=kdec[:, 4:8].unsqueeze(2).to_broadcast([128, 4, 256]), op=ALU.mult), [k_r, kdec], [kb_])
            psc = Gp.next()
            for h in range(4):
                for hf in range(2):
                    P.pe(lambda e, h=h, hf=hf, kT_=kT_, qT_=qT_, psc=psc: e.matmul(psc[:, h * 128:(h + 1) * 128], lhsT=kT_[:, 2 * h + hf, :], rhs=qT_[:, 2 * h + hf, :],
                                                                                   start=(hf == 0), stop=(hf == 1)), [kT_, qT_], [psc])
            sT_ = sT.next()
            P.dve(lambda e, sT_=sT_, psc=psc: e.tensor_tensor(out=sT_[:], in0=psc[:].rearrange("p (h t) -> p h t", h=4), in1=MT[:], op=ALU.mult), [psc, MT], [sT_])
            o_ = osb.next()
            for h in range(4):
                po = Gp.next()
                P.pe(lambda e, h=h, sT_=sT_, v_=v_, po=po: e.matmul(po[:], lhsT=sT_[:, h, :], rhs=v_[:, h * 512:(h + 1) * 512], start=True, stop=False), [sT_, v_], [po])
                for hf in range(2):
                    kt = 2 * h + hf
                    P.pe(lambda e, kt=kt, hf=hf, qTf_=qTf_, po=po: e.matmul(po[:], lhsT=qTf_[:, kt, :], rhs=Sbk[kt][:], start=False, stop=(hf == 1)),
                         [qTf_, Sbk[kt]], [po])
                P.act(lambda e, h=h, o_=o_, po=po: e.activation(out=o_[:, h * 512:(h + 1) * 512], in_=po[:], func=AF.Copy), [po], [o_])
            self.ret_state_update(P, Gp, kf_, v_, Sbk, cdI, 0)
            P.dma(self.scr_q[cidx].rearrange("p (k t) -> p k t", k=8), qTb_[:], "sq%d" % qTb.slot, reads=[qTb_], q="pool")
            P.dma(self.scr_k[cidx], kb_[:], "sk%d" % kb.slot, reads=[kb_], q="pool")
            P.dma(self.scr_v[cidx], v_[:], "sv%d" % vsb.slot, reads=[v_], q="pool")
            P.dma(self.scr_o[cidx], o_[:], "so%d" % osb.slot, reads=[o_], q="pool")
        ph.finish()

        ph = Phase(nc)
        P = ph.P
        Wzt = ph.sb("Wz", [128, 8, 2048], BF16)
        Wz = self.load_w(ph, Wzt, w["w_in"], 4096, 2048, 8)
        Wot = ph.sb("Wo", [128, 16, 1024], BF16)
        Wo = self.load_w(ph, Wot, w["w_out"], 0, 1024, 16)
        R = self.front_bufs(ph)
        R.update(self.tail_bufs(ph, last=last, pout=False))
        mm = Rot([ph.ps("mm%d" % j, [128, 512], F32) for j in range(2)])
        Gp = Rot([ph.ps("Gp%d" % j, [128, 512], F32) for j in range(3)])
        R["pout"] = Gp
        qTb = ph.rot("qTb", [128, 8, 128], BF16, 2)
        kb = ph.rot("kb", [128, 1024], BF16, 2)
        vsb = ph.rot("vsb", [128, 2048], BF16, 2)
        osb = ph.rot("osb", [128, 2048], F32, 2)
        sz = ph.rot("sz", [128, 2048], F32, 2)
        gated = ph.rot("gated", [128, 2048], BF16, 2)
        st4 = ph.rot("st4", [128, 12], F32, 2)
        Sbf = ph.sb("Sbf", [128, 8, 512], BF16)
        Sbk = [Tile(Sbf.t[:, k, :], "Sb%d" % k) for k in range(8)]
        for k in range(8):
            P.pool(lambda e, k=k: e.memset(Sbk[k][:], 0.0), [], [Sbk[k]])
        cdI = self.ret_cdI(ph)
        cd = self.rt_cd
        for ck in self.chunks_bwd():
            which = 1 if ck[0] == "c" else 0
            cidx = self.row0(ck) // CH
            need_out = not (last and ck[0] == "c")
            kb_, v_ = kb.next(), vsb.next()
            P.dma(kb_[:], self.scr_k[cidx], "lk%d" % kb.slot, writes=[kb_])
            P.dma(v_[:], self.scr_v[cidx], "lv%d" % vsb.slot, writes=[v_])
            if need_out:
                q_, o_ = qTb.next(), osb.next()
                P.dma(q_[:], self.scr_q[cidx].rearrange("p (k t) -> p k t", k=8), "lq%d" % qTb.slot, writes=[q_])
                P.dma(o_[:], self.scr_o[cidx], "lo%d" % osb.slot, writes=[o_])
                hT, xt = self.front(ph, R, self.src_ap(li, ck), which)
                sz_ = sz.next()
                for n in range(4):
                    bank = mm.next()
                    for k in range(8):
                        P.pe(lambda e, bank=bank, n=n, k=k, hT=hT: e.matmul(bank[:], lhsT=hT[:, k, :], rhs=Wz[k][:, n * 512:(n + 1) * 512],
                                                                            start=(k == 0), stop=(k == 7)), [hT, Wz[k]], [bank])
                    P.act(lambda e, bank=bank, n=n, sz_=sz_: e.activation(out=sz_[:, n * 512:(n + 1) * 512], in_=bank[:], func=AF.Silu), [bank], [sz_])
                s4 = st4.next()
                junk = R["junk"]
                for h in range(4):
                    po = Gp.next()
                    for hf in range(2):
                        kt = 2 * h + hf
                        P.pe(lambda e, kt=kt, hf=hf, q_=q_, po=po: e.matmul(po[:], lhsT=q_[:, kt, :], rhs=Sbk[kt][:], start=(hf == 0), stop=(hf == 1)),
                             [q_, Sbk[kt]], [po])
                    P.dve(lambda e, h=h, o_=o_, po=po: e.tensor_tensor(out=o_[:, h * 512:(h + 1) * 512], in0=po[:], in1=o_[:, h * 512:(h + 1) * 512], op=ALU.add),
                          [po, o_], [o_])
                    P.act(lambda e, h=h, o_=o_, s4=s4: e.activation(out=junk[:, 0:512], in_=o_[:, h * 512:(h + 1) * 512], func=AF.Square, accum_out=s4[:, h:h + 1]),
                          [o_], [junk, s4])
                P.act(lambda e, s4=s4: e.activation(out=s4[:, 4:8], in_=s4[:, 0:4], func=AF.Sqrt, scale=1.0 / 512, bias=NORM_EPS), [s4], [s4])
                P.dve(lambda e, s4=s4: e.reciprocal(out=s4[:, 8:12], in_=s4[:, 4:8]), [s4], [s4])
                g_ = gated.next()
                for h in range(4):
                    P.dve(lambda e, h=h, o_=o_, s4=s4, sz_=sz_, g_=g_: e.scalar_tensor_tensor(
                        out=g_[:, h * 512:(h + 1) * 512], in0=o_[:, h * 512:(h + 1) * 512], scalar=s4[:, 8 + h:9 + h],
                        in1=sz_[:, h * 512:(h + 1) * 512], op0=ALU.mult, op1=ALU.mult), [o_, s4, sz_], [g_])
                self.tail(ph, R, g_, Wo, li, ck, xt, last)
            self.ret_state_update(P, Gp, kb_, v_, Sbk, cdI, 4)
        ph.finish()

    def ret_state_update(self, P, pst_rot, kd, v_, Sbk, cdI, c0):
        for kt in range(8):
            h = kt // 2
            pst = pst_rot.next()
            P.pe(lambda e, kt=kt, h=h, pst=pst: e.matmul(pst[:], lhsT=cdI[:, c0 + h, :], rhs=Sbk[kt][:], start=True, stop=False), [cdI, Sbk[kt]], [pst])
            P.pe(lambda e, kt=kt, h=h, pst=pst: e.matmul(pst[:], lhsT=kd[:, kt * 128:(kt + 1) * 128], rhs=v_[:, h * 512:(h + 1) * 512], start=False, stop=True),
                 [kd, v_], [pst])
            if kt % 2 == 0:
                P.act(lambda e, kt=kt, pst=pst: e.activation(out=Sbk[kt][:], in_=pst[:], func=AF.Copy), [pst], [Sbk[kt]])
            else:
                P.dve(lambda e, kt=kt, pst=pst: e.tensor_copy(out=Sbk[kt][:], in_=pst[:]), [pst], [Sbk[kt]])

    def ret_cdI(self, ph):
        cdI = ph.sb("cdI", [128, 8, 128], BF16)
        for j in range(8):
            ph.P.dve(lambda e, j=j: e.tensor_scalar(out=cdI[:, j, :], in0=self.ident32[:], scalar1=self.rt_cd[:, j:j + 1], scalar2=None, op0=ALU.mult),
                     [self.ident32, self.rt_cd], [cdI])
        return cdI

    def gmlp_layer(self, li, i, last):
        nc, w = self.nc, self.lw[i]
        ph = Phase(nc)
        self.emit_mod(ph, i)
        ph.finish()
        ph = Phase(nc)
        P = ph.P
        Wt = ph.sb("Wuvz", [128, 8, 6144], BF16)
        Wk = self.load_w(ph, Wt, w["w_in"], 0, 6144, 8)
        Wot = ph.sb("Wo", [128, 16, 1024], BF16)
        Wo = self.load_w(ph, Wot, w["w_out"], 0, 1024, 16)
        R = self.front_bufs(ph, n_xn=1)
        R.update(self.tail_bufs(ph, n_gT=1, n_xo=1, last=last))
        mm = Rot([ph.ps("mm%d" % j, [128, 512], F32) for j in range(2)])
        spb = Rot([ph.ps("spb%d" % j, [128, 512], F32) for j in range(2)])
        wsT = ph.sb("wsT", [128, 8, 128], BF16)
        bsT = ph.sb("bsT", [128, 8], F32)
        vg = ph.sb("vg", [128, 2048], F32)
        P.dma(wsT[:], w["wsT"][:, :, :], "g0", writes=[wsT], q="pool")
        P.dma(bsT[:], w["bsT"][:, :], "g1", writes=[bsT])
        P.dma(vg[:], w["vg"][:, :], "g2", writes=[vg])
        usb = ph.rot("usb", [128, 2048], F32, 1)
        vsb = ph.rot("vsb", [128, 2048], F32, 1)
        szb = ph.rot("szb", [128, 2048], BF16, 1)
        vnb = ph.rot("vnb", [128, 2048], BF16, 1)
        gated = ph.rot("gated", [128, 2048], BF16, 1)
        stv = ph.rot("stv", [128, 16], F32, 2)
        junk = R["junk"]
        order = self.chunks_fwd()
        if last:
            order = [ck for ck in order if ck[0] == "x"]
        for ck in order:
            which = 1 if ck[0] == "c" else 0
            hT, xt = self.front(ph, R, self.src_ap(li, ck), which)
            u_, v_, z_, s_ = usb.next(), vsb.next(), szb.next(), stv.next()
            for n in range(12):
                bank = mm.next()
                for k in range(8):
                    P.pe(lambda e, bank=bank, n=n, k=k, hT=hT: e.matmul(bank[:], lhsT=hT[:, k, :], rhs=Wk[k][:, n * 512:(n + 1) * 512],
                                                                        start=(k == 0), stop=(k == 7)), [hT, Wk[k]], [bank])
                if n < 4:
                    P.act(lambda e, bank=bank, n=n, u_=u_: e.activation(out=u_[:, n * 512:(n + 1) * 512], in_=bank[:], func=AF.Copy), [bank], [u_])
                elif n < 8:
                    P.act(lambda e, bank=bank, n=n, v_=v_, s_=s_: e.activation(out=v_[:, (n - 4) * 512:(n - 3) * 512], in_=bank[:], func=AF.Identity,
                                                                              accum_out=s_[:, n - 4:n - 3]), [bank], [v_, s_])
                else:
                    P.act(lambda e, bank=bank, n=n, z_=z_: e.activation(out=z_[:, (n - 8) * 512:(n - 7) * 512], in_=bank[:], func=AF.Silu), [bank], [z_])
            for hh in range(2):
                P.act(lambda e, v_=v_, s_=s_, hh=hh: e.activation(out=junk[:], in_=v_[:, hh * 1024:(hh + 1) * 1024], func=AF.Square,
                                                                  accum_out=s_[:, 11 + hh:12 + hh]), [v_], [junk, s_])
            P.dve(lambda e, s_=s_: e.tensor_tensor(out=s_[:, 4:5], in0=s_[:, 11:12], in1=s_[:, 12:13], op=ALU.add), [s_], [s_])
            P.dve(lambda e, s_=s_: e.tensor_reduce(out=s_[:, 5:6], in_=s_[:, 0:4], axis=AX.X, op=ALU.add), [s_], [s_])
            P.dve(lambda e, s_=s_: e.tensor_scalar(out=s_[:, 5:6], in0=s_[:, 5:6], scalar1=1.0 / 2048, scalar2=None, op0=ALU.mult), [s_], [s_])
            P.dve(lambda e, s_=s_: e.tensor_tensor(out=s_[:, 6:7], in0=s_[:, 5:6], in1=s_[:, 5:6], op=ALU.mult), [s_], [s_])
            P.dve(lambda e, s_=s_: e.scalar_tensor_tensor(out=s_[:, 7:8], in0=s_[:, 4:5], scalar=1.0 / 2048, in1=s_[:, 6:7], op0=ALU.mult, op1=ALU.subtract),
                  [s_], [s_])
            P.act(lambda e, s_=s_: e.activation(out=s_[:, 8:9], in_=s_[:, 7:8], func=AF.Sqrt, scale=1.0, bias=NORM_EPS), [s_], [s_])
            P.dve(lambda e, s_=s_: e.reciprocal(out=s_[:, 9:10], in_=s_[:, 8:9]), [s_], [s_])
            P.dve(lambda e, s_=s_: e.scalar_tensor_tensor(out=s_[:, 10:11], in0=s_[:, 5:6], scalar=-1.0, in1=s_[:, 9:10], op0=ALU.mult, op1=ALU.mult),
                  [s_], [s_])
            P.act(lambda e, v_=v_, s_=s_: e.activation(out=v_[:], in_=v_[:], func=AF.Identity, scale=s_[:, 9:10], bias=s_[:, 10:11]), [v_, s_], [v_])
            vn_ = vnb.next()
            P.dve(lambda e, v_=v_, vn_=vn_: e.tensor_tensor(out=vn_[:], in0=v_[:], in1=vg[:], op=ALU.mult), [v_, vg], [vn_])
            for g in range(8):
                sb_ = spb.next()
                P.pe(lambda e, g=g, sb_=sb_, vn_=vn_: e.matmul(sb_[:, 0:256], lhsT=wsT[:, g, :], rhs=vn_[:, g * 256:(g + 1) * 256], start=True, stop=True),
                     [wsT, vn_], [sb_])
                P.dve(lambda e, g=g, sb_=sb_, u_=u_: e.scalar_tensor_tensor(out=u_[:, g * 256:(g + 1) * 256], in0=sb_[:, 0:256], scalar=bsT[:, g:g + 1],
                                                                           in1=u_[:, g * 256:(g + 1) * 256], op0=ALU.add, op1=ALU.mult), [sb_, bsT, u_], [u_])
            g_ = gated.next()
            P.pool(lambda e, u_=u_, z_=z_, g_=g_: e.tensor_tensor(out=g_[:], in0=u_[:], in1=z_[:], op=ALU.mult), [u_, z_], [g_])
            self.tail(ph, R, g_, Wo, li, ck, xt, last)
        ph.finish()

    def _rw_inputs(self, w, i, din):
        w["mu"] = din("rw_mu%d" % i, [128, 6, 8])
        w["rkvg"] = din("rw_rkvg%d" % i, [4, D, D])
        w["w1"] = din("rw_w1%d" % i, [2, D, 64])
        w["a1"] = din("rw_a1%d" % i, [2, D, 64])
        w["w2"] = din("rw_w2%d" % i, [2, 64, D])
        w["a2"] = din("rw_a2%d" % i, [2, 64, D])
        w["rows"] = din("rw_rows%d" % i, [8, D])
        w["bc"] = din("rw_bc%d" % i, [128, 5, D])
        w["w_out"] = din("rw_wout%d" % i, [D, D])
        w["masks"] = din("rw_masks%d" % i, [2, 128, 4, 128])
        w["sel8"] = din("rw_sel8%d" % i, [8, 8, 128])
        w["negc"] = din("rw_negc%d" % i, [128, 1])
        w["bmask"] = din("rw_bmask%d" % i, [128, 4, 128])
        w["cmask"] = din("rw_cmask%d" % i, [2, 128, 7, 128])

    def rw_shift(self, P, sh, hc, hp, hn, kind):
        if kind == "x":
            P.act(lambda e: e.activation(out=sh[:, 0:2, 1:128], in_=hc[:, 0:2, 0:127], func=AF.Copy), [hc], [sh])
            P.pool(lambda e: e.memset(sh[:, 0:2, :].rearrange("p k (r c) -> p k r c", c=64)[:, :, :, 0:1], 0.0), [], [sh])
            P.act(lambda e: e.activation(out=sh[:, 2:4, 0:127], in_=hc[:, 2:4, 1:128], func=AF.Copy), [hc], [sh])
            P.pool(lambda e: e.memset(sh[:, 2:4, :].rearrange("p k (r c) -> p k r c", c=64)[:, :, :, 63:64], 0.0), [], [sh])
            P.act(lambda e: e.activation(out=sh[:, 4:6, 64:128], in_=hc[:, 4:6, 0:64], func=AF.Copy), [hc], [sh])
            if hp is not None:
                P.act(lambda e: e.activation(out=sh[:, 4:6, 0:64], in_=hp[:, 4:6, 64:128], func=AF.Copy), [hp], [sh])
            else:
                P.pool(lambda e: e.memset(sh[:, 4:6, 0:64], 0.0), [], [sh])
            P.act(lambda e: e.activation(out=sh[:, 6:8, 0:64], in_=hc[:, 6:8, 64:128], func=AF.Copy), [hc], [sh])
            if hn is not None:
                P.act(lambda e: e.activation(out=sh[:, 6:8, 64:128], in_=hn[:, 6:8, 0:64], func=AF.Copy), [hn], [sh])
            else:
                P.pool(lambda e: e.memset(sh[:, 6:8, 64:128], 0.0), [], [sh])
        else:
            P.act(lambda e: e.activation(out=sh[:, 0:4, 1:128], in_=hc[:, 0:4, 0:127], func=AF.Copy), [hc], [sh])
            if hp is not None:
                P.act(lambda e: e.activation(out=sh[:, 0:4, 0:1], in_=hp[:, 0:4, 127:128], func=AF.Copy), [hp], [sh])
            else:
                P.pool(lambda e: e.memset(sh[:, 0:4, 0:1], 0.0), [], [sh])
            P.act(lambda e: e.activation(out=sh[:, 4:8, 0:127], in_=hc[:, 4:8, 1:128], func=AF.Copy), [hc], [sh])
            if hn is not None:
                P.act(lambda e: e.activation(out=sh[:, 4:8, 127:128], in_=hn[:, 4:8, 0:1], func=AF.Copy), [hn], [sh])
            else:
                P.pool(lambda e: e.memset(sh[:, 4:8, 127:128], 0.0), [], [sh])

    def rw_neighbors(self, ck):
        n = self.NC if ck[0] == "c" else self.NX
        p = (ck[0], ck[1] - 1) if ck[1] > 0 else None
        q = (ck[0], ck[1] + 1) if ck[1] < n - 1 else None
        return p, q

    def rw_hcache(self, ph, R, li):
        cache = []

        def get(ck):
            for c, v in cache:
                if c == ck:
                    return v
            which = 1 if ck[0] == "c" else 0
            v = self.front(ph, R, self.src_ap(li, ck), which)
            cache.append((ck, v))
            if len(cache) > 3:
                cache.pop(0)
            return v
        return get

    def rw_mix(self, P, mixr, tmpr, xx, hc, mu, p):
        mix = mixr.next()
        for k in range(8):
            P.dve(lambda e, k=k: e.scalar_tensor_tensor(out=mix[:, k, :], in0=xx[:, k, :], scalar=mu[:, p, k:k + 1], in1=hc[:, k, :],
                                                        op0=ALU.mult, op1=ALU.add), [xx, mu, hc], [mix])
        return mix

    def rwkv_layer(self, li, i, last):
        nc, w = self.nc, self.lw[i]
        NCH = self.NX + self.NC
        if not hasattr(self, "scr_o"):
            self.scr_o = nc.dram_tensor("scr_o", [NCH, 128, 2048], F32).ap()
        if not hasattr(self, "rw_scr_v"):
            self.rw_scr_v = nc.dram_tensor("rw_scr_v", [NCH, 128, 1024], BF16).ap()
            self.rw_dir = nc.dram_tensor("rw_dir", [2, NCH, 128, 4096], BF16).ap()
            self.rw_small = nc.dram_tensor("rw_small", [2, NCH, 128, 32], F32).ap()
        ph = Phase(nc)
        self.emit_mod(ph, i)
        ph.finish()
        self.rw_prep_phase(li, i)
        import os as _os
        for d in range(2):
            self.rw_scan_phase(li, i, d)
            if _os.environ.get("RW_STOP_AFTER_F") == "1":
                return
        self.rw_out_phase(li, i, last)

    def rw_prep_phase(self, li, i):
        nc, w = self.nc, self.lw[i]
        ph = Phase(nc)
        P = ph.P
        C0 = 0.6065306597126334
        Wr = self.load_w(ph, ph.sb("Wr", [128, 8, 1024], BF16), w["rkvg"][0], 0, 1024, 8)
        Wkk = self.load_w(ph, ph.sb("Wk", [128, 8, 1024], BF16), w["rkvg"][1], 0, 1024, 8)
        Wv = self.load_w(ph, ph.sb("Wv", [128, 8, 1024], BF16), w["rkvg"][2], 0, 1024, 8)
        w1 = [self.load_w(ph, ph.sb("w1_%d" % d, [128, 8, 64], BF16), w["w1"][d], 0, 64, 8) for d in range(2)]
        a1 = [self.load_w(ph, ph.sb("a1_%d" % d, [128, 8, 64], BF16), w["a1"][d], 0, 64, 8) for d in range(2)]
        w2 = [ph.sb("w2_%d" % d, [64, 1024], BF16) for d in range(2)]
        a2 = [ph.sb("a2_%d" % d, [64, 1024], BF16) for d in range(2)]
        tri = [ph.sb("tri%d" % d, [128, 128], F32) for d in range(2)]
        for d in range(2):
            P.dma(w2[d][:], w["w2"][d], "w2%d" % d, writes=[w2[d]], q="pool")
            P.dma(a2[d][:], w["a2"][d], "a2%d" % d, writes=[a2[d]], q="pool")
            P.dma(tri[d][:], w["masks"][d][:, 3, :], "tri%d" % d, writes=[tri[d]])
        rows = ph.sb("rows", [8, 1024], F32)
        sel8 = ph.sb("sel8", [8, 8, 128], F32)
        mu = ph.sb("mu", [128, 6, 8], F32)
        negc = ph.sb("negc", [128, 1], F32)
        kk_bc = ph.sb("kk_bc", [128, 1024], F32)
        ka_bc = ph.sb("ka_bc", [128, 1024], F32)
        rk_bc = ph.sb("rk_bc", [128, 1024], F32)
        P.dma(rows[:], w["rows"][:, :], "c0", writes=[rows])
        P.dma(sel8[:], w["sel8"][:, :, :], "c1", writes=[sel8])
        P.dma(mu[:], w["mu"][:, :, :], "c2", writes=[mu])
        P.dma(negc[:], w["negc"][:, :], "c4", writes=[negc])
        P.dma(kk_bc[:], w["bc"][:, 0, :], "c5", writes=[kk_bc])
        P.dma(ka_bc[:], w["bc"][:, 1, :], "c6", writes=[ka_bc])
        P.dma(rk_bc[:], w["bc"][:, 2, :], "c7", writes=[rk_bc])
        R = self.front_bufs(ph, n_xn=2, n_xt=2, n_hT=4)
        get_h = self.rw_hcache(ph, R, li)
        G = Rot([ph.ps("G%d" % j, [128, 512], F32) for j in range(5)])
        psm = [ph.ps("psm%d" % d, [128, 512], F32) for d in range(2)]
        sh = ph.rot("sh", [128, 8, 128], BF16, 2)
        xx = ph.rot("xx", [128, 8, 128], F32, 2)
        mixr = ph.rot("mix", [128, 8, 128], BF16, 4)
        r_sb = ph.sb("r_sb", [128, 1024], F32)
        k_sb = ph.sb("k_sb", [128, 1024], F32)
        kk = ph.sb("kk_sb", [128, 1024], F32)
        v_bf = ph.rot("v_bf", [128, 1024], BF16, 2)
        Wt = [[ph.sb("W%d_%d" % (d, j), [128, 1024], F32) for j in range(4)] for d in range(2)]
        th_bf = [ph.sb("th_bf%d" % d, [64, 128], BF16) for d in range(2)]
        la_bf = [ph.sb("la_bf%d" % d, [64, 128], BF16) for d in range(2)]
        outb = [ph.sb("outb%d" % d, [128, 4096], BF16) for d in range(2)]
        small = [ph.rot("small%d" % d, [128, 32], F32, 2) for d in range(2)]
        sm = ph.rot("sm", [128, 48], F32, 2)
        for d in range(2):
            for t_ in small[d].tiles:
                P.dve(lambda e, t_=t_: e.memset(t_[:], 0.0), [], [t_])

        def chunk_body(ck):
            cidx = self.row0(ck) // CH
            pk, nk = self.rw_neighbors(ck)
            hc = get_h(ck)[0]
            hp = get_h(pk)[0] if pk else None
            hn = get_h(nk)[0] if nk else None
            sh_, xx_ = sh.next(), xx.next()
            self.rw_shift(P, sh_, hc, hp, hn, ck[0])
            P.pool(lambda e: e.tensor_tensor(out=xx_[:], in0=sh_[:], in1=hc[:], op=ALU.subtract), [sh_, hc], [xx_])
            v_, s_ = v_bf.next(), sm.next()
            for p, Wl, dst in ((0, Wr, r_sb), (2, Wkk, k_sb), (3, Wv, v_)):
                mix = self.rw_mix(P, mixr, None, xx_, hc, mu, p)
                for n in range(2):
                    bank = G.next()
                    for k in range(8):
                        P.pe(lambda e, bank=bank, n=n, k=k, mix=mix, Wl=Wl: e.matmul(bank[:], lhsT=mix[:, k, :], rhs=Wl[k][:, n * 512:(n + 1) * 512],
                                                                                    start=(k == 0), stop=(k == 7)), [mix, Wl[k]], [bank])
                    P.act(lambda e, bank=bank, n=n, dst=dst: e.activation(out=dst[:, n * 512:(n + 1) * 512], in_=bank[:], func=AF.Copy), [bank], [dst])
            P.dma(self.rw_scr_v[cidx], v_[:], "pv%d" % v_bf.slot, reads=[v_], q="pool")
            sq = Wt[0][1]
            P.dve(lambda e: e.tensor_tensor(out=kk[:], in0=k_sb[:], in1=kk_bc[:], op=ALU.mult), [k_sb, kk_bc], [kk])
            P.pool(lambda e: e.tensor_tensor(out=sq[:], in0=kk[:], in1=kk[:], op=ALU.mult), [kk], [sq])
            P.dve(lambda e: e.tensor_reduce(out=s_[:, 0:16], in_=sq[:].rearrange("p (h c) -> p h c", h=16), axis=AX.X, op=ALU.add), [sq], [s_])
            P.act(lambda e: e.activation(out=s_[:, 16:32], in_=s_[:, 0:16], func=AF.Sqrt), [s_], [s_])
            P.dve(lambda e: e.tensor_scalar(out=s_[:, 16:32], in0=s_[:, 16:32], scalar1=1e-12, scalar2=None, op0=ALU.max), [s_], [s_])
            P.dve(lambda e: e.reciprocal(out=s_[:, 32:48], in_=s_[:, 16:32]), [s_], [s_])
            P.dve(lambda e: e.tensor_tensor(out=kk[:].rearrange("p (h c) -> p h c", h=16), in0=kk[:].rearrange("p (h c) -> p h c", h=16),
                                            in1=s_[:, 32:48].unsqueeze(2).to_broadcast([128, 16, 64]), op=ALU.mult), [kk, s_], [kk])
            mix1 = self.rw_mix(P, mixr, None, xx_, hc, mu, 1)
            mix4 = self.rw_mix(P, mixr, None, xx_, hc, mu, 4)
            sml = [small[d].next() for d in range(2)]
            for d in range(2):
                for k in range(8):
                    P.pe(lambda e, k=k, d=d: e.matmul(psm[d][0:64, 0:128], lhsT=w1[d][k][:, :], rhs=mix1[:, k, :], start=(k == 0), stop=(k == 7)),
                         [mix1, w1[d][k]], [psm[d]])
                P.act(lambda e, d=d: e.activation(out=th_bf[d][:], in_=psm[d][0:64, 0:128], func=AF.Tanh), [psm[d]], [th_bf[d]])
            for d in range(2):
                for k in range(8):
                    P.pe(lambda e, k=k, d=d: e.matmul(psm[d][0:64, 0:128], lhsT=a1[d][k][:, :], rhs=mix4[:, k, :], start=(k == 0), stop=(k == 7)),
                         [mix4, a1[d][k]], [psm[d]])
                P.act(lambda e, d=d: e.activation(out=la_bf[d][:], in_=psm[d][0:64, 0:128], func=AF.Copy), [psm[d]], [la_bf[d]])
            for d in range(2):
                sig = Wt[d][0]
                for n in range(2):
                    bank = G.next()
                    P.pe(lambda e, bank=bank, n=n, d=d: e.matmul(bank[:], lhsT=th_bf[d][:], rhs=w2[d][:, n * 512:(n + 1) * 512], start=True, stop=False),
                         [th_bf[d], w2[d]], [bank])
                    P.pe(lambda e, bank=bank, n=n, d=d: e.matmul(bank[:], lhsT=sel8[:, d, :], rhs=rows[:, n * 512:(n + 1) * 512], start=False, stop=True),
                         [sel8, rows], [bank])
                    P.act(lambda e, bank=bank, n=n, sig=sig: e.activation(out=sig[:, n * 512:(n + 1) * 512], in_=bank[:], func=AF.Sigmoid), [bank], [sig])
            for d in range(2):
                sig, ep, em, ex = Wt[d]
                for n in range(2):
                    bank = G.next()
                    sl = slice(n * 512, (n + 1) * 512)
                    P.pe(lambda e, bank=bank, sl=sl, d=d, sig=sig: e.matmul(bank[:], lhsT=tri[d][:], rhs=sig[:, sl], start=True, stop=True), [tri[d], sig], [bank])
                    P.act(lambda e, bank=bank, sl=sl, ep=ep: e.activation(out=ep[:, sl], in_=bank[:], func=AF.Exp), [bank], [ep])
                    P.act(lambda e, bank=bank, sl=sl, em=em: e.activation(out=em[:, sl], in_=bank[:], func=AF.Exp, scale=-1.0), [bank], [em])
                    P.dve(lambda e, bank=bank, sl=sl, ex=ex, sig=sig: e.scalar_tensor_tensor(out=ex[:, sl], in0=sig[:, sl], scalar=C0, in1=bank[:], op0=ALU.mult, op1=ALU.add),
                          [bank, sig], [ex])
                P.act(lambda e, ex=ex: e.activation(out=ex[:], in_=ex[:], func=AF.Exp), [ex], [ex])
            for d in range(2):
                sig = Wt[d][0]
                for h in range(16):
                    P.pe(lambda e, h=h, d=d, sig=sig: e.matmul(psm[d][0:64, 256 + h:257 + h], lhsT=sig[:, h * 64:(h + 1) * 64], rhs=negc[:, 0:1], start=True, stop=True),
                         [sig, negc], [psm[d]])
                P.act(lambda e, d=d: e.activation(out=sml[d][0:64, 0:16], in_=psm[d][0:64, 256:272], func=AF.Exp), [psm[d]], [sml[d]])
            for d in range(2):
                sig, ep, em, ex = Wt[d]
                P.pool(lambda e, d=d, ep=ep: e.tensor_tensor(out=outb[d][:, 0:1024], in0=r_sb[:], in1=ep[:], op=ALU.mult), [r_sb, ep], [outb[d]])
                P.dve(lambda e, d=d, ex=ex: e.scalar_tensor_tensor(out=outb[d][:, 1024:2048], in0=kk[:], scalar=-1.0, in1=ex[:], op0=ALU.mult, op1=ALU.mult),
                      [kk, ex], [outb[d]])
            for d in range(2):
                aa = Wt[d][1]
                for n in range(2):
                    bank = G.next()
                    P.pe(lambda e, bank=bank, n=n, d=d: e.matmul(bank[:], lhsT=la_bf[d][:], rhs=a2[d][:, n * 512:(n + 1) * 512], start=True, stop=False),
                         [la_bf[d], a2[d]], [bank])
                    P.pe(lambda e, bank=bank, n=n, d=d: e.matmul(bank[:], lhsT=sel8[:, 2 + d, :], rhs=rows[:, n * 512:(n + 1) * 512], start=False, stop=True),
                         [sel8, rows], [bank])
                    P.act(lambda e, bank=bank, n=n, aa=aa: e.activation(out=aa[:, n * 512:(n + 1) * 512], in_=bank[:], func=AF.Sigmoid), [bank], [aa])
            for d in range(2):
                aa, em, be = Wt[d][1], Wt[d][2], Wt[d][3]
                P.dve(lambda e, aa=aa, be=be: e.tensor_tensor(out=be[:], in0=kk[:], in1=aa[:], op=ALU.mult), [kk, aa], [be])
                P.pool(lambda e, d=d, be=be, em=em: e.tensor_tensor(out=outb[d][:, 2048:3072], in0=be[:], in1=em[:], op=ALU.mult), [be, em], [outb[d]])
            for d in range(2):
                kd, aa, em = Wt[d][0], Wt[d][1], Wt[d][2]
                P.dve(lambda e, kd=kd, aa=aa: e.scalar_tensor_tensor(out=kd[:], in0=aa[:], scalar=-1.0, in1=ka_bc[:], op0=ALU.add, op1=ALU.mult), [aa, ka_bc], [kd])
                P.dve(lambda e, kd=kd: e.scalar_tensor_tensor(out=kd[:], in0=kd[:], scalar=1.0, in1=k_sb[:], op0=ALU.add, op1=ALU.mult), [kd, k_sb], [kd])
                P.pool(lambda e, d=d, kd=kd, em=em: e.tensor_tensor(out=outb[d][:, 3072:4096], in0=kd[:], in1=em[:], op=ALU.mult), [kd, em], [outb[d]])
            for d in range(2):
                kd, bt = Wt[d][0], Wt[d][3]
                P.dve(lambda e, kd=kd, bt=bt: e.tensor_tensor(out=bt[:], in0=kd[:], in1=r_sb[:], op=ALU.mult), [kd, r_sb], [bt])
                P.pool(lambda e, bt=bt: e.tensor_tensor(out=bt[:], in0=bt[:], in1=rk_bc[:], op=ALU.mult), [bt, rk_bc], [bt])
                P.dve(lambda e, d=d, bt=bt: e.tensor_reduce(out=sml[d][:, 16:32], in_=bt[:].rearrange("p (h c) -> p h c", h=16), axis=AX.X, op=ALU.add), [bt], [sml[d]])
            for d in range(2):
                P.dma(self.rw_dir[d][cidx], outb[d][:], "po%d" % d, reads=[outb[d]], q="pool")
                P.dma(self.rw_small[d][cidx], sml[d][:], "ps%d_%d" % (d, small[d].slot), reads=[sml[d]], q="pool")

        for ck in self.chunks_fwd():
            chunk_body(ck)
        ph.finish()

    def rw_scan_phase(self, li, i, d):
        nc, w = self.nc, self.lw[i]
        ph = Phase(nc)
        P = ph.P
        msk = ph.sb("msk", [128, 4, 128], F32)
        cmk = ph.sb("cmk", [128, 7, 128], BF16)
        P.dma(msk[:], w["masks"][d], "c3", writes=[msk])
        P.dma(cmk[:], w["cmask"][d], "c10", writes=[cmk], q="pool")
        G = Rot([ph.ps("G%d" % j, [128, 512], F32) for j in range(6)])
        PTr = Rot([ph.ps("PT%d" % j, [128, 8, 128], BF16) for j in range(2)])
        inb = ph.rot("inb", [128, 4096], BF16, 2)
        v_rot = ph.rot("v_bf", [128, 1024], BF16, 2)
        smr = ph.rot("smr", [128, 32], F32, 2)
        if d == 1:
            smf = ph.rot("smf", [128, 32], F32, 2)
            t1 = ph.sb("t1", [128, 1024], F32)
            sm = ph.rot("sm", [128, 48], F32, 2)
        ARr = ph.rot("AR", [64, 16, 2, 128], BF16, 2)
        BTr = ph.rot("BT", [64, 16, 128], BF16, 2)
        KTr = ph.rot("KT", [64, 16, 128], BF16, 2)
        names = ("N", "NT", "NA", "NAT", "NB", "NBT", "O32", "O32T", "O64", "O64T", "O128", "T", "TT", "Aak", "Abr", "Akr")
        NU = 4
        U_ = [{n: ph.sb("%s_u%d" % (n, us), [128, 4, 128], BF16) for n in names} for us in range(NU)]
        Xb = [ph.sb("Xb%d" % us, [128, 4, 64], BF16) for us in range(NU)]
        Ub = [ph.sb("Ub%d" % us, [128, 4, 64], BF16) for us in range(NU)]
        S = [ph.sb("S%d" % u, [64, 4, 64], F32) for u in range(4)]
        Sb = [ph.sb("Sb%d" % u, [64, 4, 64], BF16) for u in range(4)]
        for u in range(4):
            P.dve(lambda e, u=u: e.memset(S[u][:], 0.0), [], [S[u]])
            P.pool(lambda e, u=u: e.memset(Sb[u][:], 0.0), [], [Sb[u]])
        ysb = ph.rot("ysb", [128, 1040], F32, 2)
        if d == 1:
            yf = ph.rot("yf", [128, 1040], F32, 2)
        ident = self.ident
        order = self.chunks_fwd() if d == 0 else self.chunks_bwd()

        def chunk_body(ck):
            cidx = self.row0(ck) // CH
            in_, v_bf, s_ = inb.next(), v_rot.next(), smr.next()
            P.dma(in_[:], self.rw_dir[d][cidx], "li%d" % inb.slot, writes=[in_])
            P.dma(v_bf[:], self.rw_scr_v[cidx], "lv%d" % v_rot.slot, writes=[v_bf])
            P.dma(s_[:], self.rw_small[d][cidx], "ls%d" % smr.slot, writes=[s_])
            WC = Tile(s_.t[0:64, 0:16], "WCview")
            WC.b = s_.b
            bh_bf = Tile(in_.t[:, 2048:3072], "bhv")
            bh_bf.b = in_.b
            kh_bf = Tile(in_.t[:, 3072:4096], "khv")
            kh_bf.b = in_.b
            AR, BT, KT = ARr.next(), BTr.next(), KTr.next()
            cnt = 0
            for g in range(2):
                for c0, dtile, dst in ((1024, AR, AR[:, g * 8:(g + 1) * 8, 0, :]), (0, AR, AR[:, g * 8:(g + 1) * 8, 1, :]),
                                       (2048, BT, BT[:, g * 8:(g + 1) * 8, :]), (3072, KT, KT[:, g * 8:(g + 1) * 8, :])):
                    PT = PTr.next()
                    for j in range(8):
                        h = g * 8 + j
                        P.pe(lambda e, PT=PT, c0=c0, h=h, j=j: e.transpose(out=PT[0:64, j, :], in_=in_[:, c0 + h * 64:c0 + (h + 1) * 64], identity=ident[:]),
                             [in_, ident], [PT])
                    if cnt % 2 == 0:
                        P.act(lambda e, PT=PT, dst=dst: e.activation(out=dst, in_=PT[0:64, :, :], func=AF.Copy), [PT], [dtile])
                    else:
                        P.dve(lambda e, PT=PT, dst=dst: e.tensor_copy(out=dst, in_=PT[0:64, :, :]), [PT], [dtile])
                    cnt += 1
            y_ = ysb.next()
            if d == 1:
                yf_ = yf.next()
                P.dma(yf_[:, 0:1024], self.scr_o[cidx][:, 0:1024], "lyf%d" % yf.slot, writes=[yf_])
                sf_ = smf.next()
                P.dma(sf_[:], self.rw_small[0][cidx], "lsf%d" % smf.slot, writes=[sf_])
            for g in range(1):
                units = [(us, us) for us in range(4)]
                for u, us in units:
                    M = U_[us]
                    h0 = u * 4
                    for pr in range(2):
                        bank = G.next()
                        for j in range(2):
                            h = h0 + pr * 2 + j
                            P.pe(lambda e, bank=bank, j=j, h=h: e.matmul(bank[:, j * 256:(j + 1) * 256], lhsT=BT[:, h, :],
                                                                         rhs=AR[:, h, :, :].rearrange("p a t -> p (a t)"), start=True, stop=True), [BT, AR], [bank])
                        bv = bank[:].rearrange("p (j a t) -> p j a t", j=2, a=2)
                        for dn, mi in (("NA", 0),):
                            P.dve(lambda e, bv=bv, M=M, pr=pr, dn=dn, mi=mi: e.tensor_tensor(out=M[dn][:, pr * 2:pr * 2 + 2, :], in0=bv[:, :, 0, :],
                                                                                            in1=cmk[:, mi, :].unsqueeze(1).to_broadcast([128, 2, 128]), op=ALU.mult),
                                  [bank, cmk], [M[dn]])
                        P.dve(lambda e, bv=bv, M=M, pr=pr: e.tensor_tensor(out=M["Abr"][:, pr * 2:pr * 2 + 2, :], in0=bv[:, :, 1, :],
                                                                          in1=msk[:, 1, :].unsqueeze(1).to_broadcast([128, 2, 128]), op=ALU.mult), [bank, msk], [M["Abr"]])
                        bank = G.next()
                        for j in range(2):
                            h = h0 + pr * 2 + j
                            P.pe(lambda e, bank=bank, j=j, h=h: e.matmul(bank[:, j * 256:(j + 1) * 256], lhsT=KT[:, h, :],
                                                                         rhs=AR[:, h, :, :].rearrange("p a t -> p (a t)"), start=True, stop=True), [KT, AR], [bank])
                        bv = bank[:].rearrange("p (j a t) -> p j a t", j=2, a=2)
                        P.dve(lambda e, bv=bv, M=M, pr=pr: e.tensor_tensor(out=M["Aak"][:, pr * 2:pr * 2 + 2, :], in0=bv[:, :, 0, :],
                                                                          in1=msk[:, 0, :].unsqueeze(1).to_broadcast([128, 2, 128]), op=ALU.mult), [bank, msk], [M["Aak"]])
                        P.dve(lambda e, bv=bv, M=M, pr=pr: e.tensor_tensor(out=M["Akr"][:, pr * 2:pr * 2 + 2, :], in0=bv[:, :, 1, :],
                                                                          in1=msk[:, 1, :].unsqueeze(1).to_broadcast([128, 2, 128]), op=ALU.mult), [bank, msk], [M["Akr"]])
                    bank = G.next()
                    for j in range(4):
                        h = h0 + j
                        P.pe(lambda e, bank=bank, j=j, h=h: e.matmul(bank[:, j * 128:(j + 1) * 128], lhsT=AR[:, h, 0, :], rhs=BT[:, h, :], start=True, stop=True),
                             [AR, BT], [bank])
                    for dn, mi in (("NAT", 3), ("O32T", 4), ("O64T", 5), ("O128", 6)):
                        P.dve(lambda e, bank=bank, M=M, dn=dn, mi=mi: e.tensor_tensor(out=M[dn][:], in0=bank[:].rearrange("p (j t) -> p j t", j=4),
                                                                                      in1=cmk[:, mi, :].unsqueeze(1).to_broadcast([128, 4, 128]), op=ALU.mult),
                              [bank, cmk], [M[dn]])
                    P.pool(lambda e, M=M: e.tensor_tensor(out=M["T"][:], in0=M["NA"][:], in1=ident[:].unsqueeze(1).to_broadcast([128, 4, 128]), op=ALU.add),
                           [M["NA"], ident], [M["T"]])

                def mm4(bank, M, lt, rt):
                    for j in range(4):
                        P.pe(lambda e, bank=bank, j=j, M=M, lt=lt, rt=rt: e.matmul(bank[:, j * 128:(j + 1) * 128], lhsT=M[lt][:, j, :], rhs=M[rt][:, j, :],
                                                                                  start=True, stop=True), [M[lt], M[rt]], [bank])

                def cp4(bank, M, dn):
                    P.act(lambda e, bank=bank, M=M, dn=dn: e.activation(out=M[dn][:], in_=bank[:].rearrange("p (j t) -> p j t", j=4), func=AF.Copy),
                          [bank], [M[dn]])

                def add4(bank, M, dn):
                    P.dve(lambda e, bank=bank, M=M, dn=dn: e.tensor_tensor(out=M[dn][:], in0=bank[:].rearrange("p (j t) -> p j t", j=4), in1=M[dn][:], op=ALU.add),
                          [bank, M[dn]], [M[dn]])
                def tr4(M):
                    PT = PTr.next()
                    for j in range(4):
                        P.pe(lambda e, PT=PT, j=j, M=M: e.transpose(out=PT[:, j, :], in_=M["T"][:, j, :], identity=ident[:]), [M["T"], ident], [PT])
                    P.act(lambda e, PT=PT, M=M: e.activation(out=M["TT"][:], in_=PT[:, 0:4, :], func=AF.Copy), [PT], [M["TT"]])
                cur = {us: ("NA", "NAT") for _, us in units}
                for lvl in range(3):
                    nxt = {}
                    for u, us in units:
                        M = U_[us]
                        nk, nkt = cur[us]
                        nb, nbt = ("NB", "NBT") if nk == "NA" else ("NA", "NAT")
                        if lvl < 2:
                            b1 = G.next(); mm4(b1, M, nkt, nk); cp4(b1, M, nb)
                        b2 = G.next(); mm4(b2, M, nk, nkt); cp4(b2, M, nbt)
                        nxt[us] = (nb, nbt)
                    for u, us in units:
                        M = U_[us]
                        nb, nbt = nxt[us]
                        b3 = G.next(); mm4(b3, M, nbt, "T"); add4(b3, M, "T")
                    cur = nxt
                for on in ("O32T", "O64T", "O128"):
                    for u, us in units:
                        tr4(U_[us])
                    for u, us in units:
                        M = U_[us]
                        b1 = G.next(); mm4(b1, M, on, "T"); cp4(b1, M, "N")
                    for u, us in units:
                        M = U_[us]
                        b3 = G.next(); mm4(b3, M, "TT", "N"); add4(b3, M, "T")
                for u, us in units:
                    M = U_[us]
                    h0 = u * 4
                    bank = G.next()
                    for j in range(4):
                        h = h0 + j
                        P.pe(lambda e, bank=bank, j=j, h=h, u=u: e.matmul(bank[:, j * 64:(j + 1) * 64], lhsT=AR[:, h, 0, :], rhs=Sb[u][:, j, :], start=True, stop=False),
                             [AR, Sb[u]], [bank])
                        P.pe(lambda e, bank=bank, j=j, h=h, M=M: e.matmul(bank[:, j * 64:(j + 1) * 64], lhsT=M["Aak"][:, j, :], rhs=v_bf[:, h * 64:(h + 1) * 64],
                                                                          start=False, stop=True), [M["Aak"], v_bf], [bank])
                    P.act(lambda e, bank=bank, us=us: e.activation(out=Xb[us][:], in_=bank[:, 0:256].rearrange("p (j v) -> p j v", j=4), func=AF.Copy),
                          [bank], [Xb[us]])
                for u, us in units:
                    M = U_[us]
                    h0 = u * 4
                    bank = G.next()
                    for j in range(4):
                        P.pe(lambda e, bank=bank, j=j, M=M, us=us: e.matmul(bank[:, j * 64:(j + 1) * 64], lhsT=M["T"][:, j, :], rhs=Xb[us][:, j, :], start=True, stop=True),
                             [M["T"], Xb[us]], [bank])
                    P.dve(lambda e, bank=bank, us=us: e.tensor_copy(out=Ub[us][:], in_=bank[:, 0:256].rearrange("p (j v) -> p j v", j=4)), [bank], [Ub[us]])
                for u, us in units:
                    M = U_[us]
                    h0 = u * 4
                    bank = G.next()
                    for j in range(4):
                        h = h0 + j
                        P.pe(lambda e, bank=bank, j=j, h=h, u=u: e.matmul(bank[:, j * 64:(j + 1) * 64], lhsT=AR[:, h, 1, :], rhs=Sb[u][:, j, :], start=True, stop=False),
                             [AR, Sb[u]], [bank])
                        P.pe(lambda e, bank=bank, j=j, M=M, us=us: e.matmul(bank[:, j * 64:(j + 1) * 64], lhsT=M["Abr"][:, j, :], rhs=Ub[us][:, j, :], start=False, stop=False),
                             [M["Abr"], Ub[us]], [bank])
                        P.pe(lambda e, bank=bank, j=j, h=h, M=M: e.matmul(bank[:, j * 64:(j + 1) * 64], lhsT=M["Akr"][:, j, :], rhs=v_bf[:, h * 64:(h + 1) * 64],
                                                                          start=False, stop=True), [M["Akr"], v_bf], [bank])
                    if d == 0:
                        P.act(lambda e, bank=bank, h0=h0, y_=y_: e.activation(out=y_[:, h0 * 64:(h0 + 4) * 64], in_=bank[:, 0:256], func=AF.Copy), [bank], [y_])
                    else:
                        P.dve(lambda e, bank=bank, h0=h0, y_=y_, yf_=yf_: e.tensor_tensor(out=y_[:, h0 * 64:(h0 + 4) * 64], in0=bank[:, 0:256],
                                                                                         in1=yf_[:, h0 * 64:(h0 + 4) * 64], op=ALU.add), [bank, yf_], [y_])
                for u, us in units:
                    M = U_[us]
                    h0 = u * 4
                    bank = G.next()
                    for j in range(4):
                        h = h0 + j
                        P.pe(lambda e, bank=bank, j=j, h=h, us=us: e.matmul(bank[0:64, j * 64:(j + 1) * 64], lhsT=bh_bf[:, h * 64:(h + 1) * 64], rhs=Ub[us][:, j, :],
                                                                            start=True, stop=False), [bh_bf, Ub[us]], [bank])
                        P.pe(lambda e, bank=bank, j=j, h=h: e.matmul(bank[0:64, j * 64:(j + 1) * 64], lhsT=kh_bf[:, h * 64:(h + 1) * 64], rhs=v_bf[:, h * 64:(h + 1) * 64],
                                                                     start=False, stop=True), [kh_bf, v_bf], [bank])
                    P.dve(lambda e, bank=bank, u=u: e.tensor_tensor(out=S[u][:], in0=bank[0:64, 0:256].rearrange("p (j v) -> p j v", j=4), in1=S[u][:], op=ALU.add),
                          [bank, S[u]], [S[u]])
                    P.dve(lambda e, u=u, h0=h0: e.tensor_tensor(out=S[u][:], in0=S[u][:], in1=WC[:, h0:h0 + 4].unsqueeze(2).to_broadcast([64, 4, 64]), op=ALU.mult),
                          [S[u], WC], [S[u]])
                    P.pool(lambda e, u=u: e.tensor_copy(out=Sb[u][:], in_=S[u][:]), [S[u]], [Sb[u]])
            if d == 0:
                P.dma(self.scr_o[cidx][:, 0:1024], y_[:, 0:1024], "sy%d" % ysb.slot, reads=[y_], q="pool")
            else:
                need_out = True
                yv = y_[:, 0:1024].rearrange("p (h c) -> p h c", h=16)
                t1v = t1[:].rearrange("p (h c) -> p h c", h=16)
                st_ = sm.next()
                P.dve(lambda e, yv=yv: e.tensor_reduce(out=st_[:, 0:16], in_=yv, axis=AX.X, op=ALU.add), [y_], [st_])
                P.dve(lambda e: e.tensor_scalar(out=st_[:, 0:16], in0=st_[:, 0:16], scalar1=-1.0 / 64, scalar2=None, op0=ALU.mult), [st_], [st_])
                P.dve(lambda e, yv=yv: e.tensor_tensor(out=yv, in0=yv, in1=st_[:, 0:16].unsqueeze(2).to_broadcast([128, 16, 64]), op=ALU.add), [y_, st_], [y_])
                P.pool(lambda e, y_=y_: e.tensor_tensor(out=t1[:], in0=y_[:, 0:1024], in1=y_[:, 0:1024], op=ALU.mult), [y_], [t1])
                P.dve(lambda e: e.tensor_reduce(out=st_[:, 16:32], in_=t1v, axis=AX.X, op=ALU.add), [t1], [st_])
                P.act(lambda e: e.activation(out=st_[:, 16:32], in_=st_[:, 16:32], func=AF.Sqrt, scale=1.0 / 64, bias=64e-5), [st_], [st_])
                P.dve(lambda e: e.reciprocal(out=st_[:, 32:48], in_=st_[:, 16:32]), [st_], [st_])
                P.dve(lambda e, yv=yv: e.tensor_tensor(out=yv, in0=yv, in1=st_[:, 32:48].unsqueeze(2).to_broadcast([128, 16, 64]), op=ALU.mult), [y_, st_], [y_])
                P.dve(lambda e: e.tensor_tensor(out=st_[:, 0:16], in0=s_[:, 16:32], in1=sf_[:, 16:32], op=ALU.add), [s_, sf_], [st_])
                P.dve(lambda e: e.tensor_tensor(out=t1v, in0=v_bf[:].rearrange("p (h c) -> p h c", h=16),
                                                in1=st_[:, 0:16].unsqueeze(2).to_broadcast([128, 16, 64]), op=ALU.mult), [v_bf, st_, t1], [t1])
                P.dma(self.scr_o[cidx][:, 0:1024], y_[:, 0:1024], "sy%d" % ysb.slot, reads=[y_], q="pool")
                P.dma(self.scr_o[cidx][:, 1024:2048], t1[:], "sbv", reads=[t1], q="pool")
        for ck in order:
            chunk_body(ck)
        ph.finish()

    def rw_out_phase(self, li, i, last):
        nc, w = self.nc, self.lw[i]
        ph = Phase(nc)
        P = ph.P
        Wg = self.load_w(ph, ph.sb("Wg", [128, 8, 1024], BF16), w["rkvg"][3], 0, 1024, 8)
        Wo = self.load_w(ph, ph.sb("Wo", [128, 8, 1024], BF16), w["w_out"], 0, 1024, 8)
        mu = ph.sb("mu", [128, 6, 8], F32)
        P.dma(mu[:], w["mu"][:, :, :], "c2", writes=[mu])
        R = self.front_bufs(ph, n_xn=1, n_xt=4, n_hT=4)
        R.update(self.tail_bufs(ph, last=last))
        get_h = self.rw_hcache(ph, R, li)
        mm = Rot([ph.ps("mm%d" % j, [128, 512], F32) for j in range(2)])
        sh = ph.sb("sh", [128, 8, 128], BF16)
        xx = ph.sb("xx", [128, 8, 128], F32)
        tmpr = ph.rot("mtmp", [128, 8, 128], F32, 2)
        mixr = ph.rot("mix", [128, 8, 128], BF16, 2)
        op = ph.rot("opre", [128, 2048], F32, 2)
        lg_bc = ph.sb("lg_bc", [128, 1024], F32)
        lb_bc = ph.sb("lb_bc", [128, 1024], F32)
        P.dma(lg_bc[:], w["bc"][:, 3, :], "c8", writes=[lg_bc])
        P.dma(lb_bc[:], w["bc"][:, 4, :], "c9", writes=[lb_bc])
        sz = ph.rot("sz", [128, 1024], F32, 2)
        gated = ph.rot("gated", [128, 1024], BF16, 2)
        order = self.chunks_fwd()
        if last:
            order = [ck for ck in order if ck[0] == "x"]
        for ck in order:
            cidx = self.row0(ck) // CH
            pk, nk = self.rw_neighbors(ck)
            hc, xt = get_h(ck)
            hp = get_h(pk)[0] if pk else None
            hn = get_h(nk)[0] if nk else None
            self.rw_shift(P, sh, hc, hp, hn, ck[0])
            P.pool(lambda e, hc=hc: e.tensor_tensor(out=xx[:], in0=sh[:], in1=hc[:], op=ALU.subtract), [sh, hc], [xx])
            mix5 = self.rw_mix(P, mixr, tmpr, xx, hc, mu, 5)
            o_ = op.next()
            P.dma(o_[:], self.scr_o[cidx][:, 0:2048], "lo%d" % op.slot, writes=[o_])
            P.dve(lambda e, o_=o_: e.tensor_tensor(out=o_[:, 0:1024], in0=o_[:, 0:1024], in1=lg_bc[:], op=ALU.mult), [o_, lg_bc], [o_])
            P.pool(lambda e, o_=o_: e.tensor_tensor(out=o_[:, 1024:2048], in0=o_[:, 1024:2048], in1=lb_bc[:], op=ALU.add), [o_, lb_bc], [o_])
            P.dve(lambda e, o_=o_: e.tensor_tensor(out=o_[:, 0:1024], in0=o_[:, 0:1024], in1=o_[:, 1024:2048], op=ALU.add), [o_], [o_])
            sz_ = sz.next()
            for n in range(2):
                bank = mm.next()
                for k in range(8):
                    P.pe(lambda e, bank=bank, n=n, k=k, mix5=mix5: e.matmul(bank[:], lhsT=mix5[:, k, :], rhs=Wg[k][:, n * 512:(n + 1) * 512],
                                                                            start=(k == 0), stop=(k == 7)), [mix5, Wg[k]], [bank])
                P.act(lambda e, bank=bank, n=n, sz_=sz_: e.activation(out=sz_[:, n * 512:(n + 1) * 512], in_=bank[:], func=AF.Silu), [bank], [sz_])
            g_ = gated.next()
            P.dve(lambda e, o_=o_, sz_=sz_, g_=g_: e.tensor_tensor(out=g_[:], in0=o_[:, 0:1024], in1=sz_[:], op=ALU.mult), [o_, sz_], [g_])
            self.tail(ph, R, g_, Wo, li, ck, xt, last, KT=8)
        ph.finish()


def _col(v, k):
    return np.ascontiguousarray(np.asarray(v, np.float32).reshape(k, 128).T)


def _consts(L, CL):
    f32 = np.float32
    t = np.arange(L)
    row, col = (t // 64).astype(f32), (t % 64).astype(f32)
    freqs = (f32(10000.0) ** (-(np.arange(64, dtype=f32)) / f32(64))).astype(f32)
    ang = np.concatenate([row[:, None] * freqs, col[:, None] * freqs], axis=-1).astype(f32)
    rope = np.zeros((CL + L, 2, 128), f32)
    rope[:CL, 0, :] = 1.0
    rope[CL:, 0, :] = np.cos(ang)
    rope[CL:, 1, :] = np.sin(ang)
    jj = np.arange(128)[:, None].astype(f32)
    ii = np.arange(128)[None, :].astype(f32)
    dmat = np.stack([np.maximum(ii - jj, 0), np.maximum(jj - ii, 0), (ii >= jj).astype(f32), (jj >= ii).astype(f32),
                     np.broadcast_to(ii + 1, (128, 128)), np.broadcast_to(128 - ii, (128, 128))], axis=1).astype(f32)
    jcol = np.stack([127 - np.arange(128), np.arange(128)], axis=1).astype(f32)
    sel = np.zeros((2, 2, 128), f32)
    sel[0, 0, :] = 1.0
    sel[1, 1, :] = 1.0
    return {"ident": np.eye(128, dtype=f32), "rope": rope, "dmat": np.ascontiguousarray(dmat), "jcol": jcol, "sel2": sel}


def host_inputs(inp, b, layers, L, CL=256, x_rows=None):
    f32 = np.float32
    m = dict(_consts(L, CL))
    xr = inp["x"][b] if x_rows is None else x_rows
    m["x"] = np.ascontiguousarray(xr[:L], f32)
    m["ctx"] = np.ascontiguousarray(inp["ctx"][b][:CL], f32)
    m["cc"] = np.ascontiguousarray(np.stack([_col(inp["c"][b], 8), _col(inp["c_ctx"], 8)], axis=-1))
    m["final_g_bc"] = np.ascontiguousarray(np.broadcast_to(np.asarray(inp["final_g"], f32), (128, D)))
    for i in layers:
        j = i // 3
        m["ada_w%d" % i] = np.ascontiguousarray(inp["ada_w"][i], f32)
        m["ada_bcol%d" % i] = _col(inp["ada_b"][i], 24)
        m["ada_brow%d" % i] = np.ascontiguousarray(np.broadcast_to(np.asarray(inp["ada_b"][i][2 * D:], f32), (2, D)))
        m["ng_col%d" % i] = _col(inp["norm_g"][i], 8)
        k = KINDS[i]
        if k == 0:
            m["ret_w_in%d" % i] = np.ascontiguousarray(inp["ret_w_in"][j], f32)
            m["ret_w_out%d" % i] = np.ascontiguousarray(inp["ret_w_out"][j], f32)
            dec = np.concatenate([inp["ret_decay"][j][0], inp["ret_decay"][j][1]]).astype(f32)
            m["ret_decay%d" % i] = np.ascontiguousarray(np.broadcast_to(dec, (128, 8)))
        elif k == 1:
            m["gm_w_in%d" % i] = np.ascontiguousarray(inp["gm_w_in"][j], f32)
            m["gm_w_out%d" % i] = np.ascontiguousarray(inp["gm_w_out"][j], f32)
            m["gm_vg%d" % i] = np.ascontiguousarray(np.broadcast_to(np.asarray(inp["gm_vnorm_g"][j], f32), (128, 2048)))
            m["gm_wsT%d" % i] = np.ascontiguousarray(np.transpose(np.asarray(inp["gm_w_s"][j], f32), (2, 0, 1)))
            m["gm_bsT%d" % i] = np.ascontiguousarray(np.asarray(inp["gm_b_s"][j], f32).T)
        else:
            _rw_host(m, inp, i, j)
    return m


def _rw_host(m, inp, i, j):
    f32 = np.float32
    g = lambda k: np.asarray(inp[k][j], f32)
    mu = g("rw_mu")
    m["rw_mu%d" % i] = np.ascontiguousarray(np.stack([_col(mu[p], 8) for p in range(6)], axis=1))
    m["rw_rkvg%d" % i] = np.ascontiguousarray(g("rw_w_rkvg"))
    m["rw_w1%d" % i] = np.ascontiguousarray(g("rw_w1"))
    m["rw_a1%d" % i] = np.ascontiguousarray(g("rw_a1"))
    m["rw_w2%d" % i] = np.ascontiguousarray(g("rw_w2"))
    m["rw_a2%d" % i] = np.ascontiguousarray(g("rw_a2"))
    rows = np.zeros((8, D), f32)
    rows[0:2] = g("rw_w0")
    rows[2:4] = g("rw_a0")
    m["rw_rows%d" % i] = rows
    bc = np.stack([g("rw_k_k"), g("rw_k_a"), g("rw_r_k").reshape(-1), g("rw_lnx_g"), g("rw_lnx_b")], axis=0)
    m["rw_bc%d" % i] = np.ascontiguousarray(np.broadcast_to(bc[None], (128, 5, D)))
    m["rw_wout%d" % i] = np.ascontiguousarray(g("rw_w_out"))
    s_ = np.arange(128)[:, None]
    t_ = np.arange(128)[None, :]
    c0 = f32(-0.6065306597126334)
    fw = np.stack([(s_ < t_), (s_ <= t_), (t_ < s_), (s_ <= t_) * c0], axis=1).astype(f32)
    bw = np.stack([(s_ > t_), (s_ >= t_), (t_ > s_), (s_ >= t_) * c0], axis=1).astype(f32)
    m["rw_masks%d" % i] = np.ascontiguousarray(np.stack([fw, bw], axis=0))
    sel = np.zeros((8, 8, 128), f32)
    for r in range(8):
        sel[r, r, :] = 1.0
    m["rw_sel8%d" % i] = sel
    m["rw_negc%d" % i] = np.full((128, 1), c0, f32)
    blk = lambda n: (s_ // n) == (t_ // n)
    bm = np.stack([blk(16)] + [blk(n) & ~blk(n // 2) for n in (32, 64, 128)], axis=1).astype(f32)
    m["rw_bmask%d" % i] = np.ascontiguousarray(bm)
    cms = []
    for dd in (fw, bw):
        st, stT = dd[:, 0, :], dd[:, 2, :]
        cms.append(np.stack([st * bm[:, 0], st * bm[:, 1], st * bm[:, 2], stT * bm[:, 0], stT * bm[:, 1], stT * bm[:, 2], stT * bm[:, 3]], axis=1))
    m["rw_cmask%d" % i] = np.ascontiguousarray(np.stack(cms, axis=0).astype(f32))


_MODEL_CACHE = {}


def kernel(**inputs):
    inp = {k: np.asarray(v) for k, v in inputs.items()}
    B, L, _ = inp["x"].shape
    layers = (0, 1, 2, 3)
    key = (L, layers)
    if key not in _MODEL_CACHE:
        _MODEL_CACHE[key] = Model(L, 256, layers)
    model = _MODEL_CACHE[key]
    maps = []
    for core in range(NCORES):
        b = core % B
        hm = host_inputs(inp, b, layers, L)
        maps.append({k: hm[k] for k in model.inputs})
    res = run_bass_kernel_spmd(model.nc, maps, core_ids=list(range(NCORES)))
    out = np.stack([np.asarray(res.results[b]["out"], np.float32) for b in range(B)], axis=0)
    return out
```

```python
from contextlib import ExitStack
import numpy as np
import concourse.bass as bass
import concourse.mybir as mybir
from concourse.bass_utils import run_bass_kernel_spmd

F32 = mybir.dt.float32
BF16 = mybir.dt.bfloat16
AF = mybir.ActivationFunctionType
ALU = mybir.AluOpType
AX = mybir.AxisListType

D = 1024
CH = 128
NCORES = 8
_UID = [0]


class Buf:
    __slots__ = ("name", "last_w", "readers", "excl")

    def __init__(self, name, excl=False):
        self.name = name
        self.last_w = None
        self.readers = []
        self.excl = excl


class Tile:
    def __init__(self, t, name, excl=False):
        self.t = t
        self.b = Buf(name, excl)

    def __getitem__(self, k):
        return self.t[k]


class Rot:
    def __init__(self, tiles):
        self.tiles = tiles
        self.i = 0

    def next(self):
        t = self.tiles[self.i % len(self.tiles)]
        self.slot = self.i % len(self.tiles)
        self.i += 1
        return t


class Op:
    __slots__ = ("eng", "fn", "dma_key", "waits", "signal", "sem", "val", "idx", "prog")


def _b(x):
    return x.b if isinstance(x, Tile) else x


class Prog:
    ENGS = ("pe", "act", "dve", "pool", "sp")

    def __init__(self):
        self.ops = []

    def add(self, eng, fn, reads=(), writes=(), dma_key=None):
        op = Op()
        op.eng, op.fn, op.dma_key = eng, fn, dma_key
        op.signal, op.sem, op.val = False, None, 0
        op.idx, op.prog = len(self.ops), self
        reads = [_b(x) for x in reads]
        writes = [_b(x) for x in writes]
        deps = []
        for b in reads:
            if b.last_w is not None:
                deps.append((b.last_w, True))
            if b.excl:
                deps.extend((r, False) for r in b.readers)
        for b in writes:
            if b.last_w is not None:
                deps.append((b.last_w, False))
            deps.extend((r, False) for r in b.readers)
        need, seen = [], set()
        for d, raw in deps:
            if d.prog is not self:
                continue
            if d.dma_key is None and dma_key is None and d.eng == eng:
                if eng == "pe" or not raw:
                    continue
            if d.idx in seen:
                continue
            seen.add(d.idx)
            d.signal = True
            need.append(d)
        op.waits = need
        for b in reads:
            b.readers.append(op)
        for b in writes:
            b.last_w = op
            b.readers = []
        self.ops.append(op)
        return op

    def pe(self, fn, reads=(), writes=()):
        return self.add("pe", fn, reads, writes)

    def act(self, fn, reads=(), writes=()):
        return self.add("act", fn, reads, writes)

    def dve(self, fn, reads=(), writes=()):
        return self.add("dve", fn, reads, writes)

    def pool(self, fn, reads=(), writes=()):
        return self.add("pool", fn, reads, writes)

    def dma(self, out, in_, key, reads=(), writes=(), q="sp", slow=False):
        if slow:
            return self.add(q, lambda e: e.dma_start(out=out, in_=in_, allow_slow_non_contiguous=True),
                            reads, writes, dma_key=key)
        return self.add(q, lambda e: e.dma_start(out=out, in_=in_), reads, writes, dma_key=key)

    def emit(self, nc, stack):
        def keyof(op):
            return ("dma", op.dma_key) if op.dma_key is not None else ("eng", op.eng)
        last = {}
        for op in self.ops:
            last[keyof(op)] = op
        for op in last.values():
            op.signal = True
        cnt, sems = {}, {}
        for op in self.ops:
            if not op.signal:
                continue
            k = keyof(op)
            cnt[k] = cnt.get(k, 0) + (16 if op.dma_key is not None else 1)
            op.val = cnt[k]
            if k not in sems:
                _UID[0] += 1
                sems[k] = nc.alloc_semaphore(name="s%d" % _UID[0])
            op.sem = sems[k]
        self.n_sems = len(sems)
        per_eng = {e: [] for e in self.ENGS}
        for op in self.ops:
            per_eng[op.eng].append(op)
        finals = [(sems[k], cnt[k]) for k in sems]

        def run(engname, e):
            waited = {}
            for op in per_eng[engname]:
                for d in op.waits:
                    key = id(d.sem)
                    if waited.get(key, 0) >= d.val:
                        continue
                    e.wait_ge(d.sem, d.val)
                    waited[key] = d.val
                ins = op.fn(e)
                if op.signal:
                    ins.then_inc(op.sem, 16 if op.dma_key is not None else 1)
            for s, v in finals:
                if waited.get(id(s), 0) < v:
                    e.wait_ge(s, v)

        with nc.Block() as block:
            block.tensor(lambda e: run("pe", e))
            block.scalar(lambda e: run("act", e))
            block.vector(lambda e: run("dve", e))
            block.gpsimd(lambda e: run("pool", e))
            block.sync(lambda e: run("sp", e))
        nc.clear_and_free_semaphores(list(sems.values()))
        nc.all_engine_barrier()


class Phase:
    def __init__(self, nc):
        self.nc = nc
        self.st = ExitStack()
        self.P = Prog()

    def sb(self, name, shape, dt):
        _UID[0] += 1
        t = self.st.enter_context(self.nc.sbuf_tensor("%s_%d" % (name, _UID[0]), list(shape), dt))
        return Tile(t, name)

    def ps(self, name, shape, dt=F32):
        _UID[0] += 1
        t = self.st.enter_context(self.nc.psum_tensor("%s_%d" % (name, _UID[0]), list(shape), dt))
        return Tile(t, name, excl=True)

    def rot(self, name, shape, dt, n):
        return Rot([self.sb("%s%d" % (name, i), shape, dt) for i in range(n)])

    def finish(self):
        self.P.emit(self.nc, self.st)
        self.st.close()


KINDS = (0, 1, 2, 0)
NORM_EPS = 1e-6


class Model:
    def __init__(self, L, CL=256, layers=(0, 1, 2, 3), final_norm=True):
        self.L, self.CL = L, CL
        self.NX, self.NC = L // CH, CL // CH
        self.layers = tuple(layers)
        self.final_norm = final_norm
        self.nc = nc = bass.Bass("TRN2", target_bir_lowering=False)
        self.inputs = {}
        self.outer = ExitStack()
        NT = L + CL

        def din(name, shape):
            self.inputs[name] = tuple(shape)
            return nc.dram_tensor(name, list(shape), F32, kind="ExternalInput").ap()

        self.x_in = din("x", [L, D])
        self.c_in = din("ctx", [CL, D])
        self.cc = din("cc", [128, 8, 2])
        self.ident_in = din("ident", [128, 128])
        self.rope_in = din("rope", [NT, 2, 128])
        self.dmat_in = din("dmat", [128, 6, 128])
        self.jcol_in = din("jcol", [128, 2])
        self.sel_in = din("sel2", [2, 2, 128])
        self.fg_in = din("final_g_bc", [128, D])
        self.lw = {}
        for i in self.layers:
            w = {}
            w["ada_w"] = din("ada_w%d" % i, [D, 3 * D])
            w["ada_bcol"] = din("ada_bcol%d" % i, [128, 24])
            w["ada_brow"] = din("ada_brow%d" % i, [2, D])
            w["ng_col"] = din("ng_col%d" % i, [128, 8])
            k = KINDS[i]
            if k == 0:
                w["w_in"] = din("ret_w_in%d" % i, [D, 6144])
                w["w_out"] = din("ret_w_out%d" % i, [2048, D])
                w["decay"] = din("ret_decay%d" % i, [128, 8])
            elif k == 1:
                w["w_in"] = din("gm_w_in%d" % i, [D, 6144])
                w["w_out"] = din("gm_w_out%d" % i, [2048, D])
                w["vg"] = din("gm_vg%d" % i, [128, 2048])
                w["wsT"] = din("gm_wsT%d" % i, [128, 8, 128])
                w["bsT"] = din("gm_bsT%d" % i, [128, 8])
            else:
                self._rw_inputs(w, i, din)
            self.lw[i] = w
        self.out = nc.dram_tensor("out", [L, D], F32, kind="ExternalOutput").ap()
        self.xs = [nc.dram_tensor("xs%d" % i, [NT, D], F32).ap() for i in range(2)]
        o = self.outer
        self.ident = Tile(o.enter_context(nc.sbuf_tensor("identb", [128, 128], BF16)), "ident")
        self.ident32 = Tile(o.enter_context(nc.sbuf_tensor("ident32", [128, 128], F32)), "ident32")
        self.mod = Tile(o.enter_context(nc.sbuf_tensor("mod", [128, 2, 2, 8], F32)), "mod")
        self.gate_dram = nc.dram_tensor("gate_dram", [2, 128, D], F32).ap()
        self.sel = Tile(o.enter_context(nc.sbuf_tensor("sel", [2, 2, 128], F32)), "sel")
        self._build()
        self.outer.close()

    def chunks_fwd(self):
        return [("c", i) for i in range(self.NC)] + [("x", i) for i in range(self.NX)]

    def chunks_bwd(self):
        return [("c", i) for i in reversed(range(self.NC))] + [("x", i) for i in reversed(range(self.NX))]

    def row0(self, ck):
        return ck[1] * CH if ck[0] == "c" else self.CL + ck[1] * CH

    def src_ap(self, li, ck):
        r0 = self.row0(ck)
        if li == 0:
            return (self.c_in if ck[0] == "c" else self.x_in)[ck[1] * CH:(ck[1] + 1) * CH, :]
        return self.xs[(li - 1) % 2][r0:r0 + CH, :]

    def dst_ap(self, li, ck):
        r0 = self.row0(ck)
        return self.xs[li % 2][r0:r0 + CH, :]

    def _build(self):
        nc = self.nc
        ph = Phase(nc)
        P = ph.P
        t32 = ph.sb("id32", [128, 128], F32)
        P.dma(self.ident32[:], self.ident_in[:, :], "c0", writes=[self.ident32])
        P.dve(lambda e: e.tensor_copy(out=self.ident[:], in_=self.ident32[:]), [self.ident32], [self.ident])
        P.dma(self.sel[:], self.sel_in[:, :, :], "c1", writes=[self.sel])
        ph.finish()
        for li, i in enumerate(self.layers):
            last = (li == len(self.layers) - 1)
            k = KINDS[i]
            if k == 0:
                self.retention_layer(li, i, last)
            elif k == 1:
                self.gmlp_layer(li, i, last)
            else:
                self.rwkv_layer(li, i, last)

    def emit_mod(self, ph, i):
        P, w = ph.P, self.lw[i]
        cc = ph.sb("cc", [128, 8, 2], F32)
        sg = ph.sb("sg", [128, 8, 2], F32)
        bcol = ph.sb("bcol", [128, 24], F32)
        brow = ph.sb("brow", [2, D], F32)
        ng = ph.sb("ng", [128, 8], F32)
        grow = ph.sb("grow", [2, D], F32)
        gate_t = [ph.sb("gate_t%d" % j, [128, D], F32) for j in range(2)]
        aw = ph.sb("adaw", [128, 8, 3 * D], F32)
        awk = [Tile(aw.t[:, k, :], "adaw%d" % k) for k in range(8)]
        pcol = ph.ps("pcol", [128, 16, 2], F32)
        prow = [ph.ps("prow%d" % n, [128, 512], F32) for n in range(2)]
        P.dma(cc[:], self.cc[:, :, :], "m0", writes=[cc])
        P.dma(bcol[:], w["ada_bcol"][:, :], "m1", writes=[bcol])
        P.dma(brow[:], w["ada_brow"][:, :], "m2", writes=[brow])
        P.dma(ng[:], w["ng_col"][:, :], "m3", writes=[ng])
        for k in range(8):
            P.dma(awk[k][:], w["ada_w"][k * 128:(k + 1) * 128, :], "ada%d" % k, writes=[awk[k]], q=("sp" if k % 2 == 0 else "pool"))
        P.act(lambda e: e.activation(out=sg[:], in_=cc[:], func=AF.Sigmoid), [cc], [sg])
        P.dve(lambda e: e.tensor_tensor(out=sg[:], in0=sg[:], in1=cc[:], op=ALU.mult), [sg, cc], [sg])
        for j in range(16):
            for k in range(8):
                P.pe(lambda e, j=j, k=k: e.matmul(pcol[:, j, :], lhsT=awk[k][:, j * 128:(j + 1) * 128], rhs=sg[:, k, :],
                                                  start=(k == 0), stop=(k == 7)), [awk[k], sg], [pcol])
        for n in range(2):
            for k in range(8):
                P.pe(lambda e, n=n, k=k: e.matmul(prow[n][0:2, :], lhsT=sg[:, k, :], rhs=awk[k][:, 2048 + n * 512:2048 + (n + 1) * 512],
                                                  start=(k == 0), stop=(k == 7)), [awk[k], sg], [prow[n]])
        tmp = ph.sb("modtmp", [128, 16, 2], F32)
        P.dve(lambda e: e.tensor_tensor(out=tmp[:], in0=pcol[:], in1=bcol[:, 0:16].unsqueeze(2).to_broadcast([128, 16, 2]), op=ALU.add),
              [pcol, bcol], [tmp])
        mod = self.mod
        for j in range(2):
            P.dve(lambda e, j=j: e.scalar_tensor_tensor(out=mod[:, 0, j, :], in0=tmp[:, 8:16, j], scalar=1.0, in1=ng[:], op0=ALU.add, op1=ALU.mult),
                  [tmp, ng], [mod])
            P.dve(lambda e, j=j: e.tensor_copy(out=mod[:, 1, j, :], in_=tmp[:, 0:8, j]), [tmp], [mod])
        for n in range(2):
            P.dve(lambda e, n=n: e.tensor_tensor(out=grow[:, n * 512:(n + 1) * 512], in0=prow[n][0:2, :], in1=brow[:, n * 512:(n + 1) * 512], op=ALU.add),
                  [prow[n], brow], [grow])
        for j in range(2):
            for n in range(2):
                P.pe(lambda e, j=j, n=n: e.matmul(prow[n][:, :], lhsT=self.sel[:, j, :], rhs=grow[:, n * 512:(n + 1) * 512], start=True, stop=True),
                     [self.sel, grow], [prow[n]])
                P.act(lambda e, j=j, n=n: e.activation(out=gate_t[j][:, n * 512:(n + 1) * 512], in_=prow[n][:, :], func=AF.Copy),
                      [prow[n]], [gate_t[j]])
        for j in range(2):
            P.dma(self.gate_dram[j], gate_t[j][:], "gst%d" % j, reads=[gate_t[j]], q="pool")

    def load_w(self, ph, Wt, src, col0, ncols, KT):
        P = ph.P
        views = []
        for k in range(KT):
            v = Tile(Wt.t[:, k, :], "%s_k%d" % (Wt.b.name, k))
            views.append(v)
            step = 2048
            for c0 in range(0, ncols, step):
                wdt = min(step, ncols - c0)
                P.dma(v.t[:, c0:c0 + wdt], src[k * 128:(k + 1) * 128, col0 + c0:col0 + c0 + wdt],
                      "w%s%d_%d" % (Wt.b.name, k, c0), writes=[v], q="pool")
        return views

    def front(self, ph, R, src, which, src_reads=()):
        P = ph.P
        mod = self.mod
        xt = R["xt"].next()
        P.dma(xt[:], src, "xt%d" % R["xt"].slot, reads=list(src_reads), writes=[xt])
        st = R["st"].next()
        junk = R["junk"]
        P.act(lambda e: e.activation(out=junk[:], in_=xt[:], func=AF.Square, accum_out=st[:, 0:1]), [xt], [junk, st])
        P.act(lambda e: e.activation(out=st[:, 1:2], in_=st[:, 0:1], func=AF.Sqrt, scale=1.0 / D, bias=NORM_EPS), [st], [st])
        P.dve(lambda e: e.reciprocal(out=st[:, 2:3], in_=st[:, 1:2]), [st], [st])
        xn = R["xn"].next()
        P.dve(lambda e: e.tensor_scalar(out=xn[:], in0=xt[:], scalar1=st[:, 2:3], scalar2=None, op0=ALU.mult), [xt, st], [xn])
        ptr = R["ptrx"]
        for k in range(8):
            P.pe(lambda e, k=k: e.transpose(out=ptr[:, k, :], in_=xn[:, k * 128:(k + 1) * 128], identity=self.ident[:]),
                 [xn, self.ident], [ptr])
        hT = R["hT"].next()
        for k in range(8):
            P.act(lambda e, k=k: e.activation(out=hT[:, k, :], in_=ptr[:, k, :], func=AF.Identity,
                                              scale=mod[:, 0, which, k:k + 1], bias=mod[:, 1, which, k:k + 1]),
                  [ptr, mod], [hT])
        return hT, xt

    def front_bufs(self, ph, n_xn=2, n_xt=2, n_hT=2, junk=True):
        return {
            "xt": ph.rot("xt", [128, D], F32, n_xt),
            "st": ph.rot("st", [128, 4], F32, 4),
            "junk": ph.sb("junk", [128, D], BF16) if junk else None,
            "xn": ph.rot("xn", [128, D], BF16, n_xn),
            "hT": ph.rot("hT", [128, 8, 128], BF16, n_hT),
            "ptrx": ph.ps("ptrx", [128, 8, 128], BF16),
        }

    def tail(self, ph, R, gated, Wo, li, ck, xres, last, KT=16):
        P = ph.P
        which = 1 if ck[0] == "c" else 0
        gT = R["gT"].next()
        for half in range(KT // 8):
            ptg = R["ptg"][half]
            for k in range(8):
                kk = half * 8 + k
                P.pe(lambda e, k=k, kk=kk, ptg=ptg: e.transpose(out=ptg[:, k, :], in_=gated[:, kk * 128:(kk + 1) * 128], identity=self.ident[:]),
                     [gated, self.ident], [ptg])
            if half == 0:
                P.act(lambda e, half=half, ptg=ptg: e.activation(out=gT[:, half * 8:(half + 1) * 8, :], in_=ptg[:], func=AF.Copy), [ptg], [gT])
            else:
                P.dve(lambda e, half=half, ptg=ptg: e.tensor_copy(out=gT[:, half * 8:(half + 1) * 8, :], in_=ptg[:]), [ptg], [gT])
        xo = R["xo"].next()
        for n in range(2):
            po = R["pout"].next() if isinstance(R["pout"], Rot) else R["pout"]
            for k in range(KT):
                P.pe(lambda e, n=n, k=k, po=po: e.matmul(po[:], lhsT=gT[:, k, :], rhs=Wo[k][:, n * 512:(n + 1) * 512], start=(k == 0), stop=(k == KT - 1)),
                     [gT, Wo[k]], [po])
            P.dve(lambda e, n=n, po=po: e.tensor_tensor(out=xo[:, n * 512:(n + 1) * 512], in0=po[:], in1=self.gate[which][:, n * 512:(n + 1) * 512], op=ALU.mult),
                  [po, self.gate[which]], [xo])
        P.pool(lambda e: e.tensor_tensor(out=xo[:], in0=xo[:], in1=xres[:], op=ALU.add), [xo, xres], [xo])
        if last and self.final_norm:
            st = R["st"].next()
            junk = R["junk"]
            P.act(lambda e: e.activation(out=junk[:], in_=xo[:], func=AF.Square, accum_out=st[:, 0:1]), [xo], [junk, st])
            P.act(lambda e: e.activation(out=st[:, 1:2], in_=st[:, 0:1], func=AF.Sqrt, scale=1.0 / D, bias=NORM_EPS), [st], [st])
            P.dve(lambda e: e.reciprocal(out=st[:, 2:3], in_=st[:, 1:2]), [st], [st])
            P.dve(lambda e: e.scalar_tensor_tensor(out=xo[:], in0=xo[:], scalar=st[:, 2:3], in1=self.fg[:], op0=ALU.mult, op1=ALU.mult),
                  [xo, st, self.fg], [xo])
            dst = self.out[ck[1] * CH:(ck[1] + 1) * CH, :]
        elif last:
            dst = self.out[ck[1] * CH:(ck[1] + 1) * CH, :]
        else:
            dst = self.dst_ap(li, ck)
        P.dma(dst, xo[:], "xo%d" % R["xo"].slot, reads=[xo], q="pool")

    def tail_bufs(self, ph, n_gT=2, n_xo=2, last=False, pout=True):
        if last and self.final_norm:
            self.fg = ph.sb("fg", [128, D], F32)
            ph.P.dma(self.fg[:], self.fg_in[:, :], "fgld", writes=[self.fg])
        self.gate = [ph.sb("gate%d" % j, [128, D], F32) for j in range(2)]
        for j in range(2):
            ph.P.dma(self.gate[j][:], self.gate_dram[j], "gld%d" % j, writes=[self.gate[j]])
        return {
            "gT": ph.rot("gT", [128, 16, 128], BF16, n_gT),
            "ptg": [ph.ps("ptg%d" % h, [128, 8, 128], BF16) for h in range(2)],
            "pout": ph.ps("pout", [128, 512], F32) if pout else None,
            "xo": ph.rot("xo", [128, D], F32, n_xo),
        }

    def retention_layer(self, li, i, last):
        nc, w = self.nc, self.lw[i]
        NCH = self.NX + self.NC
        if not hasattr(self, "scr_q"):
            self.scr_q = nc.dram_tensor("scr_q", [NCH, 128, 1024], BF16).ap()
            self.scr_k = nc.dram_tensor("scr_k", [NCH, 128, 1024], BF16).ap()
            self.scr_v = nc.dram_tensor("scr_v", [NCH, 128, 2048], BF16).ap()
            self.scr_o = nc.dram_tensor("scr_o", [NCH, 128, 2048], F32).ap()
            self.rt_cd = Tile(self.outer.enter_context(nc.sbuf_tensor("rt_cd", [128, 8], F32)), "rt_cd")
        ph = Phase(nc)
        self.emit_mod(ph, i)
        ph.finish()

        ph = Phase(nc)
        P = ph.P
        Wt = ph.sb("Wqkv", [128, 8, 4096], BF16)
        Wk = self.load_w(ph, Wt, w["w_in"], 0, 4096, 8)
        R = self.front_bufs(ph)
        dec = ph.sb("dec", [128, 8], F32)
        lg = ph.sb("lg", [128, 8], F32)
        dm = ph.sb("dm", [128, 6, 128], F32)
        jc = ph.sb("jc", [128, 2], F32)
        tA = ph.sb("tA", [128, 128], F32)
        tB = ph.sb("tB", [128, 128], F32)
        MT = ph.sb("MT", [128, 4, 128], F32)
        QDf = ph.sb("QDf", [128, 8, 128], BF16)
        QDb = ph.sb("QDb", [128, 8, 128], BF16)
        kdec = ph.sb("kdec", [128, 8], F32)
        cd = self.rt_cd
        P.dma(dec[:], w["decay"][:, :], "t0", writes=[dec])
        P.dma(dm[:], self.dmat_in[:, :, :], "t1", writes=[dm])
        P.dma(jc[:], self.jcol_in[:, :], "t2", writes=[jc])
        P.act(lambda e: e.activation(out=lg[:], in_=dec[:], func=AF.Exp, scale=-1.0), [dec], [lg])
        P.act(lambda e: e.activation(out=lg[:], in_=lg[:], func=AF.Ln, bias=1.0), [lg], [lg])
        P.dve(lambda e: e.tensor_scalar(out=lg[:], in0=lg[:], scalar1=-1.0, scalar2=None, op0=ALU.mult), [lg], [lg])
        P.act(lambda e: e.activation(out=cd[:], in_=lg[:], func=AF.Exp, scale=128.0), [lg], [cd])
        P.dve(lambda e: e.tensor_scalar(out=kdec[:, 0:4], in0=lg[:, 0:4], scalar1=jc[:, 0:1], scalar2=None, op0=ALU.mult), [lg, jc], [kdec])
        P.dve(lambda e: e.tensor_scalar(out=kdec[:, 4:8], in0=lg[:, 4:8], scalar1=jc[:, 1:2], scalar2=None, op0=ALU.mult), [lg, jc], [kdec])
        P.act(lambda e: e.activation(out=kdec[:], in_=kdec[:], func=AF.Exp), [kdec], [kdec])
        P.dve(lambda e: e.tensor_scalar(out=kdec[:], in0=kdec[:], scalar1=0.0625, scalar2=None, op0=ALU.mult), [kdec], [kdec])
        for h in range(4):
            P.act(lambda e, h=h: e.activation(out=tA[:], in_=dm[:, 0, :], func=AF.Exp, scale=lg[:, h:h + 1]), [dm, lg, MT], [tA])
            P.act(lambda e, h=h: e.activation(out=tB[:], in_=dm[:, 1, :], func=AF.Exp, scale=lg[:, 4 + h:5 + h]), [dm, lg, MT], [tB])
            P.dve(lambda e: e.tensor_tensor(out=tA[:], in0=tA[:], in1=dm[:, 2, :], op=ALU.mult), [tA, dm], [tA])
            P.dve(lambda e: e.tensor_tensor(out=tB[:], in0=tB[:], in1=dm[:, 3, :], op=ALU.mult), [tB, dm], [tB])
            P.dve(lambda e: e.tensor_tensor(out=tA[:], in0=tA[:], in1=tB[:], op=ALU.add), [tA, tB], [tA])
            P.dve(lambda e, h=h: e.tensor_scalar(out=MT[:, h, :], in0=tA[:], scalar1=0.0625, scalar2=None, op0=ALU.mult), [tA], [MT])
            for r in range(2):
                P.act(lambda e, h=h, r=r: e.activation(out=QDf[:, 2 * h + r, :], in_=dm[:, 4, :], func=AF.Exp, scale=lg[:, h:h + 1]), [dm, lg], [QDf])
                P.act(lambda e, h=h, r=r: e.activation(out=QDb[:, 2 * h + r, :], in_=dm[:, 5, :], func=AF.Exp, scale=lg[:, 4 + h:5 + h]), [dm, lg], [QDb])
        mm = Rot([ph.ps("mm%d" % j, [128, 512], F32) for j in range(2)])
        ptq = ph.ps("ptq", [128, 8, 128], BF16)
        ptk = ph.ps("ptk", [128, 8, 128], BF16)
        Gp = Rot([ph.ps("Gp%d" % j, [128, 512], F32) for j in range(3)])
        cs = ph.rot("cs", [128, 2, 2, 128], F32, 2)
        rtmp = ph.rot("rtmp", [128, 4, 2, 128], F32, 2)
        q_r = ph.sb("q_r", [128, 1024], BF16)
        k_r = ph.sb("k_r", [128, 1024], BF16)
        qT = ph.rot("qT", [128, 8, 128], BF16, 2)
        kT = ph.rot("kT", [128, 8, 128], BF16, 2)
        qTf = ph.rot("qTf", [128, 8, 128], BF16, 2)
        qTb = ph.rot("qTb", [128, 8, 128], BF16, 2)
        kf = ph.rot("kf", [128, 1024], BF16, 2)
        kb = ph.rot("kb", [128, 1024], BF16, 2)
        vsb = ph.rot("vsb", [128, 2048], BF16, 2)
        sT = ph.rot("sT", [128, 4, 128], BF16, 2)
        osb = ph.rot("osb", [128, 2048], F32, 2)
        Sbf = ph.sb("Sbf", [128, 8, 512], BF16)
        Sbk = [Tile(Sbf.t[:, k, :], "Sb%d" % k) for k in range(8)]
        for k in range(8):
            P.pool(lambda e, k=k: e.memset(Sbk[k][:], 0.0), [], [Sbk[k]])
        cdI = self.ret_cdI(ph)

        for ck in self.chunks_fwd():
            which = 1 if ck[0] == "c" else 0
            r0 = self.row0(ck)
            cidx = r0 // CH
            hT, xt = self.front(ph, R, self.src_ap(li, ck), which)
            c_ = cs.next()
            for r in range(2):
                P.dma(c_[:, :, r, :], self.rope_in[r0:r0 + CH, :, :], "cs%d_%d" % (cs.slot, r), writes=[c_])
            v_ = vsb.next()
            for n in range(8):
                bank = mm.next()
                for k in range(8):
                    P.pe(lambda e, bank=bank, n=n, k=k, hT=hT: e.matmul(bank[:], lhsT=hT[:, k, :], rhs=Wk[k][:, n * 512:(n + 1) * 512],
                                                                        start=(k == 0), stop=(k == 7)), [hT, Wk[k]], [bank])
                if n < 4:
                    dst = q_r if n < 2 else k_r
                    tmp = rtmp.next()
                    pb = bank[:].rearrange("p (h t c) -> p h t c", h=2, t=2)
                    t1, t2 = pb[:, :, 0, :], pb[:, :, 1, :]
                    cos2, sin2 = c_[:, 0, :, :], c_[:, 1, :, :]
                    P.dve(lambda e, tmp=tmp, t1=t1, cos2=cos2: e.tensor_tensor(out=tmp[:, 0], in0=t1, in1=cos2, op=ALU.mult), [bank, c_], [tmp])
                    P.dve(lambda e, tmp=tmp, t2=t2, sin2=sin2: e.tensor_tensor(out=tmp[:, 1], in0=t2, in1=sin2, op=ALU.mult), [bank, c_], [tmp])
                    P.dve(lambda e, tmp=tmp, t1=t1, sin2=sin2: e.tensor_tensor(out=tmp[:, 2], in0=t1, in1=sin2, op=ALU.mult), [bank, c_], [tmp])
                    P.dve(lambda e, tmp=tmp, t2=t2, cos2=cos2: e.tensor_tensor(out=tmp[:, 3], in0=t2, in1=cos2, op=ALU.mult), [bank, c_], [tmp])
                    dv = dst[:].rearrange("p (h t c) -> p h t c", h=4, t=2)
                    h0 = 2 * (n % 2)
                    P.pool(lambda e, tmp=tmp, dv=dv, h0=h0: e.tensor_tensor(out=dv[:, h0:h0 + 2, 0, :], in0=tmp[:, 0], in1=tmp[:, 1], op=ALU.subtract),
                           [tmp], [dst])
                    P.pool(lambda e, tmp=tmp, dv=dv, h0=h0: e.tensor_tensor(out=dv[:, h0:h0 + 2, 1, :], in0=tmp[:, 2], in1=tmp[:, 3], op=ALU.add),
                           [tmp], [dst])
                else:
                    P.act(lambda e, bank=bank, n=n, v_=v_: e.activation(out=v_[:, (n - 4) * 512:(n - 3) * 512], in_=bank[:], func=AF.Copy), [bank], [v_])
            qT_, kT_, qTf_, qTb_ = qT.next(), kT.next(), qTf.next(), qTb.next()
            for k in range(8):
                P.pe(lambda e, k=k: e.transpose(out=ptq[:, k, :], in_=q_r[:, k * 128:(k + 1) * 128], identity=self.ident[:]), [q_r, self.ident], [ptq])
            P.act(lambda e, qT_=qT_: e.activation(out=qT_[:], in_=ptq[:], func=AF.Copy), [ptq], [qT_])
            for k in range(8):
                P.pe(lambda e, k=k: e.transpose(out=ptk[:, k, :], in_=k_r[:, k * 128:(k + 1) * 128], identity=self.ident[:]), [k_r, self.ident], [ptk])
            P.act(lambda e, kT_=kT_: e.activation(out=kT_[:], in_=ptk[:], func=AF.Copy), [ptk], [kT_])
            P.dve(lambda e, qT_=qT_, qTf_=qTf_: e.tensor_tensor(out=qTf_[:], in0=qT_[:], in1=QDf[:], op=ALU.mult), [qT_, QDf], [qTf_])
            P.dve(lambda e, qT_=qT_, qTb_=qTb_: e.tensor_tensor(out=qTb_[:], in0=qT_[:], in1=QDb[:], op=ALU.mult), [qT_, QDb], [qTb_])
            kf_, kb_ = kf.next(), kb.next()
            krv = k_r[:].rearrange("p (h c) -> p h c", h=4)
            P.pool(lambda e, kf_=kf_: e.tensor_tensor(out=kf_[:].rearrange("p (h c) -> p h c", h=4), in0=krv,
                                                      in1=kdec[:, 0:4].unsqueeze(2).to_broadcast([128, 4, 256]), op=ALU.mult), [k_r, kdec], [kf_])
            P.pool(lambda e, kb_=kb_: e.tensor_tensor(out=kb_[:].rearrange("p (h c) -> p h c", h=4), in0=krv,
                                                      in1=kdec[:, 4:8].unsqueeze(2).to_broadcast([128, 4, 256]), op=ALU.mult), [k_r, kdec], [kb_])
            psc = Gp.next()
            for h in range(4):
                for hf in range(2):
                    P.pe(lambda e, h=h, hf=hf, kT_=kT_, qT_=qT_, psc=psc: e.matmul(psc[:, h * 128:(h + 1) * 128], lhsT=kT_[:, 2 * h + hf, :], rhs=qT_[:, 2 * h + hf, :],
                                                                                   start=(hf == 0), stop=(hf == 1)), [kT_, qT_], [psc])
            sT_ = sT.next()
            P.dve(lambda e, sT_=sT_, psc=psc: e.tensor_tensor(out=sT_[:], in0=psc[:].rearrange("p (h t) -> p h t", h=4), in1=MT[:], op=ALU.mult), [psc, MT], [sT_])
            o_ = osb.next()
            for h in range(4):
                po = Gp.next()
                P.pe(lambda e, h=h, sT_=sT_, v_=v_, po=po: e.matmul(po[:], lhsT=sT_[:, h, :], rhs=v_[:, h * 512:(h + 1) * 512], start=True, stop=False), [sT_, v_], [po])
                for hf in range(2):
                    kt = 2 * h + hf
                    P.pe(lambda e, kt=kt, hf=hf, qTf_=qTf_, po=po: e.matmul(po[:], lhsT=qTf_[:, kt, :], rhs=Sbk[kt][:], start=False, stop=(hf == 1)),
                         [qTf_, Sbk[kt]], [po])
                P.act(lambda e, h=h, o_=o_, po=po: e.activation(out=o_[:, h * 512:(h + 1) * 512], in_=po[:], func=AF.Copy), [po], [o_])
            self.ret_state_update(P, Gp, kf_, v_, Sbk, cdI, 0)
            P.dma(self.scr_q[cidx].rearrange("p (k t) -> p k t", k=8), qTb_[:], "sq%d" % qTb.slot, reads=[qTb_], q="pool")
            P.dma(self.scr_k[cidx], kb_[:], "sk%d" % kb.slot, reads=[kb_], q="pool")
            P.dma(self.scr_v[cidx], v_[:], "sv%d" % vsb.slot, reads=[v_], q="pool")
            P.dma(self.scr_o[cidx], o_[:], "so%d" % osb.slot, reads=[o_], q="pool")
        ph.finish()

        ph = Phase(nc)
        P = ph.P
        Wzt = ph.sb("Wz", [128, 8, 2048], BF16)
        Wz = self.load_w(ph, Wzt, w["w_in"], 4096, 2048, 8)
        Wot = ph.sb("Wo", [128, 16, 1024], BF16)
        Wo = self.load_w(ph, Wot, w["w_out"], 0, 1024, 16)
        R = self.front_bufs(ph)
        R.update(self.tail_bufs(ph, last=last, pout=False))
        mm = Rot([ph.ps("mm%d" % j, [128, 512], F32) for j in range(2)])
        Gp = Rot([ph.ps("Gp%d" % j, [128, 512], F32) for j in range(3)])
        R["pout"] = Gp
        qTb = ph.rot("qTb", [128, 8, 128], BF16, 2)
        kb = ph.rot("kb", [128, 1024], BF16, 2)
        vsb = ph.rot("vsb", [128, 2048], BF16, 2)
        osb = ph.rot("osb", [128, 2048], F32, 2)
        sz = ph.rot("sz", [128, 2048], F32, 2)
        gated = ph.rot("gated", [128, 2048], BF16, 2)
        st4 = ph.rot("st4", [128, 12], F32, 2)
        Sbf = ph.sb("Sbf", [128, 8, 512], BF16)
        Sbk = [Tile(Sbf.t[:, k, :], "Sb%d" % k) for k in range(8)]
        for k in range(8):
            P.pool(lambda e, k=k: e.memset(Sbk[k][:], 0.0), [], [Sbk[k]])
        cdI = self.ret_cdI(ph)
        cd = self.rt_cd
        for ck in self.chunks_bwd():
            which = 1 if ck[0] == "c" else 0
            cidx = self.row0(ck) // CH
            need_out = not (last and ck[0] == "c")
            kb_, v_ = kb.next(), vsb.next()
            P.dma(kb_[:], self.scr_k[cidx], "lk%d" % kb.slot, writes=[kb_])
            P.dma(v_[:], self.scr_v[cidx], "lv%d" % vsb.slot, writes=[v_])
            if need_out:
                q_, o_ = qTb.next(), osb.next()
                P.dma(q_[:], self.scr_q[cidx].rearrange("p (k t) -> p k t", k=8), "lq%d" % qTb.slot, writes=[q_])
                P.dma(o_[:], self.scr_o[cidx], "lo%d" % osb.slot, writes=[o_])
                hT, xt = self.front(ph, R, self.src_ap(li, ck), which)
                sz_ = sz.next()
                for n in range(4):
                    bank = mm.next()
                    for k in range(8):
                        P.pe(lambda e, bank=bank, n=n, k=k, hT=hT: e.matmul(bank[:], lhsT=hT[:, k, :], rhs=Wz[k][:, n * 512:(n + 1) * 512],
                                                                            start=(k == 0), stop=(k == 7)), [hT, Wz[k]], [bank])
                    P.act(lambda e, bank=bank, n=n, sz_=sz_: e.activation(out=sz_[:, n * 512:(n + 1) * 512], in_=bank[:], func=AF.Silu), [bank], [sz_])
                s4 = st4.next()
                junk = R["junk"]
                for h in range(4):
                    po = Gp.next()
                    for hf in range(2):
                        kt = 2 * h + hf
                        P.pe(lambda e, kt=kt, hf=hf, q_=q_, po=po: e.matmul(po[:], lhsT=q_[:, kt, :], rhs=Sbk[kt][:], start=(hf == 0), stop=(hf == 1)),
                             [q_, Sbk[kt]], [po])
                    P.dve(lambda e, h=h, o_=o_, po=po: e.tensor_tensor(out=o_[:, h * 512:(h + 1) * 512], in0=po[:], in1=o_[:, h * 512:(h + 1) * 512], op=ALU.add),
                          [po, o_], [o_])
                    P.act(lambda e, h=h, o_=o_, s4=s4: e.activation(out=junk[:, 0:512], in_=o_[:, h * 512:(h + 1) * 512], func=AF.Square, accum_out=s4[:, h:h + 1]),
                          [o_], [junk, s4])
                P.act(lambda e, s4=s4: e.activation(out=s4[:, 4:8], in_=s4[:, 0:4], func=AF.Sqrt, scale=1.0 / 512, bias=NORM_EPS), [s4], [s4])
                P.dve(lambda e, s4=s4: e.reciprocal(out=s4[:, 8:12], in_=s4[:, 4:8]), [s4], [s4])
                g_ = gated.next()
                for h in range(4):
                    P.dve(lambda e, h=h, o_=o_, s4=s4, sz_=sz_, g_=g_: e.scalar_tensor_tensor(
                        out=g_[:, h * 512:(h + 1) * 512], in0=o_[:, h * 512:(h + 1) * 512], scalar=s4[:, 8 + h:9 + h],
                        in1=sz_[:, h * 512:(h + 1) * 512], op0=ALU.mult, op1=ALU.mult), [o_, s4, sz_], [g_])
                self.tail(ph, R, g_, Wo, li, ck, xt, last)
            self.ret_state_update(P, Gp, kb_, v_, Sbk, cdI, 4)
        ph.finish()

    def ret_state_update(self, P, pst_rot, kd, v_, Sbk, cdI, c0):
        for kt in range(8):
            h = kt // 2
            pst = pst_rot.next()
            P.pe(lambda e, kt=kt, h=h, pst=pst: e.matmul(pst[:], lhsT=cdI[:, c0 + h, :], rhs=Sbk[kt][:], start=True, stop=False), [cdI, Sbk[kt]], [pst])
            P.pe(lambda e, kt=kt, h=h, pst=pst: e.matmul(pst[:], lhsT=kd[:, kt * 128:(kt + 1) * 128], rhs=v_[:, h * 512:(h + 1) * 512], start=False, stop=True),
                 [kd, v_], [pst])
            if kt % 2 == 0:
                P.act(lambda e, kt=kt, pst=pst: e.activation(out=Sbk[kt][:], in_=pst[:], func=AF.Copy), [pst], [Sbk[kt]])
            else:
                P.dve(lambda e, kt=kt, pst=pst: e.tensor_copy(out=Sbk[kt][:], in_=pst[:]), [pst], [Sbk[kt]])

    def ret_cdI(self, ph):
        cdI = ph.sb("cdI", [128, 8, 128], BF16)
        for j in range(8):
            ph.P.dve(lambda e, j=j: e.tensor_scalar(out=cdI[:, j, :], in0=self.ident32[:], scalar1=self.rt_cd[:, j:j + 1], scalar2=None, op0=ALU.mult),
                     [self.ident32, self.rt_cd], [cdI])
        return cdI

    def gmlp_layer(self, li, i, last):
        nc, w = self.nc, self.lw[i]
        ph = Phase(nc)
        self.emit_mod(ph, i)
        ph.finish()
        ph = Phase(nc)
        P = ph.P
        Wt = ph.sb("Wuvz", [128, 8, 6144], BF16)
        Wk = self.load_w(ph, Wt, w["w_in"], 0, 6144, 8)
        Wot = ph.sb("Wo", [128, 16, 1024], BF16)
        Wo = self.load_w(ph, Wot, w["w_out"], 0, 1024, 16)
        R = self.front_bufs(ph, n_xn=1)
        R.update(self.tail_bufs(ph, n_gT=1, n_xo=1, last=last))
        mm = Rot([ph.ps("mm%d" % j, [128, 512], F32) for j in range(2)])
        spb = Rot([ph.ps("spb%d" % j, [128, 512], F32) for j in range(2)])
        wsT = ph.sb("wsT", [128, 8, 128], BF16)
        bsT = ph.sb("bsT", [128, 8], F32)
        vg = ph.sb("vg", [128, 2048], F32)
        P.dma(wsT[:], w["wsT"][:, :, :], "g0", writes=[wsT], q="pool")
        P.dma(bsT[:], w["bsT"][:, :], "g1", writes=[bsT])
        P.dma(vg[:], w["vg"][:, :], "g2", writes=[vg])
        usb = ph.rot("usb", [128, 2048], F32, 1)
        vsb = ph.rot("vsb", [128, 2048], F32, 1)
        szb = ph.rot("szb", [128, 2048], BF16, 1)
        vnb = ph.rot("vnb", [128, 2048], BF16, 1)
        gated = ph.rot("gated", [128, 2048], BF16, 1)
        stv = ph.rot("stv", [128, 16], F32, 2)
        junk = R["junk"]
        order = self.chunks_fwd()
        if last:
            order = [ck for ck in order if ck[0] == "x"]
        for ck in order:
            which = 1 if ck[0] == "c" else 0
            hT, xt = self.front(ph, R, self.src_ap(li, ck), which)
            u_, v_, z_, s_ = usb.next(), vsb.next(), szb.next(), stv.next()
            for n in range(12):
                bank = mm.next()
                for k in range(8):
                    P.pe(lambda e, bank=bank, n=n, k=k, hT=hT: e.matmul(bank[:], lhsT=hT[:, k, :], rhs=Wk[k][:, n * 512:(n + 1) * 512],
                                                                        start=(k == 0), stop=(k == 7)), [hT, Wk[k]], [bank])
                if n < 4:
                    P.act(lambda e, bank=bank, n=n, u_=u_: e.activation(out=u_[:, n * 512:(n + 1) * 512], in_=bank[:], func=AF.Copy), [bank], [u_])
                elif n < 8:
                    P.act(lambda e, bank=bank, n=n, v_=v_, s_=s_: e.activation(out=v_[:, (n - 4) * 512:(n - 3) * 512], in_=bank[:], func=AF.Identity,
                                                                              accum_out=s_[:, n - 4:n - 3]), [bank], [v_, s_])
                else:
                    P.act(lambda e, bank=bank, n=n, z_=z_: e.activation(out=z_[:, (n - 8) * 512:(n - 7) * 512], in_=bank[:], func=AF.Silu), [bank], [z_])
            for hh in range(2):
                P.act(lambda e, v_=v_, s_=s_, hh=hh: e.activation(out=junk[:], in_=v_[:, hh * 1024:(hh + 1) * 1024], func=AF.Square,
                                                                  accum_out=s_[:, 11 + hh:12 + hh]), [v_], [junk, s_])
            P.dve(lambda e, s_=s_: e.tensor_tensor(out=s_[:, 4:5], in0=s_[:, 11:12], in1=s_[:, 12:13], op=ALU.add), [s_], [s_])
            P.dve(lambda e, s_=s_: e.tensor_reduce(out=s_[:, 5:6], in_=s_[:, 0:4], axis=AX.X, op=ALU.add), [s_], [s_])
            P.dve(lambda e, s_=s_: e.tensor_scalar(out=s_[:, 5:6], in0=s_[:, 5:6], scalar1=1.0 / 2048, scalar2=None, op0=ALU.mult), [s_], [s_])
            P.dve(lambda e, s_=s_: e.tensor_tensor(out=s_[:, 6:7], in0=s_[:, 5:6], in1=s_[:, 5:6], op=ALU.mult), [s_], [s_])
            P.dve(lambda e, s_=s_: e.scalar_tensor_tensor(out=s_[:, 7:8], in0=s_[:, 4:5], scalar=1.0 / 2048, in1=s_[:, 6:7], op0=ALU.mult, op1=ALU.subtract),
                  [s_], [s_])
            P.act(lambda e, s_=s_: e.activation(out=s_[:, 8:9], in_=s_[:, 7:8], func=AF.Sqrt, scale=1.0, bias=NORM_EPS), [s_], [s_])
            P.dve(lambda e, s_=s_: e.reciprocal(out=s_[:, 9:10], in_=s_[:, 8:9]), [s_], [s_])
            P.dve(lambda e, s_=s_: e.scalar_tensor_tensor(out=s_[:, 10:11], in0=s_[:, 5:6], scalar=-1.0, in1=s_[:, 9:10], op0=ALU.mult, op1=ALU.mult),
                  [s_], [s_])
            P.act(lambda e, v_=v_, s_=s_: e.activation(out=v_[:], in_=v_[:], func=AF.Identity, scale=s_[:, 9:10], bias=s_[:, 10:11]), [v_, s_], [v_])
            vn_ = vnb.next()
            P.dve(lambda e, v_=v_, vn_=vn_: e.tensor_tensor(out=vn_[:], in0=v_[:], in1=vg[:], op=ALU.mult), [v_, vg], [vn_])
            for g in range(8):
                sb_ = spb.next()
                P.pe(lambda e, g=g, sb_=sb_, vn_=vn_: e.matmul(sb_[:, 0:256], lhsT=wsT[:, g, :], rhs=vn_[:, g * 256:(g + 1) * 256], start=True, stop=True),
                     [wsT, vn_], [sb_])
                P.dve(lambda e, g=g, sb_=sb_, u_=u_: e.scalar_tensor_tensor(out=u_[:, g * 256:(g + 1) * 256], in0=sb_[:, 0:256], scalar=bsT[:, g:g + 1],
                                                                           in1=u_[:, g * 256:(g + 1) * 256], op0=ALU.add, op1=ALU.mult), [sb_, bsT, u_], [u_])
            g_ = gated.next()
            P.pool(lambda e, u_=u_, z_=z_, g_=g_: e.tensor_tensor(out=g_[:], in0=u_[:], in1=z_[:], op=ALU.mult), [u_, z_], [g_])
            self.tail(ph, R, g_, Wo, li, ck, xt, last)
        ph.finish()

    def _rw_inputs(self, w, i, din):
        w["mu"] = din("rw_mu%d" % i, [128, 6, 8])
        w["rkvg"] = din("rw_rkvg%d" % i, [4, D, D])
        w["w1"] = din("rw_w1%d" % i, [2, D, 64])
        w["a1"] = din("rw_a1%d" % i, [2, D, 64])
        w["w2"] = din("rw_w2%d" % i, [2, 64, D])
        w["a2"] = din("rw_a2%d" % i, [2, 64, D])
        w["rows"] = din("rw_rows%d" % i, [8, D])
        w["bc"] = din("rw_bc%d" % i, [128, 5, D])
        w["w_out"] = din("rw_wout%d" % i, [D, D])
        w["masks"] = din("rw_masks%d" % i, [2, 128, 4, 128])
        w["sel8"] = din("rw_sel8%d" % i, [8, 8, 128])
        w["negc"] = din("rw_negc%d" % i, [128, 1])
        w["bmask"] = din("rw_bmask%d" % i, [128, 4, 128])
        w["cmask"] = din("rw_cmask%d" % i, [2, 128, 7, 128])

    def rw_shift(self, P, sh, hc, hp, hn, kind):
        if kind == "x":
            P.act(lambda e: e.activation(out=sh[:, 0:2, 1:128], in_=hc[:, 0:2, 0:127], func=AF.Copy), [hc], [sh])
            P.pool(lambda e: e.memset(sh[:, 0:2, :].rearrange("p k (r c) -> p k r c", c=64)[:, :, :, 0:1], 0.0), [], [sh])
            P.act(lambda e: e.activation(out=sh[:, 2:4, 0:127], in_=hc[:, 2:4, 1:128], func=AF.Copy), [hc], [sh])
            P.pool(lambda e: e.memset(sh[:, 2:4, :].rearrange("p k (r c) -> p k r c", c=64)[:, :, :, 63:64], 0.0), [], [sh])
            P.act(lambda e: e.activation(out=sh[:, 4:6, 64:128], in_=hc[:, 4:6, 0:64], func=AF.Copy), [hc], [sh])
            if hp is not None:
                P.act(lambda e: e.activation(out=sh[:, 4:6, 0:64], in_=hp[:, 4:6, 64:128], func=AF.Copy), [hp], [sh])
            else:
                P.pool(lambda e: e.memset(sh[:, 4:6, 0:64], 0.0), [], [sh])
            P.act(lambda e: e.activation(out=sh[:, 6:8, 0:64], in_=hc[:, 6:8, 64:128], func=AF.Copy), [hc], [sh])
            if hn is not None:
                P.act(lambda e: e.activation(out=sh[:, 6:8, 64:128], in_=hn[:, 6:8, 0:64], func=AF.Copy), [hn], [sh])
            else:
                P.pool(lambda e: e.memset(sh[:, 6:8, 64:128], 0.0), [], [sh])
        else:
            P.act(lambda e: e.activation(out=sh[:, 0:4, 1:128], in_=hc[:, 0:4, 0:127], func=AF.Copy), [hc], [sh])
            if hp is not None:
                P.act(lambda e: e.activation(out=sh[:, 0:4, 0:1], in_=hp[:, 0:4, 127:128], func=AF.Copy), [hp], [sh])
            else:
                P.pool(lambda e: e.memset(sh[:, 0:4, 0:1], 0.0), [], [sh])
            P.act(lambda e: e.activation(out=sh[:, 4:8, 0:127], in_=hc[:, 4:8, 1:128], func=AF.Copy), [hc], [sh])
            if hn is not None:
                P.act(lambda e: e.activation(out=sh[:, 4:8, 127:128], in_=hn[:, 4:8, 0:1], func=AF.Copy), [hn], [sh])
            else:
                P.pool(lambda e: e.memset(sh[:, 4:8, 127:128], 0.0), [], [sh])

    def rw_neighbors(self, ck):
        n = self.NC if ck[0] == "c" else self.NX
        p = (ck[0], ck[1] - 1) if ck[1] > 0 else None
        q = (ck[0], ck[1] + 1) if ck[1] < n - 1 else None
        return p, q

    def rw_hcache(self, ph, R, li):
        cache = []

        def get(ck):
            for c, v in cache:
                if c == ck:
                    return v
            which = 1 if ck[0] == "c" else 0
            v = self.front(ph, R, self.src_ap(li, ck), which)
            cache.append((ck, v))
            if len(cache) > 3:
                cache.pop(0)
            return v
        return get

    def rw_mix(self, P, mixr, tmpr, xx, hc, mu, p):
        mix = mixr.next()
        for k in range(8):
            P.dve(lambda e, k=k: e.scalar_tensor_tensor(out=mix[:, k, :], in0=xx[:, k, :], scalar=mu[:, p, k:k + 1], in1=hc[:, k, :],
                                                        op0=ALU.mult, op1=ALU.add), [xx, mu, hc], [mix])
        return mix

    def rwkv_layer(self, li, i, last):
        nc, w = self.nc, self.lw[i]
        NCH = self.NX + self.NC
        if not hasattr(self, "scr_o"):
            self.scr_o = nc.dram_tensor("scr_o", [NCH, 128, 2048], F32).ap()
        if not hasattr(self, "rw_scr_v"):
            self.rw_scr_v = nc.dram_tensor("rw_scr_v", [NCH, 128, 1024], BF16).ap()
            self.rw_dir = nc.dram_tensor("rw_dir", [2, NCH, 128, 4096], BF16).ap()
            self.rw_small = nc.dram_tensor("rw_small", [2, NCH, 128, 32], F32).ap()
        ph = Phase(nc)
        self.emit_mod(ph, i)
        ph.finish()
        self.rw_prep_phase(li, i)
        import os as _os
        for d in range(2):
            self.rw_scan_phase(li, i, d)
            if _os.environ.get("RW_STOP_AFTER_F") == "1":
                return
        self.rw_out_phase(li, i, last)

    def rw_prep_phase(self, li, i):
        nc, w = self.nc, self.lw[i]
        ph = Phase(nc)
        P = ph.P
        C0 = 0.6065306597126334
        Wr = self.load_w(ph, ph.sb("Wr", [128, 8, 1024], BF16), w["rkvg"][0], 0, 1024, 8)
        Wkk = self.load_w(ph, ph.sb("Wk", [128, 8, 1024], BF16), w["rkvg"][1], 0, 1024, 8)
        Wv = self.load_w(ph, ph.sb("Wv", [128, 8, 1024], BF16), w["rkvg"][2], 0, 1024, 8)
        w1 = [self.load_w(ph, ph.sb("w1_%d" % d, [128, 8, 64], BF16), w["w1"][d], 0, 64, 8) for d in range(2)]
        a1 = [self.load_w(ph, ph.sb("a1_%d" % d, [128, 8, 64], BF16), w["a1"][d], 0, 64, 8) for d in range(2)]
        w2 = [ph.sb("w2_%d" % d, [64, 1024], BF16) for d in range(2)]
        a2 = [ph.sb("a2_%d" % d, [64, 1024], BF16) for d in range(2)]
        tri = [ph.sb("tri%d" % d, [128, 128], F32) for d in range(2)]
        for d in range(2):
            P.dma(w2[d][:], w["w2"][d], "w2%d" % d, writes=[w2[d]], q="pool")
            P.dma(a2[d][:], w["a2"][d], "a2%d" % d, writes=[a2[d]], q="pool")
            P.dma(tri[d][:], w["masks"][d][:, 3, :], "tri%d" % d, writes=[tri[d]])
        rows = ph.sb("rows", [8, 1024], F32)
        sel8 = ph.sb("sel8", [8, 8, 128], F32)
        mu = ph.sb("mu", [128, 6, 8], F32)
        negc = ph.sb("negc", [128, 1], F32)
        kk_bc = ph.sb("kk_bc", [128, 1024], F32)
        ka_bc = ph.sb("ka_bc", [128, 1024], F32)
        rk_bc = ph.sb("rk_bc", [128, 1024], F32)
        P.dma(rows[:], w["rows"][:, :], "c0", writes=[rows])
        P.dma(sel8[:], w["sel8"][:, :, :], "c1", writes=[sel8])
        P.dma(mu[:], w["mu"][:, :, :], "c2", writes=[mu])
        P.dma(negc[:], w["negc"][:, :], "c4", writes=[negc])
        P.dma(kk_bc[:], w["bc"][:, 0, :], "c5", writes=[kk_bc])
        P.dma(ka_bc[:], w["bc"][:, 1, :], "c6", writes=[ka_bc])
        P.dma(rk_bc[:], w["bc"][:, 2, :], "c7", writes=[rk_bc])
        R = self.front_bufs(ph, n_xn=2, n_xt=2, n_hT=4)
        get_h = self.rw_hcache(ph, R, li)
        G = Rot([ph.ps("G%d" % j, [128, 512], F32) for j in range(5)])
        psm = [ph.ps("psm%d" % d, [128, 512], F32) for d in range(2)]
        sh = ph.rot("sh", [128, 8, 128], BF16, 2)
        xx = ph.rot("xx", [128, 8, 128], F32, 2)
        mixr = ph.rot("mix", [128, 8, 128], BF16, 4)
        r_sb = ph.sb("r_sb", [128, 1024], F32)
        k_sb = ph.sb("k_sb", [128, 1024], F32)
        kk = ph.sb("kk_sb", [128, 1024], F32)
        v_bf = ph.rot("v_bf", [128, 1024], BF16, 2)
        Wt = [[ph.sb("W%d_%d" % (d, j), [128, 1024], F32) for j in range(4)] for d in range(2)]
        th_bf = [ph.sb("th_bf%d" % d, [64, 128], BF16) for d in range(2)]
        la_bf = [ph.sb("la_bf%d" % d, [64, 128], BF16) for d in range(2)]
        outb = [ph.sb("outb%d" % d, [128, 4096], BF16) for d in range(2)]
        small = [ph.rot("small%d" % d, [128, 32], F32, 2) for d in range(2)]
        sm = ph.rot("sm", [128, 48], F32, 2)
        for d in range(2):
            for t_ in small[d].tiles:
                P.dve(lambda e, t_=t_: e.memset(t_[:], 0.0), [], [t_])

        def chunk_body(ck):
            cidx = self.row0(ck) // CH
            pk, nk = self.rw_neighbors(ck)
            hc = get_h(ck)[0]
            hp = get_h(pk)[0] if pk else None
            hn = get_h(nk)[0] if nk else None
            sh_, xx_ = sh.next(), xx.next()
            self.rw_shift(P, sh_, hc, hp, hn, ck[0])
            P.dve(lambda e: e.tensor_tensor(out=xx_[:], in0=sh_[:], in1=hc[:], op=ALU.subtract), [sh_, hc], [xx_])
            v_, s_ = v_bf.next(), sm.next()
            for p, Wl, dst in ((0, Wr, r_sb), (2, Wkk, k_sb), (3, Wv, v_)):
                mix = self.rw_mix(P, mixr, None, xx_, hc, mu, p)
                for n in range(2):
                    bank = G.next()
                    for k in range(8):
                        P.pe(lambda e, bank=bank, n=n, k=k, mix=mix, Wl=Wl: e.matmul(bank[:], lhsT=mix[:, k, :], rhs=Wl[k][:, n * 512:(n + 1) * 512],
                                                                                    start=(k == 0), stop=(k == 7)), [mix, Wl[k]], [bank])
                    P.act(lambda e, bank=bank, n=n, dst=dst: e.activation(out=dst[:, n * 512:(n + 1) * 512], in_=bank[:], func=AF.Copy), [bank], [dst])
            P.dma(self.rw_scr_v[cidx], v_[:], "pv%d" % v_bf.slot, reads=[v_], q="pool")
            sq = Wt[0][1]
            P.dve(lambda e: e.tensor_tensor(out=kk[:], in0=k_sb[:], in1=kk_bc[:], op=ALU.mult), [k_sb, kk_bc], [kk])
            P.dve(lambda e: e.tensor_tensor(out=sq[:], in0=kk[:], in1=kk[:], op=ALU.mult), [kk], [sq])
            P.dve(lambda e: e.tensor_reduce(out=s_[:, 0:16], in_=sq[:].rearrange("p (h c) -> p h c", h=16), axis=AX.X, op=ALU.add), [sq], [s_])
            P.act(lambda e: e.activation(out=s_[:, 16:32], in_=s_[:, 0:16], func=AF.Sqrt), [s_], [s_])
            P.dve(lambda e: e.tensor_scalar(out=s_[:, 16:32], in0=s_[:, 16:32], scalar1=1e-12, scalar2=None, op0=ALU.max), [s_], [s_])
            P.dve(lambda e: e.reciprocal(out=s_[:, 32:48], in_=s_[:, 16:32]), [s_], [s_])
            P.dve(lambda e: e.tensor_tensor(out=kk[:].rearrange("p (h c) -> p h c", h=16), in0=kk[:].rearrange("p (h c) -> p h c", h=16),
                                            in1=s_[:, 32:48].unsqueeze(2).to_broadcast([128, 16, 64]), op=ALU.mult), [kk, s_], [kk])
            mix1 = self.rw_mix(P, mixr, None, xx_, hc, mu, 1)
            mix4 = self.rw_mix(P, mixr, None, xx_, hc, mu, 4)
            sml = [small[d].next() for d in range(2)]
            for d in range(2):
                for k in range(8):
                    P.pe(lambda e, k=k, d=d: e.matmul(psm[d][0:64, 0:128], lhsT=w1[d][k][:, :], rhs=mix1[:, k, :], start=(k == 0), stop=(k == 7)),
                         [mix1, w1[d][k]], [psm[d]])
                P.act(lambda e, d=d: e.activation(out=th_bf[d][:], in_=psm[d][0:64, 0:128], func=AF.Tanh), [psm[d]], [th_bf[d]])
            for d in range(2):
                for k in range(8):
                    P.pe(lambda e, k=k, d=d: e.matmul(psm[d][0:64, 0:128], lhsT=a1[d][k][:, :], rhs=mix4[:, k, :], start=(k == 0), stop=(k == 7)),
                         [mix4, a1[d][k]], [psm[d]])
                P.act(lambda e, d=d: e.activation(out=la_bf[d][:], in_=psm[d][0:64, 0:128], func=AF.Copy), [psm[d]], [la_bf[d]])
            for d in range(2):
                sig = Wt[d][0]
                for n in range(2):
                    bank = G.next()
                    P.pe(lambda e, bank=bank, n=n, d=d: e.matmul(bank[:], lhsT=th_bf[d][:], rhs=w2[d][:, n * 512:(n + 1) * 512], start=True, stop=False),
                         [th_bf[d], w2[d]], [bank])
                    P.pe(lambda e, bank=bank, n=n, d=d: e.matmul(bank[:], lhsT=sel8[:, d, :], rhs=rows[:, n * 512:(n + 1) * 512], start=False, stop=True),
                         [sel8, rows], [bank])
                    P.act(lambda e, bank=bank, n=n, sig=sig: e.activation(out=sig[:, n * 512:(n + 1) * 512], in_=bank[:], func=AF.Sigmoid), [bank], [sig])
            for d in range(2):
                sig, ep, em, ex = Wt[d]
                for n in range(2):
                    bank = G.next()
                    sl = slice(n * 512, (n + 1) * 512)
                    P.pe(lambda e, bank=bank, sl=sl, d=d, sig=sig: e.matmul(bank[:], lhsT=tri[d][:], rhs=sig[:, sl], start=True, stop=True), [tri[d], sig], [bank])
                    P.act(lambda e, bank=bank, sl=sl, ep=ep: e.activation(out=ep[:, sl], in_=bank[:], func=AF.Exp), [bank], [ep])
                    P.act(lambda e, bank=bank, sl=sl, em=em: e.activation(out=em[:, sl], in_=bank[:], func=AF.Exp, scale=-1.0), [bank], [em])
                    P.dve(lambda e, bank=bank, sl=sl, ex=ex, sig=sig: e.scalar_tensor_tensor(out=ex[:, sl], in0=sig[:, sl], scalar=C0, in1=bank[:], op0=ALU.mult, op1=ALU.add),
                          [bank, sig], [ex])
                P.act(lambda e, ex=ex: e.activation(out=ex[:], in_=ex[:], func=AF.Exp), [ex], [ex])
            for d in range(2):
                sig = Wt[d][0]
                for h in range(16):
                    P.pe(lambda e, h=h, d=d, sig=sig: e.matmul(psm[d][0:64, 256 + h:257 + h], lhsT=sig[:, h * 64:(h + 1) * 64], rhs=negc[:, 0:1], start=True, stop=True),
                         [sig, negc], [psm[d]])
                P.act(lambda e, d=d: e.activation(out=sml[d][0:64, 0:16], in_=psm[d][0:64, 256:272], func=AF.Exp), [psm[d]], [sml[d]])
            for d in range(2):
                sig, ep, em, ex = Wt[d]
                P.pool(lambda e, d=d, ep=ep: e.tensor_tensor(out=outb[d][:, 0:1024], in0=r_sb[:], in1=ep[:], op=ALU.mult), [r_sb, ep], [outb[d]])
                P.dve(lambda e, d=d, ex=ex: e.scalar_tensor_tensor(out=outb[d][:, 1024:2048], in0=kk[:], scalar=-1.0, in1=ex[:], op0=ALU.mult, op1=ALU.mult),
                      [kk, ex], [outb[d]])
            for d in range(2):
                aa = Wt[d][1]
                for n in range(2):
                    bank = G.next()
                    P.pe(lambda e, bank=bank, n=n, d=d: e.matmul(bank[:], lhsT=la_bf[d][:], rhs=a2[d][:, n * 512:(n + 1) * 512], start=True, stop=False),
                         [la_bf[d], a2[d]], [bank])
                    P.pe(lambda e, bank=bank, n=n, d=d: e.matmul(bank[:], lhsT=sel8[:, 2 + d, :], rhs=rows[:, n * 512:(n + 1) * 512], start=False, stop=True),
                         [sel8, rows], [bank])
                    P.act(lambda e, bank=bank, n=n, aa=aa: e.activation(out=aa[:, n * 512:(n + 1) * 512], in_=bank[:], func=AF.Sigmoid), [bank], [aa])
            for d in range(2):
                aa, em, be = Wt[d][1], Wt[d][2], Wt[d][3]
                P.dve(lambda e, aa=aa, be=be: e.tensor_tensor(out=be[:], in0=kk[:], in1=aa[:], op=ALU.mult), [kk, aa], [be])
                P.pool(lambda e, d=d, be=be, em=em: e.tensor_tensor(out=outb[d][:, 2048:3072], in0=be[:], in1=em[:], op=ALU.mult), [be, em], [outb[d]])
            for d in range(2):
                kd, aa, em = Wt[d][0], Wt[d][1], Wt[d][2]
                P.dve(lambda e, kd=kd, aa=aa: e.scalar_tensor_tensor(out=kd[:], in0=aa[:], scalar=-1.0, in1=ka_bc[:], op0=ALU.add, op1=ALU.mult), [aa, ka_bc], [kd])
                P.dve(lambda e, kd=kd: e.scalar_tensor_tensor(out=kd[:], in0=kd[:], scalar=1.0, in1=k_sb[:], op0=ALU.add, op1=ALU.mult), [kd, k_sb], [kd])
                P.pool(lambda e, d=d, kd=kd, em=em: e.tensor_tensor(out=outb[d][:, 3072:4096], in0=kd[:], in1=em[:], op=ALU.mult), [kd, em], [outb[d]])
            for d in range(2):
                kd, bt = Wt[d][0], Wt[d][3]
                P.dve(lambda e, kd=kd, bt=bt: e.tensor_tensor(out=bt[:], in0=kd[:], in1=r_sb[:], op=ALU.mult), [kd, r_sb], [bt])
                P.dve(lambda e, bt=bt: e.tensor_tensor(out=bt[:], in0=bt[:], in1=rk_bc[:], op=ALU.mult), [bt, rk_bc], [bt])
                P.dve(lambda e, d=d, bt=bt: e.tensor_reduce(out=sml[d][:, 16:32], in_=bt[:].rearrange("p (h c) -> p h c", h=16), axis=AX.X, op=ALU.add), [bt], [sml[d]])
            for d in range(2):
                P.dma(self.rw_dir[d][cidx], outb[d][:], "po%d" % d, reads=[outb[d]], q="pool")
                P.dma(self.rw_small[d][cidx], sml[d][:], "ps%d_%d" % (d, small[d].slot), reads=[sml[d]], q="pool")

        for ck in self.chunks_fwd():
            chunk_body(ck)
        ph.finish()

    def rw_scan_phase(self, li, i, d):
        nc, w = self.nc, self.lw[i]
        ph = Phase(nc)
        P = ph.P
        msk = ph.sb("msk", [128, 4, 128], F32)
        cmk = ph.sb("cmk", [128, 7, 128], BF16)
        P.dma(msk[:], w["masks"][d], "c3", writes=[msk])
        P.dma(cmk[:], w["cmask"][d], "c10", writes=[cmk], q="pool")
        G = Rot([ph.ps("G%d" % j, [128, 512], F32) for j in range(6)])
        PTr = Rot([ph.ps("PT%d" % j, [128, 8, 128], BF16) for j in range(2)])
        inb = ph.rot("inb", [128, 4096], BF16, 2)
        v_rot = ph.rot("v_bf", [128, 1024], BF16, 2)
        smr = ph.rot("smr", [128, 32], F32, 2)
        if d == 1:
            smf = ph.rot("smf", [128, 32], F32, 2)
            t1 = ph.sb("t1", [128, 1024], F32)
            sm = ph.rot("sm", [128, 48], F32, 2)
        ARr = ph.rot("AR", [64, 16, 2, 128], BF16, 2)
        BTr = ph.rot("BT", [64, 16, 128], BF16, 2)
        KTr = ph.rot("KT", [64, 16, 128], BF16, 2)
        names = ("N", "NT", "NA", "NAT", "NB", "NBT", "O32", "O32T", "O64", "O64T", "O128", "T", "TT", "Aak", "Abr", "Akr")
        NU = 4
        U_ = [{n: ph.sb("%s_u%d" % (n, us), [128, 4, 128], BF16) for n in names} for us in range(NU)]
        Xb = [ph.sb("Xb%d" % us, [128, 4, 64], BF16) for us in range(NU)]
        Ub = [ph.sb("Ub%d" % us, [128, 4, 64], BF16) for us in range(NU)]
        S = [ph.sb("S%d" % u, [64, 4, 64], F32) for u in range(4)]
        Sb = [ph.sb("Sb%d" % u, [64, 4, 64], BF16) for u in range(4)]
        for u in range(4):
            P.dve(lambda e, u=u: e.memset(S[u][:], 0.0), [], [S[u]])
            P.pool(lambda e, u=u: e.memset(Sb[u][:], 0.0), [], [Sb[u]])
        ysb = ph.rot("ysb", [128, 1040], F32, 2)
        if d == 1:
            yf = ph.rot("yf", [128, 1040], F32, 2)
        ident = self.ident
        order = self.chunks_fwd() if d == 0 else self.chunks_bwd()

        def chunk_body(ck):
            cidx = self.row0(ck) // CH
            in_, v_bf, s_ = inb.next(), v_rot.next(), smr.next()
            P.dma(in_[:], self.rw_dir[d][cidx], "li%d" % inb.slot, writes=[in_])
            P.dma(v_bf[:], self.rw_scr_v[cidx], "lv%d" % v_rot.slot, writes=[v_bf])
            P.dma(s_[:], self.rw_small[d][cidx], "ls%d" % smr.slot, writes=[s_])
            WC = Tile(s_.t[0:64, 0:16], "WCview")
            WC.b = s_.b
            bh_bf = Tile(in_.t[:, 2048:3072], "bhv")
            bh_bf.b = in_.b
            kh_bf = Tile(in_.t[:, 3072:4096], "khv")
            kh_bf.b = in_.b
            AR, BT, KT = ARr.next(), BTr.next(), KTr.next()
            cnt = 0
            for g in range(2):
                for c0, dtile, dst in ((1024, AR, AR[:, g * 8:(g + 1) * 8, 0, :]), (0, AR, AR[:, g * 8:(g + 1) * 8, 1, :]),
                                       (2048, BT, BT[:, g * 8:(g + 1) * 8, :]), (3072, KT, KT[:, g * 8:(g + 1) * 8, :])):
                    PT = PTr.next()
                    for j in range(8):
                        h = g * 8 + j
                        P.pe(lambda e, PT=PT, c0=c0, h=h, j=j: e.transpose(out=PT[0:64, j, :], in_=in_[:, c0 + h * 64:c0 + (h + 1) * 64], identity=ident[:]),
                             [in_, ident], [PT])
                    if cnt % 2 == 0:
                        P.act(lambda e, PT=PT, dst=dst: e.activation(out=dst, in_=PT[0:64, :, :], func=AF.Copy), [PT], [dtile])
                    else:
                        P.dve(lambda e, PT=PT, dst=dst: e.tensor_copy(out=dst, in_=PT[0:64, :, :]), [PT], [dtile])
                    cnt += 1
            y_ = ysb.next()
            if d == 1:
                yf_ = yf.next()
                P.dma(yf_[:, 0:1024], self.scr_o[cidx][:, 0:1024], "lyf%d" % yf.slot, writes=[yf_])
                sf_ = smf.next()
                P.dma(sf_[:], self.rw_small[0][cidx], "lsf%d" % smf.slot, writes=[sf_])
            for g in range(1):
                units = [(us, us) for us in range(4)]
                for u, us in units:
                    M = U_[us]
                    h0 = u * 4
                    for pr in range(2):
                        bank = G.next()
                        for j in range(2):
                            h = h0 + pr * 2 + j
                            P.pe(lambda e, bank=bank, j=j, h=h: e.matmul(bank[:, j * 256:(j + 1) * 256], lhsT=BT[:, h, :],
                                                                         rhs=AR[:, h, :, :].rearrange("p a t -> p (a t)"), start=True, stop=True), [BT, AR], [bank])
                        bv = bank[:].rearrange("p (j a t) -> p j a t", j=2, a=2)
                        for dn, mi in (("NA", 0),):
                            P.dve(lambda e, bv=bv, M=M, pr=pr, dn=dn, mi=mi: e.tensor_tensor(out=M[dn][:, pr * 2:pr * 2 + 2, :], in0=bv[:, :, 0, :],
                                                                                            in1=cmk[:, mi, :].unsqueeze(1).to_broadcast([128, 2, 128]), op=ALU.mult),
                                  [bank, cmk], [M[dn]])
                        P.dve(lambda e, bv=bv, M=M, pr=pr: e.tensor_tensor(out=M["Abr"][:, pr * 2:pr * 2 + 2, :], in0=bv[:, :, 1, :],
                                                                          in1=msk[:, 1, :].unsqueeze(1).to_broadcast([128, 2, 128]), op=ALU.mult), [bank, msk], [M["Abr"]])
                        bank = G.next()
                        for j in range(2):
                            h = h0 + pr * 2 + j
                            P.pe(lambda e, bank=bank, j=j, h=h: e.matmul(bank[:, j * 256:(j + 1) * 256], lhsT=KT[:, h, :],
                                                                         rhs=AR[:, h, :, :].rearrange("p a t -> p (a t)"), start=True, stop=True), [KT, AR], [bank])
                        bv = bank[:].rearrange("p (j a t) -> p j a t", j=2, a=2)
                        P.dve(lambda e, bv=bv, M=M, pr=pr: e.tensor_tensor(out=M["Aak"][:, pr * 2:pr * 2 + 2, :], in0=bv[:, :, 0, :],
                                                                          in1=msk[:, 0, :].unsqueeze(1).to_broadcast([128, 2, 128]), op=ALU.mult), [bank, msk], [M["Aak"]])
                        P.dve(lambda e, bv=bv, M=M, pr=pr: e.tensor_tensor(out=M["Akr"][:, pr * 2:pr * 2 + 2, :], in0=bv[:, :, 1, :],
                                                                          in1=msk[:, 1, :].unsqueeze(1).to_broadcast([128, 2, 128]), op=ALU.mult), [bank, msk], [M["Akr"]])
                    bank = G.next()
                    for j in range(4):
                        h = h0 + j
                        P.pe(lambda e, bank=bank, j=j, h=h: e.matmul(bank[:, j * 128:(j + 1) * 128], lhsT=AR[:, h, 0, :], rhs=BT[:, h, :], start=True, stop=True),
                             [AR, BT], [bank])
                    for dn, mi in (("NAT", 3), ("O32T", 4), ("O64T", 5), ("O128", 6)):
                        P.dve(lambda e, bank=bank, M=M, dn=dn, mi=mi: e.tensor_tensor(out=M[dn][:], in0=bank[:].rearrange("p (j t) -> p j t", j=4),
                                                                                      in1=cmk[:, mi, :].unsqueeze(1).to_broadcast([128, 4, 128]), op=ALU.mult),
                              [bank, cmk], [M[dn]])
                    P.pool(lambda e, M=M: e.tensor_tensor(out=M["T"][:], in0=M["NA"][:], in1=ident[:].unsqueeze(1).to_broadcast([128, 4, 128]), op=ALU.add),
                           [M["NA"], ident], [M["T"]])

                def mm4(bank, M, lt, rt):
                    for j in range(4):
                        P.pe(lambda e, bank=bank, j=j, M=M, lt=lt, rt=rt: e.matmul(bank[:, j * 128:(j + 1) * 128], lhsT=M[lt][:, j, :], rhs=M[rt][:, j, :],
                                                                                  start=True, stop=True), [M[lt], M[rt]], [bank])

                def cp4(bank, M, dn):
                    P.act(lambda e, bank=bank, M=M, dn=dn: e.activation(out=M[dn][:], in_=bank[:].rearrange("p (j t) -> p j t", j=4), func=AF.Copy),
                          [bank], [M[dn]])

                def add4(bank, M, dn):
                    P.dve(lambda e, bank=bank, M=M, dn=dn: e.tensor_tensor(out=M[dn][:], in0=bank[:].rearrange("p (j t) -> p j t", j=4), in1=M[dn][:], op=ALU.add),
                          [bank, M[dn]], [M[dn]])
                def tr4(M):
                    PT = PTr.next()
                    for j in range(4):
                        P.pe(lambda e, PT=PT, j=j, M=M: e.transpose(out=PT[:, j, :], in_=M["T"][:, j, :], identity=ident[:]), [M["T"], ident], [PT])
                    P.act(lambda e, PT=PT, M=M: e.activation(out=M["TT"][:], in_=PT[:, 0:4, :], func=AF.Copy), [PT], [M["TT"]])
                cur = {us: ("NA", "NAT") for _, us in units}
                for lvl in range(3):
                    nxt = {}
                    for u, us in units:
                        M = U_[us]
                        nk, nkt = cur[us]
                        nb, nbt = ("NB", "NBT") if nk == "NA" else ("NA", "NAT")
                        if lvl < 2:
                            b1 = G.next(); mm4(b1, M, nkt, nk); cp4(b1, M, nb)
                        b2 = G.next(); mm4(b2, M, nk, nkt); cp4(b2, M, nbt)
                        nxt[us] = (nb, nbt)
                    for u, us in units:
                        M = U_[us]
                        nb, nbt = nxt[us]
                        b3 = G.next(); mm4(b3, M, nbt, "T"); add4(b3, M, "T")
                    cur = nxt
                for on in ("O32T", "O64T", "O128"):
                    for u, us in units:
                        tr4(U_[us])
                    for u, us in units:
                        M = U_[us]
                        b1 = G.next(); mm4(b1, M, on, "T"); cp4(b1, M, "N")
                    for u, us in units:
                        M = U_[us]
                        b3 = G.next(); mm4(b3, M, "TT", "N"); add4(b3, M, "T")
                for u, us in units:
                    M = U_[us]
                    h0 = u * 4
                    bank = G.next()
                    for j in range(4):
                        h = h0 + j
                        P.pe(lambda e, bank=bank, j=j, h=h, u=u: e.matmul(bank[:, j * 64:(j + 1) * 64], lhsT=AR[:, h, 0, :], rhs=Sb[u][:, j, :], start=True, stop=False),
                             [AR, Sb[u]], [bank])
                        P.pe(lambda e, bank=bank, j=j, h=h, M=M: e.matmul(bank[:, j * 64:(j + 1) * 64], lhsT=M["Aak"][:, j, :], rhs=v_bf[:, h * 64:(h + 1) * 64],
                                                                          start=False, stop=True), [M["Aak"], v_bf], [bank])
                    P.act(lambda e, bank=bank, us=us: e.activation(out=Xb[us][:], in_=bank[:, 0:256].rearrange("p (j v) -> p j v", j=4), func=AF.Copy),
                          [bank], [Xb[us]])
                for u, us in units:
                    M = U_[us]
                    h0 = u * 4
                    bank = G.next()
                    for j in range(4):
                        P.pe(lambda e, bank=bank, j=j, M=M, us=us: e.matmul(bank[:, j * 64:(j + 1) * 64], lhsT=M["T"][:, j, :], rhs=Xb[us][:, j, :], start=True, stop=True),
                             [M["T"], Xb[us]], [bank])
                    P.dve(lambda e, bank=bank, us=us: e.tensor_copy(out=Ub[us][:], in_=bank[:, 0:256].rearrange("p (j v) -> p j v", j=4)), [bank], [Ub[us]])
                for u, us in units:
                    M = U_[us]
                    h0 = u * 4
                    bank = G.next()
                    for j in range(4):
                        h = h0 + j
                        P.pe(lambda e, bank=bank, j=j, h=h, u=u: e.matmul(bank[:, j * 64:(j + 1) * 64], lhsT=AR[:, h, 1, :], rhs=Sb[u][:, j, :], start=True, stop=False),
                             [AR, Sb[u]], [bank])
                        P.pe(lambda e, bank=bank, j=j, M=M, us=us: e.matmul(bank[:, j * 64:(j + 1) * 64], lhsT=M["Abr"][:, j, :], rhs=Ub[us][:, j, :], start=False, stop=False),
                             [M["Abr"], Ub[us]], [bank])
                        P.pe(lambda e, bank=bank, j=j, h=h, M=M: e.matmul(bank[:, j * 64:(j + 1) * 64], lhsT=M["Akr"][:, j, :], rhs=v_bf[:, h * 64:(h + 1) * 64],
                                                                          start=False, stop=True), [M["Akr"], v_bf], [bank])
                    if d == 0:
                        P.act(lambda e, bank=bank, h0=h0, y_=y_: e.activation(out=y_[:, h0 * 64:(h0 + 4) * 64], in_=bank[:, 0:256], func=AF.Copy), [bank], [y_])
                    else:
                        P.dve(lambda e, bank=bank, h0=h0, y_=y_, yf_=yf_: e.tensor_tensor(out=y_[:, h0 * 64:(h0 + 4) * 64], in0=bank[:, 0:256],
                                                                                         in1=yf_[:, h0 * 64:(h0 + 4) * 64], op=ALU.add), [bank, yf_], [y_])
                for u, us in units:
                    M = U_[us]
                    h0 = u * 4
                    bank = G.next()
                    for j in range(4):
                        h = h0 + j
                        P.pe(lambda e, bank=bank, j=j, h=h, us=us: e.matmul(bank[0:64, j * 64:(j + 1) * 64], lhsT=bh_bf[:, h * 64:(h + 1) * 64], rhs=Ub[us][:, j, :],
                                                                            start=True, stop=False), [bh_bf, Ub[us]], [bank])
                        P.pe(lambda e, bank=bank, j=j, h=h: e.matmul(bank[0:64, j * 64:(j + 1) * 64], lhsT=kh_bf[:, h * 64:(h + 1) * 64], rhs=v_bf[:, h * 64:(h + 1) * 64],
                                                                     start=False, stop=True), [kh_bf, v_bf], [bank])
                    P.dve(lambda e, bank=bank, u=u: e.tensor_tensor(out=S[u][:], in0=bank[0:64, 0:256].rearrange("p (j v) -> p j v", j=4), in1=S[u][:], op=ALU.add),
                          [bank, S[u]], [S[u]])
                    P.dve(lambda e, u=u, h0=h0: e.tensor_tensor(out=S[u][:], in0=S[u][:], in1=WC[:, h0:h0 + 4].unsqueeze(2).to_broadcast([64, 4, 64]), op=ALU.mult),
                          [S[u], WC], [S[u]])
                    P.pool(lambda e, u=u: e.tensor_copy(out=Sb[u][:], in_=S[u][:]), [S[u]], [Sb[u]])
            if d == 0:
                P.dma(self.scr_o[cidx][:, 0:1024], y_[:, 0:1024], "sy%d" % ysb.slot, reads=[y_], q="pool")
            else:
                need_out = True
                yv = y_[:, 0:1024].rearrange("p (h c) -> p h c", h=16)
                t1v = t1[:].rearrange("p (h c) -> p h c", h=16)
                st_ = sm.next()
                P.dve(lambda e, yv=yv: e.tensor_reduce(out=st_[:, 0:16], in_=yv, axis=AX.X, op=ALU.add), [y_], [st_])
                P.dve(lambda e: e.tensor_scalar(out=st_[:, 0:16], in0=st_[:, 0:16], scalar1=-1.0 / 64, scalar2=None, op0=ALU.mult), [st_], [st_])
                P.dve(lambda e, yv=yv: e.tensor_tensor(out=yv, in0=yv, in1=st_[:, 0:16].unsqueeze(2).to_broadcast([128, 16, 64]), op=ALU.add), [y_, st_], [y_])
                P.pool(lambda e, y_=y_: e.tensor_tensor(out=t1[:], in0=y_[:, 0:1024], in1=y_[:, 0:1024], op=ALU.mult), [y_], [t1])
                P.dve(lambda e: e.tensor_reduce(out=st_[:, 16:32], in_=t1v, axis=AX.X, op=ALU.add), [t1], [st_])
                P.act(lambda e: e.activation(out=st_[:, 16:32], in_=st_[:, 16:32], func=AF.Sqrt, scale=1.0 / 64, bias=64e-5), [st_], [st_])
                P.dve(lambda e: e.reciprocal(out=st_[:, 32:48], in_=st_[:, 16:32]), [st_], [st_])
                P.dve(lambda e, yv=yv: e.tensor_tensor(out=yv, in0=yv, in1=st_[:, 32:48].unsqueeze(2).to_broadcast([128, 16, 64]), op=ALU.mult), [y_, st_], [y_])
                P.dve(lambda e: e.tensor_tensor(out=st_[:, 0:16], in0=s_[:, 16:32], in1=sf_[:, 16:32], op=ALU.add), [s_, sf_], [st_])
                P.dve(lambda e: e.tensor_tensor(out=t1v, in0=v_bf[:].rearrange("p (h c) -> p h c", h=16),
                                                in1=st_[:, 0:16].unsqueeze(2).to_broadcast([128, 16, 64]), op=ALU.mult), [v_bf, st_, t1], [t1])
                P.dma(self.scr_o[cidx][:, 0:1024], y_[:, 0:1024], "sy%d" % ysb.slot, reads=[y_], q="pool")
                P.dma(self.scr_o[cidx][:, 1024:2048], t1[:], "sbv", reads=[t1], q="pool")
        for ck in order:
            chunk_body(ck)
        ph.finish()

    def rw_out_phase(self, li, i, last):
        nc, w = self.nc, self.lw[i]
        ph = Phase(nc)
        P = ph.P
        Wg = self.load_w(ph, ph.sb("Wg", [128, 8, 1024], BF16), w["rkvg"][3], 0, 1024, 8)
        Wo = self.load_w(ph, ph.sb("Wo", [128, 8, 1024], BF16), w["w_out"], 0, 1024, 8)
        mu = ph.sb("mu", [128, 6, 8], F32)
        P.dma(mu[:], w["mu"][:, :, :], "c2", writes=[mu])
        R = self.front_bufs(ph, n_xn=1, n_xt=4, n_hT=4)
        R.update(self.tail_bufs(ph, last=last))
        get_h = self.rw_hcache(ph, R, li)
        mm = Rot([ph.ps("mm%d" % j, [128, 512], F32) for j in range(2)])
        sh = ph.sb("sh", [128, 8, 128], BF16)
        xx = ph.sb("xx", [128, 8, 128], F32)
        tmpr = ph.rot("mtmp", [128, 8, 128], F32, 2)
        mixr = ph.rot("mix", [128, 8, 128], BF16, 2)
        op = ph.rot("opre", [128, 2048], F32, 2)
        lg_bc = ph.sb("lg_bc", [128, 1024], F32)
        lb_bc = ph.sb("lb_bc", [128, 1024], F32)
        P.dma(lg_bc[:], w["bc"][:, 3, :], "c8", writes=[lg_bc])
        P.dma(lb_bc[:], w["bc"][:, 4, :], "c9", writes=[lb_bc])
        sz = ph.rot("sz", [128, 1024], F32, 2)
        gated = ph.rot("gated", [128, 1024], BF16, 2)
        order = self.chunks_fwd()
        if last:
            order = [ck for ck in order if ck[0] == "x"]
        for ck in order:
            cidx = self.row0(ck) // CH
            pk, nk = self.rw_neighbors(ck)
            hc, xt = get_h(ck)
            hp = get_h(pk)[0] if pk else None
            hn = get_h(nk)[0] if nk else None
            self.rw_shift(P, sh, hc, hp, hn, ck[0])
            P.pool(lambda e, hc=hc: e.tensor_tensor(out=xx[:], in0=sh[:], in1=hc[:], op=ALU.subtract), [sh, hc], [xx])
            mix5 = self.rw_mix(P, mixr, tmpr, xx, hc, mu, 5)
            o_ = op.next()
            P.dma(o_[:], self.scr_o[cidx][:, 0:2048], "lo%d" % op.slot, writes=[o_])
            P.dve(lambda e, o_=o_: e.tensor_tensor(out=o_[:, 0:1024], in0=o_[:, 0:1024], in1=lg_bc[:], op=ALU.mult), [o_, lg_bc], [o_])
            P.pool(lambda e, o_=o_: e.tensor_tensor(out=o_[:, 1024:2048], in0=o_[:, 1024:2048], in1=lb_bc[:], op=ALU.add), [o_, lb_bc], [o_])
            P.dve(lambda e, o_=o_: e.tensor_tensor(out=o_[:, 0:1024], in0=o_[:, 0:1024], in1=o_[:, 1024:2048], op=ALU.add), [o_], [o_])
            sz_ = sz.next()
            for n in range(2):
                bank = mm.next()
                for k in range(8):
                    P.pe(lambda e, bank=bank, n=n, k=k, mix5=mix5: e.matmul(bank[:], lhsT=mix5[:, k, :], rhs=Wg[k][:, n * 512:(n + 1) * 512],
                                                                            start=(k == 0), stop=(k == 7)), [mix5, Wg[k]], [bank])
                P.act(lambda e, bank=bank, n=n, sz_=sz_: e.activation(out=sz_[:, n * 512:(n + 1) * 512], in_=bank[:], func=AF.Silu), [bank], [sz_])
            g_ = gated.next()
            P.dve(lambda e, o_=o_, sz_=sz_, g_=g_: e.tensor_tensor(out=g_[:], in0=o_[:, 0:1024], in1=sz_[:], op=ALU.mult), [o_, sz_], [g_])
            self.tail(ph, R, g_, Wo, li, ck, xt, last, KT=8)
        ph.finish()


def _col(v, k):
    return np.ascontiguousarray(np.asarray(v, np.float32).reshape(k, 128).T)


def _consts(L, CL):
    f32 = np.float32
    t = np.arange(L)
    row, col = (t // 64).astype(f32), (t % 64).astype(f32)
    freqs = (f32(10000.0) ** (-(np.arange(64, dtype=f32)) / f32(64))).astype(f32)
    ang = np.concatenate([row[:, None] * freqs, col[:, None] * freqs], axis=-1).astype(f32)
    rope = np.zeros((CL + L, 2, 128), f32)
    rope[:CL, 0, :] = 1.0
    rope[CL:, 0, :] = np.cos(ang)
    rope[CL:, 1, :] = np.sin(ang)
    jj = np.arange(128)[:, None].astype(f32)
    ii = np.arange(128)[None, :].astype(f32)
    dmat = np.stack([np.maximum(ii - jj, 0), np.maximum(jj - ii, 0), (ii >= jj).astype(f32), (jj >= ii).astype(f32),
                     np.broadcast_to(ii + 1, (128, 128)), np.broadcast_to(128 - ii, (128, 128))], axis=1).astype(f32)
    jcol = np.stack([127 - np.arange(128), np.arange(128)], axis=1).astype(f32)
    sel = np.zeros((2, 2, 128), f32)
    sel[0, 0, :] = 1.0
    sel[1, 1, :] = 1.0
    return {"ident": np.eye(128, dtype=f32), "rope": rope, "dmat": np.ascontiguousarray(dmat), "jcol": jcol, "sel2": sel}


def host_inputs(inp, b, layers, L, CL=256, x_rows=None):
    f32 = np.float32
    m = dict(_consts(L, CL))
    xr = inp["x"][b] if x_rows is None else x_rows
    m["x"] = np.ascontiguousarray(xr[:L], f32)
    m["ctx"] = np.ascontiguousarray(inp["ctx"][b][:CL], f32)
    m["cc"] = np.ascontiguousarray(np.stack([_col(inp["c"][b], 8), _col(inp["c_ctx"], 8)], axis=-1))
    m["final_g_bc"] = np.ascontiguousarray(np.broadcast_to(np.asarray(inp["final_g"], f32), (128, D)))
    for i in layers:
        j = i // 3
        m["ada_w%d" % i] = np.ascontiguousarray(inp["ada_w"][i], f32)
        m["ada_bcol%d" % i] = _col(inp["ada_b"][i], 24)
        m["ada_brow%d" % i] = np.ascontiguousarray(np.broadcast_to(np.asarray(inp["ada_b"][i][2 * D:], f32), (2, D)))
        m["ng_col%d" % i] = _col(inp["norm_g"][i], 8)
        k = KINDS[i]
        if k == 0:
            m["ret_w_in%d" % i] = np.ascontiguousarray(inp["ret_w_in"][j], f32)
            m["ret_w_out%d" % i] = np.ascontiguousarray(inp["ret_w_out"][j], f32)
            dec = np.concatenate([inp["ret_decay"][j][0], inp["ret_decay"][j][1]]).astype(f32)
            m["ret_decay%d" % i] = np.ascontiguousarray(np.broadcast_to(dec, (128, 8)))
        elif k == 1:
            m["gm_w_in%d" % i] = np.ascontiguousarray(inp["gm_w_in"][j], f32)
            m["gm_w_out%d" % i] = np.ascontiguousarray(inp["gm_w_out"][j], f32)
            m["gm_vg%d" % i] = np.ascontiguousarray(np.broadcast_to(np.asarray(inp["gm_vnorm_g"][j], f32), (128, 2048)))
            m["gm_wsT%d" % i] = np.ascontiguousarray(np.transpose(np.asarray(inp["gm_w_s"][j], f32), (2, 0, 1)))
            m["gm_bsT%d" % i] = np.ascontiguousarray(np.asarray(inp["gm_b_s"][j], f32).T)
        else:
            _rw_host(m, inp, i, j)
    return m


def _rw_host(m, inp, i, j):
    f32 = np.float32
    g = lambda k: np.asarray(inp[k][j], f32)
    mu = g("rw_mu")
    m["rw_mu%d" % i] = np.ascontiguousarray(np.stack([_col(mu[p], 8) for p in range(6)], axis=1))
    m["rw_rkvg%d" % i] = np.ascontiguousarray(g("rw_w_rkvg"))
    m["rw_w1%d" % i] = np.ascontiguousarray(g("rw_w1"))
    m["rw_a1%d" % i] = np.ascontiguousarray(g("rw_a1"))
    m["rw_w2%d" % i] = np.ascontiguousarray(g("rw_w2"))
    m["rw_a2%d" % i] = np.ascontiguousarray(g("rw_a2"))
    rows = np.zeros((8, D), f32)
    rows[0:2] = g("rw_w0")
    rows[2:4] = g("rw_a0")
    m["rw_rows%d" % i] = rows
    bc = np.stack([g("rw_k_k"), g("rw_k_a"), g("rw_r_k").reshape(-1), g("rw_lnx_g"), g("rw_lnx_b")], axis=0)
    m["rw_bc%d" % i] = np.ascontiguousarray(np.broadcast_to(bc[None], (128, 5, D)))
    m["rw_wout%d" % i] = np.ascontiguousarray(g("rw_w_out"))
    s_ = np.arange(128)[:, None]
    t_ = np.arange(128)[None, :]
    c0 = f32(-0.6065306597126334)
    fw = np.stack([(s_ < t_), (s_ <= t_), (t_ < s_), (s_ <= t_) * c0], axis=1).astype(f32)
    bw = np.stack([(s_ > t_), (s_ >= t_), (t_ > s_), (s_ >= t_) * c0], axis=1).astype(f32)
    m["rw_masks%d" % i] = np.ascontiguousarray(np.stack([fw, bw], axis=0))
    sel = np.zeros((8, 8, 128), f32)
    for r in range(8):
        sel[r, r, :] = 1.0
    m["rw_sel8%d" % i] = sel
    m["rw_negc%d" % i] = np.full((128, 1), c0, f32)
    blk = lambda n: (s_ // n) == (t_ // n)
    bm = np.stack([blk(16)] + [blk(n) & ~blk(n // 2) for n in (32, 64, 128)], axis=1).astype(f32)
    m["rw_bmask%d" % i] = np.ascontiguousarray(bm)
    cms = []
    for dd in (fw, bw):
        st, stT = dd[:, 0, :], dd[:, 2, :]
        cms.append(np.stack([st * bm[:, 0], st * bm[:, 1], st * bm[:, 2], stT * bm[:, 0], stT * bm[:, 1], stT * bm[:, 2], stT * bm[:, 3]], axis=1))
    m["rw_cmask%d" % i] = np.ascontiguousarray(np.stack(cms, axis=0).astype(f32))


_MODEL_CACHE = {}


def kernel(**inputs):
    inp = {k: np.asarray(v) for k, v in inputs.items()}
    B, L, _ = inp["x"].shape
    layers = (0, 1, 2, 3)
    key = (L, layers)
    if key not in _MODEL_CACHE:
        _MODEL_CACHE[key] = Model(L, 256, layers)
    model = _MODEL_CACHE[key]
    maps = []
    for core in range(NCORES):
        b = core % B
        hm = host_inputs(inp, b, layers, L)
        maps.append({k: hm[k] for k in model.inputs})
    res = run_bass_kernel_spmd(model.nc, maps, core_ids=list(range(NCORES)))
    out = np.stack([np.asarray(res.results[b]["out"], np.float32) for b in range(B)], axis=0)
    return out
```

```python
from contextlib import ExitStack
import numpy as np
import concourse.bass as bass
import concourse.mybir as mybir
from concourse.bass_utils import run_bass_kernel_spmd

F32 = mybir.dt.float32
BF16 = mybir.dt.bfloat16
AF = mybir.ActivationFunctionType
ALU = mybir.AluOpType
AX = mybir.AxisListType

D = 1024
CH = 128
NCORES = 8
_UID = [0]


class Buf:
    __slots__ = ("name", "last_w", "readers", "excl")

    def __init__(self, name, excl=False):
        self.name = name
        self.last_w = None
        self.readers = []
        self.excl = excl


class Tile:
    def __init__(self, t, name, excl=False):
        self.t = t
        self.b = Buf(name, excl)

    def __getitem__(self, k):
        return self.t[k]


class Rot:
    def __init__(self, tiles):
        self.tiles = tiles
        self.i = 0

    def next(self):
        t = self.tiles[self.i % len(self.tiles)]
        self.slot = self.i % len(self.tiles)
        self.i += 1
        return t


class Op:
    __slots__ = ("eng", "fn", "dma_key", "waits", "signal", "sem", "val", "idx", "prog")


def _b(x):
    return x.b if isinstance(x, Tile) else x


class Prog:
    ENGS = ("pe", "act", "dve", "pool", "sp")

    def __init__(self):
        self.ops = []

    def add(self, eng, fn, reads=(), writes=(), dma_key=None):
        op = Op()
        op.eng, op.fn, op.dma_key = eng, fn, dma_key
        op.signal, op.sem, op.val = False, None, 0
        op.idx, op.prog = len(self.ops), self
        reads = [_b(x) for x in reads]
        writes = [_b(x) for x in writes]
        deps = []
        for b in reads:
            if b.last_w is not None:
                deps.append((b.last_w, True))
            if b.excl:
                deps.extend((r, False) for r in b.readers)
        for b in writes:
            if b.last_w is not None:
                deps.append((b.last_w, False))
            deps.extend((r, False) for r in b.readers)
        need, seen = [], set()
        for d, raw in deps:
            if d.prog is not self:
                continue
            if d.dma_key is None and dma_key is None and d.eng == eng:
                if eng == "pe" or not raw:
                    continue
            if d.idx in seen:
                continue
            seen.add(d.idx)
            d.signal = True
            need.append(d)
        op.waits = need
        for b in reads:
            b.readers.append(op)
        for b in writes:
            b.last_w = op
            b.readers = []
        self.ops.append(op)
        return op

    def pe(self, fn, reads=(), writes=()):
        return self.add("pe", fn, reads, writes)

    def act(self, fn, reads=(), writes=()):
        return self.add("act", fn, reads, writes)

    def dve(self, fn, reads=(), writes=()):
        return self.add("dve", fn, reads, writes)

    def pool(self, fn, reads=(), writes=()):
        return self.add("pool", fn, reads, writes)

    def dma(self, out, in_, key, reads=(), writes=(), q="sp", slow=False):
        if slow:
            return self.add(q, lambda e: e.dma_start(out=out, in_=in_, allow_slow_non_contiguous=True),
                            reads, writes, dma_key=key)
        return self.add(q, lambda e: e.dma_start(out=out, in_=in_), reads, writes, dma_key=key)

    def emit(self, nc, stack):
        def keyof(op):
            return ("dma", op.dma_key) if op.dma_key is not None else ("eng", op.eng)
        last = {}
        for op in self.ops:
            last[keyof(op)] = op
        for op in last.values():
            op.signal = True
        cnt, sems = {}, {}
        for op in self.ops:
            if not op.signal:
                continue
            k = keyof(op)
            cnt[k] = cnt.get(k, 0) + (16 if op.dma_key is not None else 1)
            op.val = cnt[k]
            if k not in sems:
                _UID[0] += 1
                sems[k] = nc.alloc_semaphore(name="s%d" % _UID[0])
            op.sem = sems[k]
        self.n_sems = len(sems)
        per_eng = {e: [] for e in self.ENGS}
        for op in self.ops:
            per_eng[op.eng].append(op)
        finals = [(sems[k], cnt[k]) for k in sems]

        def run(engname, e):
            waited = {}
            for op in per_eng[engname]:
                for d in op.waits:
                    key = id(d.sem)
                    if waited.get(key, 0) >= d.val:
                        continue
                    e.wait_ge(d.sem, d.val)
                    waited[key] = d.val
                ins = op.fn(e)
                if op.signal:
                    ins.then_inc(op.sem, 16 if op.dma_key is not None else 1)
            for s, v in finals:
                if waited.get(id(s), 0) < v:
                    e.wait_ge(s, v)

        with nc.Block() as block:
            block.tensor(lambda e: run("pe", e))
            block.scalar(lambda e: run("act", e))
            block.vector(lambda e: run("dve", e))
            block.gpsimd(lambda e: run("pool", e))
            block.sync(lambda e: run("sp", e))
        nc.clear_and_free_semaphores(list(sems.values()))
        nc.all_engine_barrier()


class Phase:
    def __init__(self, nc):
        self.nc = nc
        self.st = ExitStack()
        self.P = Prog()

    def sb(self, name, shape, dt):
        _UID[0] += 1
        t = self.st.enter_context(self.nc.sbuf_tensor("%s_%d" % (name, _UID[0]), list(shape), dt))
        return Tile(t, name)

    def ps(self, name, shape, dt=F32):
        _UID[0] += 1
        t = self.st.enter_context(self.nc.psum_tensor("%s_%d" % (name, _UID[0]), list(shape), dt))
        return Tile(t, name, excl=True)

    def rot(self, name, shape, dt, n):
        return Rot([self.sb("%s%d" % (name, i), shape, dt) for i in range(n)])

    def finish(self):
        self.P.emit(self.nc, self.st)
        self.st.close()


KINDS = (0, 1, 2, 0)
NORM_EPS = 1e-6


class Model:
    def __init__(self, L, CL=256, layers=(0, 1, 2, 3), final_norm=True):
        self.L, self.CL = L, CL
        self.NX, self.NC = L // CH, CL // CH
        self.layers = tuple(layers)
        self.final_norm = final_norm
        self.nc = nc = bass.Bass("TRN2", target_bir_lowering=False)
        self.inputs = {}
        self.outer = ExitStack()
        NT = L + CL

        def din(name, shape):
            self.inputs[name] = tuple(shape)
            return nc.dram_tensor(name, list(shape), F32, kind="ExternalInput").ap()

        self.x_in = din("x", [L, D])
        self.c_in = din("ctx", [CL, D])
        self.cc = din("cc", [128, 8, 2])
        self.ident_in = din("ident", [128, 128])
        self.rope_in = din("rope", [NT, 2, 128])
        self.dmat_in = din("dmat", [128, 6, 128])
        self.jcol_in = din("jcol", [128, 2])
        self.sel_in = din("sel2", [2, 2, 128])
        self.fg_in = din("final_g_bc", [128, D])
        self.lw = {}
        for i in self.layers:
            w = {}
            w["ada_w"] = din("ada_w%d" % i, [D, 3 * D])
            w["ada_bcol"] = din("ada_bcol%d" % i, [128, 24])
            w["ada_brow"] = din("ada_brow%d" % i, [2, D])
            w["ng_col"] = din("ng_col%d" % i, [128, 8])
            k = KINDS[i]
            if k == 0:
                w["w_in"] = din("ret_w_in%d" % i, [D, 6144])
                w["w_out"] = din("ret_w_out%d" % i, [2048, D])
                w["decay"] = din("ret_decay%d" % i, [128, 8])
            elif k == 1:
                w["w_in"] = din("gm_w_in%d" % i, [D, 6144])
                w["w_out"] = din("gm_w_out%d" % i, [2048, D])
                w["vg"] = din("gm_vg%d" % i, [128, 2048])
                w["wsT"] = din("gm_wsT%d" % i, [128, 8, 128])
                w["bsT"] = din("gm_bsT%d" % i, [128, 8])
            else:
                self._rw_inputs(w, i, din)
            self.lw[i] = w
        self.out = nc.dram_tensor("out", [L, D], F32, kind="ExternalOutput").ap()
        self.xs = [nc.dram_tensor("xs%d" % i, [NT, D], F32).ap() for i in range(2)]
        o = self.outer
        self.ident = Tile(o.enter_context(nc.sbuf_tensor("identb", [128, 128], BF16)), "ident")
        self.ident32 = Tile(o.enter_context(nc.sbuf_tensor("ident32", [128, 128], F32)), "ident32")
        self.mod = Tile(o.enter_context(nc.sbuf_tensor("mod", [128, 2, 2, 8], F32)), "mod")
        self.gate_dram = nc.dram_tensor("gate_dram", [2, 128, D], F32).ap()
        self.sel = Tile(o.enter_context(nc.sbuf_tensor("sel", [2, 2, 128], F32)), "sel")
        self._build()
        self.outer.close()

    def chunks_fwd(self):
        return [("c", i) for i in range(self.NC)] + [("x", i) for i in range(self.NX)]

    def chunks_bwd(self):
        return [("c", i) for i in reversed(range(self.NC))] + [("x", i) for i in reversed(range(self.NX))]

    def row0(self, ck):
        return ck[1] * CH if ck[0] == "c" else self.CL + ck[1] * CH

    def src_ap(self, li, ck):
        r0 = self.row0(ck)
        if li == 0:
            return (self.c_in if ck[0] == "c" else self.x_in)[ck[1] * CH:(ck[1] + 1) * CH, :]
        return self.xs[(li - 1) % 2][r0:r0 + CH, :]

    def dst_ap(self, li, ck):
        r0 = self.row0(ck)
        return self.xs[li % 2][r0:r0 + CH, :]

    def _build(self):
        nc = self.nc
        ph = Phase(nc)
        P = ph.P
        t32 = ph.sb("id32", [128, 128], F32)
        P.dma(self.ident32[:], self.ident_in[:, :], "c0", writes=[self.ident32])
        P.dve(lambda e: e.tensor_copy(out=self.ident[:], in_=self.ident32[:]), [self.ident32], [self.ident])
        P.dma(self.sel[:], self.sel_in[:, :, :], "c1", writes=[self.sel])
        ph.finish()
        for li, i in enumerate(self.layers):
            last = (li == len(self.layers) - 1)
            k = KINDS[i]
            if k == 0:
                self.retention_layer(li, i, last)
            elif k == 1:
                self.gmlp_layer(li, i, last)
            else:
                self.rwkv_layer(li, i, last)

    def emit_mod(self, ph, i):
        P, w = ph.P, self.lw[i]
        cc = ph.sb("cc", [128, 8, 2], F32)
        sg = ph.sb("sg", [128, 8, 2], F32)
        bcol = ph.sb("bcol", [128, 24], F32)
        brow = ph.sb("brow", [2, D], F32)
        ng = ph.sb("ng", [128, 8], F32)
        grow = ph.sb("grow", [2, D], F32)
        gate_t = [ph.sb("gate_t%d" % j, [128, D], F32) for j in range(2)]
        aw = ph.sb("adaw", [128, 8, 3 * D], F32)
        awk = [Tile(aw.t[:, k, :], "adaw%d" % k) for k in range(8)]
        pcol = ph.ps("pcol", [128, 16, 2], F32)
        prow = [ph.ps("prow%d" % n, [128, 512], F32) for n in range(2)]
        P.dma(cc[:], self.cc[:, :, :], "m0", writes=[cc])
        P.dma(bcol[:], w["ada_bcol"][:, :], "m1", writes=[bcol])
        P.dma(brow[:], w["ada_brow"][:, :], "m2", writes=[brow])
        P.dma(ng[:], w["ng_col"][:, :], "m3", writes=[ng])
        for k in range(8):
            P.dma(awk[k][:], w["ada_w"][k * 128:(k + 1) * 128, :], "ada%d" % k, writes=[awk[k]], q=("sp" if k % 2 == 0 else "pool"))
        P.act(lambda e: e.activation(out=sg[:], in_=cc[:], func=AF.Sigmoid), [cc], [sg])
        P.dve(lambda e: e.tensor_tensor(out=sg[:], in0=sg[:], in1=cc[:], op=ALU.mult), [sg, cc], [sg])
        for j in range(16):
            for k in range(8):
                P.pe(lambda e, j=j, k=k: e.matmul(pcol[:, j, :], lhsT=awk[k][:, j * 128:(j + 1) * 128], rhs=sg[:, k, :],
                                                  start=(k == 0), stop=(k == 7)), [awk[k], sg], [pcol])
        for n in range(2):
            for k in range(8):
                P.pe(lambda e, n=n, k=k: e.matmul(prow[n][0:2, :], lhsT=sg[:, k, :], rhs=awk[k][:, 2048 + n * 512:2048 + (n + 1) * 512],
                                                  start=(k == 0), stop=(k == 7)), [awk[k], sg], [prow[n]])
        tmp = ph.sb("modtmp", [128, 16, 2], F32)
        P.dve(lambda e: e.tensor_tensor(out=tmp[:], in0=pcol[:], in1=bcol[:, 0:16].unsqueeze(2).to_broadcast([128, 16, 2]), op=ALU.add),
              [pcol, bcol], [tmp])
        mod = self.mod
        for j in range(2):
            P.dve(lambda e, j=j: e.scalar_tensor_tensor(out=mod[:, 0, j, :], in0=tmp[:, 8:16, j], scalar=1.0, in1=ng[:], op0=ALU.add, op1=ALU.mult),
                  [tmp, ng], [mod])
            P.dve(lambda e, j=j: e.tensor_copy(out=mod[:, 1, j, :], in_=tmp[:, 0:8, j]), [tmp], [mod])
        for n in range(2):
            P.dve(lambda e, n=n: e.tensor_tensor(out=grow[:, n * 512:(n + 1) * 512], in0=prow[n][0:2, :], in1=brow[:, n * 512:(n + 1) * 512], op=ALU.add),
                  [prow[n], brow], [grow])
        for j in range(2):
            for n in range(2):
                P.pe(lambda e, j=j, n=n: e.matmul(prow[n][:, :], lhsT=self.sel[:, j, :], rhs=grow[:, n * 512:(n + 1) * 512], start=True, stop=True),
                     [self.sel, grow], [prow[n]])
                P.act(lambda e, j=j, n=n: e.activation(out=gate_t[j][:, n * 512:(n + 1) * 512], in_=prow[n][:, :], func=AF.Copy),
                      [prow[n]], [gate_t[j]])
        for j in range(2):
            P.dma(self.gate_dram[j], gate_t[j][:], "gst%d" % j, reads=[gate_t[j]], q="pool")

    def load_w(self, ph, Wt, src, col0, ncols, KT):
        P = ph.P
        views = []
        for k in range(KT):
            v = Tile(Wt.t[:, k, :], "%s_k%d" % (Wt.b.name, k))
            views.append(v)
            step = 2048
            for c0 in range(0, ncols, step):
                wdt = min(step, ncols - c0)
                P.dma(v.t[:, c0:c0 + wdt], src[k * 128:(k + 1) * 128, col0 + c0:col0 + c0 + wdt],
                      "w%s%d_%d" % (Wt.b.name, k, c0), writes=[v], q="pool")
        return views

    def front(self, ph, R, src, which, src_reads=()):
        P = ph.P
        mod = self.mod
        xt = R["xt"].next()
        P.dma(xt[:], src, "xt%d" % R["xt"].slot, reads=list(src_reads), writes=[xt])
        st = R["st"].next()
        junk = R["junk"]
        P.act(lambda e: e.activation(out=junk[:], in_=xt[:], func=AF.Square, accum_out=st[:, 0:1]), [xt], [junk, st])
        P.act(lambda e: e.activation(out=st[:, 1:2], in_=st[:, 0:1], func=AF.Sqrt, scale=1.0 / D, bias=NORM_EPS), [st], [st])
        P.dve(lambda e: e.reciprocal(out=st[:, 2:3], in_=st[:, 1:2]), [st], [st])
        xn = R["xn"].next()
        P.dve(lambda e: e.tensor_scalar(out=xn[:], in0=xt[:], scalar1=st[:, 2:3], scalar2=None, op0=ALU.mult), [xt, st], [xn])
        ptr = R["ptrx"]
        for k in range(8):
            P.pe(lambda e, k=k: e.transpose(out=ptr[:, k, :], in_=xn[:, k * 128:(k + 1) * 128], identity=self.ident[:]),
                 [xn, self.ident], [ptr])
        hT = R["hT"].next()
        for k in range(8):
            P.act(lambda e, k=k: e.activation(out=hT[:, k, :], in_=ptr[:, k, :], func=AF.Identity,
                                              scale=mod[:, 0, which, k:k + 1], bias=mod[:, 1, which, k:k + 1]),
                  [ptr, mod], [hT])
        return hT, xt

    def front_bufs(self, ph, n_xn=2, n_xt=2, n_hT=2, junk=True):
        return {
            "xt": ph.rot("xt", [128, D], F32, n_xt),
            "st": ph.rot("st", [128, 4], F32, 4),
            "junk": ph.sb("junk", [128, D], BF16) if junk else None,
            "xn": ph.rot("xn", [128, D], BF16, n_xn),
            "hT": ph.rot("hT", [128, 8, 128], BF16, n_hT),
            "ptrx": ph.ps("ptrx", [128, 8, 128], BF16),
        }

    def tail(self, ph, R, gated, Wo, li, ck, xres, last, KT=16):
        P = ph.P
        which = 1 if ck[0] == "c" else 0
        gT = R["gT"].next()
        for half in range(KT // 8):
            ptg = R["ptg"][half]
            for k in range(8):
                kk = half * 8 + k
                P.pe(lambda e, k=k, kk=kk, ptg=ptg: e.transpose(out=ptg[:, k, :], in_=gated[:, kk * 128:(kk + 1) * 128], identity=self.ident[:]),
                     [gated, self.ident], [ptg])
            if half == 0:
                P.act(lambda e, half=half, ptg=ptg: e.activation(out=gT[:, half * 8:(half + 1) * 8, :], in_=ptg[:], func=AF.Copy), [ptg], [gT])
            else:
                P.dve(lambda e, half=half, ptg=ptg: e.tensor_copy(out=gT[:, half * 8:(half + 1) * 8, :], in_=ptg[:]), [ptg], [gT])
        xo = R["xo"].next()
        for n in range(2):
            po = R["pout"].next() if isinstance(R["pout"], Rot) else R["pout"]
            for k in range(KT):
                P.pe(lambda e, n=n, k=k, po=po: e.matmul(po[:], lhsT=gT[:, k, :], rhs=Wo[k][:, n * 512:(n + 1) * 512], start=(k == 0), stop=(k == KT - 1)),
                     [gT, Wo[k]], [po])
            P.dve(lambda e, n=n, po=po: e.tensor_tensor(out=xo[:, n * 512:(n + 1) * 512], in0=po[:], in1=self.gate[which][:, n * 512:(n + 1) * 512], op=ALU.mult),
                  [po, self.gate[which]], [xo])
        P.dve(lambda e: e.tensor_tensor(out=xo[:], in0=xo[:], in1=xres[:], op=ALU.add), [xo, xres], [xo])
        if last and self.final_norm:
            st = R["st"].next()
            junk = R["junk"]
            P.act(lambda e: e.activation(out=junk[:], in_=xo[:], func=AF.Square, accum_out=st[:, 0:1]), [xo], [junk, st])
            P.act(lambda e: e.activation(out=st[:, 1:2], in_=st[:, 0:1], func=AF.Sqrt, scale=1.0 / D, bias=NORM_EPS), [st], [st])
            P.dve(lambda e: e.reciprocal(out=st[:, 2:3], in_=st[:, 1:2]), [st], [st])
            P.dve(lambda e: e.scalar_tensor_tensor(out=xo[:], in0=xo[:], scalar=st[:, 2:3], in1=self.fg[:], op0=ALU.mult, op1=ALU.mult),
                  [xo, st, self.fg], [xo])
            dst = self.out[ck[1] * CH:(ck[1] + 1) * CH, :]
        elif last:
            dst = self.out[ck[1] * CH:(ck[1] + 1) * CH, :]
        else:
            dst = self.dst_ap(li, ck)
        P.dma(dst, xo[:], "xo%d" % R["xo"].slot, reads=[xo], q="pool")

    def tail_bufs(self, ph, n_gT=2, n_xo=2, last=False, pout=True):
        if last and self.final_norm:
            self.fg = ph.sb("fg", [128, D], F32)
            ph.P.dma(self.fg[:], self.fg_in[:, :], "fgld", writes=[self.fg])
        self.gate = [ph.sb("gate%d" % j, [128, D], F32) for j in range(2)]
        for j in range(2):
            ph.P.dma(self.gate[j][:], self.gate_dram[j], "gld%d" % j, writes=[self.gate[j]])
        return {
            "gT": ph.rot("gT", [128, 16, 128], BF16, n_gT),
            "ptg": [ph.ps("ptg%d" % h, [128, 8, 128], BF16) for h in range(2)],
            "pout": ph.ps("pout", [128, 512], F32) if pout else None,
            "xo": ph.rot("xo", [128, D], F32, n_xo),
        }

    def retention_layer(self, li, i, last):
        nc, w = self.nc, self.lw[i]
        NCH = self.NX + self.NC
        if not hasattr(self, "scr_q"):
            self.scr_q = nc.dram_tensor("scr_q", [NCH, 128, 1024], BF16).ap()
            self.scr_k = nc.dram_tensor("scr_k", [NCH, 128, 1024], BF16).ap()
            self.scr_v = nc.dram_tensor("scr_v", [NCH, 128, 2048], BF16).ap()
            self.scr_o = nc.dram_tensor("scr_o", [NCH, 128, 2048], F32).ap()
            self.rt_cd = Tile(self.outer.enter_context(nc.sbuf_tensor("rt_cd", [128, 8], F32)), "rt_cd")
        ph = Phase(nc)
        self.emit_mod(ph, i)
        ph.finish()

        ph = Phase(nc)
        P = ph.P
        Wt = ph.sb("Wqkv", [128, 8, 4096], BF16)
        Wk = self.load_w(ph, Wt, w["w_in"], 0, 4096, 8)
        R = self.front_bufs(ph)
        dec = ph.sb("dec", [128, 8], F32)
        lg = ph.sb("lg", [128, 8], F32)
        dm = ph.sb("dm", [128, 6, 128], F32)
        jc = ph.sb("jc", [128, 2], F32)
        tA = ph.sb("tA", [128, 128], F32)
        tB = ph.sb("tB", [128, 128], F32)
        MT = ph.sb("MT", [128, 4, 128], F32)
        QDf = ph.sb("QDf", [128, 8, 128], BF16)
        QDb = ph.sb("QDb", [128, 8, 128], BF16)
        kdec = ph.sb("kdec", [128, 8], F32)
        cd = self.rt_cd
        P.dma(dec[:], w["decay"][:, :], "t0", writes=[dec])
        P.dma(dm[:], self.dmat_in[:, :, :], "t1", writes=[dm])
        P.dma(jc[:], self.jcol_in[:, :], "t2", writes=[jc])
        P.act(lambda e: e.activation(out=lg[:], in_=dec[:], func=AF.Exp, scale=-1.0), [dec], [lg])
        P.act(lambda e: e.activation(out=lg[:], in_=lg[:], func=AF.Ln, bias=1.0), [lg], [lg])
        P.dve(lambda e: e.tensor_scalar(out=lg[:], in0=lg[:], scalar1=-1.0, scalar2=None, op0=ALU.mult), [lg], [lg])
        P.act(lambda e: e.activation(out=cd[:], in_=lg[:], func=AF.Exp, scale=128.0), [lg], [cd])
        P.dve(lambda e: e.tensor_scalar(out=kdec[:, 0:4], in0=lg[:, 0:4], scalar1=jc[:, 0:1], scalar2=None, op0=ALU.mult), [lg, jc], [kdec])
        P.dve(lambda e: e.tensor_scalar(out=kdec[:, 4:8], in0=lg[:, 4:8], scalar1=jc[:, 1:2], scalar2=None, op0=ALU.mult), [lg, jc], [kdec])
        P.act(lambda e: e.activation(out=kdec[:], in_=kdec[:], func=AF.Exp), [kdec], [kdec])
        P.dve(lambda e: e.tensor_scalar(out=kdec[:], in0=kdec[:], scalar1=0.0625, scalar2=None, op0=ALU.mult), [kdec], [kdec])
        for h in range(4):
            P.act(lambda e, h=h: e.activation(out=tA[:], in_=dm[:, 0, :], func=AF.Exp, scale=lg[:, h:h + 1]), [dm, lg, MT], [tA])
            P.act(lambda e, h=h: e.activation(out=tB[:], in_=dm[:, 1, :], func=AF.Exp, scale=lg[:, 4 + h:5 + h]), [dm, lg, MT], [tB])
            P.dve(lambda e: e.tensor_tensor(out=tA[:], in0=tA[:], in1=dm[:, 2, :], op=ALU.mult), [tA, dm], [tA])
            P.dve(lambda e: e.tensor_tensor(out=tB[:], in0=tB[:], in1=dm[:, 3, :], op=ALU.mult), [tB, dm], [tB])
            P.dve(lambda e: e.tensor_tensor(out=tA[:], in0=tA[:], in1=tB[:], op=ALU.add), [tA, tB], [tA])
            P.dve(lambda e, h=h: e.tensor_scalar(out=MT[:, h, :], in0=tA[:], scalar1=0.0625, scalar2=None, op0=ALU.mult), [tA], [MT])
            for r in range(2):
                P.act(lambda e, h=h, r=r: e.activation(out=QDf[:, 2 * h + r, :], in_=dm[:, 4, :], func=AF.Exp, scale=lg[:, h:h + 1]), [dm, lg], [QDf])
                P.act(lambda e, h=h, r=r: e.activation(out=QDb[:, 2 * h + r, :], in_=dm[:, 5, :], func=AF.Exp, scale=lg[:, 4 + h:5 + h]), [dm, lg], [QDb])
        mm = Rot([ph.ps("mm%d" % j, [128, 512], F32) for j in range(2)])
        ptq = ph.ps("ptq", [128, 8, 128], BF16)
        ptk = ph.ps("ptk", [128, 8, 128], BF16)
        Gp = Rot([ph.ps("Gp%d" % j, [128, 512], F32) for j in range(3)])
        cs = ph.rot("cs", [128, 2, 2, 128], F32, 2)
        rtmp = ph.rot("rtmp", [128, 4, 2, 128], F32, 2)
        q_r = ph.sb("q_r", [128, 1024], BF16)
        k_r = ph.sb("k_r", [128, 1024], BF16)
        qT = ph.rot("qT", [128, 8, 128], BF16, 2)
        kT = ph.rot("kT", [128, 8, 128], BF16, 2)
        qTf = ph.rot("qTf", [128, 8, 128], BF16, 2)
        qTb = ph.rot("qTb", [128, 8, 128], BF16, 2)
        kf = ph.rot("kf", [128, 1024], BF16, 2)
        kb = ph.rot("kb", [128, 1024], BF16, 2)
        vsb = ph.rot("vsb", [128, 2048], BF16, 2)
        sT = ph.rot("sT", [128, 4, 128], BF16, 2)
        osb = ph.rot("osb", [128, 2048], F32, 2)
        Sbf = ph.sb("Sbf", [128, 8, 512], BF16)
        Sbk = [Tile(Sbf.t[:, k, :], "Sb%d" % k) for k in range(8)]
        for k in range(8):
            P.pool(lambda e, k=k: e.memset(Sbk[k][:], 0.0), [], [Sbk[k]])
        cdI = self.ret_cdI(ph)

        for ck in self.chunks_fwd():
            which = 1 if ck[0] == "c" else 0
            r0 = self.row0(ck)
            cidx = r0 // CH
            hT, xt = self.front(ph, R, self.src_ap(li, ck), which)
            c_ = cs.next()
            for r in range(2):
                P.dma(c_[:, :, r, :], self.rope_in[r0:r0 + CH, :, :], "cs%d_%d" % (cs.slot, r), writes=[c_])
            v_ = vsb.next()
            for n in range(8):
                bank = mm.next()
                for k in range(8):
                    P.pe(lambda e, bank=bank, n=n, k=k, hT=hT: e.matmul(bank[:], lhsT=hT[:, k, :], rhs=Wk[k][:, n * 512:(n + 1) * 512],
                                                                        start=(k == 0), stop=(k == 7)), [hT, Wk[k]], [bank])
                if n < 4:
                    dst = q_r if n < 2 else k_r
                    tmp = rtmp.next()
                    pb = bank[:].rearrange("p (h t c) -> p h t c", h=2, t=2)
                    t1, t2 = pb[:, :, 0, :], pb[:, :, 1, :]
                    cos2, sin2 = c_[:, 0, :, :], c_[:, 1, :, :]
                    P.dve(lambda e, tmp=tmp, t1=t1, cos2=cos2: e.tensor_tensor(out=tmp[:, 0], in0=t1, in1=cos2, op=ALU.mult), [bank, c_], [tmp])
                    P.dve(lambda e, tmp=tmp, t2=t2, sin2=sin2: e.tensor_tensor(out=tmp[:, 1], in0=t2, in1=sin2, op=ALU.mult), [bank, c_], [tmp])
                    P.dve(lambda e, tmp=tmp, t1=t1, sin2=sin2: e.tensor_tensor(out=tmp[:, 2], in0=t1, in1=sin2, op=ALU.mult), [bank, c_], [tmp])
                    P.dve(lambda e, tmp=tmp, t2=t2, cos2=cos2: e.tensor_tensor(out=tmp[:, 3], in0=t2, in1=cos2, op=ALU.mult), [bank, c_], [tmp])
                    dv = dst[:].rearrange("p (h t c) -> p h t c", h=4, t=2)
                    h0 = 2 * (n % 2)
                    P.pool(lambda e, tmp=tmp, dv=dv, h0=h0: e.tensor_tensor(out=dv[:, h0:h0 + 2, 0, :], in0=tmp[:, 0], in1=tmp[:, 1], op=ALU.subtract),
                           [tmp], [dst])
                    P.pool(lambda e, tmp=tmp, dv=dv, h0=h0: e.tensor_tensor(out=dv[:, h0:h0 + 2, 1, :], in0=tmp[:, 2], in1=tmp[:, 3], op=ALU.add),
                           [tmp], [dst])
                else:
                    P.act(lambda e, bank=bank, n=n, v_=v_: e.activation(out=v_[:, (n - 4) * 512:(n - 3) * 512], in_=bank[:], func=AF.Copy), [bank], [v_])
            qT_, kT_, qTf_, qTb_ = qT.next(), kT.next(), qTf.next(), qTb.next()
            for k in range(8):
                P.pe(lambda e, k=k: e.transpose(out=ptq[:, k, :], in_=q_r[:, k * 128:(k + 1) * 128], identity=self.ident[:]), [q_r, self.ident], [ptq])
            P.act(lambda e, qT_=qT_: e.activation(out=qT_[:], in_=ptq[:], func=AF.Copy), [ptq], [qT_])
            for k in range(8):
                P.pe(lambda e, k=k: e.transpose(out=ptk[:, k, :], in_=k_r[:, k * 128:(k + 1) * 128], identity=self.ident[:]), [k_r, self.ident], [ptk])
            P.act(lambda e, kT_=kT_: e.activation(out=kT_[:], in_=ptk[:], func=AF.Copy), [ptk], [kT_])
            P.dve(lambda e, qT_=qT_, qTf_=qTf_: e.tensor_tensor(out=qTf_[:], in0=qT_[:], in1=QDf[:], op=ALU.mult), [qT_, QDf], [qTf_])
            P.dve(lambda e, qT_=qT_, qTb_=qTb_: e.tensor_tensor(out=qTb_[:], in0=qT_[:], in1=QDb[:], op=ALU.mult), [qT_, QDb], [qTb_])
            kf_, kb_ = kf.next(), kb.next()
            krv = k_r[:].rearrange("p (h c) -> p h c", h=4)
            P.pool(lambda e, kf_=kf_: e.tensor_tensor(out=kf_[:].rearrange("p (h c) -> p h c", h=4), in0=krv,
                                                      in1=kdec[:, 0:4].unsqueeze(2).to_broadcast([128, 4, 256]), op=ALU.mult), [k_r, kdec], [kf_])
            P.pool(lambda e, kb_=kb_: e.tensor_tensor(out=kb_[:].rearrange("p (h c) -> p h c", h=4), in0=krv,
                                                      in1=kdec[:, 4:8].unsqueeze(2).to_broadcast([128, 4, 256]), op=ALU.mult), [k_r, kdec], [kb_])
            psc = Gp.next()
            for h in range(4):
                for hf in range(2):
                    P.pe(lambda e, h=h, hf=hf, kT_=kT_, qT_=qT_, psc=psc: e.matmul(psc[:, h * 128:(h + 1) * 128], lhsT=kT_[:, 2 * h + hf, :], rhs=qT_[:, 2 * h + hf, :],
                                                                                   start=(hf == 0), stop=(hf == 1)), [kT_, qT_], [psc])
            sT_ = sT.next()
            P.dve(lambda e, sT_=sT_, psc=psc: e.tensor_tensor(out=sT_[:], in0=psc[:].rearrange("p (h t) -> p h t", h=4), in1=MT[:], op=ALU.mult), [psc, MT], [sT_])
            o_ = osb.next()
            for h in range(4):
                po = Gp.next()
                P.pe(lambda e, h=h, sT_=sT_, v_=v_, po=po: e.matmul(po[:], lhsT=sT_[:, h, :], rhs=v_[:, h * 512:(h + 1) * 512], start=True, stop=False), [sT_, v_], [po])
                for hf in range(2):
                    kt = 2 * h + hf
                    P.pe(lambda e, kt=kt, hf=hf, qTf_=qTf_, po=po: e.matmul(po[:], lhsT=qTf_[:, kt, :], rhs=Sbk[kt][:], start=False, stop=(hf == 1)),
                         [qTf_, Sbk[kt]], [po])
                P.act(lambda e, h=h, o_=o_, po=po: e.activation(out=o_[:, h * 512:(h + 1) * 512], in_=po[:], func=AF.Copy), [po], [o_])
            self.ret_state_update(P, Gp, kf_, v_, Sbk, cdI, 0)
            P.dma(self.scr_q[cidx].rearrange("p (k t) -> p k t", k=8), qTb_[:], "sq%d" % qTb.slot, reads=[qTb_], q="pool")
            P.dma(self.scr_k[cidx], kb_[:], "sk%d" % kb.slot, reads=[kb_], q="pool")
            P.dma(self.scr_v[cidx], v_[:], "sv%d" % vsb.slot, reads=[v_], q="pool")
            P.dma(self.scr_o[cidx], o_[:], "so%d" % osb.slot, reads=[o_], q="pool")
        ph.finish()

        ph = Phase(nc)
        P = ph.P
        Wzt = ph.sb("Wz", [128, 8, 2048], BF16)
        Wz = self.load_w(ph, Wzt, w["w_in"], 4096, 2048, 8)
        Wot = ph.sb("Wo", [128, 16, 1024], BF16)
        Wo = self.load_w(ph, Wot, w["w_out"], 0, 1024, 16)
        R = self.front_bufs(ph)
        R.update(self.tail_bufs(ph, last=last, pout=False))
        mm = Rot([ph.ps("mm%d" % j, [128, 512], F32) for j in range(2)])
        Gp = Rot([ph.ps("Gp%d" % j, [128, 512], F32) for j in range(3)])
        R["pout"] = Gp
        qTb = ph.rot("qTb", [128, 8, 128], BF16, 2)
        kb = ph.rot("kb", [128, 1024], BF16, 2)
        vsb = ph.rot("vsb", [128, 2048], BF16, 2)
        osb = ph.rot("osb", [128, 2048], F32, 2)
        sz = ph.rot("sz", [128, 2048], F32, 2)
        gated = ph.rot("gated", [128, 2048], BF16, 2)
        st4 = ph.rot("st4", [128, 12], F32, 2)
        Sbf = ph.sb("Sbf", [128, 8, 512], BF16)
        Sbk = [Tile(Sbf.t[:, k, :], "Sb%d" % k) for k in range(8)]
        for k in range(8):
            P.pool(lambda e, k=k: e.memset(Sbk[k][:], 0.0), [], [Sbk[k]])
        cdI = self.ret_cdI(ph)
        cd = self.rt_cd
        for ck in self.chunks_bwd():
            which = 1 if ck[0] == "c" else 0
            cidx = self.row0(ck) // CH
            need_out = not (last and ck[0] == "c")
            kb_, v_ = kb.next(), vsb.next()
            P.dma(kb_[:], self.scr_k[cidx], "lk%d" % kb.slot, writes=[kb_])
            P.dma(v_[:], self.scr_v[cidx], "lv%d" % vsb.slot, writes=[v_])
            if need_out:
                q_, o_ = qTb.next(), osb.next()
                P.dma(q_[:], self.scr_q[cidx].rearrange("p (k t) -> p k t", k=8), "lq%d" % qTb.slot, writes=[q_])
                P.dma(o_[:], self.scr_o[cidx], "lo%d" % osb.slot, writes=[o_])
                hT, xt = self.front(ph, R, self.src_ap(li, ck), which)
                sz_ = sz.next()
                for n in range(4):
                    bank = mm.next()
                    for k in range(8):
                        P.pe(lambda e, bank=bank, n=n, k=k, hT=hT: e.matmul(bank[:], lhsT=hT[:, k, :], rhs=Wz[k][:, n * 512:(n + 1) * 512],
                                                                            start=(k == 0), stop=(k == 7)), [hT, Wz[k]], [bank])
                    P.act(lambda e, bank=bank, n=n, sz_=sz_: e.activation(out=sz_[:, n * 512:(n + 1) * 512], in_=bank[:], func=AF.Silu), [bank], [sz_])
                s4 = st4.next()
                junk = R["junk"]
                for h in range(4):
                    po = Gp.next()
                    for hf in range(2):
                        kt = 2 * h + hf
                        P.pe(lambda e, kt=kt, hf=hf, q_=q_, po=po: e.matmul(po[:], lhsT=q_[:, kt, :], rhs=Sbk[kt][:], start=(hf == 0), stop=(hf == 1)),
                             [q_, Sbk[kt]], [po])
                    P.dve(lambda e, h=h, o_=o_, po=po: e.tensor_tensor(out=o_[:, h * 512:(h + 1) * 512], in0=po[:], in1=o_[:, h * 512:(h + 1) * 512], op=ALU.add),
                          [po, o_], [o_])
                    P.act(lambda e, h=h, o_=o_, s4=s4: e.activation(out=junk[:, 0:512], in_=o_[:, h * 512:(h + 1) * 512], func=AF.Square, accum_out=s4[:, h:h + 1]),
                          [o_], [junk, s4])
                P.act(lambda e, s4=s4: e.activation(out=s4[:, 4:8], in_=s4[:, 0:4], func=AF.Sqrt, scale=1.0 / 512, bias=NORM_EPS), [s4], [s4])
                P.dve(lambda e, s4=s4: e.reciprocal(out=s4[:, 8:12], in_=s4[:, 4:8]), [s4], [s4])
                g_ = gated.next()
                for h in range(4):
                    P.dve(lambda e, h=h, o_=o_, s4=s4, sz_=sz_, g_=g_: e.scalar_tensor_tensor(
                        out=g_[:, h * 512:(h + 1) * 512], in0=o_[:, h * 512:(h + 1) * 512], scalar=s4[:, 8 + h:9 + h],
                        in1=sz_[:, h * 512:(h + 1) * 512], op0=ALU.mult, op1=ALU.mult), [o_, s4, sz_], [g_])
                self.tail(ph, R, g_, Wo, li, ck, xt, last)
            self.ret_state_update(P, Gp, kb_, v_, Sbk, cdI, 4)
        ph.finish()

    def ret_state_update(self, P, pst_rot, kd, v_, Sbk, cdI, c0):
        for kt in range(8):
            h = kt // 2
            pst = pst_rot.next()
            P.pe(lambda e, kt=kt, h=h, pst=pst: e.matmul(pst[:], lhsT=cdI[:, c0 + h, :], rhs=Sbk[kt][:], start=True, stop=False), [cdI, Sbk[kt]], [pst])
            P.pe(lambda e, kt=kt, h=h, pst=pst: e.matmul(pst[:], lhsT=kd[:, kt * 128:(kt + 1) * 128], rhs=v_[:, h * 512:(h + 1) * 512], start=False, stop=True),
                 [kd, v_], [pst])
            if kt % 2 == 0:
                P.act(lambda e, kt=kt, pst=pst: e.activation(out=Sbk[kt][:], in_=pst[:], func=AF.Copy), [pst], [Sbk[kt]])
            else:
                P.dve(lambda e, kt=kt, pst=pst: e.tensor_copy(out=Sbk[kt][:], in_=pst[:]), [pst], [Sbk[kt]])

    def ret_cdI(self, ph):
        cdI = ph.sb("cdI", [128, 8, 128], BF16)
        for j in range(8):
            ph.P.dve(lambda e, j=j: e.tensor_scalar(out=cdI[:, j, :], in0=self.ident32[:], scalar1=self.rt_cd[:, j:j + 1], scalar2=None, op0=ALU.mult),
                     [self.ident32, self.rt_cd], [cdI])
        return cdI

    def gmlp_layer(self, li, i, last):
        nc, w = self.nc, self.lw[i]
        ph = Phase(nc)
        self.emit_mod(ph, i)
        ph.finish()
        ph = Phase(nc)
        P = ph.P
        Wt = ph.sb("Wuvz", [128, 8, 6144], BF16)
        Wk = self.load_w(ph, Wt, w["w_in"], 0, 6144, 8)
        Wot = ph.sb("Wo", [128, 16, 1024], BF16)
        Wo = self.load_w(ph, Wot, w["w_out"], 0, 1024, 16)
        R = self.front_bufs(ph, n_xn=1)
        R.update(self.tail_bufs(ph, n_gT=1, n_xo=1, last=last))
        mm = Rot([ph.ps("mm%d" % j, [128, 512], F32) for j in range(2)])
        spb = Rot([ph.ps("spb%d" % j, [128, 512], F32) for j in range(2)])
        wsT = ph.sb("wsT", [128, 8, 128], BF16)
        bsT = ph.sb("bsT", [128, 8], F32)
        vg = ph.sb("vg", [128, 2048], F32)
        P.dma(wsT[:], w["wsT"][:, :, :], "g0", writes=[wsT], q="pool")
        P.dma(bsT[:], w["bsT"][:, :], "g1", writes=[bsT])
        P.dma(vg[:], w["vg"][:, :], "g2", writes=[vg])
        usb = ph.rot("usb", [128, 2048], F32, 1)
        vsb = ph.rot("vsb", [128, 2048], F32, 1)
        szb = ph.rot("szb", [128, 2048], BF16, 1)
        vnb = ph.rot("vnb", [128, 2048], BF16, 1)
        gated = ph.rot("gated", [128, 2048], BF16, 1)
        stv = ph.rot("stv", [128, 16], F32, 2)
        junk = R["junk"]
        order = self.chunks_fwd()
        if last:
            order = [ck for ck in order if ck[0] == "x"]
        for ck in order:
            which = 1 if ck[0] == "c" else 0
            hT, xt = self.front(ph, R, self.src_ap(li, ck), which)
            u_, v_, z_, s_ = usb.next(), vsb.next(), szb.next(), stv.next()
            for n in range(12):
                bank = mm.next()
                for k in range(8):
                    P.pe(lambda e, bank=bank, n=n, k=k, hT=hT: e.matmul(bank[:], lhsT=hT[:, k, :], rhs=Wk[k][:, n * 512:(n + 1) * 512],
                                                                        start=(k == 0), stop=(k == 7)), [hT, Wk[k]], [bank])
                if n < 4:
                    P.act(lambda e, bank=bank, n=n, u_=u_: e.activation(out=u_[:, n * 512:(n + 1) * 512], in_=bank[:], func=AF.Copy), [bank], [u_])
                elif n < 8:
                    P.act(lambda e, bank=bank, n=n, v_=v_, s_=s_: e.activation(out=v_[:, (n - 4) * 512:(n - 3) * 512], in_=bank[:], func=AF.Identity,
                                                                              accum_out=s_[:, n - 4:n - 3]), [bank], [v_, s_])
                else:
                    P.act(lambda e, bank=bank, n=n, z_=z_: e.activation(out=z_[:, (n - 8) * 512:(n - 7) * 512], in_=bank[:], func=AF.Silu), [bank], [z_])
            for hh in range(2):
                P.act(lambda e, v_=v_, s_=s_, hh=hh: e.activation(out=junk[:], in_=v_[:, hh * 1024:(hh + 1) * 1024], func=AF.Square,
                                                                  accum_out=s_[:, 11 + hh:12 + hh]), [v_], [junk, s_])
            P.dve(lambda e, s_=s_: e.tensor_tensor(out=s_[:, 4:5], in0=s_[:, 11:12], in1=s_[:, 12:13], op=ALU.add), [s_], [s_])
            P.dve(lambda e, s_=s_: e.tensor_reduce(out=s_[:, 5:6], in_=s_[:, 0:4], axis=AX.X, op=ALU.add), [s_], [s_])
            P.dve(lambda e, s_=s_: e.tensor_scalar(out=s_[:, 5:6], in0=s_[:, 5:6], scalar1=1.0 / 2048, scalar2=None, op0=ALU.mult), [s_], [s_])
            P.dve(lambda e, s_=s_: e.tensor_tensor(out=s_[:, 6:7], in0=s_[:, 5:6], in1=s_[:, 5:6], op=ALU.mult), [s_], [s_])
            P.dve(lambda e, s_=s_: e.scalar_tensor_tensor(out=s_[:, 7:8], in0=s_[:, 4:5], scalar=1.0 / 2048, in1=s_[:, 6:7], op0=ALU.mult, op1=ALU.subtract),
                  [s_], [s_])
            P.act(lambda e, s_=s_: e.activation(out=s_[:, 8:9], in_=s_[:, 7:8], func=AF.Sqrt, scale=1.0, bias=NORM_EPS), [s_], [s_])
            P.dve(lambda e, s_=s_: e.reciprocal(out=s_[:, 9:10], in_=s_[:, 8:9]), [s_], [s_])
            P.dve(lambda e, s_=s_: e.scalar_tensor_tensor(out=s_[:, 10:11], in0=s_[:, 5:6], scalar=-1.0, in1=s_[:, 9:10], op0=ALU.mult, op1=ALU.mult),
                  [s_], [s_])
            P.act(lambda e, v_=v_, s_=s_: e.activation(out=v_[:], in_=v_[:], func=AF.Identity, scale=s_[:, 9:10], bias=s_[:, 10:11]), [v_, s_], [v_])
            vn_ = vnb.next()
            P.dve(lambda e, v_=v_, vn_=vn_: e.tensor_tensor(out=vn_[:], in0=v_[:], in1=vg[:], op=ALU.mult), [v_, vg], [vn_])
            for g in range(8):
                sb_ = spb.next()
                P.pe(lambda e, g=g, sb_=sb_, vn_=vn_: e.matmul(sb_[:, 0:256], lhsT=wsT[:, g, :], rhs=vn_[:, g * 256:(g + 1) * 256], start=True, stop=True),
                     [wsT, vn_], [sb_])
                P.dve(lambda e, g=g, sb_=sb_, u_=u_: e.scalar_tensor_tensor(out=u_[:, g * 256:(g + 1) * 256], in0=sb_[:, 0:256], scalar=bsT[:, g:g + 1],
                                                                           in1=u_[:, g * 256:(g + 1) * 256], op0=ALU.add, op1=ALU.mult), [sb_, bsT, u_], [u_])
            g_ = gated.next()
            P.pool(lambda e, u_=u_, z_=z_, g_=g_: e.tensor_tensor(out=g_[:], in0=u_[:], in1=z_[:], op=ALU.mult), [u_, z_], [g_])
            self.tail(ph, R, g_, Wo, li, ck, xt, last)
        ph.finish()

    def _rw_inputs(self, w, i, din):
        w["mu"] = din("rw_mu%d" % i, [128, 6, 8])
        w["rkvg"] = din("rw_rkvg%d" % i, [4, D, D])
        w["w1"] = din("rw_w1%d" % i, [2, D, 64])
        w["a1"] = din("rw_a1%d" % i, [2, D, 64])
        w["w2"] = din("rw_w2%d" % i, [2, 64, D])
        w["a2"] = din("rw_a2%d" % i, [2, 64, D])
        w["rows"] = din("rw_rows%d" % i, [8, D])
        w["bc"] = din("rw_bc%d" % i, [128, 5, D])
        w["w_out"] = din("rw_wout%d" % i, [D, D])
        w["masks"] = din("rw_masks%d" % i, [2, 128, 4, 128])
        w["sel8"] = din("rw_sel8%d" % i, [8, 8, 128])
        w["negc"] = din("rw_negc%d" % i, [128, 1])
        w["bmask"] = din("rw_bmask%d" % i, [128, 4, 128])
        w["cmask"] = din("rw_cmask%d" % i, [2, 128, 7, 128])

    def rw_shift(self, P, sh, hc, hp, hn, kind):
        if kind == "x":
            P.act(lambda e: e.activation(out=sh[:, 0:2, 1:128], in_=hc[:, 0:2, 0:127], func=AF.Copy), [hc], [sh])
            P.pool(lambda e: e.memset(sh[:, 0:2, :].rearrange("p k (r c) -> p k r c", c=64)[:, :, :, 0:1], 0.0), [], [sh])
            P.act(lambda e: e.activation(out=sh[:, 2:4, 0:127], in_=hc[:, 2:4, 1:128], func=AF.Copy), [hc], [sh])
            P.pool(lambda e: e.memset(sh[:, 2:4, :].rearrange("p k (r c) -> p k r c", c=64)[:, :, :, 63:64], 0.0), [], [sh])
            P.act(lambda e: e.activation(out=sh[:, 4:6, 64:128], in_=hc[:, 4:6, 0:64], func=AF.Copy), [hc], [sh])
            if hp is not None:
                P.act(lambda e: e.activation(out=sh[:, 4:6, 0:64], in_=hp[:, 4:6, 64:128], func=AF.Copy), [hp], [sh])
            else:
                P.pool(lambda e: e.memset(sh[:, 4:6, 0:64], 0.0), [], [sh])
            P.act(lambda e: e.activation(out=sh[:, 6:8, 0:64], in_=hc[:, 6:8, 64:128], func=AF.Copy), [hc], [sh])
            if hn is not None:
                P.act(lambda e: e.activation(out=sh[:, 6:8, 64:128], in_=hn[:, 6:8, 0:64], func=AF.Copy), [hn], [sh])
            else:
                P.pool(lambda e: e.memset(sh[:, 6:8, 64:128], 0.0), [], [sh])
        else:
            P.act(lambda e: e.activation(out=sh[:, 0:4, 1:128], in_=hc[:, 0:4, 0:127], func=AF.Copy), [hc], [sh])
            if hp is not None:
                P.act(lambda e: e.activation(out=sh[:, 0:4, 0:1], in_=hp[:, 0:4, 127:128], func=AF.Copy), [hp], [sh])
            else:
                P.pool(lambda e: e.memset(sh[:, 0:4, 0:1], 0.0), [], [sh])
            P.act(lambda e: e.activation(out=sh[:, 4:8, 0:127], in_=hc[:, 4:8, 1:128], func=AF.Copy), [hc], [sh])
            if hn is not None:
                P.act(lambda e: e.activation(out=sh[:, 4:8, 127:128], in_=hn[:, 4:8, 0:1], func=AF.Copy), [hn], [sh])
            else:
                P.pool(lambda e: e.memset(sh[:, 4:8, 127:128], 0.0), [], [sh])

    def rw_neighbors(self, ck):
        n = self.NC if ck[0] == "c" else self.NX
        p = (ck[0], ck[1] - 1) if ck[1] > 0 else None
        q = (ck[0], ck[1] + 1) if ck[1] < n - 1 else None
        return p, q

    def rw_hcache(self, ph, R, li):
        cache = []

        def get(ck):
            for c, v in cache:
                if c == ck:
                    return v
            which = 1 if ck[0] == "c" else 0
            v = self.front(ph, R, self.src_ap(li, ck), which)
            cache.append((ck, v))
            if len(cache) > 3:
                cache.pop(0)
            return v
        return get

    def rw_mix(self, P, mixr, tmpr, xx, hc, mu, p):
        mix = mixr.next()
        for k in range(8):
            P.dve(lambda e, k=k: e.scalar_tensor_tensor(out=mix[:, k, :], in0=xx[:, k, :], scalar=mu[:, p, k:k + 1], in1=hc[:, k, :],
                                                        op0=ALU.mult, op1=ALU.add), [xx, mu, hc], [mix])
        return mix

    def rwkv_layer(self, li, i, last):
        nc, w = self.nc, self.lw[i]
        NCH = self.NX + self.NC
        if not hasattr(self, "scr_o"):
            self.scr_o = nc.dram_tensor("scr_o", [NCH, 128, 2048], F32).ap()
        if not hasattr(self, "rw_scr_v"):
            self.rw_scr_v = nc.dram_tensor("rw_scr_v", [NCH, 128, 1024], BF16).ap()
            self.rw_dir = nc.dram_tensor("rw_dir", [2, NCH, 128, 4096], BF16).ap()
            self.rw_small = nc.dram_tensor("rw_small", [2, NCH, 128, 32], F32).ap()
        ph = Phase(nc)
        self.emit_mod(ph, i)
        ph.finish()
        self.rw_prep_phase(li, i)
        import os as _os
        for d in range(2):
            self.rw_scan_phase(li, i, d)
            if _os.environ.get("RW_STOP_AFTER_F") == "1":
                return
        self.rw_out_phase(li, i, last)

    def rw_prep_phase(self, li, i):
        nc, w = self.nc, self.lw[i]
        ph = Phase(nc)
        P = ph.P
        C0 = 0.6065306597126334
        Wr = self.load_w(ph, ph.sb("Wr", [128, 8, 1024], BF16), w["rkvg"][0], 0, 1024, 8)
        Wkk = self.load_w(ph, ph.sb("Wk", [128, 8, 1024], BF16), w["rkvg"][1], 0, 1024, 8)
        Wv = self.load_w(ph, ph.sb("Wv", [128, 8, 1024], BF16), w["rkvg"][2], 0, 1024, 8)
        w1 = [self.load_w(ph, ph.sb("w1_%d" % d, [128, 8, 64], BF16), w["w1"][d], 0, 64, 8) for d in range(2)]
        a1 = [self.load_w(ph, ph.sb("a1_%d" % d, [128, 8, 64], BF16), w["a1"][d], 0, 64, 8) for d in range(2)]
        w2 = [ph.sb("w2_%d" % d, [64, 1024], BF16) for d in range(2)]
        a2 = [ph.sb("a2_%d" % d, [64, 1024], BF16) for d in range(2)]
        tri = [ph.sb("tri%d" % d, [128, 128], F32) for d in range(2)]
        for d in range(2):
            P.dma(w2[d][:], w["w2"][d], "w2%d" % d, writes=[w2[d]], q="pool")
            P.dma(a2[d][:], w["a2"][d], "a2%d" % d, writes=[a2[d]], q="pool")
            P.dma(tri[d][:], w["masks"][d][:, 3, :], "tri%d" % d, writes=[tri[d]])
        rows = ph.sb("rows", [8, 1024], F32)
        sel8 = ph.sb("sel8", [8, 8, 128], F32)
        mu = ph.sb("mu", [128, 6, 8], F32)
        negc = ph.sb("negc", [128, 1], F32)
        kk_bc = ph.sb("kk_bc", [128, 1024], F32)
        ka_bc = ph.sb("ka_bc", [128, 1024], F32)
        rk_bc = ph.sb("rk_bc", [128, 1024], F32)
        P.dma(rows[:], w["rows"][:, :], "c0", writes=[rows])
        P.dma(sel8[:], w["sel8"][:, :, :], "c1", writes=[sel8])
        P.dma(mu[:], w["mu"][:, :, :], "c2", writes=[mu])
        P.dma(negc[:], w["negc"][:, :], "c4", writes=[negc])
        P.dma(kk_bc[:], w["bc"][:, 0, :], "c5", writes=[kk_bc])
        P.dma(ka_bc[:], w["bc"][:, 1, :], "c6", writes=[ka_bc])
        P.dma(rk_bc[:], w["bc"][:, 2, :], "c7", writes=[rk_bc])
        R = self.front_bufs(ph, n_xn=2, n_xt=2, n_hT=4)
        get_h = self.rw_hcache(ph, R, li)
        G = Rot([ph.ps("G%d" % j, [128, 512], F32) for j in range(5)])
        psm = [ph.ps("psm%d" % d, [128, 512], F32) for d in range(2)]
        sh = ph.rot("sh", [128, 8, 128], BF16, 2)
        xx = ph.rot("xx", [128, 8, 128], F32, 2)
        mixr = ph.rot("mix", [128, 8, 128], BF16, 4)
        r_sb = ph.sb("r_sb", [128, 1024], F32)
        k_sb = ph.sb("k_sb", [128, 1024], F32)
        kk = ph.sb("kk_sb", [128, 1024], F32)
        v_bf = ph.rot("v_bf", [128, 1024], BF16, 2)
        Wt = [[ph.sb("W%d_%d" % (d, j), [128, 1024], F32) for j in range(4)] for d in range(2)]
        th_bf = [ph.sb("th_bf%d" % d, [64, 128], BF16) for d in range(2)]
        la_bf = [ph.sb("la_bf%d" % d, [64, 128], BF16) for d in range(2)]
        outb = [ph.sb("outb%d" % d, [128, 4096], BF16) for d in range(2)]
        small = [ph.rot("small%d" % d, [128, 32], F32, 2) for d in range(2)]
        sm = ph.rot("sm", [128, 48], F32, 2)
        for d in range(2):
            for t_ in small[d].tiles:
                P.dve(lambda e, t_=t_: e.memset(t_[:], 0.0), [], [t_])

        def chunk_body(ck):
            cidx = self.row0(ck) // CH
            pk, nk = self.rw_neighbors(ck)
            hc = get_h(ck)[0]
            hp = get_h(pk)[0] if pk else None
            hn = get_h(nk)[0] if nk else None
            sh_, xx_ = sh.next(), xx.next()
            self.rw_shift(P, sh_, hc, hp, hn, ck[0])
            P.dve(lambda e: e.tensor_tensor(out=xx_[:], in0=sh_[:], in1=hc[:], op=ALU.subtract), [sh_, hc], [xx_])
            v_, s_ = v_bf.next(), sm.next()
            for p, Wl, dst in ((0, Wr, r_sb), (2, Wkk, k_sb), (3, Wv, v_)):
                mix = self.rw_mix(P, mixr, None, xx_, hc, mu, p)
                for n in range(2):
                    bank = G.next()
                    for k in range(8):
                        P.pe(lambda e, bank=bank, n=n, k=k, mix=mix, Wl=Wl: e.matmul(bank[:], lhsT=mix[:, k, :], rhs=Wl[k][:, n * 512:(n + 1) * 512],
                                                                                    start=(k == 0), stop=(k == 7)), [mix, Wl[k]], [bank])
                    P.act(lambda e, bank=bank, n=n, dst=dst: e.activation(out=dst[:, n * 512:(n + 1) * 512], in_=bank[:], func=AF.Copy), [bank], [dst])
            P.dma(self.rw_scr_v[cidx], v_[:], "pv%d" % v_bf.slot, reads=[v_], q="pool")
            sq = Wt[0][1]
            P.dve(lambda e: e.tensor_tensor(out=kk[:], in0=k_sb[:], in1=kk_bc[:], op=ALU.mult), [k_sb, kk_bc], [kk])
            P.dve(lambda e: e.tensor_tensor(out=sq[:], in0=kk[:], in1=kk[:], op=ALU.mult), [kk], [sq])
            P.dve(lambda e: e.tensor_reduce(out=s_[:, 0:16], in_=sq[:].rearrange("p (h c) -> p h c", h=16), axis=AX.X, op=ALU.add), [sq], [s_])
            P.act(lambda e: e.activation(out=s_[:, 16:32], in_=s_[:, 0:16], func=AF.Sqrt), [s_], [s_])
            P.dve(lambda e: e.tensor_scalar(out=s_[:, 16:32], in0=s_[:, 16:32], scalar1=1e-12, scalar2=None, op0=ALU.max), [s_], [s_])
            P.dve(lambda e: e.reciprocal(out=s_[:, 32:48], in_=s_[:, 16:32]), [s_], [s_])
            P.dve(lambda e: e.tensor_tensor(out=kk[:].rearrange("p (h c) -> p h c", h=16), in0=kk[:].rearrange("p (h c) -> p h c", h=16),
                                            in1=s_[:, 32:48].unsqueeze(2).to_broadcast([128, 16, 64]), op=ALU.mult), [kk, s_], [kk])
            mix1 = self.rw_mix(P, mixr, None, xx_, hc, mu, 1)
            mix4 = self.rw_mix(P, mixr, None, xx_, hc, mu, 4)
            sml = [small[d].next() for d in range(2)]
            for d in range(2):
                for k in range(8):
                    P.pe(lambda e, k=k, d=d: e.matmul(psm[d][0:64, 0:128], lhsT=w1[d][k][:, :], rhs=mix1[:, k, :], start=(k == 0), stop=(k == 7)),
                         [mix1, w1[d][k]], [psm[d]])
                P.act(lambda e, d=d: e.activation(out=th_bf[d][:], in_=psm[d][0:64, 0:128], func=AF.Tanh), [psm[d]], [th_bf[d]])
            for d in range(2):
                for k in range(8):
                    P.pe(lambda e, k=k, d=d: e.matmul(psm[d][0:64, 0:128], lhsT=a1[d][k][:, :], rhs=mix4[:, k, :], start=(k == 0), stop=(k == 7)),
                         [mix4, a1[d][k]], [psm[d]])
                P.act(lambda e, d=d: e.activation(out=la_bf[d][:], in_=psm[d][0:64, 0:128], func=AF.Copy), [psm[d]], [la_bf[d]])
            for d in range(2):
                sig = Wt[d][0]
                for n in range(2):
                    bank = G.next()
                    P.pe(lambda e, bank=bank, n=n, d=d: e.matmul(bank[:], lhsT=th_bf[d][:], rhs=w2[d][:, n * 512:(n + 1) * 512], start=True, stop=False),
                         [th_bf[d], w2[d]], [bank])
                    P.pe(lambda e, bank=bank, n=n, d=d: e.matmul(bank[:], lhsT=sel8[:, d, :], rhs=rows[:, n * 512:(n + 1) * 512], start=False, stop=True),
                         [sel8, rows], [bank])
                    P.act(lambda e, bank=bank, n=n, sig=sig: e.activation(out=sig[:, n * 512:(n + 1) * 512], in_=bank[:], func=AF.Sigmoid), [bank], [sig])
            for d in range(2):
                sig, ep, em, ex = Wt[d]
                for n in range(2):
                    bank = G.next()
                    sl = slice(n * 512, (n + 1) * 512)
                    P.pe(lambda e, bank=bank, sl=sl, d=d, sig=sig: e.matmul(bank[:], lhsT=tri[d][:], rhs=sig[:, sl], start=True, stop=True), [tri[d], sig], [bank])
                    P.act(lambda e, bank=bank, sl=sl, ep=ep: e.activation(out=ep[:, sl], in_=bank[:], func=AF.Exp), [bank], [ep])
                    P.act(lambda e, bank=bank, sl=sl, em=em: e.activation(out=em[:, sl], in_=bank[:], func=AF.Exp, scale=-1.0), [bank], [em])
                    P.dve(lambda e, bank=bank, sl=sl, ex=ex, sig=sig: e.scalar_tensor_tensor(out=ex[:, sl], in0=sig[:, sl], scalar=C0, in1=bank[:], op0=ALU.mult, op1=ALU.add),
                          [bank, sig], [ex])
                P.act(lambda e, ex=ex: e.activation(out=ex[:], in_=ex[:], func=AF.Exp), [ex], [ex])
            for d in range(2):
                sig = Wt[d][0]
                for h in range(16):
                    P.pe(lambda e, h=h, d=d, sig=sig: e.matmul(psm[d][0:64, 256 + h:257 + h], lhsT=sig[:, h * 64:(h + 1) * 64], rhs=negc[:, 0:1], start=True, stop=True),
                         [sig, negc], [psm[d]])
                P.act(lambda e, d=d: e.activation(out=sml[d][0:64, 0:16], in_=psm[d][0:64, 256:272], func=AF.Exp), [psm[d]], [sml[d]])
            for d in range(2):
                sig, ep, em, ex = Wt[d]
                P.pool(lambda e, d=d, ep=ep: e.tensor_tensor(out=outb[d][:, 0:1024], in0=r_sb[:], in1=ep[:], op=ALU.mult), [r_sb, ep], [outb[d]])
                P.dve(lambda e, d=d, ex=ex: e.scalar_tensor_tensor(out=outb[d][:, 1024:2048], in0=kk[:], scalar=-1.0, in1=ex[:], op0=ALU.mult, op1=ALU.mult),
                      [kk, ex], [outb[d]])
            for d in range(2):
                aa = Wt[d][1]
                for n in range(2):
                    bank = G.next()
                    P.pe(lambda e, bank=bank, n=n, d=d: e.matmul(bank[:], lhsT=la_bf[d][:], rhs=a2[d][:, n * 512:(n + 1) * 512], start=True, stop=False),
                         [la_bf[d], a2[d]], [bank])
                    P.pe(lambda e, bank=bank, n=n, d=d: e.matmul(bank[:], lhsT=sel8[:, 2 + d, :], rhs=rows[:, n * 512:(n + 1) * 512], start=False, stop=True),
                         [sel8, rows], [bank])
                    P.act(lambda e, bank=bank, n=n, aa=aa: e.activation(out=aa[:, n * 512:(n + 1) * 512], in_=bank[:], func=AF.Sigmoid), [bank], [aa])
            for d in range(2):
                aa, em, be = Wt[d][1], Wt[d][2], Wt[d][3]
                P.dve(lambda e, aa=aa, be=be: e.tensor_tensor(out=be[:], in0=kk[:], in1=aa[:], op=ALU.mult), [kk, aa], [be])
                P.pool(lambda e, d=d, be=be, em=em: e.tensor_tensor(out=outb[d][:, 2048:3072], in0=be[:], in1=em[:], op=ALU.mult), [be, em], [outb[d]])
            for d in range(2):
                kd, aa, em = Wt[d][0], Wt[d][1], Wt[d][2]
                P.dve(lambda e, kd=kd, aa=aa: e.scalar_tensor_tensor(out=kd[:], in0=aa[:], scalar=-1.0, in1=ka_bc[:], op0=ALU.add, op1=ALU.mult), [aa, ka_bc], [kd])
                P.dve(lambda e, kd=kd: e.scalar_tensor_tensor(out=kd[:], in0=kd[:], scalar=1.0, in1=k_sb[:], op0=ALU.add, op1=ALU.mult), [kd, k_sb], [kd])
                P.pool(lambda e, d=d, kd=kd, em=em: e.tensor_tensor(out=outb[d][:, 3072:4096], in0=kd[:], in1=em[:], op=ALU.mult), [kd, em], [outb[d]])
            for d in range(2):
                kd, bt = Wt[d][0], Wt[d][3]
                P.dve(lambda e, kd=kd, bt=bt: e.tensor_tensor(out=bt[:], in0=kd[:], in1=r_sb[:], op=ALU.mult), [kd, r_sb], [bt])
                P.dve(lambda e, bt=bt: e.tensor_tensor(out=bt[:], in0=bt[:], in1=rk_bc[:], op=ALU.mult), [bt, rk_bc], [bt])
                P.dve(lambda e, d=d, bt=bt: e.tensor_reduce(out=sml[d][:, 16:32], in_=bt[:].rearrange("p (h c) -> p h c", h=16), axis=AX.X, op=ALU.add), [bt], [sml[d]])
            for d in range(2):
                P.dma(self.rw_dir[d][cidx], outb[d][:], "po%d" % d, reads=[outb[d]], q="pool")
                P.dma(self.rw_small[d][cidx], sml[d][:], "ps%d_%d" % (d, small[d].slot), reads=[sml[d]], q="pool")

        for ck in self.chunks_fwd():
            chunk_body(ck)
        ph.finish()

    def rw_scan_phase(self, li, i, d):
        nc, w = self.nc, self.lw[i]
        ph = Phase(nc)
        P = ph.P
        msk = ph.sb("msk", [128, 4, 128], F32)
        cmk = ph.sb("cmk", [128, 7, 128], BF16)
        P.dma(msk[:], w["masks"][d], "c3", writes=[msk])
        P.dma(cmk[:], w["cmask"][d], "c10", writes=[cmk], q="pool")
        G = Rot([ph.ps("G%d" % j, [128, 512], F32) for j in range(6)])
        PTr = Rot([ph.ps("PT%d" % j, [128, 8, 128], BF16) for j in range(2)])
        inb = ph.rot("inb", [128, 4096], BF16, 2)
        v_rot = ph.rot("v_bf", [128, 1024], BF16, 2)
        smr = ph.rot("smr", [128, 32], F32, 2)
        if d == 1:
            smf = ph.rot("smf", [128, 32], F32, 2)
            t1 = ph.sb("t1", [128, 1024], F32)
            sm = ph.rot("sm", [128, 48], F32, 2)
        ARr = ph.rot("AR", [64, 16, 2, 128], BF16, 2)
        BTr = ph.rot("BT", [64, 16, 128], BF16, 2)
        KTr = ph.rot("KT", [64, 16, 128], BF16, 2)
        names = ("N", "NT", "NA", "NAT", "NB", "NBT", "O32", "O32T", "O64", "O64T", "O128", "T", "TT", "Aak", "Abr", "Akr")
        NU = 4
        U_ = [{n: ph.sb("%s_u%d" % (n, us), [128, 4, 128], BF16) for n in names} for us in range(NU)]
        Xb = [ph.sb("Xb%d" % us, [128, 4, 64], BF16) for us in range(NU)]
        Ub = [ph.sb("Ub%d" % us, [128, 4, 64], BF16) for us in range(NU)]
        S = [ph.sb("S%d" % u, [64, 4, 64], F32) for u in range(4)]
        Sb = [ph.sb("Sb%d" % u, [64, 4, 64], BF16) for u in range(4)]
        for u in range(4):
            P.dve(lambda e, u=u: e.memset(S[u][:], 0.0), [], [S[u]])
            P.pool(lambda e, u=u: e.memset(Sb[u][:], 0.0), [], [Sb[u]])
        ysb = ph.rot("ysb", [128, 1040], F32, 2)
        if d == 1:
            yf = ph.rot("yf", [128, 1040], F32, 2)
        ident = self.ident
        order = self.chunks_fwd() if d == 0 else self.chunks_bwd()

        def chunk_body(ck):
            cidx = self.row0(ck) // CH
            in_, v_bf, s_ = inb.next(), v_rot.next(), smr.next()
            P.dma(in_[:], self.rw_dir[d][cidx], "li%d" % inb.slot, writes=[in_])
            P.dma(v_bf[:], self.rw_scr_v[cidx], "lv%d" % v_rot.slot, writes=[v_bf])
            P.dma(s_[:], self.rw_small[d][cidx], "ls%d" % smr.slot, writes=[s_])
            WC = Tile(s_.t[0:64, 0:16], "WCview")
            WC.b = s_.b
            bh_bf = Tile(in_.t[:, 2048:3072], "bhv")
            bh_bf.b = in_.b
            kh_bf = Tile(in_.t[:, 3072:4096], "khv")
            kh_bf.b = in_.b
            AR, BT, KT = ARr.next(), BTr.next(), KTr.next()
            cnt = 0
            for g in range(2):
                for c0, dtile, dst in ((1024, AR, AR[:, g * 8:(g + 1) * 8, 0, :]), (0, AR, AR[:, g * 8:(g + 1) * 8, 1, :]),
                                       (2048, BT, BT[:, g * 8:(g + 1) * 8, :]), (3072, KT, KT[:, g * 8:(g + 1) * 8, :])):
                    PT = PTr.next()
                    for j in range(8):
                        h = g * 8 + j
                        P.pe(lambda e, PT=PT, c0=c0, h=h, j=j: e.transpose(out=PT[0:64, j, :], in_=in_[:, c0 + h * 64:c0 + (h + 1) * 64], identity=ident[:]),
                             [in_, ident], [PT])
                    if cnt % 2 == 0:
                        P.act(lambda e, PT=PT, dst=dst: e.activation(out=dst, in_=PT[0:64, :, :], func=AF.Copy), [PT], [dtile])
                    else:
                        P.dve(lambda e, PT=PT, dst=dst: e.tensor_copy(out=dst, in_=PT[0:64, :, :]), [PT], [dtile])
                    cnt += 1
            y_ = ysb.next()
            if d == 1:
                yf_ = yf.next()
                P.dma(yf_[:, 0:1024], self.scr_o[cidx][:, 0:1024], "lyf%d" % yf.slot, writes=[yf_])
                sf_ = smf.next()
                P.dma(sf_[:], self.rw_small[0][cidx], "lsf%d" % smf.slot, writes=[sf_])
            for g in range(1):
                units = [(us, us) for us in range(4)]
                for u, us in units:
                    M = U_[us]
                    h0 = u * 4
                    for pr in range(2):
                        bank = G.next()
                        for j in range(2):
                            h = h0 + pr * 2 + j
                            P.pe(lambda e, bank=bank, j=j, h=h: e.matmul(bank[:, j * 256:(j + 1) * 256], lhsT=BT[:, h, :],
                                                                         rhs=AR[:, h, :, :].rearrange("p a t -> p (a t)"), start=True, stop=True), [BT, AR], [bank])
                        bv = bank[:].rearrange("p (j a t) -> p j a t", j=2, a=2)
                        for dn, mi in (("NA", 0),):
                            P.dve(lambda e, bv=bv, M=M, pr=pr, dn=dn, mi=mi: e.tensor_tensor(out=M[dn][:, pr * 2:pr * 2 + 2, :], in0=bv[:, :, 0, :],
                                                                                            in1=cmk[:, mi, :].unsqueeze(1).to_broadcast([128, 2, 128]), op=ALU.mult),
                                  [bank, cmk], [M[dn]])
                        P.dve(lambda e, bv=bv, M=M, pr=pr: e.tensor_tensor(out=M["Abr"][:, pr * 2:pr * 2 + 2, :], in0=bv[:, :, 1, :],
                                                                          in1=msk[:, 1, :].unsqueeze(1).to_broadcast([128, 2, 128]), op=ALU.mult), [bank, msk], [M["Abr"]])
                        bank = G.next()
                        for j in range(2):
                            h = h0 + pr * 2 + j
                            P.pe(lambda e, bank=bank, j=j, h=h: e.matmul(bank[:, j * 256:(j + 1) * 256], lhsT=KT[:, h, :],
                                                                         rhs=AR[:, h, :, :].rearrange("p a t -> p (a t)"), start=True, stop=True), [KT, AR], [bank])
                        bv = bank[:].rearrange("p (j a t) -> p j a t", j=2, a=2)
                        P.dve(lambda e, bv=bv, M=M, pr=pr: e.tensor_tensor(out=M["Aak"][:, pr * 2:pr * 2 + 2, :], in0=bv[:, :, 0, :],
                                                                          in1=msk[:, 0, :].unsqueeze(1).to_broadcast([128, 2, 128]), op=ALU.mult), [bank, msk], [M["Aak"]])
                        P.dve(lambda e, bv=bv, M=M, pr=pr: e.tensor_tensor(out=M["Akr"][:, pr * 2:pr * 2 + 2, :], in0=bv[:, :, 1, :],
                                                                          in1=msk[:, 1, :].unsqueeze(1).to_broadcast([128, 2, 128]), op=ALU.mult), [bank, msk], [M["Akr"]])
                    bank = G.next()
                    for j in range(4):
                        h = h0 + j
                        P.pe(lambda e, bank=bank, j=j, h=h: e.matmul(bank[:, j * 128:(j + 1) * 128], lhsT=AR[:, h, 0, :], rhs=BT[:, h, :], start=True, stop=True),
                             [AR, BT], [bank])
                    for dn, mi in (("NAT", 3), ("O32T", 4), ("O64T", 5), ("O128", 6)):
                        P.dve(lambda e, bank=bank, M=M, dn=dn, mi=mi: e.tensor_tensor(out=M[dn][:], in0=bank[:].rearrange("p (j t) -> p j t", j=4),
                                                                                      in1=cmk[:, mi, :].unsqueeze(1).to_broadcast([128, 4, 128]), op=ALU.mult),
                              [bank, cmk], [M[dn]])
                    P.pool(lambda e, M=M: e.tensor_tensor(out=M["T"][:], in0=M["NA"][:], in1=ident[:].unsqueeze(1).to_broadcast([128, 4, 128]), op=ALU.add),
                           [M["NA"], ident], [M["T"]])

                def mm4(bank, M, lt, rt):
                    for j in range(4):
                        P.pe(lambda e, bank=bank, j=j, M=M, lt=lt, rt=rt: e.matmul(bank[:, j * 128:(j + 1) * 128], lhsT=M[lt][:, j, :], rhs=M[rt][:, j, :],
                                                                                  start=True, stop=True), [M[lt], M[rt]], [bank])

                def cp4(bank, M, dn):
                    P.act(lambda e, bank=bank, M=M, dn=dn: e.activation(out=M[dn][:], in_=bank[:].rearrange("p (j t) -> p j t", j=4), func=AF.Copy),
                          [bank], [M[dn]])

                def add4(bank, M, dn):
                    P.dve(lambda e, bank=bank, M=M, dn=dn: e.tensor_tensor(out=M[dn][:], in0=bank[:].rearrange("p (j t) -> p j t", j=4), in1=M[dn][:], op=ALU.add),
                          [bank, M[dn]], [M[dn]])
                def tr4(M):
                    PT = PTr.next()
                    for j in range(4):
                        P.pe(lambda e, PT=PT, j=j, M=M: e.transpose(out=PT[:, j, :], in_=M["T"][:, j, :], identity=ident[:]), [M["T"], ident], [PT])
                    P.act(lambda e, PT=PT, M=M: e.activation(out=M["TT"][:], in_=PT[:, 0:4, :], func=AF.Copy), [PT], [M["TT"]])
                cur = {us: ("NA", "NAT") for _, us in units}
                for lvl in range(3):
                    nxt = {}
                    for u, us in units:
                        M = U_[us]
                        nk, nkt = cur[us]
                        nb, nbt = ("NB", "NBT") if nk == "NA" else ("NA", "NAT")
                        if lvl < 2:
                            b1 = G.next(); mm4(b1, M, nkt, nk); cp4(b1, M, nb)
                        b2 = G.next(); mm4(b2, M, nk, nkt); cp4(b2, M, nbt)
                        nxt[us] = (nb, nbt)
                    for u, us in units:
                        M = U_[us]
                        nb, nbt = nxt[us]
                        b3 = G.next(); mm4(b3, M, nbt, "T"); add4(b3, M, "T")
                    cur = nxt
                for on in ("O32T", "O64T", "O128"):
                    for u, us in units:
                        tr4(U_[us])
                    for u, us in units:
                        M = U_[us]
                        b1 = G.next(); mm4(b1, M, on, "T"); cp4(b1, M, "N")
                    for u, us in units:
                        M = U_[us]
                        b3 = G.next(); mm4(b3, M, "TT", "N"); add4(b3, M, "T")
                for u, us in units:
                    M = U_[us]
                    h0 = u * 4
                    bank = G.next()
                    for j in range(4):
                        h = h0 + j
                        P.pe(lambda e, bank=bank, j=j, h=h, u=u: e.matmul(bank[:, j * 64:(j + 1) * 64], lhsT=AR[:, h, 0, :], rhs=Sb[u][:, j, :], start=True, stop=False),
                             [AR, Sb[u]], [bank])
                        P.pe(lambda e, bank=bank, j=j, h=h, M=M: e.matmul(bank[:, j * 64:(j + 1) * 64], lhsT=M["Aak"][:, j, :], rhs=v_bf[:, h * 64:(h + 1) * 64],
                                                                          start=False, stop=True), [M["Aak"], v_bf], [bank])
                    P.act(lambda e, bank=bank, us=us: e.activation(out=Xb[us][:], in_=bank[:, 0:256].rearrange("p (j v) -> p j v", j=4), func=AF.Copy),
                          [bank], [Xb[us]])
                for u, us in units:
                    M = U_[us]
                    h0 = u * 4
                    bank = G.next()
                    for j in range(4):
                        P.pe(lambda e, bank=bank, j=j, M=M, us=us: e.matmul(bank[:, j * 64:(j + 1) * 64], lhsT=M["T"][:, j, :], rhs=Xb[us][:, j, :], start=True, stop=True),
                             [M["T"], Xb[us]], [bank])
                    P.dve(lambda e, bank=bank, us=us: e.tensor_copy(out=Ub[us][:], in_=bank[:, 0:256].rearrange("p (j v) -> p j v", j=4)), [bank], [Ub[us]])
                for u, us in units:
                    M = U_[us]
                    h0 = u * 4
                    bank = G.next()
                    for j in range(4):
                        h = h0 + j
                        P.pe(lambda e, bank=bank, j=j, h=h, u=u: e.matmul(bank[:, j * 64:(j + 1) * 64], lhsT=AR[:, h, 1, :], rhs=Sb[u][:, j, :], start=True, stop=False),
                             [AR, Sb[u]], [bank])
                        P.pe(lambda e, bank=bank, j=j, M=M, us=us: e.matmul(bank[:, j * 64:(j + 1) * 64], lhsT=M["Abr"][:, j, :], rhs=Ub[us][:, j, :], start=False, stop=False),
                             [M["Abr"], Ub[us]], [bank])
                        P.pe(lambda e, bank=bank, j=j, h=h, M=M: e.matmul(bank[:, j * 64:(j + 1) * 64], lhsT=M["Akr"][:, j, :], rhs=v_bf[:, h * 64:(h + 1) * 64],
                                                                          start=False, stop=True), [M["Akr"], v_bf], [bank])
                    if d == 0:
                        P.act(lambda e, bank=bank, h0=h0, y_=y_: e.activation(out=y_[:, h0 * 64:(h0 + 4) * 64], in_=bank[:, 0:256], func=AF.Copy), [bank], [y_])
                    else:
                        P.dve(lambda e, bank=bank, h0=h0, y_=y_, yf_=yf_: e.tensor_tensor(out=y_[:, h0 * 64:(h0 + 4) * 64], in0=bank[:, 0:256],
                                                                                         in1=yf_[:, h0 * 64:(h0 + 4) * 64], op=ALU.add), [bank, yf_], [y_])
                for u, us in units:
                    M = U_[us]
                    h0 = u * 4
                    bank = G.next()
                    for j in range(4):
                        h = h0 + j
                        P.pe(lambda e, bank=bank, j=j, h=h, us=us: e.matmul(bank[0:64, j * 64:(j + 1) * 64], lhsT=bh_bf[:, h * 64:(h + 1) * 64], rhs=Ub[us][:, j, :],
                                                                            start=True, stop=False), [bh_bf, Ub[us]], [bank])
                        P.pe(lambda e, bank=bank, j=j, h=h: e.matmul(bank[0:64, j * 64:(j + 1) * 64], lhsT=kh_bf[:, h * 64:(h + 1) * 64], rhs=v_bf[:, h * 64:(h + 1) * 64],
                                                                     start=False, stop=True), [kh_bf, v_bf], [bank])
                    P.dve(lambda e, bank=bank, u=u: e.tensor_tensor(out=S[u][:], in0=bank[0:64, 0:256].rearrange("p (j v) -> p j v", j=4), in1=S[u][:], op=ALU.add),
                          [bank, S[u]], [S[u]])
                    P.dve(lambda e, u=u, h0=h0: e.tensor_tensor(out=S[u][:], in0=S[u][:], in1=WC[:, h0:h0 + 4].unsqueeze(2).to_broadcast([64, 4, 64]), op=ALU.mult),
                          [S[u], WC], [S[u]])
                    P.pool(lambda e, u=u: e.tensor_copy(out=Sb[u][:], in_=S[u][:]), [S[u]], [Sb[u]])
            if d == 0:
                P.dma(self.scr_o[cidx][:, 0:1024], y_[:, 0:1024], "sy%d" % ysb.slot, reads=[y_], q="pool")
            else:
                need_out = True
                yv = y_[:, 0:1024].rearrange("p (h c) -> p h c", h=16)
                t1v = t1[:].rearrange("p (h c) -> p h c", h=16)
                st_ = sm.next()
                P.dve(lambda e, yv=yv: e.tensor_reduce(out=st_[:, 0:16], in_=yv, axis=AX.X, op=ALU.add), [y_], [st_])
                P.dve(lambda e: e.tensor_scalar(out=st_[:, 0:16], in0=st_[:, 0:16], scalar1=-1.0 / 64, scalar2=None, op0=ALU.mult), [st_], [st_])
                P.dve(lambda e, yv=yv: e.tensor_tensor(out=yv, in0=yv, in1=st_[:, 0:16].unsqueeze(2).to_broadcast([128, 16, 64]), op=ALU.add), [y_, st_], [y_])
                P.pool(lambda e, y_=y_: e.tensor_tensor(out=t1[:], in0=y_[:, 0:1024], in1=y_[:, 0:1024], op=ALU.mult), [y_], [t1])
                P.dve(lambda e: e.tensor_reduce(out=st_[:, 16:32], in_=t1v, axis=AX.X, op=ALU.add), [t1], [st_])
                P.act(lambda e: e.activation(out=st_[:, 16:32], in_=st_[:, 16:32], func=AF.Sqrt, scale=1.0 / 64, bias=64e-5), [st_], [st_])
                P.dve(lambda e: e.reciprocal(out=st_[:, 32:48], in_=st_[:, 16:32]), [st_], [st_])
                P.dve(lambda e, yv=yv: e.tensor_tensor(out=yv, in0=yv, in1=st_[:, 32:48].unsqueeze(2).to_broadcast([128, 16, 64]), op=ALU.mult), [y_, st_], [y_])
                P.dve(lambda e: e.tensor_tensor(out=st_[:, 0:16], in0=s_[:, 16:32], in1=sf_[:, 16:32], op=ALU.add), [s_, sf_], [st_])
                P.dve(lambda e: e.tensor_tensor(out=t1v, in0=v_bf[:].rearrange("p (h c) -> p h c", h=16),
                                                in1=st_[:, 0:16].unsqueeze(2).to_broadcast([128, 16, 64]), op=ALU.mult), [v_bf, st_, t1], [t1])
                P.dma(self.scr_o[cidx][:, 0:1024], y_[:, 0:1024], "sy%d" % ysb.slot, reads=[y_], q="pool")
                P.dma(self.scr_o[cidx][:, 1024:2048], t1[:], "sbv", reads=[t1], q="pool")
        for ck in order:
            chunk_body(ck)
        ph.finish()

    def rw_out_phase(self, li, i, last):
        nc, w = self.nc, self.lw[i]
        ph = Phase(nc)
        P = ph.P
        Wg = self.load_w(ph, ph.sb("Wg", [128, 8, 1024], BF16), w["rkvg"][3], 0, 1024, 8)
        Wo = self.load_w(ph, ph.sb("Wo", [128, 8, 1024], BF16), w["w_out"], 0, 1024, 8)
        mu = ph.sb("mu", [128, 6, 8], F32)
        P.dma(mu[:], w["mu"][:, :, :], "c2", writes=[mu])
        R = self.front_bufs(ph, n_xn=1, n_xt=4, n_hT=4)
        R.update(self.tail_bufs(ph, last=last))
        get_h = self.rw_hcache(ph, R, li)
        mm = Rot([ph.ps("mm%d" % j, [128, 512], F32) for j in range(2)])
        sh = ph.sb("sh", [128, 8, 128], BF16)
        xx = ph.sb("xx", [128, 8, 128], F32)
        tmpr = ph.rot("mtmp", [128, 8, 128], F32, 2)
        mixr = ph.rot("mix", [128, 8, 128], BF16, 2)
        op = ph.rot("opre", [128, 2048], F32, 2)
        lg_bc = ph.sb("lg_bc", [128, 1024], F32)
        lb_bc = ph.sb("lb_bc", [128, 1024], F32)
        P.dma(lg_bc[:], w["bc"][:, 3, :], "c8", writes=[lg_bc])
        P.dma(lb_bc[:], w["bc"][:, 4, :], "c9", writes=[lb_bc])
        sz = ph.rot("sz", [128, 1024], F32, 2)
        gated = ph.rot("gated", [128, 1024], BF16, 2)
        order = self.chunks_fwd()
        if last:
            order = [ck for ck in order if ck[0] == "x"]
        for ck in order:
            cidx = self.row0(ck) // CH
            pk, nk = self.rw_neighbors(ck)
            hc, xt = get_h(ck)
            hp = get_h(pk)[0] if pk else None
            hn = get_h(nk)[0] if nk else None
            self.rw_shift(P, sh, hc, hp, hn, ck[0])
            P.dve(lambda e, hc=hc: e.tensor_tensor(out=xx[:], in0=sh[:], in1=hc[:], op=ALU.subtract), [sh, hc], [xx])
            mix5 = self.rw_mix(P, mixr, tmpr, xx, hc, mu, 5)
            o_ = op.next()
            P.dma(o_[:], self.scr_o[cidx][:, 0:2048], "lo%d" % op.slot, writes=[o_])
            P.dve(lambda e, o_=o_: e.tensor_tensor(out=o_[:, 0:1024], in0=o_[:, 0:1024], in1=lg_bc[:], op=ALU.mult), [o_, lg_bc], [o_])
            P.dve(lambda e, o_=o_: e.tensor_tensor(out=o_[:, 1024:2048], in0=o_[:, 1024:2048], in1=lb_bc[:], op=ALU.add), [o_, lb_bc], [o_])
            P.dve(lambda e, o_=o_: e.tensor_tensor(out=o_[:, 0:1024], in0=o_[:, 0:1024], in1=o_[:, 1024:2048], op=ALU.add), [o_], [o_])
            sz_ = sz.next()
            for n in range(2):
                bank = mm.next()
                for k in range(8):
                    P.pe(lambda e, bank=bank, n=n, k=k, mix5=mix5: e.matmul(bank[:], lhsT=mix5[:, k, :], rhs=Wg[k][:, n * 512:(n + 1) * 512],
                                                                            start=(k == 0), stop=(k == 7)), [mix5, Wg[k]], [bank])
                P.act(lambda e, bank=bank, n=n, sz_=sz_: e.activation(out=sz_[:, n * 512:(n + 1) * 512], in_=bank[:], func=AF.Silu), [bank], [sz_])
            g_ = gated.next()
            P.dve(lambda e, o_=o_, sz_=sz_, g_=g_: e.tensor_tensor(out=g_[:], in0=o_[:, 0:1024], in1=sz_[:], op=ALU.mult), [o_, sz_], [g_])
            self.tail(ph, R, g_, Wo, li, ck, xt, last, KT=8)
        ph.finish()


def _col(v, k):
    return np.ascontiguousarray(np.asarray(v, np.float32).reshape(k, 128).T)


def _consts(L, CL):
    f32 = np.float32
    t = np.arange(L)
    row, col = (t // 64).astype(f32), (t % 64).astype(f32)
    freqs = (f32(10000.0) ** (-(np.arange(64, dtype=f32)) / f32(64))).astype(f32)
    ang = np.concatenate([row[:, None] * freqs, col[:, None] * freqs], axis=-1).astype(f32)
    rope = np.zeros((CL + L, 2, 128), f32)
    rope[:CL, 0, :] = 1.0
    rope[CL:, 0, :] = np.cos(ang)
    rope[CL:, 1, :] = np.sin(ang)
    jj = np.arange(128)[:, None].astype(f32)
    ii = np.arange(128)[None, :].astype(f32)
    dmat = np.stack([np.maximum(ii - jj, 0), np.maximum(jj - ii, 0), (ii >= jj).astype(f32), (jj >= ii).astype(f32),
                     np.broadcast_to(ii + 1, (128, 128)), np.broadcast_to(128 - ii, (128, 128))], axis=1).astype(f32)
    jcol = np.stack([127 - np.arange(128), np.arange(128)], axis=1).astype(f32)
    sel = np.zeros((2, 2, 128), f32)
    sel[0, 0, :] = 1.0
    sel[1, 1, :] = 1.0
    return {"ident": np.eye(128, dtype=f32), "rope": rope, "dmat": np.ascontiguousarray(dmat), "jcol": jcol, "sel2": sel}


def host_inputs(inp, b, layers, L, CL=256, x_rows=None):
    f32 = np.float32
    m = dict(_consts(L, CL))
    xr = inp["x"][b] if x_rows is None else x_rows
    m["x"] = np.ascontiguousarray(xr[:L], f32)
    m["ctx"] = np.ascontiguousarray(inp["ctx"][b][:CL], f32)
    m["cc"] = np.ascontiguousarray(np.stack([_col(inp["c"][b], 8), _col(inp["c_ctx"], 8)], axis=-1))
    m["final_g_bc"] = np.ascontiguousarray(np.broadcast_to(np.asarray(inp["final_g"], f32), (128, D)))
    for i in layers:
        j = i // 3
        m["ada_w%d" % i] = np.ascontiguousarray(inp["ada_w"][i], f32)
        m["ada_bcol%d" % i] = _col(inp["ada_b"][i], 24)
        m["ada_brow%d" % i] = np.ascontiguousarray(np.broadcast_to(np.asarray(inp["ada_b"][i][2 * D:], f32), (2, D)))
        m["ng_col%d" % i] = _col(inp["norm_g"][i], 8)
        k = KINDS[i]
        if k == 0:
            m["ret_w_in%d" % i] = np.ascontiguousarray(inp["ret_w_in"][j], f32)
            m["ret_w_out%d" % i] = np.ascontiguousarray(inp["ret_w_out"][j], f32)
            dec = np.concatenate([inp["ret_decay"][j][0], inp["ret_decay"][j][1]]).astype(f32)
            m["ret_decay%d" % i] = np.ascontiguousarray(np.broadcast_to(dec, (128, 8)))
        elif k == 1:
            m["gm_w_in%d" % i] = np.ascontiguousarray(inp["gm_w_in"][j], f32)
            m["gm_w_out%d" % i] = np.ascontiguousarray(inp["gm_w_out"][j], f32)
            m["gm_vg%d" % i] = np.ascontiguousarray(np.broadcast_to(np.asarray(inp["gm_vnorm_g"][j], f32), (128, 2048)))
            m["gm_wsT%d" % i] = np.ascontiguousarray(np.transpose(np.asarray(inp["gm_w_s"][j], f32), (2, 0, 1)))
            m["gm_bsT%d" % i] = np.ascontiguousarray(np.asarray(inp["gm_b_s"][j], f32).T)
        else:
            _rw_host(m, inp, i, j)
    return m


def _rw_host(m, inp, i, j):
    f32 = np.float32
    g = lambda k: np.asarray(inp[k][j], f32)
    mu = g("rw_mu")
    m["rw_mu%d" % i] = np.ascontiguousarray(np.stack([_col(mu[p], 8) for p in range(6)], axis=1))
    m["rw_rkvg%d" % i] = np.ascontiguousarray(g("rw_w_rkvg"))
    m["rw_w1%d" % i] = np.ascontiguousarray(g("rw_w1"))
    m["rw_a1%d" % i] = np.ascontiguousarray(g("rw_a1"))
    m["rw_w2%d" % i] = np.ascontiguousarray(g("rw_w2"))
    m["rw_a2%d" % i] = np.ascontiguousarray(g("rw_a2"))
    rows = np.zeros((8, D), f32)
    rows[0:2] = g("rw_w0")
    rows[2:4] = g("rw_a0")
    m["rw_rows%d" % i] = rows
    bc = np.stack([g("rw_k_k"), g("rw_k_a"), g("rw_r_k").reshape(-1), g("rw_lnx_g"), g("rw_lnx_b")], axis=0)
    m["rw_bc%d" % i] = np.ascontiguousarray(np.broadcast_to(bc[None], (128, 5, D)))
    m["rw_wout%d" % i] = np.ascontiguousarray(g("rw_w_out"))
    s_ = np.arange(128)[:, None]
    t_ = np.arange(128)[None, :]
    c0 = f32(-0.6065306597126334)
    fw = np.stack([(s_ < t_), (s_ <= t_), (t_ < s_), (s_ <= t_) * c0], axis=1).astype(f32)
    bw = np.stack([(s_ > t_), (s_ >= t_), (t_ > s_), (s_ >= t_) * c0], axis=1).astype(f32)
    m["rw_masks%d" % i] = np.ascontiguousarray(np.stack([fw, bw], axis=0))
    sel = np.zeros((8, 8, 128), f32)
    for r in range(8):
        sel[r, r, :] = 1.0
    m["rw_sel8%d" % i] = sel
    m["rw_negc%d" % i] = np.full((128, 1), c0, f32)
    blk = lambda n: (s_ // n) == (t_ // n)
    bm = np.stack([blk(16)] + [blk(n) & ~blk(n // 2) for n in (32, 64, 128)], axis=1).astype(f32)
    m["rw_bmask%d" % i] = np.ascontiguousarray(bm)
    cms = []
    for dd in (fw, bw):
        st, stT = dd[:, 0, :], dd[:, 2, :]
        cms.append(np.stack([st * bm[:, 0], st * bm[:, 1], st * bm[:, 2], stT * bm[:, 0], stT * bm[:, 1], stT * bm[:, 2], stT * bm[:, 3]], axis=1))
    m["rw_cmask%d" % i] = np.ascontiguousarray(np.stack(cms, axis=0).astype(f32))


_MODEL_CACHE = {}


def kernel(**inputs):
    inp = {k: np.asarray(v) for k, v in inputs.items()}
    B, L, _ = inp["x"].shape
    layers = (0, 1, 2, 3)
    key = (L, layers)
    if key not in _MODEL_CACHE:
        _MODEL_CACHE[key] = Model(L, 256, layers)
    model = _MODEL_CACHE[key]
    maps = []
    for core in range(NCORES):
        b = core % B
        hm = host_inputs(inp, b, layers, L)
        maps.append({k: hm[k] for k in model.inputs})
    res = run_bass_kernel_spmd(model.nc, maps, core_ids=list(range(NCORES)))
    out = np.stack([np.asarray(res.results[b]["out"], np.float32) for b in range(B)], axis=0)
    return out
```
